# Optimizing a Trainium2 kernel written in Bass

```python
import math
import jax, jax.numpy as jnp
from jax import lax
import numpy as np

D_MODEL = 2048
BATCH = 4
SEQ = 4096
DEPTH = 4

N_MEM = 256
RMS_EPS = 1e-5
CONV_WIDTH = 3
A_WIDTH = D_MODEL // 2
S5_WIDTH = D_MODEL // 2
S5_GROUP = 16
S5_GROUPS = S5_WIDTH // S5_GROUP
S5_STATE = 64
EVEN_IN = 3 * A_WIDTH + S5_WIDTH
HEAD_DIM = 64
N_Q_HEADS = D_MODEL // HEAD_DIM
Q_PER_KV = 8
N_KV_HEADS = N_Q_HEADS // Q_PER_KV
WINDOW = 128
BLOCK = 128
ODD_IN = (N_Q_HEADS + 2 * N_KV_HEADS) * HEAD_DIM
N_BUCKETS = 32
MAX_DISTANCE = 128
X_HEADS = 4
X_HEAD_DIM = D_MODEL // X_HEADS
D_FF = 5632
NEG_INF = -1e30
N_EVEN = (DEPTH + 1) // 2
N_ODD = DEPTH // 2

kernel_name = "hybrid_shortconv_s5_swa_sink_trunk"


def rms_norm(x, g):
    xf = x.astype(jnp.float32)
    y = xf * lax.rsqrt(jnp.mean(xf * xf, axis=-1, keepdims=True) + RMS_EPS)
    return (y * g.astype(jnp.float32)).astype(x.dtype)


def causal_dwconv(u, w):
    L = u.shape[1]
    up = jnp.pad(u, ((0, 0), (CONV_WIDTH - 1, 0), (0, 0)))
    y = up[:, 0:L] * w[0]
    for k in range(1, CONV_WIDTH):
        y = y + up[:, k:k + L] * w[k]
    return y


def t5_causal_bucket(rel):
    max_exact = N_BUCKETS // 2
    n = jnp.maximum(rel, 0)
    nf = jnp.maximum(n, max_exact).astype(jnp.float32)
    large = max_exact + (jnp.log(nf / max_exact) / math.log(MAX_DISTANCE / max_exact)
                         * (N_BUCKETS - max_exact)).astype(jnp.int32)
    large = jnp.minimum(large, N_BUCKETS - 1)
    return jnp.where(n < max_exact, n, large)


def s5_branch(u, a_re, a_im, log_dt, b_re, b_im, c_re, c_im, d, glu_w):
    f32 = jnp.float32
    Bsz, L, _ = u.shape
    ug = u.astype(f32).reshape(Bsz, L, S5_GROUPS, S5_GROUP)
    lam = lax.complex(a_re.astype(f32), a_im.astype(f32))
    dt = jnp.exp(log_dt.astype(f32))[:, None]
    a_bar = jnp.exp(lam * dt)
    b = lax.complex(b_re.astype(f32), b_im.astype(f32))
    b_bar = ((a_bar - 1.0) / lam)[..., None] * b
    bu = jnp.einsum('gph,blgh->blgp', b_bar, ug.astype(jnp.complex64))
    a_elems = jnp.broadcast_to(a_bar, (1, L) + a_bar.shape)

    def combine(e1, e2):
        a1, s1 = e1
        a2, s2 = e2
        return a1 * a2, a2 * s1 + s2

    _, states = lax.associative_scan(combine, (a_elems, bu), axis=1)
    c = lax.complex(c_re.astype(f32), c_im.astype(f32))
    y = jnp.real(jnp.einsum('ghp,blgp->blgh', c, states)) \
        + d.astype(f32).reshape(S5_GROUPS, S5_GROUP) * ug
    yg = jax.nn.gelu(y)
    gate = jnp.einsum('blgh,gho->blgo', yg, glu_w.astype(f32))
    out = yg * jax.nn.sigmoid(gate)
    return out.reshape(Bsz, L, S5_WIDTH).astype(u.dtype)


def conv_ssm_mixer(h, w_in, conv_w, a_re, a_im, log_dt, b_re, b_im, c_re, c_im, d, glu_w, w_out):
    z = h @ w_in
    gate_b, gate_c, xa, u = jnp.split(z, [A_WIDTH, 2 * A_WIDTH, 3 * A_WIDTH], axis=-1)
    ya = gate_b * causal_dwconv(gate_c * xa, conv_w)
    ys = s5_branch(u, a_re, a_im, log_dt, b_re, b_im, c_re, c_im, d, glu_w)
    return jnp.concatenate([ya, ys], axis=-1) @ w_out


def swa_sink_attention(h, w_qkv, b_qkv, sinks, rel_bias, w_out):
    Bsz, L, _ = h.shape
    nblk = L // BLOCK
    z = h @ w_qkv + b_qkv
    q, k, v = jnp.split(z, [N_Q_HEADS * HEAD_DIM, (N_Q_HEADS + N_KV_HEADS) * HEAD_DIM], axis=-1)
    q = q.reshape(Bsz, nblk, BLOCK, N_KV_HEADS, Q_PER_KV, HEAD_DIM)
    k = k.reshape(Bsz, nblk, BLOCK, N_KV_HEADS, HEAD_DIM)
    v = v.reshape(Bsz, nblk, BLOCK, N_KV_HEADS, HEAD_DIM)

    def with_prev(t):
        prev = jnp.pad(t, ((0, 0), (1, 0), (0, 0), (0, 0), (0, 0)))[:, :-1]
        return jnp.concatenate([prev, t], axis=2)

    kb, vb = with_prev(k), with_prev(v)
    s = jnp.einsum('bnqkgd,bnskd->bnkgqs', q, kb).astype(jnp.float32) * (HEAD_DIM ** -0.5)
    qi = jnp.arange(BLOCK, dtype=jnp.int32)[:, None]
    kj = jnp.arange(2 * BLOCK, dtype=jnp.int32)[None, :]
    rel = qi + BLOCK - kj
    bias = rel_bias.astype(jnp.float32)[t5_causal_bucket(rel)]
    bias = jnp.transpose(bias, (2, 0, 1)).reshape(N_KV_HEADS, Q_PER_KV, BLOCK, 2 * BLOCK)
    blk = jnp.arange(nblk, dtype=jnp.int32)[:, None, None]
    valid = (rel >= 0)[None] & (rel < WINDOW)[None] & (blk * BLOCK + kj[None] - BLOCK >= 0)
    s = jnp.where(valid[None, :, None, None], s + bias[None, None], NEG_INF)
    sink = jnp.broadcast_to(sinks.astype(jnp.float32).reshape(N_KV_HEADS, Q_PER_KV)[None, None, :, :, None, None],
                            s.shape[:-1] + (1,))
    p = jax.nn.softmax(jnp.concatenate([s, sink], axis=-1), axis=-1)[..., :-1]
    o = jnp.einsum('bnkgqs,bnskd->bnqkgd', p.astype(vb.dtype), vb)
    return o.reshape(Bsz, L, N_Q_HEADS * HEAD_DIM) @ w_out


def memory_cross_attention(h, mem_n, w_q, w_kv, w_o):
    Bsz, L, _ = h.shape
    q = (h @ w_q).reshape(Bsz, L, X_HEADS, X_HEAD_DIM)
    k, v = jnp.split(mem_n @ w_kv, 2, axis=-1)
    k = k.reshape(Bsz, -1, X_HEADS, X_HEAD_DIM)
    v = v.reshape(Bsz, -1, X_HEADS, X_HEAD_DIM)
    s = jnp.einsum('blhd,bmhd->bhlm', q, k).astype(jnp.float32) * (X_HEAD_DIM ** -0.5)
    p = jax.nn.softmax(s, axis=-1).astype(v.dtype)
    o = jnp.einsum('bhlm,bmhd->blhd', p, v).reshape(Bsz, L, D_MODEL)
    return o @ w_o


def conv_gated_mlp(h, w_gate, w_up, conv_w, conv_b, w_down):
    g = causal_dwconv(h @ w_gate, conv_w) + conv_b
    return (jax.nn.silu(g) * (h @ w_up)) @ w_down


def setup_inputs(seed: int = 0) -> dict:
    key = jax.random.key(seed)
    ks = jax.random.split(key, 40)
    f32 = jnp.float32
    D = D_MODEL

    def nrm(k, shape, fan_in):
        return jax.random.normal(k, shape, f32) * (fan_in ** -0.5)

    def gain(k, shape):
        return 1.0 + 0.02 * jax.random.normal(k, shape, f32)

    a_re = -0.5 * jnp.exp(0.05 * jax.random.normal(ks[8], (N_EVEN, S5_GROUPS, S5_STATE), f32))
    a_im = math.pi * jnp.arange(S5_STATE, dtype=f32) + 0.01 * jax.random.normal(ks[9], (N_EVEN, S5_GROUPS, S5_STATE), f32)
    log_dt = jax.random.uniform(ks[10], (N_EVEN, S5_GROUPS), f32, math.log(1e-3), math.log(1e-1))
    return {
        "x": jax.random.normal(ks[0], (BATCH, SEQ, D), f32),
        "mem": jax.random.normal(ks[1], (BATCH, N_MEM, D), f32),
        "norm_mix": gain(ks[2], (DEPTH, D)),
        "norm_xattn": gain(ks[3], (DEPTH, D)),
        "norm_ffn": gain(ks[4], (DEPTH, D)),
        "norm_final": gain(ks[5], (D,)),
        "norm_mem": gain(ks[6], (D,)),
        "rel_bias": 0.5 * jax.random.normal(ks[7], (N_BUCKETS, N_Q_HEADS), f32),
        "ev_w_in": nrm(ks[11], (N_EVEN, D, EVEN_IN), D),
        "ev_conv_w": nrm(ks[12], (N_EVEN, CONV_WIDTH, A_WIDTH), CONV_WIDTH),
        "s5_a_re": a_re,
        "s5_a_im": a_im,
        "s5_log_dt": log_dt,
        "s5_b_re": nrm(ks[13], (N_EVEN, S5_GROUPS, S5_STATE, S5_GROUP), 2 * S5_GROUP),
        "s5_b_im": nrm(ks[14], (N_EVEN, S5_GROUPS, S5_STATE, S5_GROUP), 2 * S5_GROUP),
        "s5_c_re": nrm(ks[15], (N_EVEN, S5_GROUPS, S5_GROUP, S5_STATE), 2 * S5_STATE),
        "s5_c_im": nrm(ks[16], (N_EVEN, S5_GROUPS, S5_GROUP, S5_STATE), 2 * S5_STATE),
        "s5_d": jax.random.normal(ks[17], (N_EVEN, S5_WIDTH), f32),
        "s5_glu_w": nrm(ks[18], (N_EVEN, S5_GROUPS, S5_GROUP, S5_GROUP), S5_GROUP),
        "ev_w_out": nrm(ks[19], (N_EVEN, D, D), D),
        "od_w_qkv": nrm(ks[20], (N_ODD, D, ODD_IN), D),
        "od_b_qkv": 0.02 * jax.random.normal(ks[21], (N_ODD, ODD_IN), f32),
        "od_sinks": jax.random.normal(ks[22], (N_ODD, N_Q_HEADS), f32),
        "od_w_out": nrm(ks[23], (N_ODD, N_Q_HEADS * HEAD_DIM, D), D),
        "xa_w_q": nrm(ks[24], (DEPTH, D, D), D),
        "xa_w_kv": nrm(ks[25], (DEPTH, D, 2 * D), D),
        "xa_w_o": nrm(ks[26], (DEPTH, D, D), D),
        "ff_w_gate": nrm(ks[27], (DEPTH, D, D_FF), D),
        "ff_w_up": nrm(ks[28], (DEPTH, D, D_FF), D),
        "ff_conv_w": nrm(ks[29], (DEPTH, CONV_WIDTH, D_FF), CONV_WIDTH),
        "ff_conv_b": 0.02 * jax.random.normal(ks[30], (DEPTH, D_FF), f32),
        "ff_w_down": nrm(ks[31], (DEPTH, D_FF, D), D_FF),
    }


def reference(x, mem, norm_mix, norm_xattn, norm_ffn, norm_final, norm_mem, rel_bias,
              ev_w_in, ev_conv_w, s5_a_re, s5_a_im, s5_log_dt, s5_b_re, s5_b_im,
              s5_c_re, s5_c_im, s5_d, s5_glu_w, ev_w_out,
              od_w_qkv, od_b_qkv, od_sinks, od_w_out,
              xa_w_q, xa_w_kv, xa_w_o,
              ff_w_gate, ff_w_up, ff_conv_w, ff_conv_b, ff_w_down):
    mem_n = rms_norm(mem, norm_mem)
    h = x
    for l in range(DEPTH):
        i = l // 2
        hn = rms_norm(h, norm_mix[l])
        if l % 2 == 0:
            h = h + conv_ssm_mixer(hn, ev_w_in[i], ev_conv_w[i], s5_a_re[i], s5_a_im[i],
                                   s5_log_dt[i], s5_b_re[i], s5_b_im[i], s5_c_re[i],
                                   s5_c_im[i], s5_d[i], s5_glu_w[i], ev_w_out[i])
        else:
            h = h + swa_sink_attention(hn, od_w_qkv[i], od_b_qkv[i], od_sinks[i], rel_bias, od_w_out[i])
        h = h + memory_cross_attention(rms_norm(h, norm_xattn[l]), mem_n, xa_w_q[l], xa_w_kv[l], xa_w_o[l])
        h = h + conv_gated_mlp(rms_norm(h, norm_ffn[l]), ff_w_gate[l], ff_w_up[l],
                               ff_conv_w[l], ff_conv_b[l], ff_w_down[l])
    return rms_norm(h, norm_final)
```

```python
import math
from contextlib import ExitStack

import numpy as np
import concourse.bass as bass
import concourse.mybir as mybir
from concourse.bass_utils import run_bass_kernel_spmd

F32 = mybir.dt.float32
BF16 = mybir.dt.bfloat16
AF = mybir.ActivationFunctionType
ALU = mybir.AluOpType

P = 128
D = 2048
KC = D // P
NTOK = 2048
TT = 512
NT = NTOK // TT
N_MEM = 256
D_FF = 5632
FC = D_FF // P
RMS_EPS = 1e-5


class T:
    __slots__ = ("name", "w", "r", "sem", "semval", "last_dma")

    def __init__(self, name):
        self.name = name
        self.w = None
        self.r = {}
        self.sem = None
        self.semval = 0
        self.last_dma = None


ENGS = ("pe", "act", "dve", "pool", "sp")


class Sched:
    def __init__(self, nc, stack):
        self.nc = nc
        self.stack = stack
        self.q = {e: [] for e in ENGS}
        self.cnt = {e: 0 for e in ENGS}
        self.waited = {e: {} for e in ENGS}
        self.semh = {}
        for e in ENGS:
            self.semh[e] = stack.enter_context(nc.semaphore("sem_" + e))
        self.ndma_sem = 0
        self.n_instr = 0
        self.dmaval = {}

    def barrier(self):
        cur = {e: self.cnt[e] for e in ENGS if self.cnt[e] > 0}
        cur.update(self.dmaval)
        for e in ENGS:
            self._need(e, {k: v for k, v in cur.items() if k != e})

    def _tile_sem(self, t):
        if t.sem is None:
            key = "dsem%d" % self.ndma_sem
            self.ndma_sem += 1
            self.semh[key] = self.stack.enter_context(self.nc.semaphore(key))
            t.sem = key
        return t.sem

    def _need(self, eng, needs):
        for key, val in needs.items():
            if key == "pe" and eng == "pe":
                continue
            if self.waited[eng].get(key, 0) >= val:
                continue
            self.waited[eng][key] = val
            h = self.semh[key]
            self.q[eng].append(lambda e, h=h, val=val: e.wait_ge(h, val))

    def _deps(self, reads, writes):
        needs = {}

        def add(m):
            if m is not None:
                if needs.get(m[0], 0) < m[1]:
                    needs[m[0]] = m[1]
        for t in reads:
            add(t.w)
        for t in writes:
            add(t.w)
            for m in t.r.items():
                add(m)
        return needs

    def op(self, eng, fn, reads=(), writes=(), inc=True):
        needs = self._deps(reads, writes)
        self._need(eng, needs)
        if inc:
            self.cnt[eng] += 1
            h = self.semh[eng]
            self.q[eng].append(lambda e, fn=fn, h=h: fn(e).then_inc(h, 1))
            mark = (eng, self.cnt[eng])
        else:
            self.q[eng].append(lambda e, fn=fn: fn(e))
            mark = (eng, self.cnt[eng] + 1)
        self._mark(reads, writes, mark)
        self.n_instr += 1

    def dma(self, eng, fns, owner, reads=(), writes=()):
        key = self._tile_sem(owner)
        needs = self._deps(reads, writes)
        if owner.last_dma is not None:
            if needs.get(owner.last_dma[0], 0) < owner.last_dma[1]:
                needs[owner.last_dma[0]] = owner.last_dma[1]
        self._need(eng, needs)
        h = self.semh[key]
        for fn in fns:
            owner.semval += 16
            self.q[eng].append(lambda e, fn=fn, h=h: fn(e).then_inc(h, 16))
        mark = (key, owner.semval)
        self.dmaval[key] = owner.semval
        owner.last_dma = mark
        self._mark(reads, writes, mark)
        self.n_instr += len(fns)

    @staticmethod
    def _mark(reads, writes, mark):
        for t in reads:
            if t.r.get(mark[0], 0) < mark[1]:
                t.r[mark[0]] = mark[1]
        for t in writes:
            t.w = mark
            t.r = {}

    def wait_all(self, eng, tiles):
        needs = {}
        for t in tiles:
            for m in ([t.w] if t.w else []) + list(t.r.items()):
                if needs.get(m[0], 0) < m[1]:
                    needs[m[0]] = m[1]
        self._need(eng, needs)

    def emit(self):
        nc = self.nc
        with nc.Block() as block:
            @block.tensor
            def _(e):
                for f in self.q["pe"]:
                    f(e)

            @block.scalar
            def _(e):
                for f in self.q["act"]:
                    f(e)

            @block.vector
            def _(e):
                for f in self.q["dve"]:
                    f(e)

            @block.gpsimd
            def _(e):
                for f in self.q["pool"]:
                    f(e)

            @block.sync
            def _(e):
                for f in self.q["sp"]:
                    f(e)


WSLOT = 5632
NWB = 4
NPS = 8


class Prog:
    def __init__(self, nc, stack):
        self.nc = nc
        self.st = stack
        self.S = Sched(nc, stack)
        self.pstack = None
        self.uid = 0

        def sb(name, shape, dt):
            self.uid += 1
            stk = self.pstack if self.pstack is not None else stack
            return stk.enter_context(nc.sbuf_tensor("%s_%d" % (name, self.uid), shape, dt))
        self.sb = sb
        self.hT = sb("hT", [P, KC, TT], F32)
        self.t_hT = [T("hT%d" % k) for k in range(KC)]
        self.hn = sb("hn", [P, KC, TT], BF16)
        self.t_hn = [T("hn%d" % k) for k in range(KC)]
        self.wb = [sb("wb%d" % i, [P, WSLOT], BF16) for i in range(NWB)]
        self.t_wb = [T("wb%d" % i) for i in range(NWB)]
        self.wi = 0
        self.ps = [stack.enter_context(nc.psum_tensor("ps%d" % i, [P, 512], F32)) for i in range(NPS)]
        self.t_ps = [T("ps%d" % i) for i in range(NPS)]
        self.pi = 0
        self.sq = [sb("sq%d" % i, [P, TT], BF16) for i in range(2)]
        self.t_sq = [T("sq0"), T("sq1")]
        self.rstd = sb("rstd", [P, TT], F32)
        self.t_rstd = T("rstd")
        self.rtmp = sb("rtmp", [P, TT], F32)
        self.t_rtmp = T("rtmp")
        self.ones = sb("ones", [P, P], BF16)
        self.t_ones = T("ones")
        self.S.op("dve", lambda e: e.memset(self.ones[:], 1.0), writes=[self.t_ones])
        self.epsc = sb("epsc", [P, 1], F32)
        self.t_eps = T("eps")
        self.S.op("dve", lambda e: e.memset(self.epsc[:], RMS_EPS), writes=[self.t_eps])
        self.evi = 0

    def phase(self):
        prog = self

        class _Ph:
            def __enter__(s):
                prog.S.barrier()
                s.es = ExitStack()
                s.es.__enter__()
                s.prev = prog.pstack
                prog.pstack = s.es
                return s

            def __exit__(s, *a):
                prog.S.barrier()
                prog.pstack = s.prev
                return s.es.__exit__(*a)
        return _Ph()

    def alloc_A(self, n):
        self.bufA = self.sb("bufA", [P, n, TT], BF16)
        self.t_A = [T("A%d" % k) for k in range(n)]

    def psum(self):
        i = self.pi
        self.pi = (i + 1) % NPS
        return self.ps[i], self.t_ps[i]

    def load_w(self, src3, k, m):
        i = self.wi
        self.wi = (i + 1) % NWB
        view = self.wb[i][:, 0:k * m].rearrange("p (k m) -> p k m", k=k)
        t = self.t_wb[i]
        if k > 22:
            h = k // 2
            fns = [lambda e: e.dma_start(out=view[:, 0:h, :], in_=src3[:, 0:h, :]),
                   lambda e: e.dma_start(out=view[:, h:, :], in_=src3[:, h:, :])]
        else:
            fns = [lambda e: e.dma_start(out=view, in_=src3)]
        self.S.dma("pool", fns, t, writes=[t])
        return view, t

    def mm_group(self, out_ap, t_out, pairs, reads, tp=None):
        n = len(pairs)
        for i, (l, r) in enumerate(pairs):
            if tp is None:
                self.S.op("pe", lambda e, l=l, r=r, i=i: e.matmul(out_ap, l, r, start=(i == 0), stop=(i == n - 1)),
                          reads=reads, writes=[t_out], inc=(i == n - 1))
            else:
                self.S.op("pe", lambda e, l=l, r=r, i=i: e.matmul(out_ap, l, r, start=(i == 0), stop=(i == n - 1), tile_position=tp),
                          reads=reads, writes=[t_out], inc=(i == n - 1))

    def load_small(self, name, dram_ap, shape, dt=F32):
        t = self.sb(name, shape, dt)
        tt = T(name)
        self.S.dma("sp", [lambda e: e.dma_start(out=t[:], in_=dram_ap)], tt, writes=[tt])
        return t, tt

    def rmsnorm(self, src, t_src, gain, t_gain, gcol, dst, t_dst, w, nk=KC):
        S = self.S
        ps, t_ps = self.psum()
        for k in range(nk):
            b = k % 2
            S.op("act", lambda e, k=k, b=b: e.activation(out=self.sq[b][:, 0:w], in_=src[:, k, 0:w], func=AF.Square),
                 reads=[t_src[k]], writes=[self.t_sq[b]])
            S.op("pe", lambda e, k=k, b=b: e.matmul(ps[:, 0:w], self.ones[:], self.sq[b][:, 0:w], start=(k == 0), stop=(k == nk - 1)),
                 reads=[self.t_sq[b], self.t_ones], writes=[t_ps], inc=True)
        S.op("act", lambda e: e.activation(out=self.rtmp[:, 0:w], in_=ps[:, 0:w], func=AF.Sqrt, bias=self.epsc[:], scale=1.0 / (nk * P)),
             reads=[t_ps, self.t_eps], writes=[self.t_rtmp])
        S.op("dve", lambda e: e.reciprocal(out=self.rstd[:, 0:w], in_=self.rtmp[:, 0:w]), reads=[self.t_rtmp], writes=[self.t_rstd])
        for k in range(nk):
            S.op("dve", lambda e, k=k: e.scalar_tensor_tensor(out=dst[:, k, 0:w], in0=src[:, k, 0:w], scalar=gain[:, gcol + k:gcol + k + 1],
                                                              in1=self.rstd[:, 0:w], op0=ALU.mult, op1=ALU.mult),
                 reads=[t_src[k], t_gain, self.t_rstd], writes=[t_dst[k]])

    def load_h(self, h_dram, tok0):
        S = self.S
        src = h_dram[:, tok0:tok0 + TT].rearrange("(k p) t -> p k t", p=P)
        for g in range(4):
            S.dma("sp", [lambda e, g=g: e.dma_start(out=self.hT[:, 4 * g:4 * g + 4, :], in_=src[:, 4 * g:4 * g + 4, :])],
                  self.t_hT[4 * g], writes=self.t_hT[4 * g:4 * g + 4])

    def store_h(self, h_dram, tok0):
        S = self.S
        dst = h_dram[:, tok0:tok0 + TT].rearrange("(k p) t -> p k t", p=P)
        for g in range(4):
            S.dma("sp", [lambda e, g=g: e.dma_start(out=dst[:, 4 * g:4 * g + 4, :], in_=self.hT[:, 4 * g:4 * g + 4, :])],
                  self.t_hT[4 * g], reads=self.t_hT[4 * g:4 * g + 4])

    def linear_resid(self, W, kin, t_in_list, in_buf, in_off):
        S = self.S
        mb = 256 if kin <= 22 else 128
        per = mb // P
        for blk in range(D // mb):
            w, t_w = self.load_w(W[:, blk * mb:(blk + 1) * mb].rearrange("(k p) m -> p k m", p=P), kin, mb)
            for m in range(per):
                ps, t_ps = self.psum()
                self.mm_group(ps[:, 0:TT], t_ps,
                              [(w[:, k, m * P:(m + 1) * P], in_buf[:, in_off + k, :]) for k in range(kin)],
                              reads=[t_w] + t_in_list)
                c = blk * per + m
                S.op("dve", lambda e, c=c, ps=ps: e.tensor_tensor(out=self.hT[:, c, :], in0=self.hT[:, c, :], in1=ps[:, 0:TT], op=ALU.add),
                     reads=[t_ps, self.t_hT[c]], writes=[self.t_hT[c]])

    def linear_to(self, W, col0, ncols, t_in_list, in_buf, out_buf, out_off, t_out_list, evac=None):
        S = self.S
        mb = 256
        for blk in range(ncols // mb):
            w, t_w = self.load_w(W[:, col0 + blk * mb:col0 + (blk + 1) * mb].rearrange("(k p) m -> p k m", p=P), KC, mb)
            for m in range(2):
                ps, t_ps = self.psum()
                self.mm_group(ps[:, 0:TT], t_ps, [(w[:, k, m * P:(m + 1) * P], in_buf[:, k, :]) for k in range(KC)],
                              reads=[t_w] + t_in_list)
                c = blk * 2 + m
                if evac is not None:
                    evac(c, ps, t_ps)
                else:
                    S.op("act", lambda e, c=c, ps=ps: e.activation(out=out_buf[:, out_off + c, :], in_=ps[:, 0:TT], func=AF.Copy),
                         reads=[t_ps], writes=[t_out_list[out_off + c]])

    def prep_mem(self, memT_d, gmem_d):
        S = self.S
        self.memn = self.sb("memn", [P, KC, N_MEM], BF16)
        self.t_memn = [T("memn%d" % k) for k in range(KC)]
        gm, t_gm = self.load_small("gmem_s", gmem_d, [P, KC])
        src = memT_d.rearrange("(k p) t -> p k t", p=P)
        S.dma("sp", [lambda e: e.dma_start(out=self.hT[:, :, 0:N_MEM], in_=src)], self.t_hT[0], writes=self.t_hT)
        self.rmsnorm(self.hT, self.t_hT, gm, t_gm, 0, self.memn, self.t_memn, N_MEM)

    def prep_kv(self, Wkv):
        S = self.S
        for blk in range(8):
            w, t_w = self.load_w(Wkv[:, blk * 256:(blk + 1) * 256].rearrange("(k p) m -> p k m", p=P), KC, 256)
            for m in range(2):
                ps, t_ps = self.psum()
                self.mm_group(ps[:, 0:N_MEM], t_ps, [(w[:, k, m * P:(m + 1) * P], self.memn[:, k, :]) for k in range(KC)],
                              reads=[t_w] + self.t_memn)
                c = blk * 2 + m
                S.op("act", lambda e, c=c, ps=ps: e.activation(out=self.kT[:, c, :], in_=ps[:, 0:N_MEM], func=AF.Copy),
                     reads=[t_ps], writes=[self.t_kT[c]])
        for blk in range(8):
            w, t_w = self.load_w(Wkv[:, D + blk * 256:D + (blk + 1) * 256].rearrange("(k p) m -> p k m", p=P), KC, 256)
            for mc in range(2):
                ps, t_ps = self.psum()
                self.mm_group(ps[:, 0:256], t_ps, [(self.memn[:, k, mc * P:(mc + 1) * P], w[:, k, :]) for k in range(KC)],
                              reads=[t_w] + self.t_memn)
                S.op("act", lambda e, mc=mc, blk=blk, ps=ps: e.activation(out=self.vv[:, mc, blk * 256:(blk + 1) * 256], in_=ps[:, 0:256], func=AF.Copy),
                     reads=[t_ps], writes=[self.t_vv[mc]])

    def xattn(self, h_in, h_out, Wq, Wkv, Wo, gain, t_gain, gcol):
        S = self.S
        self.alloc_A(2 * KC)
        self.kT = self.sb("kT", [P, KC, N_MEM], BF16)
        self.t_kT = [T("kT%d" % k) for k in range(KC)]
        self.vv = self.sb("vv", [P, 2, D], BF16)
        self.t_vv = [T("vv0"), T("vv1")]
        self.pT = self.sb("pT", [P, 2, TT], BF16)
        self.t_pT = [T("pT0"), T("pT1")]
        self.rden = self.sb("rden", [P, TT], F32)
        self.t_rden = T("rden")
        self.prep_kv(Wkv)
        qoff, ooff = 0, KC
        scale = 1.0 / math.sqrt(512.0)
        for it in range(NT):
            tok0 = it * TT
            self.load_h(h_in, tok0)
            self.rmsnorm(self.hT, self.t_hT, gain, t_gain, gcol, self.hn, self.t_hn, TT)
            self.linear_to(Wq, 0, D, self.t_hn, self.hn, self.bufA, qoff, self.t_A)
            for hd in range(4):
                for mc in range(2):
                    ps, t_ps = self.psum()
                    self.mm_group(ps[:, 0:TT], t_ps,
                                  [(self.kT[:, hd * 4 + j, mc * P:(mc + 1) * P], self.bufA[:, qoff + hd * 4 + j, :]) for j in range(4)],
                                  reads=self.t_kT[hd * 4:hd * 4 + 4] + self.t_A[qoff + hd * 4:qoff + hd * 4 + 4])
                    S.op("act", lambda e, mc=mc, ps=ps: e.activation(out=self.pT[:, mc, :], in_=ps[:, 0:TT], func=AF.Exp, scale=scale),
                         reads=[t_ps], writes=[self.t_pT[mc]])
                ps, t_ps = self.psum()
                self.mm_group(ps[:, 0:TT], t_ps, [(self.ones[:], self.pT[:, mc, :]) for mc in range(2)],
                              reads=[self.t_ones] + self.t_pT)
                S.op("dve", lambda e, ps=ps: e.reciprocal(out=self.rden[:], in_=ps[:, 0:TT]), reads=[t_ps], writes=[self.t_rden])
                for j in range(4):
                    ps, t_ps = self.psum()
                    f0 = hd * 512 + j * P
                    self.mm_group(ps[:, 0:TT], t_ps, [(self.vv[:, mc, f0:f0 + P], self.pT[:, mc, :]) for mc in range(2)],
                                  reads=self.t_vv + self.t_pT)
                    c = ooff + hd * 4 + j
                    S.op("dve", lambda e, c=c, ps=ps: e.tensor_tensor(out=self.bufA[:, c, :], in0=ps[:, 0:TT], in1=self.rden[:], op=ALU.mult),
                         reads=[t_ps, self.t_rden], writes=[self.t_A[c]])
            self.linear_resid(Wo, KC, self.t_A[ooff:ooff + KC], self.bufA, ooff)
            self.store_h(h_out, tok0)

    def ffn(self, h_in, h_out, hhalo_d, Wg, Wu, Wd, gain, t_gain, gcol, cw, t_cw, cb, t_cb):
        S = self.S
        self.alloc_A(FC)
        self.gfull = [self.sb("gfull%d" % i, [P, TT + 2], F32) for i in range(2)]
        self.t_gfull = [T("gfull0"), T("gfull1")]
        self.ghalo = self.sb("ghalo", [P, FC, 2], F32)
        self.t_ghalo = [T("ghalo%d" % c) for c in range(FC)]
        self.c1 = [self.sb("c1_%d" % i, [P, TT], F32) for i in range(2)]
        self.t_c1 = [T("c1_0"), T("c1_1")]
        self.hhT = self.sb("hhT", [P, KC, 2], F32)
        self.t_hhT = [T("hhT%d" % k) for k in range(KC)]
        self.hhn = self.sb("hhn", [P, KC, 2], BF16)
        self.t_hhn = [T("hhn%d" % k) for k in range(KC)]
        S.dma("sp", [lambda e: e.dma_start(out=self.hhT[:], in_=hhalo_d.rearrange("(k p) t -> p k t", p=P))], self.t_hhT[0], writes=self.t_hhT)
        self.rmsnorm(self.hhT, self.t_hhT, gain, t_gain, gcol, self.hhn, self.t_hhn, 2)
        for it in range(NT):
            tok0 = it * TT
            self.load_h(h_in, tok0)
            self.rmsnorm(self.hT, self.t_hT, gain, t_gain, gcol, self.hn, self.t_hn, TT)
            for blk in range(FC // 2):
                wg, t_wg = self.load_w(Wg[:, blk * 256:(blk + 1) * 256].rearrange("(k p) m -> p k m", p=P), KC, 256)
                wu, t_wu = self.load_w(Wu[:, blk * 256:(blk + 1) * 256].rearrange("(k p) m -> p k m", p=P), KC, 256)
                for m in range(2):
                    c = blk * 2 + m
                    b = c % 2
                    gf, t_gf = self.gfull[b], self.t_gfull[b]
                    if it == 0:
                        psh, t_psh = self.psum()
                        self.mm_group(psh[:, 0:2], t_psh, [(wg[:, k, m * P:(m + 1) * P], self.hhn[:, k, :]) for k in range(KC)],
                                      reads=[t_wg] + self.t_hhn)
                        S.op("act", lambda e, gf=gf, psh=psh: e.activation(out=gf[:, 0:2], in_=psh[:, 0:2], func=AF.Copy),
                             reads=[t_psh], writes=[t_gf])
                    else:
                        S.op("act", lambda e, gf=gf, c=c: e.activation(out=gf[:, 0:2], in_=self.ghalo[:, c, :], func=AF.Copy),
                             reads=[self.t_ghalo[c]], writes=[t_gf])
                    psg, t_psg = self.psum()
                    self.mm_group(psg[:, 0:TT], t_psg, [(wg[:, k, m * P:(m + 1) * P], self.hn[:, k, :]) for k in range(KC)],
                                  reads=[t_wg] + self.t_hn)
                    psu, t_psu = self.psum()
                    self.mm_group(psu[:, 0:TT], t_psu, [(wu[:, k, m * P:(m + 1) * P], self.hn[:, k, :]) for k in range(KC)],
                                  reads=[t_wu] + self.t_hn)
                    S.op("act", lambda e, gf=gf, psg=psg: e.activation(out=gf[:, 2:TT + 2], in_=psg[:, 0:TT], func=AF.Copy),
                         reads=[t_psg], writes=[t_gf])
                    if it < NT - 1:
                        S.op("act", lambda e, gf=gf, c=c: e.activation(out=self.ghalo[:, c, :], in_=gf[:, TT:TT + 2], func=AF.Copy),
                             reads=[t_gf], writes=[self.t_ghalo[c]])
                    c1, t_c1 = self.c1[b], self.t_c1[b]
                    S.op("act", lambda e, gf=gf, c=c, c1=c1: e.activation(out=c1[:], in_=gf[:, 2:TT + 2], func=AF.Identity,
                                                                        bias=cb[:, c:c + 1], scale=cw[:, 2, c:c + 1]),
                         reads=[t_gf, t_cw, t_cb], writes=[t_c1])
                    S.op("dve", lambda e, gf=gf, c=c, c1=c1: e.scalar_tensor_tensor(out=c1[:], in0=gf[:, 1:TT + 1], scalar=cw[:, 1, c:c + 1], in1=c1[:],
                                                                                  op0=ALU.mult, op1=ALU.add),
                         reads=[t_gf, t_cw, t_c1], writes=[t_c1])
                    S.op("dve", lambda e, gf=gf, c=c, c1=c1: e.scalar_tensor_tensor(out=c1[:], in0=gf[:, 0:TT], scalar=cw[:, 0, c:c + 1], in1=c1[:],
                                                                                  op0=ALU.mult, op1=ALU.add),
                         reads=[t_gf, t_cw, t_c1], writes=[t_c1])
                    S.op("act", lambda e, c1=c1: e.activation(out=c1[:], in_=c1[:], func=AF.Silu), reads=[t_c1], writes=[t_c1])
                    S.op("dve", lambda e, c=c, c1=c1, psu=psu: e.tensor_tensor(out=self.bufA[:, c, :], in0=c1[:], in1=psu[:, 0:TT], op=ALU.mult),
                         reads=[t_c1, t_psu], writes=[self.t_A[c]])
            self.linear_resid(Wd, FC, self.t_A, self.bufA, 0)
            self.store_h(h_out, tok0)

    def final_norm(self, h_in, out_d, gain, t_gain, gcol):
        S = self.S
        self.fo = self.sb("fo", [P, KC, TT], F32)
        self.t_fo = [T("fo%d" % k) for k in range(KC)]
        for it in range(NT):
            tok0 = it * TT
            self.load_h(h_in, tok0)
            self.rmsnorm(self.hT, self.t_hT, gain, t_gain, gcol, self.fo, self.t_fo, TT)
            dst = out_d[:, tok0:tok0 + TT].rearrange("(k p) t -> p k t", p=P)
            S.dma("sp", [lambda e, dst=dst: e.dma_start(out=dst, in_=self.fo[:])], self.t_fo[0], reads=self.t_fo)

    def finish(self, out_tiles):
        self.S.wait_all("sp", out_tiles)
        self.S.emit()


def pk(v):
    v = np.asarray(v, dtype=np.float32)
    lead = v.shape[:-1]
    n = v.shape[-1] // P
    a = v.reshape(lead + (n, P))
    a = np.moveaxis(a, -1, 0)
    return np.ascontiguousarray(a)


def build_xattn_launch():
    nc = bass.Bass("TRN2", target_bir_lowering=False)
    dt = lambda name, shape, kind="ExternalInput": nc.dram_tensor(name, shape, F32, kind=kind).ap()
    h_in = dt("h_in", [D, NTOK]); h_out = dt("h_out", [D, NTOK], "ExternalOutput")
    memT = dt("memT", [D, N_MEM]); gmem = dt("gmem", [P, KC]); gx = dt("gx", [P, KC])
    Wq = dt("Wq", [D, D]); Wkv = dt("Wkv", [D, 2 * D]); Wo = dt("Wo", [D, D])
    with ExitStack() as st:
        pr = Prog(nc, st)
        g, t_g = pr.load_small("gx_s", gx, [P, KC])
        pr.prep_mem(memT, gmem)
        with pr.phase():
            pr.xattn(h_in, h_out, Wq, Wkv, Wo, g, t_g, 0)
        pr.finish(pr.t_hT)
    return nc


def build_ffn_launch(final=False):
    nc = bass.Bass("TRN2", target_bir_lowering=False)
    dt = lambda name, shape, kind="ExternalInput": nc.dram_tensor(name, shape, F32, kind=kind).ap()
    h_in = dt("h_in", [D, NTOK]); h_out = dt("h_out", [D, NTOK], "ExternalOutput")
    hhalo = dt("hhalo", [D, 2]); gf = dt("gf", [P, KC])
    cw = dt("cw", [P, 3, FC]); cb = dt("cb", [P, FC])
    Wg = dt("Wg", [D, D_FF]); Wu = dt("Wu", [D, D_FF]); Wd = dt("Wd", [D_FF, D])
    with ExitStack() as st:
        pr = Prog(nc, st)
        g, t_g = pr.load_small("gf_s", gf, [P, KC])
        cws, t_cw = pr.load_small("cw_s", cw, [P, 3, FC])
        cbs, t_cb = pr.load_small("cb_s", cb, [P, FC])
        with pr.phase():
            pr.ffn(h_in, h_out, hhalo, Wg, Wu, Wd, g, t_g, 0, cws, t_cw, cbs, t_cb)
        pr.finish(pr.t_hT)
    return nc


def t5_bucket_np(rel):
    n = np.maximum(rel, 0)
    nf = np.maximum(n, 16).astype(np.float32)
    large = 16 + (np.log(nf / np.float32(16)) / np.float32(math.log(128 / 16)) * np.float32(16)).astype(np.int32)
    large = np.minimum(large, 31)
    return np.where(n < 16, n, large)


def swa_consts():
    k = np.arange(128)[:, None, None]
    j = np.arange(2)[None, :, None]
    q = np.arange(128)[None, None, :]
    rel = 128 + q - j * 128 - k
    valid = (rel >= 0) & (rel < 128)
    bk = t5_bucket_np(rel)
    oh = np.zeros((128, 32, 2, 128), np.float32)
    for b in range(32):
        oh[:, b] = ((bk == b) & valid).astype(np.float32)
    maskT = np.where(valid, 0.0, -30000.0).astype(np.float32)
    return oh, maskT


def _swa_setup(self, oh_d, mask_d, relb_d, sinkrow_d):
    S = self.S
    self.biasT = self.sb("biasT", [P, 2, 4, 4, 256], F32)
    self.t_biasT = T("biasT")
    self.esrow = self.sb("esrow", [1, 32 * 128], BF16)
    self.t_esrow = T("esrow")
    with self.phase():
        oh = self.sb("oh_s", [P, 8, 256], F32)
        t_oh = T("oh")
        mk, t_mk = self.load_small("mask_s", mask_d.rearrange("p j q -> p (j q)"), [P, 256])
        rb, t_rb = self.load_small("relb_s", relb_d, [P, 32, 32])
        ohv = oh_d.rearrange("p b j q -> p b (j q)")
        for bg in range(4):
            S.dma("sp", [lambda e, bg=bg: e.dma_start(out=oh[:], in_=ohv[:, bg * 8:(bg + 1) * 8, :])], t_oh, writes=[t_oh])
            for h in range(32):
                kh, g = h // 8, h % 8
                par, i = g % 2, g // 2
                dst = self.biasT[:, par, kh, i, :]
                if bg == 0:
                    S.op("dve", lambda e, dst=dst: e.tensor_copy(out=dst, in_=mk[:]), reads=[t_mk], writes=[self.t_biasT])
                for b8 in range(8):
                    b = bg * 8 + b8
                    S.op("dve", lambda e, dst=dst, b=b, b8=b8, h=h: e.scalar_tensor_tensor(out=dst, in0=oh[:, b8, :], scalar=rb[:, b, h:h + 1], in1=dst,
                                                                                         op0=ALU.mult, op1=ALU.add),
                         reads=[t_oh, t_rb, self.t_biasT], writes=[self.t_biasT])
        sk, t_sk = self.load_small("sink_s", sinkrow_d, [1, 32])
        z128 = self.sb("z128", [1, 128], F32)
        t_z = T("z128")
        S.op("dve", lambda e: e.memset(z128[:], 0.0), writes=[t_z])
        for hh in range(32):
            S.op("act", lambda e, hh=hh: e.activation(out=self.esrow[0:1, hh * 128:(hh + 1) * 128], in_=z128[:], func=AF.Exp,
                                                     bias=sk[0:1, hh:hh + 1], scale=1.0),
                 reads=[t_z, t_sk], writes=[self.t_esrow])
    self.kbuf = self.sb("kbuf", [P, 4, 128 + TT], BF16)
    self.t_kbuf = [T("kbuf%d" % i) for i in range(4)]
    self.vdup = self.sb("vdup", [P, 5, 512], BF16)
    self.t_vdup = [T("vdup%d" % i) for i in range(5)]
    self.hh128 = self.hT
    self.t_hh128 = self.t_hT
    self.alloc_A(2 * KC)
    self.hhn128 = self.sb("hhn128", [P, KC, 128], BF16)
    self.t_hhn128 = [T("hhn128_%d" % k) for k in range(KC)]
    self.sc = [self.sb("sc%d" % i, [P, 512], F32) for i in range(2)]
    self.t_sc = [T("sc0"), T("sc1")]
    self.pS = [self.sb("pS%d" % i, [P, 512], BF16) for i in range(2)]
    self.t_pS = [T("pS0"), T("pS1")]
    self.rdn = self.sb("rdn", [P, 512], F32)
    self.t_rdn = T("rdn")


def _swa(self, h_in, h_out, hhalo_d, hmask, t_hmask, Wq, Wkd, Wvd, Wo, bq, t_bq, bkd, t_bkd, bvd, t_bvd, gain, t_gain, gcol):
    S = self.S
    qoff, ooff = 0, KC
    S.dma("sp", [lambda e: e.dma_start(out=self.hh128[:, :, 0:128], in_=hhalo_d.rearrange("(k p) t -> p k t", p=P))], self.t_hh128[0], writes=self.t_hh128)
    self.rmsnorm(self.hh128, self.t_hh128, gain, t_gain, gcol, self.hhn128, self.t_hhn128, 128)

    def kv_proj(src, t_src, w, kcol0, vblk):
        for half in range(2):
            wk, t_wk = self.load_w(Wkd[:, half * 256:(half + 1) * 256].rearrange("(k p) m -> p k m", p=P), KC, 256)
            for m in range(2):
                kh = half * 2 + m
                ps, t_ps = self.psum()
                self.mm_group(ps[:, 0:w], t_ps, [(wk[:, k, m * P:(m + 1) * P], src[:, k, 0:w]) for k in range(KC)], reads=[t_wk] + t_src)
                S.op("act", lambda e, kh=kh, ps=ps: e.activation(out=self.kbuf[:, kh, kcol0:kcol0 + w], in_=ps[:, 0:w], func=AF.Identity,
                                                                bias=bkd[:, kh:kh + 1], scale=1.0),
                     reads=[t_ps, t_bkd], writes=[self.t_kbuf[kh]])
        wv = []
        for half in range(2):
            wv.append(self.load_w(Wvd[:, half * 256:(half + 1) * 256].rearrange("(k p) m -> p k m", p=P), KC, 256))
        for qb in range(w // 128):
            ps, t_ps = self.psum()
            for half in range(2):
                self.mm_group(ps[:, half * 256:(half + 1) * 256], t_ps,
                              [(src[:, k, qb * 128:(qb + 1) * 128], wv[half][0][:, k, :]) for k in range(KC)], reads=[wv[half][1]] + t_src)
            S.op("dve", lambda e, qb=qb, ps=ps: e.tensor_tensor(out=self.vdup[:, vblk + qb, :], in0=ps[:, 0:512], in1=bvd[:], op=ALU.add),
                 reads=[t_ps, t_bvd], writes=[self.t_vdup[vblk + qb]])

    kv_proj(self.hhn128, self.t_hhn128, 128, 0, 0)
    for it in range(NT):
        tok0 = it * TT
        self.load_h(h_in, tok0)
        self.rmsnorm(self.hT, self.t_hT, gain, t_gain, gcol, self.hn, self.t_hn, TT)
        if it > 0:
            for kh in range(4):
                S.op("act", lambda e, kh=kh: e.activation(out=self.kbuf[:, kh, 0:128], in_=self.kbuf[:, kh, TT:TT + 128], func=AF.Copy),
                     reads=[self.t_kbuf[kh]], writes=[self.t_kbuf[kh]])
            S.op("act", lambda e: e.activation(out=self.vdup[:, 0, :], in_=self.vdup[:, 4, :], func=AF.Copy),
                 reads=[self.t_vdup[4]], writes=[self.t_vdup[0]])
        kv_proj(self.hn, self.t_hn, TT, 128, 1)

        def q_evac(c, ps, t_ps):
            S.op("act", lambda e, c=c, ps=ps: e.activation(out=self.bufA[:, qoff + c, :], in_=ps[:, 0:TT], func=AF.Identity,
                                                          bias=bq[:, c:c + 1], scale=0.125),
                 reads=[t_ps, t_bq], writes=[self.t_A[qoff + c]])
        self.linear_to(Wq, 0, D, self.t_hn, self.hn, self.bufA, qoff, self.t_A, evac=q_evac)
        for qb in range(TT // 128):
            qs = slice(qb * 128, (qb + 1) * 128)
            for kh in range(4):
                for par in range(2):
                    pr = slice(par * 64, (par + 1) * 64)
                    t_q4 = self.t_A[qoff + 4 * kh:qoff + 4 * kh + 4]
                    for j in range(2):
                        ps, t_ps = self.psum()
                        kc0 = (qb + j) * 128
                        self.mm_group(ps[:, 0:512], t_ps,
                                      [(self.kbuf[pr, kh, kc0:kc0 + 128], self.bufA[pr, qoff + 4 * kh:qoff + 4 * kh + 4, qs])],
                                      reads=[self.t_kbuf[kh]] + t_q4)
                        bsl = self.biasT[:, par, kh, :, j * 128:(j + 1) * 128]
                        psv = ps[:, 0:512].rearrange("p (i q) -> p i q", i=4)
                        scv = self.sc[j][:].rearrange("p (i q) -> p i q", i=4)
                        if it == 0 and qb == 0 and j == 0:
                            S.op("dve", lambda e, psv=psv, scv=scv, bsl=bsl: e.scalar_tensor_tensor(out=scv, in0=psv, scalar=hmask[:, 0:1], in1=bsl,
                                                                                                  op0=ALU.add, op1=ALU.add),
                                 reads=[t_ps, self.t_biasT, t_hmask], writes=[self.t_sc[j]])
                        else:
                            S.op("dve", lambda e, psv=psv, scv=scv, bsl=bsl: e.tensor_tensor(out=scv, in0=psv, in1=bsl, op=ALU.add),
                                 reads=[t_ps, self.t_biasT], writes=[self.t_sc[j]])
                        S.op("act", lambda e, j=j: e.activation(out=self.pS[j][:], in_=self.sc[j][:], func=AF.Exp),
                             reads=[self.t_sc[j]], writes=[self.t_pS[j]])
                    psd, t_psd = self.psum()
                    e0 = (par * 16 + kh * 4) * 128
                    self.mm_group(psd[:, 0:512], t_psd,
                                  [(self.ones[:], self.pS[0][:]), (self.ones[:], self.pS[1][:]), (self.ones[0:1, :], self.esrow[0:1, e0:e0 + 512])],
                                  reads=self.t_pS + [self.t_ones, self.t_esrow])
                    S.op("dve", lambda e, psd=psd: e.reciprocal(out=self.rdn[:], in_=psd[:, 0:512]), reads=[t_psd], writes=[self.t_rdn])
                    pso, t_pso = self.psum()
                    self.mm_group(pso[:, 0:512], t_pso,
                                  [(self.vdup[:, qb + j, kh * 128:(kh + 1) * 128], self.pS[j][:]) for j in range(2)],
                                  reads=self.t_pS + [self.t_vdup[qb], self.t_vdup[qb + 1]])
                    ov = self.bufA[pr, ooff + 4 * kh:ooff + 4 * kh + 4, qs]
                    S.op("dve", lambda e, ov=ov, pso=pso, pr=pr: e.tensor_tensor(out=ov, in0=pso[pr, 0:512].rearrange("p (i q) -> p i q", i=4),
                                                                                in1=self.rdn[pr, :].rearrange("p (i q) -> p i q", i=4), op=ALU.mult),
                         reads=[t_pso, self.t_rdn], writes=self.t_A[ooff + 4 * kh:ooff + 4 * kh + 4])
        self.linear_resid(Wo, KC, self.t_A[ooff:ooff + KC], self.bufA, ooff)
        self.store_h(h_out, tok0)


Prog.swa_setup = _swa_setup
Prog.swa = _swa


def swa_host_params(z, i):
    b = np.asarray(z["od_b_qkv"][i], np.float32)
    bq = pk(b[0:2048])
    bk = b[2048:2304].reshape(4, 64)
    bkd = np.ascontiguousarray(np.concatenate([bk, bk], axis=1).T)
    bv = b[2304:2560].reshape(4, 64)
    bvd = np.concatenate([bv, bv], axis=1).reshape(1, 512)
    bvd = np.ascontiguousarray(np.broadcast_to(bvd, (P, 512)))
    W = np.asarray(z["od_w_qkv"][i], np.float32)
    Wk = W[:, 2048:2304].reshape(D, 4, 64)
    Wkd = np.ascontiguousarray(np.concatenate([Wk, Wk], axis=2).reshape(D, 512))
    Wv = W[:, 2304:2560].reshape(D, 4, 64)
    Wvd = np.ascontiguousarray(np.concatenate([Wv, Wv], axis=2).reshape(D, 512))
    Wq = np.ascontiguousarray(W[:, 0:2048])
    sk = np.asarray(z["od_sinks"][i], np.float32)
    order = [kh * 8 + 2 * ii + par for par in range(2) for kh in range(4) for ii in range(4)]
    sinkrow = np.ascontiguousarray(sk[order].reshape(1, 32))
    relb = np.ascontiguousarray(np.broadcast_to(np.asarray(z["rel_bias"], np.float32)[None], (P, 32, 32)))
    return dict(bq=bq, bkd=bkd, bvd=bvd, Wq=Wq, Wkd=Wkd, Wvd=Wvd, sinkrow=sinkrow, relb=relb)


def build_swa_launch():
    nc = bass.Bass("TRN2", target_bir_lowering=False)
    dt = lambda name, shape, kind="ExternalInput": nc.dram_tensor(name, shape, F32, kind=kind).ap()
    h_in = dt("h_in", [D, NTOK]); h_out = dt("h_out", [D, NTOK], "ExternalOutput")
    hhalo = dt("hhalo", [D, 128]); gm = dt("gm", [P, KC]); hmask_d = dt("hmask", [P, 1])
    oh_d = dt("oh", [P, 32, 2, 128]); mask_d = dt("maskT", [P, 2, 128]); relb_d = dt("relb", [P, 32, 32]); sinkrow_d = dt("sinkrow", [1, 32])
    bq_d = dt("bq", [P, KC]); bkd_d = dt("bkd", [P, 4]); bvd_d = dt("bvd", [P, 512])
    Wq = dt("Wq", [D, D]); Wkd = dt("Wkd", [D, 512]); Wvd = dt("Wvd", [D, 512]); Wo = dt("Wo", [D, D])
    with ExitStack() as st:
        pr = Prog(nc, st)
        g, t_g = pr.load_small("gm_s", gm, [P, KC])
        hm, t_hm = pr.load_small("hmask_s", hmask_d, [P, 1])
        bq, t_bq = pr.load_small("bq_s", bq_d, [P, KC])
        pr.S.op("dve", lambda e: e.tensor_scalar(out=bq[:], in0=bq[:], scalar1=0.125, scalar2=None, op0=ALU.mult), reads=[t_bq], writes=[t_bq])
        bkd, t_bkd = pr.load_small("bkd_s", bkd_d, [P, 4])
        bvd, t_bvd = pr.load_small("bvd_s", bvd_d, [P, 512])
        with pr.phase():
            pr.swa_setup(oh_d, mask_d, relb_d, sinkrow_d)
            pr.swa(h_in, h_out, hhalo, hm, t_hm, Wq, Wkd, Wvd, Wo, bq, t_bq, bkd, t_bkd, bvd, t_bvd, g, t_g, 0)
        pr.finish(pr.t_hT)
    return nc


class Ew:
    def __init__(self, prog, shape, tag):
        self.pr = prog
        self.shape = shape
        self.tag = tag
        self.n = 0

    def new(self, dt=F32, shape=None):
        self.n += 1
        t = self.pr.sb("%s%d" % (self.tag, self.n), shape or self.shape, dt)
        return (t[:], T("%s%d" % (self.tag, self.n)))

    def tt(self, o, a, b, op):
        self.pr.S.op("dve", lambda e: e.tensor_tensor(out=o[0], in0=a[0], in1=b[0], op=op), reads=[a[1], b[1]], writes=[o[1]])
        return o

    def ts(self, o, a, s1, op0, s2=None, op1=None):
        if op1 is None:
            self.pr.S.op("dve", lambda e: e.tensor_scalar(out=o[0], in0=a[0], scalar1=s1, scalar2=None, op0=op0), reads=[a[1]], writes=[o[1]])
        else:
            self.pr.S.op("dve", lambda e: e.tensor_scalar(out=o[0], in0=a[0], scalar1=s1, scalar2=s2, op0=op0, op1=op1), reads=[a[1]], writes=[o[1]])
        return o

    def act(self, o, a, func, scale=1.0, bias=None, breads=()):
        if bias is None:
            self.pr.S.op("act", lambda e: e.activation(out=o[0], in_=a[0], func=func, scale=scale), reads=[a[1]], writes=[o[1]])
        else:
            self.pr.S.op("act", lambda e: e.activation(out=o[0], in_=a[0], func=func, scale=scale, bias=bias), reads=[a[1]] + list(breads), writes=[o[1]])
        return o

    def cmul(self, orr, oi, ar, ai, br, bi, t1, t2):
        self.tt(t1, ar, br, ALU.mult)
        self.tt(t2, ai, bi, ALU.mult)
        self.tt(orr, t1, t2, ALU.subtract)
        self.tt(t1, ar, bi, ALU.mult)
        self.tt(t2, ai, br, ALU.mult)
        self.tt(oi, t1, t2, ALU.add)

    def abar(self, are, aim, ldt, halfpi):
        n = self.new
        dt_ = self.act(n(), ldt, AF.Exp)
        x1 = self.tt(n(), are, dt_, ALU.mult)
        th = self.tt(n(), aim, dt_, ALU.mult)
        rho = self.act(n(), x1, AF.Exp, scale=1.0 / 32)
        sn = self.act(n(), th, AF.Sin, scale=1.0 / 32)
        cs = self.act(n(), th, AF.Sin, scale=1.0 / 32, bias=halfpi[0], breads=[halfpi[1]])
        er = self.tt(n(), rho, cs, ALU.mult)
        ei = self.tt(n(), rho, sn, ALU.mult)
        t1, t2, t3 = n(), n(), n()
        for _ in range(5):
            self.tt(t1, er, er, ALU.mult)
            self.tt(t2, ei, ei, ALU.mult)
            self.tt(t3, er, ei, ALU.mult)
            self.tt(er, t1, t2, ALU.subtract)
            self.ts(ei, t3, 2.0, ALU.mult)
        nr = self.ts(n(), er, -1.0, ALU.add)
        self.tt(t1, are, are, ALU.mult)
        self.tt(t2, aim, aim, ALU.mult)
        self.tt(t3, t1, t2, ALU.add)
        rd = n()
        self.pr.S.op("dve", lambda e: e.reciprocal(out=rd[0], in_=t3[0]), reads=[t3[1]], writes=[rd[1]])
        qr, qi = n(), n()
        self.tt(t1, nr, are, ALU.mult)
        self.tt(t2, ei, aim, ALU.mult)
        self.tt(t3, t1, t2, ALU.add)
        self.tt(qr, t3, rd, ALU.mult)
        self.tt(t1, ei, are, ALU.mult)
        self.tt(t2, nr, aim, ALU.mult)
        self.tt(t3, t1, t2, ALU.subtract)
        self.tt(qi, t3, rd, ALU.mult)
        return er, ei, qr, qi


def _s5_precompute(self, prm, dr):
    S = self.S
    self.A8r = self.sb("A8r", [P, 32], F32)
    self.A8i = self.sb("A8i", [P, 32], F32)
    self.t_A8 = T("A8")
    with self.phase():
        hp_t = self.sb("halfpi", [P, 1], F32)
        t_hp = T("halfpi")
        S.op("dve", lambda e: e.memset(hp_t[:], math.pi / 2), writes=[t_hp])
        halfpi = (hp_t[:], t_hp)
        ld = lambda name, shape: (lambda r: (r[0][:], r[1]))(self.load_small(name + "_s", prm[name], shape))
        es = Ew(self, [P, 32], "es")
        are, aim, ldt = ld("are_S", [P, 32]), ld("aim_S", [P, 32]), ld("ldt_S", [P, 32])
        ar, ai, qr, qi = es.abar(are, aim, ldt, halfpi)
        bre, bim = ld("bre_S", [P, 32, 16]), ld("bim_S", [P, 32, 16])
        cre, cim = ld("cre_S", [P, 32, 16]), ld("cim_S", [P, 32, 16])
        e3 = Ew(self, [P, 32, 16], "e3")
        bc = lambda v: (v[0].unsqueeze(2).to_broadcast([P, 32, 16]), v[1])
        Br, Bi, u1, u2 = e3.new(), e3.new(), e3.new(), e3.new()
        e3.cmul(Br, Bi, bc(qr), bc(qi), bre, bim, u1, u2)
        pw = [(es.new(), es.new()) for _ in range(9)]
        S.op("dve", lambda e: e.memset(pw[0][0][0], 1.0), writes=[pw[0][0][1]])
        S.op("dve", lambda e: e.memset(pw[0][1][0], 0.0), writes=[pw[0][1][1]])
        s1, s2 = es.new(), es.new()
        for k in range(8):
            es.cmul(pw[k + 1][0], pw[k + 1][1], pw[k][0], pw[k][1], ar, ai, s1, s2)
        S.op("dve", lambda e: e.tensor_copy(out=self.A8r[:], in_=pw[8][0][0]), reads=[pw[8][0][1]], writes=[self.t_A8])
        S.op("dve", lambda e: e.tensor_copy(out=self.A8i[:], in_=pw[8][1][0]), reads=[pw[8][1][1]], writes=[self.t_A8])
        def padded(name, dt):
            t = self.sb(name, [P, 32, 32], dt)
            tt = T(name)
            S.op("dve", lambda e: e.memset(t[:], 0.0), writes=[tt])
            return t, tt

        def to_pad(dst, t_dst, src):
            for m in range(2):
                S.op("dve", lambda e, m=m: e.tensor_copy(out=dst[m * 64:(m + 1) * 64, :, m * 16:(m + 1) * 16], in_=src[0][m * 64:(m + 1) * 64, :, :]),
                     reads=[src[1]], writes=[t_dst])
        Bpr, t_Bpr = padded("Bpr", BF16)
        Bpn, t_Bpn = padded("Bpn", BF16)
        to_pad(Bpr, t_Bpr, Br)
        nBi = e3.ts(e3.new(), Bi, -1.0, ALU.mult)
        to_pad(Bpn, t_Bpn, nBi)
        Cpr, t_Cpr = padded("Cpr", BF16)
        Cpi, t_Cpi = padded("Cpi", BF16)
        ident, t_ident = self.load_small("ident_s", prm["ident"], [P, P])
        dcol, t_dcol = self.load_small("dcol_s", prm["dcol"], [P, 8])
        Ck_r, Ck_i = e3.new(), e3.new()
        stg = self.sb("wc_stg", [P, 8, 3072], BF16)
        t_stg = T("wc_stg")
        S.op("dve", lambda e: e.memset(stg[:], 0.0), writes=[t_stg])
        f32blk = self.sb("f32blk", [P, P], F32)
        t_f32blk = T("f32blk")
        for k in range(9):
            e3.cmul(Ck_r, Ck_i, cre, cim, bc(pw[k][0]), bc(pw[k][1]), u1, u2)
            if k < 8:
                to_pad(Cpr, t_Cpr, Ck_r)
                to_pad(Cpi, t_Cpi, Ck_i)
                for c8 in range(8):
                    ps, t_ps = self.psum()
                    for q in range(4):
                        pair = c8 * 4 + q
                        self.mm_group(ps[q * 32:(q + 1) * 32, q * 32:(q + 1) * 32], t_ps,
                                      [(Bpr[:, pair, :], Cpr[:, pair, :]), (Bpn[:, pair, :], Cpi[:, pair, :])],
                                      reads=[t_Bpr, t_Bpn, t_Cpr, t_Cpi], tp=(0, q * 32))
                    dstv = stg[:, c8, k * 128:(k + 1) * 128]
                    for q in range(4):
                        sl = slice(q * 32, (q + 1) * 32)
                        if k == 0:
                            S.op("dve", lambda e, sl=sl, c8=c8, ps=ps, dstv=dstv: e.scalar_tensor_tensor(
                                out=dstv[sl, sl], in0=ident[sl, sl], scalar=dcol[sl, c8:c8 + 1], in1=ps[sl, sl], op0=ALU.mult, op1=ALU.add),
                                reads=[t_ps, t_ident, t_dcol], writes=[t_stg])
                        else:
                            S.op("act", lambda e, sl=sl, ps=ps, dstv=dstv: e.activation(out=dstv[sl, sl], in_=ps[sl, sl], func=AF.Copy),
                                 reads=[t_ps], writes=[t_stg])
            if k >= 1:
                r = k - 1
                wov = stg[:, :, 1024:3072].rearrange("p c (q r i o) -> p c q r i o", q=4, r=8, i=2)
                for m in range(2):
                    ms = slice(m * 64, (m + 1) * 64)
                    S.op("dve", lambda e, ms=ms, m=m, r=r: e.tensor_copy(
                        out=wov[ms, :, :, r, 0, m * 16:(m + 1) * 16], in_=Ck_r[0][ms, :, :].rearrange("p (c q) h -> p c q h", q=4)),
                        reads=[Ck_r[1]], writes=[t_stg])
                    S.op("dve", lambda e, ms=ms, m=m, r=r: e.tensor_scalar(
                        out=wov[ms, :, :, r, 1, m * 16:(m + 1) * 16], in0=Ck_i[0][ms, :, :].rearrange("p (c q) h -> p c q h", q=4),
                        scalar1=-1.0, scalar2=None, op0=ALU.mult),
                        reads=[Ck_i[1]], writes=[t_stg])
        S.dma("sp", [lambda e: e.dma_start(out=dr["wc"], in_=stg[:])], t_stg, reads=[t_stg], writes=[dr["t_wc"]])
    with self.phase():
        hp_t = self.sb("halfpi2", [P, 1], F32)
        t_hp = T("halfpi2")
        S.op("dve", lambda e: e.memset(hp_t[:], math.pi / 2), writes=[t_hp])
        halfpi = (hp_t[:], t_hp)
        ld = lambda name, shape: (lambda r: (r[0][:], r[1]))(self.load_small(name + "_s", prm[name], shape))
        et = Ew(self, [P, 8, 64], "et")
        are, aim, ldt = ld("are_T", [P, 8, 64]), ld("aim_T", [P, 8, 64]), ld("ldt_T", [P, 8, 64])
        ar, ai, qr, qi = et.abar(are, aim, ldt, halfpi)
        bre, bim = ld("bre_T", [P, 8, 64]), ld("bim_T", [P, 8, 64])
        QBr, QBi, u1, u2 = et.new(), et.new(), et.new(), et.new()
        et.cmul(QBr, QBi, qr, qi, bre, bim, u1, u2)
        m01, t_m01 = self.load_small("mask01_s", prm["mask01"], [P, 2])
        wzs = self.sb("wz_stg", [P, 8, 8, 2, 128], BF16)
        t_wzs = T("wz_stg")
        pr_, pi_ = et.new(), et.new()
        S.op("dve", lambda e: e.memset(pr_[0], 1.0), writes=[pr_[1]])
        S.op("dve", lambda e: e.memset(pi_[0], 0.0), writes=[pi_[1]])
        Wr, Wi, nr_, ni_ = et.new(), et.new(), et.new(), et.new()
        for k in range(8):
            s = 7 - k
            et.cmul(Wr, Wi, pr_, pi_, QBr, QBi, u1, u2)
            for ri, W in ((0, Wr), (1, Wi)):
                for m in range(2):
                    S.op("dve", lambda e, s=s, ri=ri, m=m, W=W: e.tensor_scalar(out=wzs[:, :, s, ri, m * 64:(m + 1) * 64], in0=W[0],
                                                                                 scalar1=m01[:, m:m + 1], scalar2=None, op0=ALU.mult),
                         reads=[W[1], t_m01], writes=[t_wzs])
            if k < 7:
                et.cmul(nr_, ni_, pr_, pi_, ar, ai, u1, u2)
                S.op("dve", lambda e: e.tensor_copy(out=pr_[0], in_=nr_[0]), reads=[nr_[1]], writes=[pr_[1]])
                S.op("dve", lambda e: e.tensor_copy(out=pi_[0], in_=ni_[0]), reads=[ni_[1]], writes=[pi_[1]])
        S.dma("sp", [lambda e: e.dma_start(out=dr["wz"], in_=wzs[:].rearrange("p c s i o -> p c (s i o)"))], t_wzs, reads=[t_wzs], writes=[dr["t_wz"]])


Prog.s5_precompute = _s5_precompute


def ev_host_params(z, i):
    f = lambda k: np.asarray(z[k][i], np.float32)
    a_re, a_im, ldt = f("s5_a_re"), f("s5_a_im"), f("s5_log_dt")
    b_re, b_im = f("s5_b_re"), f("s5_b_im")
    c_re, c_im = f("s5_c_re"), f("s5_c_im")
    def S2(v):
        return np.ascontiguousarray(v.reshape(32, 2, 64).transpose(1, 2, 0).reshape(128, 32))
    def S3(v):
        return np.ascontiguousarray(v.reshape(32, 2, 64, 16).transpose(1, 2, 0, 3).reshape(128, 32, 16))
    def T2(v):
        w = v.reshape(8, 4, 2, 1, 64)
        w = np.broadcast_to(w, (8, 4, 2, 16, 64)).transpose(1, 2, 3, 0, 4).reshape(128, 8, 64)
        return np.ascontiguousarray(w)
    def T3(v):
        w = v.reshape(8, 4, 2, 64, 16).transpose(1, 2, 4, 0, 3).reshape(128, 8, 64)
        return np.ascontiguousarray(w)
    ldt2 = np.broadcast_to(ldt[:, None], (64, 64))
    out = dict(are_S=S2(a_re), aim_S=S2(a_im), ldt_S=S2(ldt2), bre_S=S3(b_re), bim_S=S3(b_im),
               cre_S=S3(c_re.transpose(0, 2, 1)), cim_S=S3(c_im.transpose(0, 2, 1)),
               are_T=T2(a_re), aim_T=T2(a_im), ldt_T=T2(ldt2), bre_T=T3(b_re), bim_T=T3(b_im))
    mask01 = np.zeros((128, 2), np.float32)
    for p_ in range(128):
        mask01[p_, (p_ // 16) % 2] = 1.0
    out["mask01"] = mask01
    out["ident"] = np.eye(128, dtype=np.float32)
    out["dcol"] = pk(f("s5_d"))
    glu = f("s5_glu_w")
    gb = np.zeros((128, 8, 128), np.float32)
    for g in range(64):
        c8, gg = g // 8, g % 8
        gb[gg * 16:(gg + 1) * 16, c8, gg * 16:(gg + 1) * 16] = glu[g]
    out["gblk"] = gb
    out["convw"] = pk(f("ev_conv_w"))
    return out


def _even_mixer(self, h_in, h_out, hhalo_d, s5in_d, s5out_d, Win, Wout, prm, dr, gain, t_gain, gcol):
    S = self.S
    self.s5_precompute(prm, dr)
    self.alloc_A(24)
    cw, t_cw = self.load_small("convw_s", prm["convw"], [P, 3, 8])
    gblk = self.sb("gblk", [P, 8, P], BF16)
    t_gblk = T("gblk")
    S.dma("pool", [lambda e: e.dma_start(out=gblk[:], in_=prm["gblk"])], t_gblk, writes=[t_gblk])
    Xb = [self.sb("Xb%d" % i, [P, 32, 65], F32) for i in range(2)]
    t_Xb = [T("Xb0"), T("Xb1")]
    X16 = [self.sb("X16_%d" % i, [P, 32, 64], BF16) for i in range(2)]
    t_X16 = [T("X16_0"), T("X16_1")]
    sstg = [self.sb("sstg%d" % i, [P, 32], F32) for i in range(2)]
    t_sstg = [T("sstg0"), T("sstg1")]
    for ri in range(2):
        S.dma("sp", [lambda e, ri=ri: e.dma_start(out=sstg[ri][:], in_=s5in_d[ri])], t_sstg[ri], writes=[t_sstg[ri]])
        S.op("act", lambda e, ri=ri: e.activation(out=Xb[ri][:, :, 0], in_=sstg[ri][:], func=AF.Copy), reads=[t_sstg[ri]], writes=[t_Xb[ri]])
    st = [self.sb("st%d" % i, [P, 32], F32) for i in range(4)]
    t_st = [T("st%d" % i) for i in range(4)]
    prod = [self.sb("prod%d" % i, [P, TT + 2], F32) for i in range(2)]
    t_prod = [T("prod0"), T("prod1")]
    phalo = self.sb("phalo", [P, 8, 2], F32)
    t_phalo = [T("phalo%d" % c) for c in range(8)]
    gcs = [self.sb("gcs%d" % i, [P, TT], F32) for i in range(2)]
    t_gcs = [T("gcs0"), T("gcs1")]
    cv = [self.sb("cv%d" % i, [P, TT], F32) for i in range(2)]
    t_cv = [T("cv0"), T("cv1")]
    hhT = self.sb("ehhT", [P, KC, 2], F32)
    t_hhT = [T("ehhT%d" % k) for k in range(KC)]
    hhn = self.sb("ehhn", [P, KC, 2], BF16)
    t_hhn = [T("ehhn%d" % k) for k in range(KC)]
    yx = [self.sb("yx%d" % i, [P, TT], F32) for i in range(2)]
    t_yx = [T("yx0"), T("yx1")]
    yg = [self.sb("yg%d" % i, [P, TT], F32) for i in range(2)]
    t_yg = [T("yg0"), T("yg1")]
    yb = [self.sb("yb%d" % i, [P, TT], BF16) for i in range(2)]
    t_yb = [T("yb0"), T("yb1")]
    S.dma("sp", [lambda e: e.dma_start(out=hhT[:], in_=hhalo_d.rearrange("(k p) t -> p k t", p=P))], t_hhT[0], writes=t_hhT)
    self.rmsnorm(hhT, t_hhT, gain, t_gain, gcol, hhn, t_hhn, 2)
    NJ = TT // 8
    UO = 16
    for it in range(NT):
        tok0 = it * TT
        self.load_h(h_in, tok0)
        self.rmsnorm(self.hT, self.t_hT, gain, t_gain, gcol, self.hn, self.t_hn, TT)
        for blk in range(4):
            wblk = lambda col0: self.load_w(Win[:, col0 + blk * 256:col0 + (blk + 1) * 256].rearrange("(k p) m -> p k m", p=P), KC, 256)
            wgb, t_wgb = wblk(0)
            wgc, t_wgc = wblk(1024)
            wxa, t_wxa = wblk(2048)
            for m in range(2):
                c = blk * 2 + m
                b = c % 2
                ms = slice(m * P, (m + 1) * P)
                pd, t_pd = prod[b], t_prod[b]
                if it == 0:
                    ph1, t_ph1 = self.psum()
                    self.mm_group(ph1[:, 0:2], t_ph1, [(wgc[:, k, ms], hhn[:, k, :]) for k in range(KC)], reads=[t_wgc] + t_hhn)
                    ph2, t_ph2 = self.psum()
                    self.mm_group(ph2[:, 0:2], t_ph2, [(wxa[:, k, ms], hhn[:, k, :]) for k in range(KC)], reads=[t_wxa] + t_hhn)
                    S.op("act", lambda e, ph1=ph1, b=b: e.activation(out=gcs[b][:, 0:2], in_=ph1[:, 0:2], func=AF.Copy), reads=[t_ph1], writes=[t_gcs[b]])
                    S.op("dve", lambda e, ph2=ph2, b=b, pd=pd: e.tensor_tensor(out=pd[:, 0:2], in0=gcs[b][:, 0:2], in1=ph2[:, 0:2], op=ALU.mult),
                         reads=[t_ph2, t_gcs[b]], writes=[t_pd])
                else:
                    S.op("act", lambda e, c=c, pd=pd: e.activation(out=pd[:, 0:2], in_=phalo[:, c, :], func=AF.Copy), reads=[t_phalo[c]], writes=[t_pd])
                pgc, t_pgc = self.psum()
                self.mm_group(pgc[:, 0:TT], t_pgc, [(wgc[:, k, ms], self.hn[:, k, :]) for k in range(KC)], reads=[t_wgc] + self.t_hn)
                pxa, t_pxa = self.psum()
                self.mm_group(pxa[:, 0:TT], t_pxa, [(wxa[:, k, ms], self.hn[:, k, :]) for k in range(KC)], reads=[t_wxa] + self.t_hn)
                pgb, t_pgb = self.psum()
                self.mm_group(pgb[:, 0:TT], t_pgb, [(wgb[:, k, ms], self.hn[:, k, :]) for k in range(KC)], reads=[t_wgb] + self.t_hn)
                S.op("act", lambda e, pgc=pgc, b=b: e.activation(out=gcs[b][:], in_=pgc[:, 0:TT], func=AF.Copy), reads=[t_pgc], writes=[t_gcs[b]])
                S.op("dve", lambda e, pxa=pxa, b=b, pd=pd: e.tensor_tensor(out=pd[:, 2:TT + 2], in0=gcs[b][:], in1=pxa[:, 0:TT], op=ALU.mult),
                     reads=[t_pxa, t_gcs[b]], writes=[t_pd])
                if it < NT - 1:
                    S.op("act", lambda e, c=c, pd=pd: e.activation(out=phalo[:, c, :], in_=pd[:, TT:TT + 2], func=AF.Copy), reads=[t_pd], writes=[t_phalo[c]])
                S.op("act", lambda e, c=c, b=b, pd=pd: e.activation(out=cv[b][:], in_=pd[:, 2:TT + 2], func=AF.Copy, scale=cw[:, 2, c:c + 1]),
                     reads=[t_pd, t_cw], writes=[t_cv[b]])
                S.op("dve", lambda e, c=c, b=b, pd=pd: e.scalar_tensor_tensor(out=cv[b][:], in0=pd[:, 1:TT + 1], scalar=cw[:, 1, c:c + 1], in1=cv[b][:],
                                                                           op0=ALU.mult, op1=ALU.add), reads=[t_pd, t_cw, t_cv[b]], writes=[t_cv[b]])
                S.op("dve", lambda e, c=c, b=b, pd=pd: e.scalar_tensor_tensor(out=cv[b][:], in0=pd[:, 0:TT], scalar=cw[:, 0, c:c + 1], in1=cv[b][:],
                                                                           op0=ALU.mult, op1=ALU.add), reads=[t_pd, t_cw, t_cv[b]], writes=[t_cv[b]])
                S.op("dve", lambda e, c=c, b=b, pgb=pgb: e.tensor_tensor(out=self.bufA[:, c, :], in0=cv[b][:], in1=pgb[:, 0:TT], op=ALU.mult),
                     reads=[t_cv[b], t_pgb], writes=[self.t_A[c]])
        self.linear_to(Win, 3072, 1024, self.t_hn, self.hn, self.bufA, UO, self.t_A)
        for c8 in range(8):
            wz, t_wz = self.load_wz(dr, c8)
            uv = self.bufA[:, UO + c8, :].rearrange("p (j s) -> p j s", s=8)
            for q in range(4):
                pair = c8 * 4 + q
                ps, t_ps = self.psum()
                for ri in range(2):
                    n = 8
                    for s in range(8):
                        self.S.op("pe", lambda e, ps=ps, q=q, s=s, ri=ri, wz=wz, uv=uv: e.matmul(
                            ps[:, ri * NJ:(ri + 1) * NJ], wz[q * 32:(q + 1) * 32, s, ri, :], uv[q * 32:(q + 1) * 32, :, s],
                            start=(s == 0), stop=(s == n - 1), tile_position=(q * 32, 0)),
                            reads=[t_wz, self.t_A[UO + c8]], writes=[t_ps], inc=(s == n - 1))
                    S.op("act", lambda e, ps=ps, ri=ri, pair=pair: e.activation(out=Xb[ri][:, pair, 1:NJ + 1], in_=ps[:, ri * NJ:(ri + 1) * NJ], func=AF.Copy),
                         reads=[t_ps], writes=[t_Xb[ri]])
        for j in range(NJ):
            xr, xi = Xb[0][:, :, j], Xb[1][:, :, j]
            nr_, ni_ = Xb[0][:, :, j + 1], Xb[1][:, :, j + 1]
            tt_ = lambda o, t_o, a, b_, op, rd: S.op("dve", lambda e: e.tensor_tensor(out=o, in0=a, in1=b_, op=op), reads=rd, writes=[t_o])
            tt_(st[0][:], t_st[0], self.A8r[:], xr, ALU.mult, [self.t_A8, t_Xb[0]])
            tt_(st[1][:], t_st[1], self.A8i[:], xi, ALU.mult, [self.t_A8, t_Xb[1]])
            tt_(st[2][:], t_st[2], self.A8i[:], xr, ALU.mult, [self.t_A8, t_Xb[0]])
            tt_(st[3][:], t_st[3], self.A8r[:], xi, ALU.mult, [self.t_A8, t_Xb[1]])
            tt_(st[0][:], t_st[0], st[0][:], st[1][:], ALU.subtract, [t_st[0], t_st[1]])
            tt_(st[2][:], t_st[2], st[2][:], st[3][:], ALU.add, [t_st[2], t_st[3]])
            tt_(nr_, t_Xb[0], nr_, st[0][:], ALU.add, [t_Xb[0], t_st[0]])
            tt_(ni_, t_Xb[1], ni_, st[2][:], ALU.add, [t_Xb[1], t_st[2]])
        for ri in range(2):
            S.op("act", lambda e, ri=ri: e.activation(out=X16[ri][:], in_=Xb[ri][:, :, 0:NJ], func=AF.Copy), reads=[t_Xb[ri]], writes=[t_X16[ri]])
            if it < NT - 1:
                S.op("act", lambda e, ri=ri: e.activation(out=Xb[ri][:, :, 0:1], in_=Xb[ri][:, :, NJ:NJ + 1], func=AF.Copy), reads=[t_Xb[ri]], writes=[t_Xb[ri]])
            else:
                S.op("act", lambda e, ri=ri: e.activation(out=sstg[ri][:], in_=Xb[ri][:, :, NJ], func=AF.Copy), reads=[t_Xb[ri]], writes=[t_sstg[ri]])
                S.dma("sp", [lambda e, ri=ri: e.dma_start(out=s5out_d[ri], in_=sstg[ri][:])], t_sstg[ri], reads=[t_sstg[ri]])
        for c8 in range(8):
            wc, t_wc = self.load_wc(dr, c8)
            fir = wc[:, 0:1024].rearrange("p (k o) -> p k o", k=8)
            wo = wc[:, 1024:3072].rearrange("p (q r i o) -> p q r i o", q=4, r=8, i=2)
            ut = self.bufA[:, UO + c8, :]
            uv = ut.rearrange("p (j s) -> p j s", s=8)
            py, t_py = self.psum()
            pyv = py[:, 0:TT].rearrange("p (j s) -> p j s", s=8)
            rd = [t_wc, self.t_A[UO + c8]] + t_X16
            S.op("pe", lambda e, py=py, fir=fir, ut=ut: e.matmul(py[:, 0:TT], fir[:, 0, :], ut, start=True, stop=False), reads=rd, writes=[t_py], inc=False)
            for k in range(1, 8):
                S.op("pe", lambda e, pyv=pyv, fir=fir, uv=uv, k=k: e.matmul(pyv[:, :, k:8], fir[:, k, :], uv[:, :, 0:8 - k], start=False, stop=False),
                     reads=rd, writes=[t_py], inc=False)
            for q in range(4):
                for r in range(8):
                    for ri in range(2):
                        last = (q == 3 and r == 7 and ri == 1)
                        S.op("pe", lambda e, pyv=pyv, wo=wo, q=q, r=r, ri=ri, c8=c8, last=last: e.matmul(
                            pyv[q * 32:(q + 1) * 32, :, r], wo[:, q, r, ri, :], X16[ri][:, c8 * 4 + q, :], start=False, stop=last,
                            tile_position=(0, q * 32)), reads=rd, writes=[t_py], inc=last)
            b = c8 % 2
            S.op("act", lambda e, py=py, b=b: e.activation(out=yx[b][:], in_=py[:, 0:TT], func=AF.Square), reads=[t_py], writes=[t_yx[b]])
            S.op("dve", lambda e, b=b: e.tensor_scalar(out=yx[b][:], in0=yx[b][:], scalar1=0.044715, scalar2=1.0, op0=ALU.mult, op1=ALU.add),
                 reads=[t_yx[b]], writes=[t_yx[b]])
            S.op("dve", lambda e, py=py, b=b: e.tensor_tensor(out=yx[b][:], in0=yx[b][:], in1=py[:, 0:TT], op=ALU.mult), reads=[t_yx[b], t_py], writes=[t_yx[b]])
            S.op("act", lambda e, b=b: e.activation(out=yx[b][:], in_=yx[b][:], func=AF.Sigmoid, scale=1.5957691216057308), reads=[t_yx[b]], writes=[t_yx[b]])
            S.op("dve", lambda e, py=py, b=b: e.tensor_tensor(out=yg[b][:], in0=yx[b][:], in1=py[:, 0:TT], op=ALU.mult), reads=[t_yx[b], t_py], writes=[t_yg[b]])
            S.op("act", lambda e, b=b: e.activation(out=yb[b][:], in_=yg[b][:], func=AF.Copy), reads=[t_yg[b]], writes=[t_yb[b]])
            pg, t_pg = self.psum()
            self.mm_group(pg[:, 0:TT], t_pg, [(gblk[:, c8, :], yb[b][:])], reads=[t_gblk, t_yb[b]])
            S.op("act", lambda e, pg=pg, b=b: e.activation(out=yx[b][:], in_=pg[:, 0:TT], func=AF.Sigmoid), reads=[t_pg], writes=[t_yx[b]])
            S.op("dve", lambda e, b=b, c8=c8: e.tensor_tensor(out=self.bufA[:, 8 + c8, :], in0=yg[b][:], in1=yx[b][:], op=ALU.mult),
                 reads=[t_yg[b], t_yx[b]], writes=[self.t_A[8 + c8]])
        self.linear_resid(Wout, KC, self.t_A[0:KC], self.bufA, 0)
        self.store_h(h_out, tok0)
    return t_sstg


def _load_wz(self, dr, c8):
    i = self.wi
    self.wi = (i + 1) % NWB
    view = self.wb[i][:, 0:2048].rearrange("p (s i o) -> p s i o", s=8, i=2)
    t = self.t_wb[i]
    self.S.dma("sp", [lambda e: e.dma_start(out=self.wb[i][:, 0:2048], in_=dr["wz"][:, c8, :])], t, reads=[dr["t_wz"]], writes=[t])
    return view, t


def _load_wc(self, dr, c8):
    i = self.wi
    self.wi = (i + 1) % NWB
    view = self.wb[i][:, 0:3072]
    t = self.t_wb[i]
    self.S.dma("sp", [lambda e: e.dma_start(out=self.wb[i][:, 0:3072], in_=dr["wc"][:, c8, :])], t, reads=[dr["t_wc"]], writes=[t])
    return view, t


Prog.even_mixer = _even_mixer
Prog.load_wz = _load_wz
Prog.load_wc = _load_wc

EV_SMALL = dict(are_S=[P, 32], aim_S=[P, 32], ldt_S=[P, 32], bre_S=[P, 32, 16], bim_S=[P, 32, 16], cre_S=[P, 32, 16], cim_S=[P, 32, 16],
                are_T=[P, 8, 64], aim_T=[P, 8, 64], ldt_T=[P, 8, 64], bre_T=[P, 8, 64], bim_T=[P, 8, 64], mask01=[P, 2], ident=[P, P],
                dcol=[P, 8], gblk=[P, 8, P], convw=[P, 3, 8])


def build_even_launch():
    nc = bass.Bass("TRN2", target_bir_lowering=False)
    dt = lambda name, shape, kind="ExternalInput": nc.dram_tensor(name, shape, F32, kind=kind).ap()
    h_in = dt("h_in", [D, NTOK]); h_out = dt("h_out", [D, NTOK], "ExternalOutput")
    hhalo = dt("hhalo", [D, 2]); gm = dt("gm", [P, KC])
    s5in = dt("s5in", [2, P, 32]); s5out = dt("s5out", [2, P, 32], "ExternalOutput")
    Win = dt("Win", [D, 4096]); Wout = dt("Wout", [D, D])
    prm = {k: dt(k, shp) for k, shp in EV_SMALL.items()}
    dr = dict(wz=nc.dram_tensor("wz_scr", [P, 8, 2048], BF16, kind="Internal").ap(), t_wz=T("wz_scr"),
              wc=nc.dram_tensor("wc_scr", [P, 8, 3072], BF16, kind="Internal").ap(), t_wc=T("wc_scr"))
    with ExitStack() as st:
        pr = Prog(nc, st)
        g, t_g = pr.load_small("gm_s", gm, [P, KC])
        with pr.phase():
            t_Xb = pr.even_mixer(h_in, h_out, hhalo, s5in, s5out, Win, Wout, prm, dr, g, t_g, 0)
            pr.S.wait_all("sp", t_Xb)
        pr.finish(pr.t_hT)
    return nc


def build_final_launch():
    nc = bass.Bass("TRN2", target_bir_lowering=False)
    dt = lambda name, shape, kind="ExternalInput": nc.dram_tensor(name, shape, F32, kind=kind).ap()
    h_in = dt("h_in", [D, NTOK]); out = dt("out", [D, NTOK], "ExternalOutput"); gfin = dt("gfin", [P, KC])
    with ExitStack() as st:
        pr = Prog(nc, st)
        g, t_g = pr.load_small("gfin_s", gfin, [P, KC])
        with pr.phase():
            pr.final_norm(h_in, out, g, t_g, 0)
            pr.S.wait_all("sp", pr.t_fo)
        pr.finish([])
    return nc


NCORES = 8


def _run(nc, maps):
    res = run_bass_kernel_spmd(nc, maps, core_ids=list(range(NCORES)))
    return res.results


def kernel_unfused(**inp):
    z = {k: np.asarray(v) for k, v in inp.items()}
    x = z["x"].astype(np.float32)
    B = x.shape[0]
    cores = [(b, hf) for b in range(B) for hf in range(2)]
    h = [np.ascontiguousarray(x[b, hf * NTOK:(hf + 1) * NTOK, :].T) for (b, hf) in cores]

    def halo(w):
        out = []
        for ci, (b, hf) in enumerate(cores):
            if hf == 0:
                out.append(np.zeros((D, w), np.float32))
            else:
                out.append(np.ascontiguousarray(h[ci - 1][:, -w:]))
        return out
    oh, maskT = swa_consts()
    nc_x = build_xattn_launch()
    nc_f = build_ffn_launch()
    nc_e = build_even_launch()
    nc_s = build_swa_launch()
    for l in range(4):
        i = l // 2
        if l % 2 == 0:
            hp = ev_host_params(z, i)
            hal = halo(2)
            s5 = [np.zeros((2, P, 32), np.float32) for _ in cores]
            for rep in range(2):
                maps = [dict(h_in=h[ci], hhalo=hal[ci], gm=pk(z["norm_mix"][l]), s5in=s5[ci], Win=z["ev_w_in"][i], Wout=z["ev_w_out"][i], **hp)
                        for ci in range(len(cores))]
                r = _run(nc_e, maps)
                if rep == 0:
                    s5 = [np.zeros((2, P, 32), np.float32) if hf == 0 else r[ci - 1]["s5out"] for ci, (b, hf) in enumerate(cores)]
            h = [r[ci]["h_out"] for ci in range(len(cores))]
        else:
            hp = swa_host_params(z, i)
            hal = halo(128)
            maps = [dict(h_in=h[ci], hhalo=hal[ci], hmask=(np.full((P, 1), -30000.0, np.float32) if hf == 0 else np.zeros((P, 1), np.float32)),
                         gm=pk(z["norm_mix"][l]), oh=oh, maskT=maskT, relb=hp["relb"], sinkrow=hp["sinkrow"], bq=hp["bq"], bkd=hp["bkd"], bvd=hp["bvd"],
                         Wq=hp["Wq"], Wkd=hp["Wkd"], Wvd=hp["Wvd"], Wo=z["od_w_out"][i]) for ci, (b, hf) in enumerate(cores)]
            r = _run(nc_s, maps)
            h = [r[ci]["h_out"] for ci in range(len(cores))]
        maps = [dict(h_in=h[ci], memT=np.ascontiguousarray(z["mem"][b].T), gmem=pk(z["norm_mem"]), gx=pk(z["norm_xattn"][l]),
                     Wq=z["xa_w_q"][l], Wkv=z["xa_w_kv"][l], Wo=z["xa_w_o"][l]) for ci, (b, hf) in enumerate(cores)]
        r = _run(nc_x, maps)
        h = [r[ci]["h_out"] for ci in range(len(cores))]
        hal = halo(2)
        maps = [dict(h_in=h[ci], hhalo=hal[ci], gf=pk(z["norm_ffn"][l]), cw=pk(z["ff_conv_w"][l]), cb=pk(z["ff_conv_b"][l]),
                     Wg=z["ff_w_gate"][l], Wu=z["ff_w_up"][l], Wd=z["ff_w_down"][l]) for ci in range(len(cores))]
        r = _run(nc_f, maps)
        h = [r[ci]["h_out"] for ci in range(len(cores))]
    nc_n = build_final_launch()
    r = _run(nc_n, [dict(h_in=h[ci], gfin=pk(z["norm_final"])) for ci in range(len(cores))])
    out = np.empty((B, 2 * NTOK, D), np.float32)
    for ci, (b, hf) in enumerate(cores):
        out[b, hf * NTOK:(hf + 1) * NTOK, :] = r[ci]["out"].T
    return out


def kernel(**inp):
    return kernel_unfused(**inp)
```

```python
import math
from contextlib import ExitStack

import numpy as np
import concourse.bass as bass
import concourse.mybir as mybir
from concourse.bass_utils import run_bass_kernel_spmd

F32 = mybir.dt.float32
BF16 = mybir.dt.bfloat16
AF = mybir.ActivationFunctionType
ALU = mybir.AluOpType

P = 128
D = 2048
KC = D // P
NTOK = 2048
TT = 512
NT = NTOK // TT
N_MEM = 256
D_FF = 5632
FC = D_FF // P
RMS_EPS = 1e-5


class T:
    __slots__ = ("name", "w", "r", "sem", "semval", "last_dma")

    def __init__(self, name):
        self.name = name
        self.w = None
        self.r = {}
        self.sem = None
        self.semval = 0
        self.last_dma = None


ENGS = ("pe", "act", "dve", "pool", "sp")


class Sched:
    def __init__(self, nc, stack):
        self.nc = nc
        self.stack = stack
        self.q = {e: [] for e in ENGS}
        self.cnt = {e: 0 for e in ENGS}
        self.waited = {e: {} for e in ENGS}
        self.semh = {}
        for e in ENGS:
            self.semh[e] = stack.enter_context(nc.semaphore("sem_" + e))
        self.ndma_sem = 0
        self.n_instr = 0
        self.dmaval = {}
        self.gd = [stack.enter_context(nc.sbuf_tensor("gd%d" % i, [P, 1], F32)) for i in range(3)]
        self.q["dve"].append(lambda e: e.memset(self.gd[2][:], 0.0))

    def barrier(self):
        cur = {e: self.cnt[e] for e in ENGS if self.cnt[e] > 0}
        cur.update(self.dmaval)
        for e in ENGS:
            self._need(e, {k: v for k, v in cur.items() if k != e})

    NDSEM = 40

    def _tile_sem(self, t):
        if t.sem is None:
            i = self.ndma_sem % self.NDSEM
            key = "dsem%d" % i
            self.ndma_sem += 1
            if key not in self.semh:
                self.semh[key] = self.stack.enter_context(self.nc.semaphore(key))
            t.sem = key
        return t.sem

    def _need(self, eng, needs):
        for key, val in needs.items():
            if key == "pe" and eng == "pe":
                continue
            if self.waited[eng].get(key, 0) >= val:
                continue
            self.waited[eng][key] = val
            h = self.semh[key]
            self.q[eng].append(lambda e, h=h, val=val: e.wait_ge(h, val))

    def _deps(self, reads, writes):
        needs = {}

        def add(m):
            if m is not None:
                if needs.get(m[0], 0) < m[1]:
                    needs[m[0]] = m[1]
        for t in reads:
            add(t.w)
        for t in writes:
            add(t.w)
            for m in t.r.items():
                add(m)
        return needs

    def op(self, eng, fn, reads=(), writes=(), inc=True, guard=False):
        needs = self._deps(reads, writes)
        self._need(eng, needs)
        if guard and eng in ("dve", "act") and inc:
            self.q[eng].append(lambda e, fn=fn: fn(e))
            self.cnt[eng] += 1
            h = self.semh[eng]
            g = self.gd
            if eng == "dve":
                self.q[eng].append(lambda e, h=h, g=g: e.memset(g[0][:], 0.0).then_inc(h, 1))
            else:
                self.q[eng].append(lambda e, h=h, g=g: e.activation(out=g[1][:], in_=g[2][:], func=AF.Copy).then_inc(h, 1))
            mark = (eng, self.cnt[eng])
            self._mark(reads, writes, mark)
            self.n_instr += 2
            return
        if inc:
            self.cnt[eng] += 1
            h = self.semh[eng]
            self.q[eng].append(lambda e, fn=fn, h=h: fn(e).then_inc(h, 1))
            mark = (eng, self.cnt[eng])
        else:
            self.q[eng].append(lambda e, fn=fn: fn(e))
            mark = (eng, self.cnt[eng] + 1)
        self._mark(reads, writes, mark)
        self.n_instr += 1

    def dma(self, eng, fns, owner, reads=(), writes=(), step=16):
        key = self._tile_sem(owner)
        needs = self._deps(reads, writes)
        cur = self.dmaval.get(key, 0)
        if cur > 0 and needs.get(key, 0) < cur:
            needs[key] = cur
        self._need(eng, needs)
        h = self.semh[key]
        for fn in fns:
            cur += step
            self.q[eng].append(lambda e, fn=fn, h=h: fn(e).then_inc(h, step))
        mark = (key, cur)
        self.dmaval[key] = cur
        self._mark(reads, writes, mark)
        self.n_instr += len(fns)

    def cc(self, eng, fn, reads=(), writes=()):
        key = "ccsem%d" % len([k for k in self.semh if k.startswith("ccsem")])
        self.semh[key] = self.stack.enter_context(self.nc.semaphore(key))
        needs = self._deps(reads, writes)
        self._need(eng, needs)
        h = self.semh[key]
        self.q[eng].append(lambda e, fn=fn, h=h: fn(e).then_inc(h, 1))
        mark = (key, 1)
        self.dmaval[key] = 1
        self._mark(reads, writes, mark)
        self.n_instr += 1

    @staticmethod
    def _mark(reads, writes, mark):
        for t in reads:
            if t.r.get(mark[0], 0) < mark[1]:
                t.r[mark[0]] = mark[1]
        for t in writes:
            t.w = mark
            t.r = {}

    def wait_all(self, eng, tiles):
        needs = {}
        for t in tiles:
            for m in ([t.w] if t.w else []) + list(t.r.items()):
                if needs.get(m[0], 0) < m[1]:
                    needs[m[0]] = m[1]
        self._need(eng, needs)

    def emit(self):
        nc = self.nc
        if not any(self.q[e] for e in ENGS):
            return
        qs = {e: self.q[e] for e in ENGS}
        self.q = {e: [] for e in ENGS}
        self._emit_block(nc, qs)

    def _emit_block(self, nc, qs):
        self_q = qs
        with nc.Block() as block:
            @block.tensor
            def _(e):
                for f in self_q["pe"]:
                    f(e)

            @block.scalar
            def _(e):
                for f in self_q["act"]:
                    f(e)

            @block.vector
            def _(e):
                for f in self_q["dve"]:
                    f(e)

            @block.gpsimd
            def _(e):
                for f in self_q["pool"]:
                    f(e)

            @block.sync
            def _(e):
                for f in self_q["sp"]:
                    f(e)


WSLOT = 5632
NWB = 4
NPS = 8


class Prog:
    def __init__(self, nc, stack):
        self.nc = nc
        self.st = stack
        self.S = Sched(nc, stack)
        self.pstack = None
        self.uid = 0

        def sb(name, shape, dt):
            self.uid += 1
            stk = self.pstack if self.pstack is not None else stack
            return stk.enter_context(nc.sbuf_tensor("%s_%d" % (name, self.uid), shape, dt))
        self.sb = sb
        self.hT = sb("hT", [P, KC, TT], F32)
        self.t_hT = [T("hT%d" % k) for k in range(KC)]
        self.hn = sb("hn", [P, KC, TT], BF16)
        self.t_hn = [T("hn%d" % k) for k in range(KC)]
        self.wb = [sb("wb%d" % i, [P, WSLOT], BF16) for i in range(NWB)]
        self.t_wb = [T("wb%d" % i) for i in range(NWB)]
        self.wi = 0
        self.ps = [stack.enter_context(nc.psum_tensor("ps%d" % i, [P, 512], F32)) for i in range(NPS)]
        self.t_ps = [T("ps%d" % i) for i in range(NPS)]
        self.pi = 0
        self.sq = [sb("sq%d" % i, [P, TT], BF16) for i in range(2)]
        self.t_sq = [T("sq0"), T("sq1")]
        self.rstd = sb("rstd", [P, TT], F32)
        self.t_rstd = T("rstd")
        self.rtmp = sb("rtmp", [P, TT], F32)
        self.t_rtmp = T("rtmp")
        self.ones = sb("ones", [P, P], BF16)
        self.t_ones = T("ones")
        self.S.op("dve", lambda e: e.memset(self.ones[:], 1.0), writes=[self.t_ones])
        self.epsc = sb("epsc", [P, 1], F32)
        self.t_eps = T("eps")
        self.S.op("dve", lambda e: e.memset(self.epsc[:], RMS_EPS), writes=[self.t_eps])
        self.evi = 0

    def phase(self):
        prog = self

        class _Ph:
            def __enter__(s):
                prog.S.barrier()
                prog.S.emit()
                s.es = ExitStack()
                s.es.__enter__()
                s.prev = prog.pstack
                prog.pstack = s.es
                return s

            def __exit__(s, *a):
                prog.S.barrier()
                prog.S.emit()
                prog.pstack = s.prev
                return s.es.__exit__(*a)
        return _Ph()

    DBG = False

    def dbg(self, name, ap, tiles, shape, dt=F32):
        if not self.DBG:
            return
        d = self.nc.dram_tensor("dbg_" + name, list(shape), dt, kind="ExternalOutput").ap()
        t = T("dbg_" + name)
        self.S.dma("sp", [lambda e: e.dma_start(out=d, in_=ap)], t, reads=list(tiles))
        self.dbg_tiles = getattr(self, "dbg_tiles", []) + [t]

    def alloc_A(self, n):
        self.bufA = self.sb("bufA", [P, n, TT], BF16)
        self.t_A = [T("A%d" % k) for k in range(n)]

    def psum(self):
        i = self.pi
        self.pi = (i + 1) % NPS
        return self.ps[i], self.t_ps[i]

    def load_w(self, src3, k, m):
        i = self.wi
        self.wi = (i + 1) % NWB
        view = self.wb[i][:, 0:k * m].rearrange("p (k m) -> p k m", k=k)
        t = self.t_wb[i]
        if k > 22:
            h = k // 2
            fns = [lambda e: e.dma_start(out=view[:, 0:h, :], in_=src3[:, 0:h, :]),
                   lambda e: e.dma_start(out=view[:, h:, :], in_=src3[:, h:, :])]
        else:
            fns = [lambda e: e.dma_start(out=view, in_=src3)]
        self.S.dma("pool", fns, t, writes=[t])
        return view, t

    def mm_group(self, out_ap, t_out, pairs, reads, tp=None):
        n = len(pairs)
        for i, (l, r) in enumerate(pairs):
            if tp is None:
                self.S.op("pe", lambda e, l=l, r=r, i=i: e.matmul(out_ap, l, r, start=(i == 0), stop=(i == n - 1)),
                          reads=reads, writes=[t_out], inc=(i == n - 1))
            else:
                self.S.op("pe", lambda e, l=l, r=r, i=i: e.matmul(out_ap, l, r, start=(i == 0), stop=(i == n - 1), tile_position=tp),
                          reads=reads, writes=[t_out], inc=(i == n - 1))

    def load_small(self, name, dram_ap, shape, dt=F32):
        t = self.sb(name, shape, dt)
        tt = T(name)
        self.S.dma("sp", [lambda e: e.dma_start(out=t[:], in_=dram_ap)], tt, writes=[tt])
        return t, tt

    def rmsnorm(self, src, t_src, gain, t_gain, gcol, dst, t_dst, w, nk=KC):
        S = self.S
        ps, t_ps = self.psum()
        for k in range(nk):
            b = k % 2
            S.op("act", lambda e, k=k, b=b: e.activation(out=self.sq[b][:, 0:w], in_=src[:, k, 0:w], func=AF.Square),
                 reads=[t_src[k]], writes=[self.t_sq[b]])
            S.op("pe", lambda e, k=k, b=b: e.matmul(ps[:, 0:w], self.ones[:], self.sq[b][:, 0:w], start=(k == 0), stop=(k == nk - 1)),
                 reads=[self.t_sq[b], self.t_ones], writes=[t_ps], inc=True)
        S.op("act", lambda e: e.activation(out=self.rtmp[:, 0:w], in_=ps[:, 0:w], func=AF.Sqrt, bias=self.epsc[:], scale=1.0 / (nk * P)),
             reads=[t_ps, self.t_eps], writes=[self.t_rtmp])
        S.op("dve", lambda e: e.reciprocal(out=self.rstd[:, 0:w], in_=self.rtmp[:, 0:w]), reads=[self.t_rtmp], writes=[self.t_rstd])
        if getattr(self, "dbg_norm", False):
            self.dbg_norm = False
            self.dbg("n_hT0", src[:, 0, 0:w], [t_src[0]], [P, w], F32)
            self.dbg("n_hT15", src[:, 15, 0:w], [t_src[15]], [P, w], F32)
            self.dbg("n_rtmp", self.rtmp[:, 0:w], [self.t_rtmp], [P, w], F32)
            self.dbg("n_rstd", self.rstd[:, 0:w], [self.t_rstd], [P, w], F32)
        for k in range(nk):
            S.op("dve", lambda e, k=k: e.scalar_tensor_tensor(out=dst[:, k, 0:w], in0=src[:, k, 0:w], scalar=gain[:, gcol + k:gcol + k + 1],
                                                              in1=self.rstd[:, 0:w], op0=ALU.mult, op1=ALU.mult),
                 reads=[t_src[k], t_gain, self.t_rstd], writes=[t_dst[k]])

    def load_h(self, h_dram, tok0):
        S = self.S
        src = h_dram[:, tok0:tok0 + TT].rearrange("(k p) t -> p k t", p=P)
        for g in range(4):
            S.dma("sp", [lambda e, g=g: e.dma_start(out=self.hT[:, 4 * g:4 * g + 4, :], in_=src[:, 4 * g:4 * g + 4, :])],
                  self.t_hT[4 * g], writes=self.t_hT[4 * g:4 * g + 4])

    def store_h(self, h_dram, tok0):
        S = self.S
        dst = h_dram[:, tok0:tok0 + TT].rearrange("(k p) t -> p k t", p=P)
        for g in range(4):
            S.dma("sp", [lambda e, g=g: e.dma_start(out=dst[:, 4 * g:4 * g + 4, :], in_=self.hT[:, 4 * g:4 * g + 4, :])],
                  self.t_hT[4 * g], reads=self.t_hT[4 * g:4 * g + 4])

    def linear_resid(self, W, kin, t_in_list, in_buf, in_off):
        S = self.S
        mb = 256
        per = mb // P
        nh = 1 if kin <= 22 else 2
        kh = kin // nh
        for blk in range(D // mb):
            ws = []
            for hf in range(nh):
                ws.append(self.load_w(W[hf * kh * P:(hf + 1) * kh * P, blk * mb:(blk + 1) * mb].rearrange("(k p) m -> p k m", p=P), kh, mb))
            for m in range(per):
                ps, t_ps = self.psum()
                self.mm_group(ps[:, 0:TT], t_ps,
                              [(ws[k // kh][0][:, k % kh, m * P:(m + 1) * P], in_buf[:, in_off + k, :]) for k in range(kin)],
                              reads=[x[1] for x in ws] + t_in_list)
                c = blk * per + m
                S.op("dve", lambda e, c=c, ps=ps: e.tensor_tensor(out=self.hT[:, c, :], in0=self.hT[:, c, :], in1=ps[:, 0:TT], op=ALU.add),
                     reads=[t_ps, self.t_hT[c]], writes=[self.t_hT[c]])

    def linear_to(self, W, col0, ncols, t_in_list, in_buf, out_buf, out_off, t_out_list, evac=None):
        S = self.S
        mb = 256
        for blk in range(ncols // mb):
            w, t_w = self.load_w(W[:, col0 + blk * mb:col0 + (blk + 1) * mb].rearrange("(k p) m -> p k m", p=P), KC, mb)
            for m in range(2):
                ps, t_ps = self.psum()
                self.mm_group(ps[:, 0:TT], t_ps, [(w[:, k, m * P:(m + 1) * P], in_buf[:, k, :]) for k in range(KC)],
                              reads=[t_w] + t_in_list)
                c = blk * 2 + m
                if evac is not None:
                    evac(c, ps, t_ps)
                else:
                    S.op("act", lambda e, c=c, ps=ps: e.activation(out=out_buf[:, out_off + c, :], in_=ps[:, 0:TT], func=AF.Copy),
                         reads=[t_ps], writes=[t_out_list[out_off + c]])

    def prep_mem(self, memT_d, gmem_d):
        S = self.S
        self.memn = self.sb("memn", [P, KC, N_MEM], BF16)
        self.t_memn = [T("memn%d" % k) for k in range(KC)]
        gm, t_gm = self.load_small("gmem_s", gmem_d, [P, KC])
        src = memT_d.rearrange("(k p) t -> p k t", p=P)
        S.dma("sp", [lambda e: e.dma_start(out=self.hT[:, :, 0:N_MEM], in_=src)], self.t_hT[0], writes=self.t_hT)
        self.rmsnorm(self.hT, self.t_hT, gm, t_gm, 0, self.memn, self.t_memn, N_MEM)

    def prep_kv(self, Wkv):
        S = self.S
        for blk in range(8):
            w, t_w = self.load_w(Wkv[:, blk * 256:(blk + 1) * 256].rearrange("(k p) m -> p k m", p=P), KC, 256)
            for m in range(2):
                ps, t_ps = self.psum()
                self.mm_group(ps[:, 0:N_MEM], t_ps, [(w[:, k, m * P:(m + 1) * P], self.memn[:, k, :]) for k in range(KC)],
                              reads=[t_w] + self.t_memn)
                c = blk * 2 + m
                S.op("act", lambda e, c=c, ps=ps: e.activation(out=self.kT[:, c, :], in_=ps[:, 0:N_MEM], func=AF.Copy),
                     reads=[t_ps], writes=[self.t_kT[c]])
        for blk in range(8):
            w, t_w = self.load_w(Wkv[:, D + blk * 256:D + (blk + 1) * 256].rearrange("(k p) m -> p k m", p=P), KC, 256)
            for mc in range(2):
                ps, t_ps = self.psum()
                self.mm_group(ps[:, 0:256], t_ps, [(self.memn[:, k, mc * P:(mc + 1) * P], w[:, k, :]) for k in range(KC)],
                              reads=[t_w] + self.t_memn)
                S.op("act", lambda e, mc=mc, blk=blk, ps=ps: e.activation(out=self.vv[:, mc, blk * 256:(blk + 1) * 256], in_=ps[:, 0:256], func=AF.Copy),
                     reads=[t_ps], writes=[self.t_vv[mc]])

    def xattn(self, h_in, h_out, Wq, Wkv, Wo, gain, t_gain, gcol):
        S = self.S
        self.alloc_A(2 * KC)
        self.kT = self.sb("kT", [P, KC, N_MEM], BF16)
        self.t_kT = [T("kT%d" % k) for k in range(KC)]
        self.vv = self.sb("vv", [P, 2, D], BF16)
        self.t_vv = [T("vv0"), T("vv1")]
        self.pT = self.sb("pT", [P, 2, TT], BF16)
        self.t_pT = [T("pT0"), T("pT1")]
        self.rden = self.sb("rden", [P, TT], F32)
        self.t_rden = T("rden")
        self.prep_kv(Wkv)
        qoff, ooff = 0, KC
        scale = 1.0 / math.sqrt(512.0)
        for it in range(NT):
            tok0 = it * TT
            self.load_h(h_in, tok0)
            self.dbg_norm = (it == 0)
            self.rmsnorm(self.hT, self.t_hT, gain, t_gain, gcol, self.hn, self.t_hn, TT)
            self.linear_to(Wq, 0, D, self.t_hn, self.hn, self.bufA, qoff, self.t_A)
            for hd in range(4):
                for mc in range(2):
                    ps, t_ps = self.psum()
                    self.mm_group(ps[:, 0:TT], t_ps,
                                  [(self.kT[:, hd * 4 + j, mc * P:(mc + 1) * P], self.bufA[:, qoff + hd * 4 + j, :]) for j in range(4)],
                                  reads=self.t_kT[hd * 4:hd * 4 + 4] + self.t_A[qoff + hd * 4:qoff + hd * 4 + 4])
                    S.op("act", lambda e, mc=mc, ps=ps: e.activation(out=self.pT[:, mc, :], in_=ps[:, 0:TT], func=AF.Exp, scale=scale),
                         reads=[t_ps], writes=[self.t_pT[mc]])
                ps, t_ps = self.psum()
                self.mm_group(ps[:, 0:TT], t_ps, [(self.ones[:], self.pT[:, mc, :]) for mc in range(2)],
                              reads=[self.t_ones] + self.t_pT)
                S.op("dve", lambda e, ps=ps: e.reciprocal(out=self.rden[:], in_=ps[:, 0:TT]), reads=[t_ps], writes=[self.t_rden])
                for j in range(4):
                    ps, t_ps = self.psum()
                    f0 = hd * 512 + j * P
                    self.mm_group(ps[:, 0:TT], t_ps, [(self.vv[:, mc, f0:f0 + P], self.pT[:, mc, :]) for mc in range(2)],
                                  reads=self.t_vv + self.t_pT)
                    c = ooff + hd * 4 + j
                    S.op("dve", lambda e, c=c, ps=ps: e.tensor_tensor(out=self.bufA[:, c, :], in0=ps[:, 0:TT], in1=self.rden[:], op=ALU.mult),
                         reads=[t_ps, self.t_rden], writes=[self.t_A[c]])
                if it == 0 and hd == 0:
                    self.dbg("hn0", self.hn[:, 0, :], [self.t_hn[0]], [P, TT], BF16)
                    self.dbg("q0", self.bufA[:, 0, :], [self.t_A[0]], [P, TT], BF16)
                    self.dbg("kT0", self.kT[:, 0, :], [self.t_kT[0]], [P, N_MEM], BF16)
                    self.dbg("vv0", self.vv[:, 0, 0:512], [self.t_vv[0]], [P, 512], BF16)
                    self.dbg("memn0", self.memn[:, 0, :], [self.t_memn[0]], [P, N_MEM], BF16)
                    self.dbg("pT0", self.pT[:, 0, :], [self.t_pT[0]], [P, TT], BF16)
                    self.dbg("rden", self.rden[:], [self.t_rden], [P, TT], F32)
                    self.dbg("o0", self.bufA[:, ooff, :], [self.t_A[ooff]], [P, TT], BF16)
            self.linear_resid(Wo, KC, self.t_A[ooff:ooff + KC], self.bufA, ooff)
            self.store_h(h_out, tok0)

    def halo_load(self, dst, t_dst, src_ap, w, flag, rd=()):
        S = self.S
        S.dma("sp", [lambda e: e.dma_start(out=dst[:, :, 0:w], in_=src_ap.rearrange("(k p) t -> p k t", p=P))], t_dst[0], reads=list(rd), writes=t_dst)
        if flag is not None:
            S.op("dve", lambda e: e.tensor_scalar(out=dst[:, :, 0:w], in0=dst[:, :, 0:w], scalar1=flag[0][:, 0:1], scalar2=None, op0=ALU.mult),
                 reads=t_dst + [flag[1]], writes=t_dst)

    def ffn(self, h_in, h_out, hhalo_d, Wg, Wu, Wd, gain, t_gain, gcol, cw, t_cw, cb, t_cb, flag=None, hrd=()):
        S = self.S
        self.alloc_A(FC)
        self.gfull = [self.sb("gfull%d" % i, [P, TT + 2], F32) for i in range(2)]
        self.t_gfull = [T("gfull0"), T("gfull1")]
        self.ghalo = self.sb("ghalo", [P, FC, 2], F32)
        self.t_ghalo = [T("ghalo%d" % c) for c in range(FC)]
        self.c1 = [self.sb("c1_%d" % i, [P, TT], F32) for i in range(2)]
        self.t_c1 = [T("c1_0"), T("c1_1")]
        self.hhT = self.sb("hhT", [P, KC, 2], F32)
        self.t_hhT = [T("hhT%d" % k) for k in range(KC)]
        self.hhn = self.sb("hhn", [P, KC, 2], BF16)
        self.t_hhn = [T("hhn%d" % k) for k in range(KC)]
        self.halo_load(self.hhT, self.t_hhT, hhalo_d, 2, flag, hrd)
        self.rmsnorm(self.hhT, self.t_hhT, gain, t_gain, gcol, self.hhn, self.t_hhn, 2)
        for it in range(NT):
            tok0 = it * TT
            self.load_h(h_in, tok0)
            self.rmsnorm(self.hT, self.t_hT, gain, t_gain, gcol, self.hn, self.t_hn, TT)
            for blk in range(FC // 2):
                wg, t_wg = self.load_w(Wg[:, blk * 256:(blk + 1) * 256].rearrange("(k p) m -> p k m", p=P), KC, 256)
                wu, t_wu = self.load_w(Wu[:, blk * 256:(blk + 1) * 256].rearrange("(k p) m -> p k m", p=P), KC, 256)
                for m in range(2):
                    c = blk * 2 + m
                    b = c % 2
                    gf, t_gf = self.gfull[b], self.t_gfull[b]
                    if it == 0:
                        psh, t_psh = self.psum()
                        self.mm_group(psh[:, 0:2], t_psh, [(wg[:, k, m * P:(m + 1) * P], self.hhn[:, k, :]) for k in range(KC)],
                                      reads=[t_wg] + self.t_hhn)
                        S.op("act", lambda e, gf=gf, psh=psh: e.activation(out=gf[:, 0:2], in_=psh[:, 0:2], func=AF.Copy),
                             reads=[t_psh], writes=[t_gf])
                    else:
                        S.op("act", lambda e, gf=gf, c=c: e.activation(out=gf[:, 0:2], in_=self.ghalo[:, c, :], func=AF.Copy),
                             reads=[self.t_ghalo[c]], writes=[t_gf])
                    psg, t_psg = self.psum()
                    self.mm_group(psg[:, 0:TT], t_psg, [(wg[:, k, m * P:(m + 1) * P], self.hn[:, k, :]) for k in range(KC)],
                                  reads=[t_wg] + self.t_hn)
                    psu, t_psu = self.psum()
                    self.mm_group(psu[:, 0:TT], t_psu, [(wu[:, k, m * P:(m + 1) * P], self.hn[:, k, :]) for k in range(KC)],
                                  reads=[t_wu] + self.t_hn)
                    S.op("act", lambda e, gf=gf, psg=psg: e.activation(out=gf[:, 2:TT + 2], in_=psg[:, 0:TT], func=AF.Copy),
                         reads=[t_psg], writes=[t_gf])
                    if it < NT - 1:
                        S.op("act", lambda e, gf=gf, c=c: e.activation(out=self.ghalo[:, c, :], in_=gf[:, TT:TT + 2], func=AF.Copy),
                             reads=[t_gf], writes=[self.t_ghalo[c]])
                    c1, t_c1 = self.c1[b], self.t_c1[b]
                    S.op("act", lambda e, gf=gf, c=c, c1=c1: e.activation(out=c1[:], in_=gf[:, 2:TT + 2], func=AF.Identity,
                                                                        bias=cb[:, c:c + 1], scale=cw[:, 2, c:c + 1]),
                         reads=[t_gf, t_cw, t_cb], writes=[t_c1])
                    S.op("dve", lambda e, gf=gf, c=c, c1=c1: e.scalar_tensor_tensor(out=c1[:], in0=gf[:, 1:TT + 1], scalar=cw[:, 1, c:c + 1], in1=c1[:],
                                                                                  op0=ALU.mult, op1=ALU.add),
                         reads=[t_gf, t_cw, t_c1], writes=[t_c1])
                    S.op("dve", lambda e, gf=gf, c=c, c1=c1: e.scalar_tensor_tensor(out=c1[:], in0=gf[:, 0:TT], scalar=cw[:, 0, c:c + 1], in1=c1[:],
                                                                                  op0=ALU.mult, op1=ALU.add),
                         reads=[t_gf, t_cw, t_c1], writes=[t_c1])
                    S.op("act", lambda e, c1=c1: e.activation(out=c1[:], in_=c1[:], func=AF.Silu), reads=[t_c1], writes=[t_c1])
                    S.op("dve", lambda e, c=c, c1=c1, psu=psu: e.tensor_tensor(out=self.bufA[:, c, :], in0=c1[:], in1=psu[:, 0:TT], op=ALU.mult),
                         reads=[t_c1, t_psu], writes=[self.t_A[c]])
            self.linear_resid(Wd, FC, self.t_A, self.bufA, 0)
            self.store_h(h_out, tok0)

    def final_norm(self, h_in, out_d, gain, t_gain, gcol):
        S = self.S
        self.fo = self.sb("fo", [P, KC, TT], F32)
        self.t_fo = [T("fo%d" % k) for k in range(KC)]
        for it in range(NT):
            tok0 = it * TT
            self.load_h(h_in, tok0)
            self.rmsnorm(self.hT, self.t_hT, gain, t_gain, gcol, self.fo, self.t_fo, TT)
            dst = out_d[:, tok0:tok0 + TT].rearrange("(k p) t -> p k t", p=P)
            S.dma("sp", [lambda e, dst=dst: e.dma_start(out=dst, in_=self.fo[:])], self.t_fo[0], reads=self.t_fo)

    def finish(self, out_tiles):
        self.S.wait_all("sp", out_tiles)
        self.S.emit()


def pk(v):
    v = np.asarray(v, dtype=np.float32)
    lead = v.shape[:-1]
    n = v.shape[-1] // P
    a = v.reshape(lead + (n, P))
    a = np.moveaxis(a, -1, 0)
    return np.ascontiguousarray(a)


def build_xattn_launch():
    nc = bass.Bass("TRN2", target_bir_lowering=False)
    dt = lambda name, shape, kind="ExternalInput": nc.dram_tensor(name, shape, F32, kind=kind).ap()
    h_in = dt("h_in", [D, NTOK]); h_out = dt("h_out", [D, NTOK], "ExternalOutput")
    memT = dt("memT", [D, N_MEM]); gmem = dt("gmem", [P, KC]); gx = dt("gx", [P, KC])
    Wq = dt("Wq", [D, D]); Wkv = dt("Wkv", [D, 2 * D]); Wo = dt("Wo", [D, D])
    with ExitStack() as st:
        pr = Prog(nc, st)
        g, t_g = pr.load_small("gx_s", gx, [P, KC])
        pr.prep_mem(memT, gmem)
        with pr.phase():
            pr.xattn(h_in, h_out, Wq, Wkv, Wo, g, t_g, 0)
        pr.finish(pr.t_hT)
    return nc


def build_ffn_launch(final=False):
    nc = bass.Bass("TRN2", target_bir_lowering=False)
    dt = lambda name, shape, kind="ExternalInput": nc.dram_tensor(name, shape, F32, kind=kind).ap()
    h_in = dt("h_in", [D, NTOK]); h_out = dt("h_out", [D, NTOK], "ExternalOutput")
    hhalo = dt("hhalo", [D, 2]); gf = dt("gf", [P, KC])
    cw = dt("cw", [P, 3, FC]); cb = dt("cb", [P, FC])
    Wg = dt("Wg", [D, D_FF]); Wu = dt("Wu", [D, D_FF]); Wd = dt("Wd", [D_FF, D])
    with ExitStack() as st:
        pr = Prog(nc, st)
        g, t_g = pr.load_small("gf_s", gf, [P, KC])
        cws, t_cw = pr.load_small("cw_s", cw, [P, 3, FC])
        cbs, t_cb = pr.load_small("cb_s", cb, [P, FC])
        with pr.phase():
            pr.ffn(h_in, h_out, hhalo, Wg, Wu, Wd, g, t_g, 0, cws, t_cw, cbs, t_cb)
        pr.finish(pr.t_hT)
    return nc


def t5_bucket_np(rel):
    n = np.maximum(rel, 0)
    nf = np.maximum(n, 16).astype(np.float32)
    large = 16 + (np.log(nf / np.float32(16)) / np.float32(math.log(128 / 16)) * np.float32(16)).astype(np.int32)
    large = np.minimum(large, 31)
    return np.where(n < 16, n, large)


def swa_consts():
    k = np.arange(128)[:, None, None]
    j = np.arange(2)[None, :, None]
    q = np.arange(128)[None, None, :]
    rel = 128 + q - j * 128 - k
    valid = (rel >= 0) & (rel < 128)
    bk = t5_bucket_np(rel)
    oh = np.zeros((128, 32, 2, 128), np.float32)
    for b in range(32):
        oh[:, b] = ((bk == b) & valid).astype(np.float32)
    maskT = np.where(valid, 0.0, -30000.0).astype(np.float32)
    return oh, maskT


def _swa_setup(self, oh_d, mask_d, relb_d, sinkrow_d):
    S = self.S
    self.biasT = self.sb("biasT", [P, 2, 4, 4, 256], F32)
    self.t_biasT = T("biasT")
    self.esrow = self.sb("esrow", [1, 32 * 128], BF16)
    self.t_esrow = T("esrow")
    with self.phase():
        oh = self.sb("oh_s", [P, 8, 256], F32)
        t_oh = T("oh")
        mk, t_mk = self.load_small("mask_s", mask_d.rearrange("p j q -> p (j q)"), [P, 256])
        rb, t_rb = self.load_small("relb_s", relb_d, [P, 32, 32])
        ohv = oh_d.rearrange("p b j q -> p b (j q)")
        for bg in range(4):
            S.dma("sp", [lambda e, bg=bg: e.dma_start(out=oh[:], in_=ohv[:, bg * 8:(bg + 1) * 8, :])], t_oh, writes=[t_oh])
            for h in range(32):
                kh, g = h // 8, h % 8
                par, i = g % 2, g // 2
                dst = self.biasT[:, par, kh, i, :]
                if bg == 0:
                    S.op("dve", lambda e, dst=dst: e.tensor_copy(out=dst, in_=mk[:]), reads=[t_mk], writes=[self.t_biasT])
                for b8 in range(8):
                    b = bg * 8 + b8
                    S.op("dve", lambda e, dst=dst, b=b, b8=b8, h=h: e.scalar_tensor_tensor(out=dst, in0=oh[:, b8, :], scalar=rb[:, b, h:h + 1], in1=dst,
                                                                                         op0=ALU.mult, op1=ALU.add),
                         reads=[t_oh, t_rb, self.t_biasT], writes=[self.t_biasT])
        sk, t_sk = self.load_small("sink_s", sinkrow_d, [1, 32])
        z128 = self.sb("z128", [1, 128], F32)
        t_z = T("z128")
        S.op("dve", lambda e: e.memset(z128[:], 0.0), writes=[t_z])
        for hh in range(32):
            S.op("act", lambda e, hh=hh: e.activation(out=self.esrow[0:1, hh * 128:(hh + 1) * 128], in_=z128[:], func=AF.Exp,
                                                     bias=sk[0:1, hh:hh + 1], scale=1.0),
                 reads=[t_z, t_sk], writes=[self.t_esrow])
    self.kbuf = self.sb("kbuf", [P, 4, 128 + TT], BF16)
    self.t_kbuf = [T("kbuf%d" % i) for i in range(4)]
    self.vdup = self.sb("vdup", [P, 5, 512], BF16)
    self.t_vdup = [T("vdup%d" % i) for i in range(5)]
    self.hh128 = self.hT
    self.t_hh128 = self.t_hT
    self.alloc_A(2 * KC)
    self.hhn128 = self.sb("hhn128", [P, KC, 128], BF16)
    self.t_hhn128 = [T("hhn128_%d" % k) for k in range(KC)]
    self.sc = [self.sb("sc%d" % i, [P, 512], F32) for i in range(2)]
    self.t_sc = [T("sc0"), T("sc1")]
    self.pS = [self.sb("pS%d" % i, [P, 512], BF16) for i in range(2)]
    self.t_pS = [T("pS0"), T("pS1")]
    self.rdn = self.sb("rdn", [P, 512], F32)
    self.t_rdn = T("rdn")


def _swa(self, h_in, h_out, hhalo_d, hmask, t_hmask, Wq, Wkd, Wvd, Wo, bq, t_bq, bkd, t_bkd, bvd, t_bvd, gain, t_gain, gcol, flag=None, hrd=()):
    S = self.S
    qoff, ooff = 0, KC
    self.halo_load(self.hh128, self.t_hh128, hhalo_d, 128, flag, hrd)
    self.rmsnorm(self.hh128, self.t_hh128, gain, t_gain, gcol, self.hhn128, self.t_hhn128, 128)

    def kv_proj(src, t_src, w, kcol0, vblk):
        for half in range(2):
            wk, t_wk = self.load_w(Wkd[:, half * 256:(half + 1) * 256].rearrange("(k p) m -> p k m", p=P), KC, 256)
            for m in range(2):
                kh = half * 2 + m
                ps, t_ps = self.psum()
                self.mm_group(ps[:, 0:w], t_ps, [(wk[:, k, m * P:(m + 1) * P], src[:, k, 0:w]) for k in range(KC)], reads=[t_wk] + t_src)
                S.op("act", lambda e, kh=kh, ps=ps: e.activation(out=self.kbuf[:, kh, kcol0:kcol0 + w], in_=ps[:, 0:w], func=AF.Identity,
                                                                bias=bkd[:, kh:kh + 1], scale=1.0),
                     reads=[t_ps, t_bkd], writes=[self.t_kbuf[kh]])
        wv = []
        for half in range(2):
            wv.append(self.load_w(Wvd[:, half * 256:(half + 1) * 256].rearrange("(k p) m -> p k m", p=P), KC, 256))
        for qb in range(w // 128):
            ps, t_ps = self.psum()
            for half in range(2):
                self.mm_group(ps[:, half * 256:(half + 1) * 256], t_ps,
                              [(src[:, k, qb * 128:(qb + 1) * 128], wv[half][0][:, k, :]) for k in range(KC)], reads=[wv[half][1]] + t_src)
            S.op("dve", lambda e, qb=qb, ps=ps: e.tensor_tensor(out=self.vdup[:, vblk + qb, :], in0=ps[:, 0:512], in1=bvd[:], op=ALU.add),
                 reads=[t_ps, t_bvd], writes=[self.t_vdup[vblk + qb]])

    kv_proj(self.hhn128, self.t_hhn128, 128, 0, 0)
    for it in range(NT):
        tok0 = it * TT
        self.load_h(h_in, tok0)
        self.rmsnorm(self.hT, self.t_hT, gain, t_gain, gcol, self.hn, self.t_hn, TT)
        if it > 0:
            for kh in range(4):
                S.op("act", lambda e, kh=kh: e.activation(out=self.kbuf[:, kh, 0:128], in_=self.kbuf[:, kh, TT:TT + 128], func=AF.Copy),
                     reads=[self.t_kbuf[kh]], writes=[self.t_kbuf[kh]])
            S.op("act", lambda e: e.activation(out=self.vdup[:, 0, :], in_=self.vdup[:, 4, :], func=AF.Copy),
                 reads=[self.t_vdup[4]], writes=[self.t_vdup[0]])
        kv_proj(self.hn, self.t_hn, TT, 128, 1)

        def q_evac(c, ps, t_ps):
            S.op("act", lambda e, c=c, ps=ps: e.activation(out=self.bufA[:, qoff + c, :], in_=ps[:, 0:TT], func=AF.Identity,
                                                          bias=bq[:, c:c + 1], scale=0.125),
                 reads=[t_ps, t_bq], writes=[self.t_A[qoff + c]])
        self.linear_to(Wq, 0, D, self.t_hn, self.hn, self.bufA, qoff, self.t_A, evac=q_evac)
        for qb in range(TT // 128):
            qs = slice(qb * 128, (qb + 1) * 128)
            for kh in range(4):
                for par in range(2):
                    pr = slice(par * 64, (par + 1) * 64)
                    t_q4 = self.t_A[qoff + 4 * kh:qoff + 4 * kh + 4]
                    for j in range(2):
                        ps, t_ps = self.psum()
                        kc0 = (qb + j) * 128
                        self.mm_group(ps[:, 0:512], t_ps,
                                      [(self.kbuf[pr, kh, kc0:kc0 + 128], self.bufA[pr, qoff + 4 * kh:qoff + 4 * kh + 4, qs])],
                                      reads=[self.t_kbuf[kh]] + t_q4)
                        bsl = self.biasT[:, par, kh, :, j * 128:(j + 1) * 128]
                        psv = ps[:, 0:512].rearrange("p (i q) -> p i q", i=4)
                        scv = self.sc[j][:].rearrange("p (i q) -> p i q", i=4)
                        if it == 0 and qb == 0 and j == 0:
                            S.op("dve", lambda e, psv=psv, scv=scv, bsl=bsl: e.scalar_tensor_tensor(out=scv, in0=psv, scalar=hmask[:, 0:1], in1=bsl,
                                                                                                  op0=ALU.add, op1=ALU.add),
                                 reads=[t_ps, self.t_biasT, t_hmask], writes=[self.t_sc[j]])
                        else:
                            S.op("dve", lambda e, psv=psv, scv=scv, bsl=bsl: e.tensor_tensor(out=scv, in0=psv, in1=bsl, op=ALU.add),
                                 reads=[t_ps, self.t_biasT], writes=[self.t_sc[j]])
                        S.op("act", lambda e, j=j: e.activation(out=self.pS[j][:], in_=self.sc[j][:], func=AF.Exp),
                             reads=[self.t_sc[j]], writes=[self.t_pS[j]])
                    psd, t_psd = self.psum()
                    e0 = (par * 16 + kh * 4) * 128
                    self.mm_group(psd[:, 0:512], t_psd,
                                  [(self.ones[:], self.pS[0][:]), (self.ones[:], self.pS[1][:]), (self.ones[0:1, :], self.esrow[0:1, e0:e0 + 512])],
                                  reads=self.t_pS + [self.t_ones, self.t_esrow])
                    S.op("dve", lambda e, psd=psd: e.reciprocal(out=self.rdn[:], in_=psd[:, 0:512]), reads=[t_psd], writes=[self.t_rdn])
                    pso, t_pso = self.psum()
                    self.mm_group(pso[:, 0:512], t_pso,
                                  [(self.vdup[:, qb + j, kh * 128:(kh + 1) * 128], self.pS[j][:]) for j in range(2)],
                                  reads=self.t_pS + [self.t_vdup[qb], self.t_vdup[qb + 1]])
                    ov = self.bufA[pr, ooff + 4 * kh:ooff + 4 * kh + 4, qs]
                    S.op("dve", lambda e, ov=ov, pso=pso, pr=pr: e.tensor_tensor(out=ov, in0=pso[pr, 0:512].rearrange("p (i q) -> p i q", i=4),
                                                                                in1=self.rdn[pr, :].rearrange("p (i q) -> p i q", i=4), op=ALU.mult),
                         reads=[t_pso, self.t_rdn], writes=self.t_A[ooff + 4 * kh:ooff + 4 * kh + 4])
        self.linear_resid(Wo, KC, self.t_A[ooff:ooff + KC], self.bufA, ooff)
        self.store_h(h_out, tok0)


Prog.swa_setup = _swa_setup
Prog.swa = _swa


def swa_host_params(z, i):
    b = np.asarray(z["od_b_qkv"][i], np.float32)
    bq = pk(b[0:2048])
    bk = b[2048:2304].reshape(4, 64)
    bkd = np.ascontiguousarray(np.concatenate([bk, bk], axis=1).T)
    bv = b[2304:2560].reshape(4, 64)
    bvd = np.concatenate([bv, bv], axis=1).reshape(1, 512)
    bvd = np.ascontiguousarray(np.broadcast_to(bvd, (P, 512)))
    W = np.asarray(z["od_w_qkv"][i], np.float32)
    Wk = W[:, 2048:2304].reshape(D, 4, 64)
    Wkd = np.ascontiguousarray(np.concatenate([Wk, Wk], axis=2).reshape(D, 512))
    Wv = W[:, 2304:2560].reshape(D, 4, 64)
    Wvd = np.ascontiguousarray(np.concatenate([Wv, Wv], axis=2).reshape(D, 512))
    Wq = np.ascontiguousarray(W[:, 0:2048])
    sk = np.asarray(z["od_sinks"][i], np.float32)
    order = [kh * 8 + 2 * ii + par for par in range(2) for kh in range(4) for ii in range(4)]
    sinkrow = np.ascontiguousarray(sk[order].reshape(1, 32))
    relb = np.ascontiguousarray(np.broadcast_to(np.asarray(z["rel_bias"], np.float32)[None], (P, 32, 32)))
    return dict(bq=bq, bkd=bkd, bvd=bvd, Wq=Wq, Wkd=Wkd, Wvd=Wvd, sinkrow=sinkrow, relb=relb)


def build_swa_launch():
    nc = bass.Bass("TRN2", target_bir_lowering=False)
    dt = lambda name, shape, kind="ExternalInput": nc.dram_tensor(name, shape, F32, kind=kind).ap()
    h_in = dt("h_in", [D, NTOK]); h_out = dt("h_out", [D, NTOK], "ExternalOutput")
    hhalo = dt("hhalo", [D, 128]); gm = dt("gm", [P, KC]); hmask_d = dt("hmask", [P, 1])
    oh_d = dt("oh", [P, 32, 2, 128]); mask_d = dt("maskT", [P, 2, 128]); relb_d = dt("relb", [P, 32, 32]); sinkrow_d = dt("sinkrow", [1, 32])
    bq_d = dt("bq", [P, KC]); bkd_d = dt("bkd", [P, 4]); bvd_d = dt("bvd", [P, 512])
    Wq = dt("Wq", [D, D]); Wkd = dt("Wkd", [D, 512]); Wvd = dt("Wvd", [D, 512]); Wo = dt("Wo", [D, D])
    with ExitStack() as st:
        pr = Prog(nc, st)
        g, t_g = pr.load_small("gm_s", gm, [P, KC])
        hm, t_hm = pr.load_small("hmask_s", hmask_d, [P, 1])
        bq, t_bq = pr.load_small("bq_s", bq_d, [P, KC])
        pr.S.op("dve", lambda e: e.tensor_scalar(out=bq[:], in0=bq[:], scalar1=0.125, scalar2=None, op0=ALU.mult), reads=[t_bq], writes=[t_bq])
        bkd, t_bkd = pr.load_small("bkd_s", bkd_d, [P, 4])
        bvd, t_bvd = pr.load_small("bvd_s", bvd_d, [P, 512])
        with pr.phase():
            pr.swa_setup(oh_d, mask_d, relb_d, sinkrow_d)
            pr.swa(h_in, h_out, hhalo, hm, t_hm, Wq, Wkd, Wvd, Wo, bq, t_bq, bkd, t_bkd, bvd, t_bvd, g, t_g, 0)
        pr.finish(pr.t_hT)
    return nc


class Ew:
    def __init__(self, prog, shape, tag):
        self.pr = prog
        self.shape = shape
        self.tag = tag
        self.n = 0

    def new(self, dt=F32, shape=None):
        self.n += 1
        t = self.pr.sb("%s%d" % (self.tag, self.n), shape or self.shape, dt)
        return (t[:], T("%s%d" % (self.tag, self.n)))

    def tt(self, o, a, b, op):
        self.pr.S.op("dve", lambda e: e.tensor_tensor(out=o[0], in0=a[0], in1=b[0], op=op), reads=[a[1], b[1]], writes=[o[1]], guard=True)
        return o

    def ts(self, o, a, s1, op0, s2=None, op1=None):
        if op1 is None:
            self.pr.S.op("dve", lambda e: e.tensor_scalar(out=o[0], in0=a[0], scalar1=s1, scalar2=None, op0=op0), reads=[a[1]], writes=[o[1]], guard=True)
        else:
            self.pr.S.op("dve", lambda e: e.tensor_scalar(out=o[0], in0=a[0], scalar1=s1, scalar2=s2, op0=op0, op1=op1), reads=[a[1]], writes=[o[1]], guard=True)
        return o

    def act(self, o, a, func, scale=1.0, bias=None, breads=()):
        if bias is None:
            self.pr.S.op("act", lambda e: e.activation(out=o[0], in_=a[0], func=func, scale=scale), reads=[a[1]], writes=[o[1]], guard=True)
        else:
            self.pr.S.op("act", lambda e: e.activation(out=o[0], in_=a[0], func=func, scale=scale, bias=bias), reads=[a[1]] + list(breads), writes=[o[1]], guard=True)
        return o

    def cmul(self, orr, oi, ar, ai, br, bi, t1, t2):
        self.tt(t1, ar, br, ALU.mult)
        self.tt(t2, ai, bi, ALU.mult)
        self.tt(orr, t1, t2, ALU.subtract)
        self.tt(t1, ar, bi, ALU.mult)
        self.tt(t2, ai, br, ALU.mult)
        self.tt(oi, t1, t2, ALU.add)

    def abar(self, are, aim, ldt, halfpi):
        n = self.new
        dt_ = self.act(n(), ldt, AF.Exp)
        x1 = self.tt(n(), are, dt_, ALU.mult)
        th = self.tt(n(), aim, dt_, ALU.mult)
        rho = self.act(n(), x1, AF.Exp, scale=1.0 / 32)
        sn = self.act(n(), th, AF.Sin, scale=1.0 / 32)
        cs = self.act(n(), th, AF.Sin, scale=1.0 / 32, bias=halfpi[0], breads=[halfpi[1]])
        er = self.tt(n(), rho, cs, ALU.mult)
        ei = self.tt(n(), rho, sn, ALU.mult)
        t1, t2, t3 = n(), n(), n()
        for _ in range(5):
            self.tt(t1, er, er, ALU.mult)
            self.tt(t2, ei, ei, ALU.mult)
            self.tt(t3, er, ei, ALU.mult)
            self.tt(er, t1, t2, ALU.subtract)
            self.ts(ei, t3, 2.0, ALU.mult)
        nr = self.ts(n(), er, -1.0, ALU.add)
        self.tt(t1, are, are, ALU.mult)
        self.tt(t2, aim, aim, ALU.mult)
        self.tt(t3, t1, t2, ALU.add)
        rd = n()
        self.pr.S.op("dve", lambda e: e.reciprocal(out=rd[0], in_=t3[0]), reads=[t3[1]], writes=[rd[1]])
        qr, qi = n(), n()
        self.tt(t1, nr, are, ALU.mult)
        self.tt(t2, ei, aim, ALU.mult)
        self.tt(t3, t1, t2, ALU.add)
        self.tt(qr, t3, rd, ALU.mult)
        self.tt(t1, ei, are, ALU.mult)
        self.tt(t2, nr, aim, ALU.mult)
        self.tt(t3, t1, t2, ALU.subtract)
        self.tt(qi, t3, rd, ALU.mult)
        return er, ei, qr, qi


def _s5_precompute(self, prm, dr):
    S = self.S
    self.A8r = self.sb("A8r", [P, 32], F32)
    self.A8i = self.sb("A8i", [P, 32], F32)
    self.t_A8 = T("A8")

    with self.phase():
        hp_t = self.sb("halfpi", [P, 1], F32)
        t_hp = T("halfpi")
        S.op("dve", lambda e: e.memset(hp_t[:], math.pi / 2), writes=[t_hp])
        halfpi = (hp_t[:], t_hp)
        ld = lambda name, shape: (lambda r: (r[0][:], r[1]))(self.load_small(name + "_s", prm[name], shape))
        es = Ew(self, [P, 32], "es")
        are, aim, ldt = ld("are_S", [P, 32]), ld("aim_S", [P, 32]), ld("ldt_S", [P, 32])
        ar, ai, qr, qi = es.abar(are, aim, ldt, halfpi)
        bre, bim = ld("bre_S", [P, 32, 16]), ld("bim_S", [P, 32, 16])
        cre, cim = ld("cre_S", [P, 32, 16]), ld("cim_S", [P, 32, 16])
        e3 = Ew(self, [P, 32, 16], "e3")
        bc = lambda v: (v[0].unsqueeze(2).to_broadcast([P, 32, 16]), v[1])
        Br, Bi, u1, u2 = e3.new(), e3.new(), e3.new(), e3.new()
        e3.cmul(Br, Bi, bc(qr), bc(qi), bre, bim, u1, u2)
        pw = [(es.new(), es.new()) for _ in range(9)]
        S.op("dve", lambda e: e.memset(pw[0][0][0], 1.0), writes=[pw[0][0][1]])
        S.op("dve", lambda e: e.memset(pw[0][1][0], 0.0), writes=[pw[0][1][1]])
        s1, s2 = es.new(), es.new()
        for k in range(8):
            es.cmul(pw[k + 1][0], pw[k + 1][1], pw[k][0], pw[k][1], ar, ai, s1, s2)
        S.op("dve", lambda e: e.tensor_copy(out=self.A8r[:], in_=pw[8][0][0]), reads=[pw[8][0][1]], writes=[self.t_A8])
        S.op("dve", lambda e: e.tensor_copy(out=self.A8i[:], in_=pw[8][1][0]), reads=[pw[8][1][1]], writes=[self.t_A8])
        def padded(name, dt):
            t = self.sb(name, [P, 32, 32], dt)
            tt = T(name)
            S.op("dve", lambda e: e.memset(t[:], 0.0), writes=[tt])
            return t, tt

        def to_pad(dst, t_dst, src):
            for m in range(2):
                S.op("dve", lambda e, m=m: e.tensor_copy(out=dst[m * 64:(m + 1) * 64, :, m * 16:(m + 1) * 16], in_=src[0][m * 64:(m + 1) * 64, :, :]),
                     reads=[src[1]], writes=[t_dst])
        Bpr, t_Bpr = padded("Bpr", BF16)
        Bpn, t_Bpn = padded("Bpn", BF16)
        to_pad(Bpr, t_Bpr, Br)
        nBi = e3.ts(e3.new(), Bi, -1.0, ALU.mult)
        to_pad(Bpn, t_Bpn, nBi)
        Cpr, t_Cpr = padded("Cpr", BF16)
        Cpi, t_Cpi = padded("Cpi", BF16)
        ident, t_ident = self.load_small("ident_s", prm["ident"], [P, P])
        dcol, t_dcol = self.load_small("dcol_s", prm["dcol"], [P, 8])
        Ck_r, Ck_i = e3.new(), e3.new()
        stg = self.sb("wc_stg", [P, 8, 3072], BF16)
        t_stg = T("wc_stg")
        S.op("dve", lambda e: e.memset(stg[:], 0.0), writes=[t_stg])
        f32blk = self.sb("f32blk", [P, P], F32)
        t_f32blk = T("f32blk")
        for k in range(9):
            e3.cmul(Ck_r, Ck_i, cre, cim, bc(pw[k][0]), bc(pw[k][1]), u1, u2)
            if k < 8:
                to_pad(Cpr, t_Cpr, Ck_r)
                to_pad(Cpi, t_Cpi, Ck_i)
                for c8 in range(8):
                    ps, t_ps = self.psum()
                    for q in range(4):
                        pair = c8 * 4 + q
                        self.mm_group(ps[q * 32:(q + 1) * 32, q * 32:(q + 1) * 32], t_ps,
                                      [(Bpr[:, pair, :], Cpr[:, pair, :]), (Bpn[:, pair, :], Cpi[:, pair, :])],
                                      reads=[t_Bpr, t_Bpn, t_Cpr, t_Cpi], tp=(0, q * 32))
                    dstv = stg[:, c8, k * 128:(k + 1) * 128]
                    for q in range(4):
                        sl = slice(q * 32, (q + 1) * 32)
                        if k == 0:
                            S.op("dve", lambda e, sl=sl, c8=c8, ps=ps, dstv=dstv: e.scalar_tensor_tensor(
                                out=dstv[sl, sl], in0=ident[sl, sl], scalar=dcol[sl, c8:c8 + 1], in1=ps[sl, sl], op0=ALU.mult, op1=ALU.add),
                                reads=[t_ps, t_ident, t_dcol], writes=[t_stg])
                        else:
                            S.op("act", lambda e, sl=sl, ps=ps, dstv=dstv: e.activation(out=dstv[sl, sl], in_=ps[sl, sl], func=AF.Copy),
                                 reads=[t_ps], writes=[t_stg])
            if k >= 1:
                r = k - 1
                wov = stg[:, :, 1024:3072].rearrange("p c (q r i o) -> p c q r i o", q=4, r=8, i=2)
                for m in range(2):
                    ms = slice(m * 64, (m + 1) * 64)
                    S.op("dve", lambda e, ms=ms, m=m, r=r: e.tensor_copy(
                        out=wov[ms, :, :, r, 0, m * 16:(m + 1) * 16], in_=Ck_r[0][ms, :, :].rearrange("p (c q) h -> p c q h", q=4)),
                        reads=[Ck_r[1]], writes=[t_stg])
                    S.op("dve", lambda e, ms=ms, m=m, r=r: e.tensor_scalar(
                        out=wov[ms, :, :, r, 1, m * 16:(m + 1) * 16], in0=Ck_i[0][ms, :, :].rearrange("p (c q) h -> p c q h", q=4),
                        scalar1=-1.0, scalar2=None, op0=ALU.mult),
                        reads=[Ck_i[1]], writes=[t_stg])
        S.dma("sp", [lambda e: e.dma_start(out=dr["wc"], in_=stg[:])], t_stg, reads=[t_stg], writes=[dr["t_wc"]])
    with self.phase():
        hp_t = self.sb("halfpi2", [P, 1], F32)
        t_hp = T("halfpi2")
        S.op("dve", lambda e: e.memset(hp_t[:], math.pi / 2), writes=[t_hp])
        halfpi = (hp_t[:], t_hp)
        ld = lambda name, shape: (lambda r: (r[0][:], r[1]))(self.load_small(name + "_s", prm[name], shape))
        et = Ew(self, [P, 8, 64], "et")
        are, aim, ldt = ld("are_T", [P, 8, 64]), ld("aim_T", [P, 8, 64]), ld("ldt_T", [P, 8, 64])
        ar, ai, qr, qi = et.abar(are, aim, ldt, halfpi)
        bre, bim = ld("bre_T", [P, 8, 64]), ld("bim_T", [P, 8, 64])
        QBr, QBi, u1, u2 = et.new(), et.new(), et.new(), et.new()
        et.cmul(QBr, QBi, qr, qi, bre, bim, u1, u2)
        m01, t_m01 = self.load_small("mask01_s", prm["mask01"], [P, 2])
        wzs = self.sb("wz_stg", [P, 8, 8, 2, 128], BF16)
        t_wzs = T("wz_stg")
        pr_, pi_ = et.new(), et.new()
        S.op("dve", lambda e: e.memset(pr_[0], 1.0), writes=[pr_[1]])
        S.op("dve", lambda e: e.memset(pi_[0], 0.0), writes=[pi_[1]])
        Wr, Wi, nr_, ni_ = et.new(), et.new(), et.new(), et.new()
        for k in range(8):
            s = 7 - k
            et.cmul(Wr, Wi, pr_, pi_, QBr, QBi, u1, u2)
            for ri, W in ((0, Wr), (1, Wi)):
                for m in range(2):
                    S.op("dve", lambda e, s=s, ri=ri, m=m, W=W: e.tensor_scalar(out=wzs[:, :, s, ri, m * 64:(m + 1) * 64], in0=W[0],
                                                                                 scalar1=m01[:, m:m + 1], scalar2=None, op0=ALU.mult),
                         reads=[W[1], t_m01], writes=[t_wzs])
            if k < 7:
                et.cmul(nr_, ni_, pr_, pi_, ar, ai, u1, u2)
                S.op("dve", lambda e: e.tensor_copy(out=pr_[0], in_=nr_[0]), reads=[nr_[1]], writes=[pr_[1]])
                S.op("dve", lambda e: e.tensor_copy(out=pi_[0], in_=ni_[0]), reads=[ni_[1]], writes=[pi_[1]])
        S.dma("sp", [lambda e: e.dma_start(out=dr["wz"], in_=wzs[:].rearrange("p c s i o -> p c (s i o)"))], t_wzs, reads=[t_wzs], writes=[dr["t_wz"]])


Prog.s5_precompute = _s5_precompute


def ev_host_params(z, i):
    f = lambda k: np.asarray(z[k][i], np.float32)
    a_re, a_im, ldt = f("s5_a_re"), f("s5_a_im"), f("s5_log_dt")
    b_re, b_im = f("s5_b_re"), f("s5_b_im")
    c_re, c_im = f("s5_c_re"), f("s5_c_im")
    def S2(v):
        return np.ascontiguousarray(v.reshape(32, 2, 64).transpose(1, 2, 0).reshape(128, 32))
    def S3(v):
        return np.ascontiguousarray(v.reshape(32, 2, 64, 16).transpose(1, 2, 0, 3).reshape(128, 32, 16))
    def T2(v):
        w = v.reshape(8, 4, 2, 1, 64)
        w = np.broadcast_to(w, (8, 4, 2, 16, 64)).transpose(1, 2, 3, 0, 4).reshape(128, 8, 64)
        return np.ascontiguousarray(w)
    def T3(v):
        w = v.reshape(8, 4, 2, 64, 16).transpose(1, 2, 4, 0, 3).reshape(128, 8, 64)
        return np.ascontiguousarray(w)
    ldt2 = np.broadcast_to(ldt[:, None], (64, 64))
    out = dict(are_S=S2(a_re), aim_S=S2(a_im), ldt_S=S2(ldt2), bre_S=S3(b_re), bim_S=S3(b_im),
               cre_S=S3(c_re.transpose(0, 2, 1)), cim_S=S3(c_im.transpose(0, 2, 1)),
               are_T=T2(a_re), aim_T=T2(a_im), ldt_T=T2(ldt2), bre_T=T3(b_re), bim_T=T3(b_im))
    mask01 = np.zeros((128, 2), np.float32)
    for p_ in range(128):
        mask01[p_, (p_ // 16) % 2] = 1.0
    out["mask01"] = mask01
    out["ident"] = np.eye(128, dtype=np.float32)
    out["dcol"] = pk(f("s5_d"))
    glu = f("s5_glu_w")
    gb = np.zeros((128, 8, 128), np.float32)
    for g in range(64):
        c8, gg = g // 8, g % 8
        gb[gg * 16:(gg + 1) * 16, c8, gg * 16:(gg + 1) * 16] = glu[g]
    out["gblk"] = gb
    out["convw"] = pk(f("ev_conv_w"))
    return out


def _even_mixer(self, h_in, h_out, hhalo_d, Win, Wout, prm, dr, gain, t_gain, gcol, flag, hrd, xch):
    S = self.S
    self.s5_precompute(prm, dr)
    self.alloc_A(24)
    cw, t_cw = self.load_small("convw_s", prm["convw"], [P, 3, 8])
    gblk = self.sb("gblk", [P, 8, P], BF16)
    t_gblk = T("gblk")
    S.dma("pool", [lambda e: e.dma_start(out=gblk[:], in_=prm["gblk"])], t_gblk, writes=[t_gblk])
    Xb = [self.sb("Xb%d" % i, [P, 32, 65], F32) for i in range(2)]
    t_Xb = [T("Xb0"), T("Xb1")]
    X16 = [self.sb("X16_%d" % i, [P, 32, 64], BF16) for i in range(2)]
    t_X16 = [T("X16_0"), T("X16_1")]
    X0 = [self.sb("X0_%d" % i, [P, 32, NT], F32) for i in range(2)]
    t_X0 = [T("X0_0"), T("X0_1")]
    sstg = [self.sb("sstg%d" % i, [P, 32], F32) for i in range(2)]
    t_sstg = [T("sstg0"), T("sstg1")]
    st = [self.sb("st%d" % i, [P, 32], F32) for i in range(4)]
    t_st = [T("st%d" % i) for i in range(4)]
    prod = [self.sb("prod%d" % i, [P, TT + 2], F32) for i in range(2)]
    t_prod = [T("prod0"), T("prod1")]
    phalo = self.sb("phalo", [P, 8, 2], F32)
    t_phalo = [T("phalo%d" % c) for c in range(8)]
    gcs = [self.sb("gcs%d" % i, [P, TT], F32) for i in range(2)]
    t_gcs = [T("gcs0"), T("gcs1")]
    cv = [self.sb("cv%d" % i, [P, TT], F32) for i in range(2)]
    t_cv = [T("cv0"), T("cv1")]
    hhT = self.sb("ehhT", [P, KC, 2], F32)
    t_hhT = [T("ehhT%d" % k) for k in range(KC)]
    hhn = self.sb("ehhn", [P, KC, 2], BF16)
    t_hhn = [T("ehhn%d" % k) for k in range(KC)]
    yx, t_yx = gcs, t_gcs
    yg, t_yg = cv, t_cv
    yb = [self.sb("yb%d" % i, [P, TT], BF16) for i in range(2)]
    t_yb = [T("yb0"), T("yb1")]
    NJ = TT // 8
    UO = 16
    tt_ = lambda o, t_o, a, b_, op, rd: S.op("dve", lambda e: e.tensor_tensor(out=o, in0=a, in1=b_, op=op), reads=rd, writes=[t_o])

    def zscan(it):
        for c8 in range(8):
            wz, t_wz = self.load_wz(dr, c8)
            uv = self.bufA[:, UO + c8, :].rearrange("p (j s) -> p j s", s=8)
            for q in range(4):
                pair = c8 * 4 + q
                ps, t_ps = self.psum()
                for ri in range(2):
                    n = 8
                    for s in range(8):
                        self.S.op("pe", lambda e, ps=ps, q=q, s=s, ri=ri, wz=wz, uv=uv: e.matmul(
                            ps[:, ri * NJ:(ri + 1) * NJ], wz[q * 32:(q + 1) * 32, s, ri, :], uv[q * 32:(q + 1) * 32, :, s],
                            start=(s == 0), stop=(s == n - 1), tile_position=(q * 32, 0)),
                            reads=[t_wz, self.t_A[UO + c8]], writes=[t_ps], inc=(s == n - 1))
                    S.op("act", lambda e, ps=ps, ri=ri, pair=pair: e.activation(out=Xb[ri][:, pair, 1:NJ + 1], in_=ps[:, ri * NJ:(ri + 1) * NJ], func=AF.Copy),
                         reads=[t_ps], writes=[t_Xb[ri]])
        for j in range(NJ):
            xr, xi = Xb[0][:, :, j], Xb[1][:, :, j]
            nr_, ni_ = Xb[0][:, :, j + 1], Xb[1][:, :, j + 1]
            tt_ = lambda o, t_o, a, b_, op, rd: S.op("dve", lambda e: e.tensor_tensor(out=o, in0=a, in1=b_, op=op), reads=rd, writes=[t_o], guard=True)
            tt_(st[0][:], t_st[0], self.A8r[:], xr, ALU.mult, [self.t_A8, t_Xb[0]])
            tt_(st[1][:], t_st[1], self.A8i[:], xi, ALU.mult, [self.t_A8, t_Xb[1]])
            tt_(st[2][:], t_st[2], self.A8i[:], xr, ALU.mult, [self.t_A8, t_Xb[0]])
            tt_(st[3][:], t_st[3], self.A8r[:], xi, ALU.mult, [self.t_A8, t_Xb[1]])
            tt_(st[0][:], t_st[0], st[0][:], st[1][:], ALU.subtract, [t_st[0], t_st[1]])
            tt_(st[2][:], t_st[2], st[2][:], st[3][:], ALU.add, [t_st[2], t_st[3]])
            tt_(nr_, t_Xb[0], nr_, st[0][:], ALU.add, [t_Xb[0], t_st[0]])
            tt_(ni_, t_Xb[1], ni_, st[2][:], ALU.add, [t_Xb[1], t_st[2]])
    for ri in range(2):
        S.op("dve", lambda e, ri=ri: e.memset(Xb[ri][:, :, 0], 0.0), writes=[t_Xb[ri]])
    for it in range(NT):
        self.load_h(h_in, it * TT)
        self.rmsnorm(self.hT, self.t_hT, gain, t_gain, gcol, self.hn, self.t_hn, TT)
        self.linear_to(Win, 3072, 1024, self.t_hn, self.hn, self.bufA, UO, self.t_A)
        for ri in range(2):
            S.op("act", lambda e, ri=ri, it=it: e.activation(out=X0[ri][:, :, it], in_=Xb[ri][:, :, 0], func=AF.Copy), reads=[t_Xb[ri]], writes=[t_X0[ri]])
        zscan(it)
        for ri in range(2):
            if it < NT - 1:
                S.op("act", lambda e, ri=ri: e.activation(out=Xb[ri][:, :, 0], in_=Xb[ri][:, :, NJ], func=AF.Copy), reads=[t_Xb[ri]], writes=[t_Xb[ri]])
            else:
                S.op("act", lambda e, ri=ri: e.activation(out=sstg[ri][:], in_=Xb[ri][:, :, NJ], func=AF.Copy), reads=[t_Xb[ri]], writes=[t_sstg[ri]])
    for ri in range(2):
        S.dma("sp", [lambda e, ri=ri: e.dma_start(out=xch["snd_s"][ri * P:(ri + 1) * P, :], in_=sstg[ri][:])], t_sstg[ri], reads=[t_sstg[ri]], writes=[xch["t_snd_s"]])
    S.cc("pool", lambda e: e.collective_compute("AllGather", ALU.bypass, replica_groups=xch["rg"], ins=[xch["snd_s"]], outs=[xch["rcv_s"]]),
         reads=[xch["t_snd_s"]], writes=[xch["t_rcv_s"]])
    Dr = [self.sb("Dst%d" % i, [P, 32], F32) for i in range(2)]
    t_Dr = [T("Dst0"), T("Dst1")]
    for ri in range(2):
        S.dma("sp", [lambda e, ri=ri: e.dma_start(out=Dr[ri][:], in_=xch["rcv_s"][ri * P:(ri + 1) * P, :])], t_Dr[ri], reads=[xch["t_rcv_s"]], writes=[t_Dr[ri]])
        S.op("dve", lambda e, ri=ri: e.tensor_scalar(out=Dr[ri][:], in0=Dr[ri][:], scalar1=flag[0][:, 0:1], scalar2=None, op0=ALU.mult),
             reads=[t_Dr[ri], flag[1]], writes=[t_Dr[ri]])
    self.dbg("m_A8r", self.A8r[:], [self.t_A8], [P, 32])
    self.dbg("m_Dr0", Dr[0][:], [t_Dr[0]], [P, 32])
    self.dbg("m_X0", X0[0][:].rearrange("p a b -> p (a b)"), [t_X0[0]], [P, 32 * NT])
    es = Ew(self, [P, 32], "e5")
    Ar, Ai = es.new(), es.new()
    S.op("dve", lambda e: e.tensor_copy(out=Ar[0], in_=self.A8r[:]), reads=[self.t_A8], writes=[Ar[1]])
    S.op("dve", lambda e: e.tensor_copy(out=Ai[0], in_=self.A8i[:]), reads=[self.t_A8], writes=[Ai[1]])
    q1, q2, q3 = es.new(), es.new(), es.new()
    for _ in range(6):
        es.tt(q1, Ar, Ar, ALU.mult)
        es.tt(q2, Ai, Ai, ALU.mult)
        es.tt(q3, Ar, Ai, ALU.mult)
        es.tt(Ar, q1, q2, ALU.subtract)
        es.ts(Ai, q3, 2.0, ALU.mult)
    D0, D1 = (Dr[0][:], t_Dr[0]), (Dr[1][:], t_Dr[1])
    n0, n1 = es.new(), es.new()
    for it in range(NT):
        for ri, Dv in ((0, D0), (1, D1)):
            S.op("dve", lambda e, ri=ri, it=it, Dv=Dv: e.tensor_tensor(out=X0[ri][:, :, it], in0=X0[ri][:, :, it], in1=Dv[0], op=ALU.add),
                 reads=[t_X0[ri], Dv[1]], writes=[t_X0[ri]])
        if it < NT - 1:
            es.cmul(n0, n1, D0, D1, Ar, Ai, q1, q2)
            S.op("dve", lambda e: e.tensor_copy(out=D0[0], in_=n0[0]), reads=[n0[1]], writes=[D0[1]])
            S.op("dve", lambda e: e.tensor_copy(out=D1[0], in_=n1[0]), reads=[n1[1]], writes=[D1[1]])
    self.halo_load(hhT, t_hhT, hhalo_d, 2, flag, hrd)
    self.rmsnorm(hhT, t_hhT, gain, t_gain, gcol, hhn, t_hhn, 2)
    for it in range(NT):
        tok0 = it * TT
        self.load_h(h_in, tok0)
        self.rmsnorm(self.hT, self.t_hT, gain, t_gain, gcol, self.hn, self.t_hn, TT)
        for blk in range(4):
            wblk = lambda col0: self.load_w(Win[:, col0 + blk * 256:col0 + (blk + 1) * 256].rearrange("(k p) m -> p k m", p=P), KC, 256)
            wgb, t_wgb = wblk(0)
            wgc, t_wgc = wblk(1024)
            wxa, t_wxa = wblk(2048)
            for m in range(2):
                c = blk * 2 + m
                b = c % 2
                ms = slice(m * P, (m + 1) * P)
                pd, t_pd = prod[b], t_prod[b]
                if it == 0:
                    ph1, t_ph1 = self.psum()
                    self.mm_group(ph1[:, 0:2], t_ph1, [(wgc[:, k, ms], hhn[:, k, :]) for k in range(KC)], reads=[t_wgc] + t_hhn)
                    ph2, t_ph2 = self.psum()
                    self.mm_group(ph2[:, 0:2], t_ph2, [(wxa[:, k, ms], hhn[:, k, :]) for k in range(KC)], reads=[t_wxa] + t_hhn)
                    S.op("act", lambda e, ph1=ph1, b=b: e.activation(out=gcs[b][:, 0:2], in_=ph1[:, 0:2], func=AF.Copy), reads=[t_ph1], writes=[t_gcs[b]])
                    S.op("dve", lambda e, ph2=ph2, b=b, pd=pd: e.tensor_tensor(out=pd[:, 0:2], in0=gcs[b][:, 0:2], in1=ph2[:, 0:2], op=ALU.mult),
                         reads=[t_ph2, t_gcs[b]], writes=[t_pd])
                else:
                    S.op("act", lambda e, c=c, pd=pd: e.activation(out=pd[:, 0:2], in_=phalo[:, c, :], func=AF.Copy), reads=[t_phalo[c]], writes=[t_pd])
                pgc, t_pgc = self.psum()
                self.mm_group(pgc[:, 0:TT], t_pgc, [(wgc[:, k, ms], self.hn[:, k, :]) for k in range(KC)], reads=[t_wgc] + self.t_hn)
                pxa, t_pxa = self.psum()
                self.mm_group(pxa[:, 0:TT], t_pxa, [(wxa[:, k, ms], self.hn[:, k, :]) for k in range(KC)], reads=[t_wxa] + self.t_hn)
                pgb, t_pgb = self.psum()
                self.mm_group(pgb[:, 0:TT], t_pgb, [(wgb[:, k, ms], self.hn[:, k, :]) for k in range(KC)], reads=[t_wgb] + self.t_hn)
                S.op("act", lambda e, pgc=pgc, b=b: e.activation(out=gcs[b][:], in_=pgc[:, 0:TT], func=AF.Copy), reads=[t_pgc], writes=[t_gcs[b]])
                S.op("dve", lambda e, pxa=pxa, b=b, pd=pd: e.tensor_tensor(out=pd[:, 2:TT + 2], in0=gcs[b][:], in1=pxa[:, 0:TT], op=ALU.mult),
                     reads=[t_pxa, t_gcs[b]], writes=[t_pd])
                if it < NT - 1:
                    S.op("act", lambda e, c=c, pd=pd: e.activation(out=phalo[:, c, :], in_=pd[:, TT:TT + 2], func=AF.Copy), reads=[t_pd], writes=[t_phalo[c]])
                S.op("act", lambda e, c=c, b=b, pd=pd: e.activation(out=cv[b][:], in_=pd[:, 2:TT + 2], func=AF.Copy, scale=cw[:, 2, c:c + 1]),
                     reads=[t_pd, t_cw], writes=[t_cv[b]])
                S.op("dve", lambda e, c=c, b=b, pd=pd: e.scalar_tensor_tensor(out=cv[b][:], in0=pd[:, 1:TT + 1], scalar=cw[:, 1, c:c + 1], in1=cv[b][:],
                                                                           op0=ALU.mult, op1=ALU.add), reads=[t_pd, t_cw, t_cv[b]], writes=[t_cv[b]])
                S.op("dve", lambda e, c=c, b=b, pd=pd: e.scalar_tensor_tensor(out=cv[b][:], in0=pd[:, 0:TT], scalar=cw[:, 0, c:c + 1], in1=cv[b][:],
                                                                           op0=ALU.mult, op1=ALU.add), reads=[t_pd, t_cw, t_cv[b]], writes=[t_cv[b]])
                S.op("dve", lambda e, c=c, b=b, pgb=pgb: e.tensor_tensor(out=self.bufA[:, c, :], in0=cv[b][:], in1=pgb[:, 0:TT], op=ALU.mult),
                     reads=[t_cv[b], t_pgb], writes=[self.t_A[c]])
        self.linear_to(Win, 3072, 1024, self.t_hn, self.hn, self.bufA, UO, self.t_A)
        for ri in range(2):
            S.op("act", lambda e, ri=ri, it=it: e.activation(out=Xb[ri][:, :, 0], in_=X0[ri][:, :, it], func=AF.Copy), reads=[t_X0[ri]], writes=[t_Xb[ri]])
        zscan(it)
        for ri in range(2):
            S.op("act", lambda e, ri=ri: e.activation(out=X16[ri][:], in_=Xb[ri][:, :, 0:NJ], func=AF.Copy), reads=[t_Xb[ri]], writes=[t_X16[ri]])
        for c8 in range(8):
            wc, t_wc = self.load_wc(dr, c8)
            fir = wc[:, 0:1024].rearrange("p (k o) -> p k o", k=8)
            wo = wc[:, 1024:3072].rearrange("p (q r i o) -> p q r i o", q=4, r=8, i=2)
            ut = self.bufA[:, UO + c8, :]
            uv = ut.rearrange("p (j s) -> p j s", s=8)
            py, t_py = self.psum()
            pyv = py[:, 0:TT].rearrange("p (j s) -> p j s", s=8)
            rd = [t_wc, self.t_A[UO + c8]] + t_X16
            S.op("pe", lambda e, py=py, fir=fir, ut=ut: e.matmul(py[:, 0:TT], fir[:, 0, :], ut, start=True, stop=False), reads=rd, writes=[t_py], inc=False)
            for k in range(1, 8):
                S.op("pe", lambda e, pyv=pyv, fir=fir, uv=uv, k=k: e.matmul(pyv[:, :, k:8], fir[:, k, :], uv[:, :, 0:8 - k], start=False, stop=False),
                     reads=rd, writes=[t_py], inc=False)
            for q in range(4):
                for r in range(8):
                    for ri in range(2):
                        last = (q == 3 and r == 7 and ri == 1)
                        S.op("pe", lambda e, pyv=pyv, wo=wo, q=q, r=r, ri=ri, c8=c8, last=last: e.matmul(
                            pyv[q * 32:(q + 1) * 32, :, r], wo[:, q, r, ri, :], X16[ri][:, c8 * 4 + q, :], start=False, stop=last,
                            tile_position=(0, q * 32)), reads=rd, writes=[t_py], inc=last)
            b = c8 % 2
            S.op("act", lambda e, py=py, b=b: e.activation(out=yx[b][:], in_=py[:, 0:TT], func=AF.Square), reads=[t_py], writes=[t_yx[b]])
            S.op("dve", lambda e, b=b: e.tensor_scalar(out=yx[b][:], in0=yx[b][:], scalar1=0.044715, scalar2=1.0, op0=ALU.mult, op1=ALU.add),
                 reads=[t_yx[b]], writes=[t_yx[b]])
            S.op("dve", lambda e, py=py, b=b: e.tensor_tensor(out=yx[b][:], in0=yx[b][:], in1=py[:, 0:TT], op=ALU.mult), reads=[t_yx[b], t_py], writes=[t_yx[b]])
            S.op("act", lambda e, b=b: e.activation(out=yx[b][:], in_=yx[b][:], func=AF.Sigmoid, scale=1.5957691216057308), reads=[t_yx[b]], writes=[t_yx[b]])
            S.op("dve", lambda e, py=py, b=b: e.tensor_tensor(out=yg[b][:], in0=yx[b][:], in1=py[:, 0:TT], op=ALU.mult), reads=[t_yx[b], t_py], writes=[t_yg[b]])
            S.op("act", lambda e, b=b: e.activation(out=yb[b][:], in_=yg[b][:], func=AF.Copy), reads=[t_yg[b]], writes=[t_yb[b]])
            pg, t_pg = self.psum()
            self.mm_group(pg[:, 0:TT], t_pg, [(gblk[:, c8, :], yb[b][:])], reads=[t_gblk, t_yb[b]])
            S.op("act", lambda e, pg=pg, b=b: e.activation(out=yx[b][:], in_=pg[:, 0:TT], func=AF.Sigmoid), reads=[t_pg], writes=[t_yx[b]])
            S.op("dve", lambda e, b=b, c8=c8: e.tensor_tensor(out=self.bufA[:, 8 + c8, :], in0=yg[b][:], in1=yx[b][:], op=ALU.mult),
                 reads=[t_yg[b], t_yx[b]], writes=[self.t_A[8 + c8]])
        if it == 0:
            self.dbg("m_ya0", self.bufA[:, 0, :], [self.t_A[0]], [P, TT], BF16)
            self.dbg("m_ys0", self.bufA[:, 8, :], [self.t_A[8]], [P, TT], BF16)
            self.dbg("m_u0", self.bufA[:, 16, :], [self.t_A[16]], [P, TT], BF16)
            self.dbg("m_X16", X16[0][:].rearrange("p a b -> p (a b)"), [t_X16[0]], [P, 32 * 64], BF16)
        self.linear_resid(Wout, KC, self.t_A[0:KC], self.bufA, 0)
        if it == 0:
            self.dbg("m_hT0", self.hT[:, 0, :], [self.t_hT[0]], [P, TT])
        self.store_h(h_out, tok0)


def _load_wz(self, dr, c8):
    i = self.wi
    self.wi = (i + 1) % NWB
    view = self.wb[i][:, 0:2048].rearrange("p (s i o) -> p s i o", s=8, i=2)
    t = self.t_wb[i]
    self.S.dma("sp", [lambda e: e.dma_start(out=self.wb[i][:, 0:2048], in_=dr["wz"][:, c8, :])], t, reads=[dr["t_wz"]], writes=[t])
    return view, t


def _load_wc(self, dr, c8):
    i = self.wi
    self.wi = (i + 1) % NWB
    view = self.wb[i][:, 0:3072]
    t = self.t_wb[i]
    self.S.dma("sp", [lambda e: e.dma_start(out=self.wb[i][:, 0:3072], in_=dr["wc"][:, c8, :])], t, reads=[dr["t_wc"]], writes=[t])
    return view, t


Prog.even_mixer = _even_mixer
Prog.load_wz = _load_wz
Prog.load_wc = _load_wc

EV_SMALL = dict(are_S=[P, 32], aim_S=[P, 32], ldt_S=[P, 32], bre_S=[P, 32, 16], bim_S=[P, 32, 16], cre_S=[P, 32, 16], cim_S=[P, 32, 16],
                are_T=[P, 8, 64], aim_T=[P, 8, 64], ldt_T=[P, 8, 64], bre_T=[P, 8, 64], bim_T=[P, 8, 64], mask01=[P, 2], ident=[P, P],
                dcol=[P, 8], gblk=[P, 8, P], convw=[P, 3, 8])


def build_even_launch():
    nc = bass.Bass("TRN2", target_bir_lowering=False)
    dt = lambda name, shape, kind="ExternalInput": nc.dram_tensor(name, shape, F32, kind=kind).ap()
    h_in = dt("h_in", [D, NTOK]); h_out = dt("h_out", [D, NTOK], "ExternalOutput")
    hhalo = dt("hhalo", [D, 2]); gm = dt("gm", [P, KC])
    s5in = dt("s5in", [2, P, 32]); s5out = dt("s5out", [2, P, 32], "ExternalOutput")
    Win = dt("Win", [D, 4096]); Wout = dt("Wout", [D, D])
    prm = {k: dt(k, shp) for k, shp in EV_SMALL.items()}
    dr = dict(wz=nc.dram_tensor("wz_scr", [P, 8, 2048], BF16, kind="Internal").ap(), t_wz=T("wz_scr"),
              wc=nc.dram_tensor("wc_scr", [P, 8, 3072], BF16, kind="Internal").ap(), t_wc=T("wc_scr"))
    with ExitStack() as st:
        pr = Prog(nc, st)
        g, t_g = pr.load_small("gm_s", gm, [P, KC])
        with pr.phase():
            t_Xb = pr.even_mixer(h_in, h_out, hhalo, s5in, s5out, Win, Wout, prm, dr, g, t_g, 0)
            pr.S.wait_all("sp", t_Xb)
        pr.finish(pr.t_hT)
    return nc


def build_final_launch():
    nc = bass.Bass("TRN2", target_bir_lowering=False)
    dt = lambda name, shape, kind="ExternalInput": nc.dram_tensor(name, shape, F32, kind=kind).ap()
    h_in = dt("h_in", [D, NTOK]); out = dt("out", [D, NTOK], "ExternalOutput"); gfin = dt("gfin", [P, KC])
    with ExitStack() as st:
        pr = Prog(nc, st)
        g, t_g = pr.load_small("gfin_s", gfin, [P, KC])
        with pr.phase():
            pr.final_norm(h_in, out, g, t_g, 0)
            pr.S.wait_all("sp", pr.t_fo)
        pr.finish([])
    return nc


NCORES = 8


def _run(nc, maps):
    res = run_bass_kernel_spmd(nc, maps, core_ids=list(range(NCORES)))
    return res.results


def kernel_unfused(**inp):
    z = {k: np.asarray(v) for k, v in inp.items()}
    x = z["x"].astype(np.float32)
    B = x.shape[0]
    cores = [(b, hf) for b in range(B) for hf in range(2)]
    h = [np.ascontiguousarray(x[b, hf * NTOK:(hf + 1) * NTOK, :].T) for (b, hf) in cores]

    def halo(w):
        out = []
        for ci, (b, hf) in enumerate(cores):
            if hf == 0:
                out.append(np.zeros((D, w), np.float32))
            else:
                out.append(np.ascontiguousarray(h[ci - 1][:, -w:]))
        return out
    oh, maskT = swa_consts()
    nc_x = build_xattn_launch()
    nc_f = build_ffn_launch()
    nc_e = build_even_launch()
    nc_s = build_swa_launch()
    for l in range(4):
        i = l // 2
        if l % 2 == 0:
            hp = ev_host_params(z, i)
            hal = halo(2)
            s5 = [np.zeros((2, P, 32), np.float32) for _ in cores]
            for rep in range(2):
                maps = [dict(h_in=h[ci], hhalo=hal[ci], gm=pk(z["norm_mix"][l]), s5in=s5[ci], Win=z["ev_w_in"][i], Wout=z["ev_w_out"][i], **hp)
                        for ci in range(len(cores))]
                r = _run(nc_e, maps)
                if rep == 0:
                    s5 = [np.zeros((2, P, 32), np.float32) if hf == 0 else r[ci - 1]["s5out"] for ci, (b, hf) in enumerate(cores)]
            h = [r[ci]["h_out"] for ci in range(len(cores))]
        else:
            hp = swa_host_params(z, i)
            hal = halo(128)
            maps = [dict(h_in=h[ci], hhalo=hal[ci], hmask=(np.full((P, 1), -30000.0, np.float32) if hf == 0 else np.zeros((P, 1), np.float32)),
                         gm=pk(z["norm_mix"][l]), oh=oh, maskT=maskT, relb=hp["relb"], sinkrow=hp["sinkrow"], bq=hp["bq"], bkd=hp["bkd"], bvd=hp["bvd"],
                         Wq=hp["Wq"], Wkd=hp["Wkd"], Wvd=hp["Wvd"], Wo=z["od_w_out"][i]) for ci, (b, hf) in enumerate(cores)]
            r = _run(nc_s, maps)
            h = [r[ci]["h_out"] for ci in range(len(cores))]
        maps = [dict(h_in=h[ci], memT=np.ascontiguousarray(z["mem"][b].T), gmem=pk(z["norm_mem"]), gx=pk(z["norm_xattn"][l]),
                     Wq=z["xa_w_q"][l], Wkv=z["xa_w_kv"][l], Wo=z["xa_w_o"][l]) for ci, (b, hf) in enumerate(cores)]
        r = _run(nc_x, maps)
        h = [r[ci]["h_out"] for ci in range(len(cores))]
        hal = halo(2)
        maps = [dict(h_in=h[ci], hhalo=hal[ci], gf=pk(z["norm_ffn"][l]), cw=pk(z["ff_conv_w"][l]), cb=pk(z["ff_conv_b"][l]),
                     Wg=z["ff_w_gate"][l], Wu=z["ff_w_up"][l], Wd=z["ff_w_down"][l]) for ci in range(len(cores))]
        r = _run(nc_f, maps)
        h = [r[ci]["h_out"] for ci in range(len(cores))]
    nc_n = build_final_launch()
    r = _run(nc_n, [dict(h_in=h[ci], gfin=pk(z["norm_final"])) for ci in range(len(cores))])
    out = np.empty((B, 2 * NTOK, D), np.float32)
    for ci, (b, hf) in enumerate(cores):
        out[b, hf * NTOK:(hf + 1) * NTOK, :] = r[ci]["out"].T
    return out


def kernel(**inp):
    return kernel_unfused(**inp)


RG_PAIRS = [[0, 1], [2, 3], [4, 5], [6, 7]]
SWA_SMALL = dict(bq=[P, KC], bkd=[P, 4], bvd=[P, 512], sinkrow=[1, 32])


def build_fused(depth=4, final=True, stop_after=None):
    nc = bass.Bass("TRN2", target_bir_lowering=False)
    n_ev = (depth + 1) // 2
    n_od = depth // 2
    global LAST_INPUT_NAMES
    LAST_INPUT_NAMES = []

    def dt(name, shape, kind="ExternalInput"):
        if kind == "ExternalInput":
            LAST_INPUT_NAMES.append(name)
        return nc.dram_tensor(name, list(shape), F32, kind=kind).ap()
    xT = dt("xT", [D, NTOK]); xhalo = dt("xhalo", [D, 128]); flag_d = dt("flag", [P, 1]); hmask_d = dt("hmask", [P, 1])
    memT = dt("memT", [D, N_MEM]); gmem = dt("gmem", [P, KC]); gfin = dt("gfin", [P, KC])
    gmix = dt("gmix", [P, 4, KC]); gxa = dt("gxa", [P, 4, KC]); gff = dt("gff", [P, 4, KC])
    oh_d = dt("oh", [P, 32, 2, 128]); mask_d = dt("maskT", [P, 2, 128]); relb_d = dt("relb", [P, 32, 32])
    ev = []
    for i in range(n_ev):
        e = dict(Win=dt("ev_w_in%d" % i, [D, 4096]), Wout=dt("ev_w_out%d" % i, [D, D]))
        e["prm"] = {k: dt("ev%d_%s" % (i, k), shp) for k, shp in EV_SMALL.items()}
        e["dr"] = dict(wz=nc.dram_tensor("wz_scr%d" % i, [P, 8, 2048], BF16, kind="Internal").ap(), t_wz=T("wz_scr"),
                       wc=nc.dram_tensor("wc_scr%d" % i, [P, 8, 3072], BF16, kind="Internal").ap(), t_wc=T("wc_scr"))
        ev.append(e)
    od = []
    for i in range(n_od):
        o = dict(Wq=dt("od_wq%d" % i, [D, D]), Wkd=dt("od_wkd%d" % i, [D, 512]), Wvd=dt("od_wvd%d" % i, [D, 512]), Wo=dt("od_wo%d" % i, [D, D]))
        o["prm"] = {k: dt("od%d_%s" % (i, k), shp) for k, shp in SWA_SMALL.items()}
        od.append(o)
    xa = [dict(Wq=dt("xa_wq%d" % l, [D, D]), Wkv=dt("xa_wkv%d" % l, [D, 2 * D]), Wo=dt("xa_wo%d" % l, [D, D])) for l in range(depth)]
    ff = [dict(Wg=dt("ff_wg%d" % l, [D, D_FF]), Wu=dt("ff_wu%d" % l, [D, D_FF]), Wd=dt("ff_wd%d" % l, [D_FF, D]),
               cw=dt("ff_cw%d" % l, [P, 3, FC]), cb=dt("ff_cb%d" % l, [P, FC])) for l in range(depth)]
    out = dt("out", [D, NTOK], "ExternalOutput")
    if stop_after is None:
        h_scr = nc.dram_tensor("h_scr", [D, NTOK], F32, kind="Internal").ap()
    else:
        h_scr = dt("h_dbg", [D, NTOK], "ExternalOutput")
    xch = dict(rg=RG_PAIRS,
               snd_h=nc.dram_tensor("snd_h", [D, 128], F32, kind="Internal").ap(), t_snd_h=T("snd_h"),
               rcv_h=nc.dram_tensor("rcv_h", [2 * D, 128], F32, kind="Internal").ap(), t_rcv_h=T("rcv_h"),
               snd_s=nc.dram_tensor("snd_s", [2 * P, 32], F32, kind="Internal").ap(), t_snd_s=T("snd_s"),
               rcv_s=nc.dram_tensor("rcv_s", [4 * P, 32], F32, kind="Internal").ap(), t_rcv_s=T("rcv_s"))
    with ExitStack() as st:
        pr = Prog(nc, st)
        S = pr.S
        gm, t_gm = pr.load_small("gmix_s", gmix, [P, 4, KC])
        gx, t_gx = pr.load_small("gxa_s", gxa, [P, 4, KC])
        gf, t_gf = pr.load_small("gff_s", gff, [P, 4, KC])
        gfn, t_gfn = pr.load_small("gfin_s", gfin, [P, KC])
        flag = pr.load_small("flag_s", flag_d, [P, 1])
        hm, t_hm = pr.load_small("hmask_s", hmask_d, [P, 1])
        gmv = gm[:].rearrange("p l k -> p (l k)")
        gxv = gx[:].rearrange("p l k -> p (l k)")
        gfv = gf[:].rearrange("p l k -> p (l k)")
        pr.prep_mem(memT, gmem)

        def exchange_h():
            S.dma("sp", [lambda e: e.dma_start(out=xch["snd_h"].rearrange("(k p) t -> p k t", p=P), in_=pr.hT[:, :, TT - 128:TT])],
                  xch["t_snd_h"], reads=pr.t_hT, writes=[xch["t_snd_h"]])
            S.cc("pool", lambda e: e.collective_compute("AllGather", ALU.bypass, replica_groups=xch["rg"], ins=[xch["snd_h"]], outs=[xch["rcv_h"]]),
                 reads=[xch["t_snd_h"]], writes=[xch["t_rcv_h"]])

        for l in range(depth):
            i = l // 2
            h_in = xT if l == 0 else h_scr
            halo_src = xhalo if l == 0 else xch["rcv_h"][0:D, :]
            hrd = [] if l == 0 else [xch["t_rcv_h"]]
            with pr.phase():
                if l % 2 == 0:
                    pr.even_mixer(h_in, h_scr, halo_src[:, 126:128], ev[i]["Win"], ev[i]["Wout"], ev[i]["prm"], ev[i]["dr"],
                                  gmv, t_gm, l * KC, flag, hrd, xch)
                else:
                    o = od[i]
                    bq, t_bq = pr.load_small("bq_s", o["prm"]["bq"], [P, KC])
                    S.op("dve", lambda e, bq=bq: e.tensor_scalar(out=bq[:], in0=bq[:], scalar1=0.125, scalar2=None, op0=ALU.mult), reads=[t_bq], writes=[t_bq])
                    bkd, t_bkd = pr.load_small("bkd_s", o["prm"]["bkd"], [P, 4])
                    bvd, t_bvd = pr.load_small("bvd_s", o["prm"]["bvd"], [P, 512])
                    pr.swa_setup(oh_d, mask_d, relb_d, o["prm"]["sinkrow"])
                    pr.swa(h_in, h_scr, halo_src, hm, t_hm, o["Wq"], o["Wkd"], o["Wvd"], o["Wo"], bq, t_bq, bkd, t_bkd, bvd, t_bvd,
                           gmv, t_gm, l * KC, flag=flag, hrd=hrd)
            if stop_after == (l, "mix"):
                break
            with pr.phase():
                pr.xattn(h_scr, h_scr, xa[l]["Wq"], xa[l]["Wkv"], xa[l]["Wo"], gxv, t_gx, l * KC)
                exchange_h()
            if stop_after == (l, "xa"):
                break
            with pr.phase():
                cws, t_cw = pr.load_small("cw_s", ff[l]["cw"], [P, 3, FC])
                cbs, t_cb = pr.load_small("cb_s", ff[l]["cb"], [P, FC])
                pr.ffn(h_scr, h_scr, xch["rcv_h"][0:D, 126:128], ff[l]["Wg"], ff[l]["Wu"], ff[l]["Wd"], gfv, t_gf, l * KC, cws, t_cw, cbs, t_cb,
                       flag=flag, hrd=[xch["t_rcv_h"]])
                if l < depth - 1:
                    exchange_h()
        with pr.phase():
            pr.final_norm(h_scr, out, gfn, t_gfn, 0)
            S.wait_all("sp", pr.t_fo)
        pr.finish(getattr(pr, "dbg_tiles", []))
    return nc


def fused_inputs(z):
    x = np.asarray(z["x"], np.float32)
    B = x.shape[0]
    cores = [(b, hf) for b in range(B) for hf in range(2)]
    oh, maskT = swa_consts()
    common = dict(gmem=pk(z["norm_mem"]), gfin=pk(z["norm_final"]), gmix=pk(z["norm_mix"]), gxa=pk(z["norm_xattn"]), gff=pk(z["norm_ffn"]),
                  oh=oh, maskT=maskT)
    for i in range(2):
        hp = ev_host_params(z, i)
        common["ev_w_in%d" % i] = np.asarray(z["ev_w_in"][i], np.float32)
        common["ev_w_out%d" % i] = np.asarray(z["ev_w_out"][i], np.float32)
        for k in EV_SMALL:
            common["ev%d_%s" % (i, k)] = hp[k]
        sp = swa_host_params(z, i)
        common["relb"] = sp["relb"]
        common["od_wq%d" % i] = sp["Wq"]
        common["od_wkd%d" % i] = sp["Wkd"]
        common["od_wvd%d" % i] = sp["Wvd"]
        common["od_wo%d" % i] = np.asarray(z["od_w_out"][i], np.float32)
        for k in SWA_SMALL:
            common["od%d_%s" % (i, k)] = sp[k]
    for l in range(4):
        common["xa_wq%d" % l] = np.asarray(z["xa_w_q"][l], np.float32)
        common["xa_wkv%d" % l] = np.asarray(z["xa_w_kv"][l], np.float32)
        common["xa_wo%d" % l] = np.asarray(z["xa_w_o"][l], np.float32)
        common["ff_wg%d" % l] = np.asarray(z["ff_w_gate"][l], np.float32)
        common["ff_wu%d" % l] = np.asarray(z["ff_w_up"][l], np.float32)
        common["ff_wd%d" % l] = np.asarray(z["ff_w_down"][l], np.float32)
        common["ff_cw%d" % l] = pk(z["ff_conv_w"][l])
        common["ff_cb%d" % l] = pk(z["ff_conv_b"][l])
    maps = []
    for (b, hf) in cores:
        m = dict(common)
        m["xT"] = np.ascontiguousarray(x[b, hf * NTOK:(hf + 1) * NTOK, :].T)
        m["xhalo"] = np.zeros((D, 128), np.float32) if hf == 0 else np.ascontiguousarray(x[b, NTOK - 128:NTOK, :].T)
        m["flag"] = np.full((P, 1), float(hf), np.float32)
        m["hmask"] = np.full((P, 1), -30000.0 if hf == 0 else 0.0, np.float32)
        m["memT"] = np.ascontiguousarray(np.asarray(z["mem"], np.float32)[b].T)
        maps.append(m)
    return cores, maps


def kernel_fused(**inp):
    z = {k: np.asarray(v) for k, v in inp.items()}
    cores, maps = fused_inputs(z)
    nc = build_fused()
    decl = set(LAST_INPUT_NAMES)
    maps = [{k: v for k, v in m.items() if k in decl} for m in maps]
    res = run_bass_kernel_spmd(nc, maps, core_ids=list(range(len(cores))))
    B = z["x"].shape[0]
    out = np.empty((B, 2 * NTOK, D), np.float32)
    for ci, (b, hf) in enumerate(cores):
        out[b, hf * NTOK:(hf + 1) * NTOK, :] = res.results[ci]["out"].T
    return out


def kernel(**inp):
    return kernel_fused(**inp)
```

```python
import math
from contextlib import ExitStack

import numpy as np
import concourse.bass as bass
import concourse.mybir as mybir
from concourse.bass_utils import run_bass_kernel_spmd

F32 = mybir.dt.float32
BF16 = mybir.dt.bfloat16
AF = mybir.ActivationFunctionType
ALU = mybir.AluOpType

P = 128
D = 2048
KC = D // P
NTOK = 2048
TT = 512
NT = NTOK // TT
N_MEM = 256
D_FF = 5632
FC = D_FF // P
RMS_EPS = 1e-5


class T:
    __slots__ = ("name", "w", "r", "sem", "semval", "last_dma")

    def __init__(self, name):
        self.name = name
        self.w = None
        self.r = {}
        self.sem = None
        self.semval = 0
        self.last_dma = None


ENGS = ("pe", "act", "dve", "pool", "sp")


class Sched:
    def __init__(self, nc, stack):
        self.nc = nc
        self.stack = stack
        self.q = {e: [] for e in ENGS}
        self.cnt = {e: 0 for e in ENGS}
        self.waited = {e: {} for e in ENGS}
        self.semh = {}
        for e in ENGS:
            self.semh[e] = stack.enter_context(nc.semaphore("sem_" + e))
        self.ndma_sem = 0
        self.n_instr = 0
        self.dmaval = {}
        self.gd = [stack.enter_context(nc.sbuf_tensor("gd%d" % i, [P, 1], F32)) for i in range(3)]
        self.q["dve"].append(lambda e: e.memset(self.gd[2][:], 0.0))

    def barrier(self):
        cur = {e: self.cnt[e] for e in ENGS if self.cnt[e] > 0}
        cur.update(self.dmaval)
        for e in ENGS:
            self._need(e, {k: v for k, v in cur.items() if k != e})

    NDSEM = 40

    def _tile_sem(self, t):
        if t.sem is None:
            i = self.ndma_sem % self.NDSEM
            key = "dsem%d" % i
            self.ndma_sem += 1
            if key not in self.semh:
                self.semh[key] = self.stack.enter_context(self.nc.semaphore(key))
            t.sem = key
        return t.sem

    def _need(self, eng, needs):
        for key, val in needs.items():
            if key == "pe" and eng == "pe":
                continue
            if self.waited[eng].get(key, 0) >= val:
                continue
            self.waited[eng][key] = val
            h = self.semh[key]
            self.q[eng].append(lambda e, h=h, val=val: e.wait_ge(h, val))

    def _deps(self, reads, writes):
        needs = {}

        def add(m):
            if m is not None:
                if needs.get(m[0], 0) < m[1]:
                    needs[m[0]] = m[1]
        for t in reads:
            add(t.w)
        for t in writes:
            add(t.w)
            for m in t.r.items():
                add(m)
        return needs

    def op(self, eng, fn, reads=(), writes=(), inc=True, guard=False):
        needs = self._deps(reads, writes)
        self._need(eng, needs)
        if guard and eng in ("dve", "act") and inc:
            self.q[eng].append(lambda e, fn=fn: fn(e))
            self.cnt[eng] += 1
            h = self.semh[eng]
            g = self.gd
            if eng == "dve":
                self.q[eng].append(lambda e, h=h, g=g: e.memset(g[0][:], 0.0).then_inc(h, 1))
            else:
                self.q[eng].append(lambda e, h=h, g=g: e.activation(out=g[1][:], in_=g[2][:], func=AF.Copy).then_inc(h, 1))
            mark = (eng, self.cnt[eng])
            self._mark(reads, writes, mark)
            self.n_instr += 2
            return
        if inc:
            self.cnt[eng] += 1
            h = self.semh[eng]
            self.q[eng].append(lambda e, fn=fn, h=h: fn(e).then_inc(h, 1))
            mark = (eng, self.cnt[eng])
        else:
            self.q[eng].append(lambda e, fn=fn: fn(e))
            mark = (eng, self.cnt[eng] + 1)
        self._mark(reads, writes, mark)
        self.n_instr += 1

    def dma(self, eng, fns, owner, reads=(), writes=(), step=16):
        key = self._tile_sem(owner)
        needs = self._deps(reads, writes)
        cur = self.dmaval.get(key, 0)
        if cur > 0 and needs.get(key, 0) < cur:
            needs[key] = cur
        self._need(eng, needs)
        h = self.semh[key]
        for fn in fns:
            cur += step
            self.q[eng].append(lambda e, fn=fn, h=h: fn(e).then_inc(h, step))
        mark = (key, cur)
        self.dmaval[key] = cur
        self._mark(reads, writes, mark)
        self.n_instr += len(fns)

    def cc(self, eng, fn, reads=(), writes=()):
        key = "ccsem%d" % len([k for k in self.semh if k.startswith("ccsem")])
        self.semh[key] = self.stack.enter_context(self.nc.semaphore(key))
        needs = self._deps(reads, writes)
        self._need(eng, needs)
        h = self.semh[key]
        self.q[eng].append(lambda e, fn=fn, h=h: fn(e).then_inc(h, 1))
        mark = (key, 1)
        self.dmaval[key] = 1
        self._mark(reads, writes, mark)
        self.n_instr += 1

    @staticmethod
    def _mark(reads, writes, mark):
        for t in reads:
            if t.r.get(mark[0], 0) < mark[1]:
                t.r[mark[0]] = mark[1]
        for t in writes:
            t.w = mark
            t.r = {}

    def wait_all(self, eng, tiles):
        needs = {}
        for t in tiles:
            for m in ([t.w] if t.w else []) + list(t.r.items()):
                if needs.get(m[0], 0) < m[1]:
                    needs[m[0]] = m[1]
        self._need(eng, needs)

    def emit(self):
        nc = self.nc
        if not any(self.q[e] for e in ENGS):
            return
        qs = {e: self.q[e] for e in ENGS}
        self.q = {e: [] for e in ENGS}
        self._emit_block(nc, qs)

    def _emit_block(self, nc, qs):
        self_q = qs
        with nc.Block() as block:
            @block.tensor
            def _(e):
                for f in self_q["pe"]:
                    f(e)

            @block.scalar
            def _(e):
                for f in self_q["act"]:
                    f(e)

            @block.vector
            def _(e):
                for f in self_q["dve"]:
                    f(e)

            @block.gpsimd
            def _(e):
                for f in self_q["pool"]:
                    f(e)

            @block.sync
            def _(e):
                for f in self_q["sp"]:
                    f(e)


WSLOT = 5632
NWB = 4
NPS = 8


class Prog:
    def __init__(self, nc, stack):
        self.nc = nc
        self.st = stack
        self.S = Sched(nc, stack)
        self.pstack = None
        self.uid = 0

        def sb(name, shape, dt):
            self.uid += 1
            stk = self.pstack if self.pstack is not None else stack
            return stk.enter_context(nc.sbuf_tensor("%s_%d" % (name, self.uid), shape, dt))
        self.sb = sb
        self.hT = sb("hT", [P, KC, TT], F32)
        self.t_hT = [T("hT%d" % k) for k in range(KC)]
        self.hn = sb("hn", [P, KC, TT], BF16)
        self.t_hn = [T("hn%d" % k) for k in range(KC)]
        self.wb = [sb("wb%d" % i, [P, WSLOT], BF16) for i in range(NWB)]
        self.t_wb = [T("wb%d" % i) for i in range(NWB)]
        self.wi = 0
        self.ps = [stack.enter_context(nc.psum_tensor("ps%d" % i, [P, 512], F32)) for i in range(NPS)]
        self.t_ps = [T("ps%d" % i) for i in range(NPS)]
        self.pi = 0
        self.sq = [sb("sq%d" % i, [P, TT], BF16) for i in range(2)]
        self.t_sq = [T("sq0"), T("sq1")]
        self.rstd = sb("rstd", [P, TT], F32)
        self.t_rstd = T("rstd")
        self.rtmp = sb("rtmp", [P, TT], F32)
        self.t_rtmp = T("rtmp")
        self.ones = sb("ones", [P, P], BF16)
        self.t_ones = T("ones")
        self.S.op("dve", lambda e: e.memset(self.ones[:], 1.0), writes=[self.t_ones])
        self.epsc = sb("epsc", [P, 1], F32)
        self.t_eps = T("eps")
        self.S.op("dve", lambda e: e.memset(self.epsc[:], RMS_EPS), writes=[self.t_eps])
        self.evi = 0

    def phase(self):
        prog = self

        class _Ph:
            def __enter__(s):
                prog.S.barrier()
                prog.S.emit()
                s.es = ExitStack()
                s.es.__enter__()
                s.prev = prog.pstack
                prog.pstack = s.es
                return s

            def __exit__(s, *a):
                prog.S.barrier()
                prog.S.emit()
                prog.pstack = s.prev
                return s.es.__exit__(*a)
        return _Ph()

    DBG = False

    def dbg(self, name, ap, tiles, shape, dt=F32):
        if not self.DBG:
            return
        d = self.nc.dram_tensor("dbg_" + name, list(shape), dt, kind="ExternalOutput").ap()
        t = T("dbg_" + name)
        self.S.dma("sp", [lambda e: e.dma_start(out=d, in_=ap)], t, reads=list(tiles))
        self.dbg_tiles = getattr(self, "dbg_tiles", []) + [t]

    def alloc_A(self, n):
        self.bufA = self.sb("bufA", [P, n, TT], BF16)
        self.t_A = [T("A%d" % k) for k in range(n)]

    def psum(self):
        i = self.pi
        self.pi = (i + 1) % NPS
        return self.ps[i], self.t_ps[i]

    def load_w(self, src3, k, m):
        i = self.wi
        self.wi = (i + 1) % NWB
        view = self.wb[i][:, 0:k * m].rearrange("p (k m) -> p k m", k=k)
        t = self.t_wb[i]
        if k > 22:
            h = k // 2
            fns = [lambda e: e.dma_start(out=view[:, 0:h, :], in_=src3[:, 0:h, :]),
                   lambda e: e.dma_start(out=view[:, h:, :], in_=src3[:, h:, :])]
        else:
            fns = [lambda e: e.dma_start(out=view, in_=src3)]
        self.S.dma("pool", fns, t, writes=[t])
        return view, t

    def mm_group(self, out_ap, t_out, pairs, reads, tp=None):
        n = len(pairs)
        for i, (l, r) in enumerate(pairs):
            if tp is None:
                self.S.op("pe", lambda e, l=l, r=r, i=i: e.matmul(out_ap, l, r, start=(i == 0), stop=(i == n - 1)),
                          reads=reads, writes=[t_out], inc=(i == n - 1))
            else:
                self.S.op("pe", lambda e, l=l, r=r, i=i: e.matmul(out_ap, l, r, start=(i == 0), stop=(i == n - 1), tile_position=tp),
                          reads=reads, writes=[t_out], inc=(i == n - 1))

    def load_small(self, name, dram_ap, shape, dt=F32):
        t = self.sb(name, shape, dt)
        tt = T(name)
        self.S.dma("sp", [lambda e: e.dma_start(out=t[:], in_=dram_ap)], tt, writes=[tt])
        return t, tt

    def rmsnorm(self, src, t_src, gain, t_gain, gcol, dst, t_dst, w, nk=KC):
        S = self.S
        ps, t_ps = self.psum()
        for k in range(nk):
            b = k % 2
            S.op("act", lambda e, k=k, b=b: e.activation(out=self.sq[b][:, 0:w], in_=src[:, k, 0:w], func=AF.Square),
                 reads=[t_src[k]], writes=[self.t_sq[b]])
            S.op("pe", lambda e, k=k, b=b: e.matmul(ps[:, 0:w], self.ones[:], self.sq[b][:, 0:w], start=(k == 0), stop=(k == nk - 1)),
                 reads=[self.t_sq[b], self.t_ones], writes=[t_ps], inc=True)
        S.op("act", lambda e: e.activation(out=self.rtmp[:, 0:w], in_=ps[:, 0:w], func=AF.Sqrt, bias=self.epsc[:], scale=1.0 / (nk * P)),
             reads=[t_ps, self.t_eps], writes=[self.t_rtmp])
        S.op("dve", lambda e: e.reciprocal(out=self.rstd[:, 0:w], in_=self.rtmp[:, 0:w]), reads=[self.t_rtmp], writes=[self.t_rstd])
        if getattr(self, "dbg_norm", False):
            self.dbg_norm = False
            self.dbg("n_hT0", src[:, 0, 0:w], [t_src[0]], [P, w], F32)
            self.dbg("n_hT15", src[:, 15, 0:w], [t_src[15]], [P, w], F32)
            self.dbg("n_rtmp", self.rtmp[:, 0:w], [self.t_rtmp], [P, w], F32)
            self.dbg("n_rstd", self.rstd[:, 0:w], [self.t_rstd], [P, w], F32)
        for k in range(nk):
            S.op("dve", lambda e, k=k: e.scalar_tensor_tensor(out=dst[:, k, 0:w], in0=src[:, k, 0:w], scalar=gain[:, gcol + k:gcol + k + 1],
                                                              in1=self.rstd[:, 0:w], op0=ALU.mult, op1=ALU.mult),
                 reads=[t_src[k], t_gain, self.t_rstd], writes=[t_dst[k]])

    def load_h(self, h_dram, tok0):
        S = self.S
        src = h_dram[:, tok0:tok0 + TT].rearrange("(k p) t -> p k t", p=P)
        for g in range(4):
            S.dma("sp", [lambda e, g=g: e.dma_start(out=self.hT[:, 4 * g:4 * g + 4, :], in_=src[:, 4 * g:4 * g + 4, :])],
                  self.t_hT[4 * g], writes=self.t_hT[4 * g:4 * g + 4])

    def store_h(self, h_dram, tok0):
        S = self.S
        dst = h_dram[:, tok0:tok0 + TT].rearrange("(k p) t -> p k t", p=P)
        for g in range(4):
            S.dma("sp", [lambda e, g=g: e.dma_start(out=dst[:, 4 * g:4 * g + 4, :], in_=self.hT[:, 4 * g:4 * g + 4, :])],
                  self.t_hT[4 * g], reads=self.t_hT[4 * g:4 * g + 4])

    def linear_resid(self, W, kin, t_in_list, in_buf, in_off):
        S = self.S
        mb = 256
        per = mb // P
        nh = 1 if kin <= 22 else 2
        kh = kin // nh
        for blk in range(D // mb):
            ws = []
            for hf in range(nh):
                ws.append(self.load_w(W[hf * kh * P:(hf + 1) * kh * P, blk * mb:(blk + 1) * mb].rearrange("(k p) m -> p k m", p=P), kh, mb))
            for m in range(per):
                ps, t_ps = self.psum()
                self.mm_group(ps[:, 0:TT], t_ps,
                              [(ws[k // kh][0][:, k % kh, m * P:(m + 1) * P], in_buf[:, in_off + k, :]) for k in range(kin)],
                              reads=[x[1] for x in ws] + t_in_list)
                c = blk * per + m
                S.op("dve", lambda e, c=c, ps=ps: e.tensor_tensor(out=self.hT[:, c, :], in0=self.hT[:, c, :], in1=ps[:, 0:TT], op=ALU.add),
                     reads=[t_ps, self.t_hT[c]], writes=[self.t_hT[c]])

    def linear_to(self, W, col0, ncols, t_in_list, in_buf, out_buf, out_off, t_out_list, evac=None):
        S = self.S
        mb = 256
        for blk in range(ncols // mb):
            w, t_w = self.load_w(W[:, col0 + blk * mb:col0 + (blk + 1) * mb].rearrange("(k p) m -> p k m", p=P), KC, mb)
            for m in range(2):
                ps, t_ps = self.psum()
                self.mm_group(ps[:, 0:TT], t_ps, [(w[:, k, m * P:(m + 1) * P], in_buf[:, k, :]) for k in range(KC)],
                              reads=[t_w] + t_in_list)
                c = blk * 2 + m
                if evac is not None:
                    evac(c, ps, t_ps)
                else:
                    S.op("act", lambda e, c=c, ps=ps: e.activation(out=out_buf[:, out_off + c, :], in_=ps[:, 0:TT], func=AF.Copy),
                         reads=[t_ps], writes=[t_out_list[out_off + c]])

    def prep_mem(self, memT_d, gmem_d):
        S = self.S
        self.memn = self.sb("memn", [P, KC, N_MEM], BF16)
        self.t_memn = [T("memn%d" % k) for k in range(KC)]
        gm, t_gm = self.load_small("gmem_s", gmem_d, [P, KC])
        src = memT_d.rearrange("(k p) t -> p k t", p=P)
        S.dma("sp", [lambda e: e.dma_start(out=self.hT[:, :, 0:N_MEM], in_=src)], self.t_hT[0], writes=self.t_hT)
        self.rmsnorm(self.hT, self.t_hT, gm, t_gm, 0, self.memn, self.t_memn, N_MEM)

    def prep_kv(self, Wkv):
        S = self.S
        for blk in range(8):
            w, t_w = self.load_w(Wkv[:, blk * 256:(blk + 1) * 256].rearrange("(k p) m -> p k m", p=P), KC, 256)
            for m in range(2):
                ps, t_ps = self.psum()
                self.mm_group(ps[:, 0:N_MEM], t_ps, [(w[:, k, m * P:(m + 1) * P], self.memn[:, k, :]) for k in range(KC)],
                              reads=[t_w] + self.t_memn)
                c = blk * 2 + m
                S.op("act", lambda e, c=c, ps=ps: e.activation(out=self.kT[:, c, :], in_=ps[:, 0:N_MEM], func=AF.Copy),
                     reads=[t_ps], writes=[self.t_kT[c]])
        for blk in range(8):
            w, t_w = self.load_w(Wkv[:, D + blk * 256:D + (blk + 1) * 256].rearrange("(k p) m -> p k m", p=P), KC, 256)
            for mc in range(2):
                ps, t_ps = self.psum()
                self.mm_group(ps[:, 0:256], t_ps, [(self.memn[:, k, mc * P:(mc + 1) * P], w[:, k, :]) for k in range(KC)],
                              reads=[t_w] + self.t_memn)
                S.op("act", lambda e, mc=mc, blk=blk, ps=ps: e.activation(out=self.vv[:, mc, blk * 256:(blk + 1) * 256], in_=ps[:, 0:256], func=AF.Copy),
                     reads=[t_ps], writes=[self.t_vv[mc]])

    def xattn(self, h_in, h_out, Wq, Wkv, Wo, gain, t_gain, gcol):
        S = self.S
        self.alloc_A(2 * KC)
        self.kT = self.sb("kT", [P, KC, N_MEM], BF16)
        self.t_kT = [T("kT%d" % k) for k in range(KC)]
        self.vv = self.sb("vv", [P, 2, D], BF16)
        self.t_vv = [T("vv0"), T("vv1")]
        self.pT = self.sb("pT", [P, 2, TT], BF16)
        self.t_pT = [T("pT0"), T("pT1")]
        self.rden = self.sb("rden", [P, TT], F32)
        self.t_rden = T("rden")
        self.prep_kv(Wkv)
        qoff, ooff = 0, KC
        scale = 1.0 / math.sqrt(512.0)
        for it in range(NT):
            tok0 = it * TT
            self.load_h(h_in, tok0)
            self.dbg_norm = (it == 0)
            self.rmsnorm(self.hT, self.t_hT, gain, t_gain, gcol, self.hn, self.t_hn, TT)
            self.linear_to(Wq, 0, D, self.t_hn, self.hn, self.bufA, qoff, self.t_A)
            for hd in range(4):
                for mc in range(2):
                    ps, t_ps = self.psum()
                    self.mm_group(ps[:, 0:TT], t_ps,
                                  [(self.kT[:, hd * 4 + j, mc * P:(mc + 1) * P], self.bufA[:, qoff + hd * 4 + j, :]) for j in range(4)],
                                  reads=self.t_kT[hd * 4:hd * 4 + 4] + self.t_A[qoff + hd * 4:qoff + hd * 4 + 4])
                    S.op("act", lambda e, mc=mc, ps=ps: e.activation(out=self.pT[:, mc, :], in_=ps[:, 0:TT], func=AF.Exp, scale=scale),
                         reads=[t_ps], writes=[self.t_pT[mc]])
                ps, t_ps = self.psum()
                self.mm_group(ps[:, 0:TT], t_ps, [(self.ones[:], self.pT[:, mc, :]) for mc in range(2)],
                              reads=[self.t_ones] + self.t_pT)
                S.op("dve", lambda e, ps=ps: e.reciprocal(out=self.rden[:], in_=ps[:, 0:TT]), reads=[t_ps], writes=[self.t_rden])
                for j in range(4):
                    ps, t_ps = self.psum()
                    f0 = hd * 512 + j * P
                    self.mm_group(ps[:, 0:TT], t_ps, [(self.vv[:, mc, f0:f0 + P], self.pT[:, mc, :]) for mc in range(2)],
                                  reads=self.t_vv + self.t_pT)
                    c = ooff + hd * 4 + j
                    S.op("dve", lambda e, c=c, ps=ps: e.tensor_tensor(out=self.bufA[:, c, :], in0=ps[:, 0:TT], in1=self.rden[:], op=ALU.mult),
                         reads=[t_ps, self.t_rden], writes=[self.t_A[c]])
                if it == 0 and hd == 0:
                    self.dbg("hn0", self.hn[:, 0, :], [self.t_hn[0]], [P, TT], BF16)
                    self.dbg("q0", self.bufA[:, 0, :], [self.t_A[0]], [P, TT], BF16)
                    self.dbg("kT0", self.kT[:, 0, :], [self.t_kT[0]], [P, N_MEM], BF16)
                    self.dbg("vv0", self.vv[:, 0, 0:512], [self.t_vv[0]], [P, 512], BF16)
                    self.dbg("memn0", self.memn[:, 0, :], [self.t_memn[0]], [P, N_MEM], BF16)
                    self.dbg("pT0", self.pT[:, 0, :], [self.t_pT[0]], [P, TT], BF16)
                    self.dbg("rden", self.rden[:], [self.t_rden], [P, TT], F32)
                    self.dbg("o0", self.bufA[:, ooff, :], [self.t_A[ooff]], [P, TT], BF16)
            self.linear_resid(Wo, KC, self.t_A[ooff:ooff + KC], self.bufA, ooff)
            self.store_h(h_out, tok0)

    def halo_load(self, dst, t_dst, src_ap, w, flag, rd=()):
        S = self.S
        S.dma("sp", [lambda e: e.dma_start(out=dst[:, :, 0:w], in_=src_ap.rearrange("(k p) t -> p k t", p=P))], t_dst[0], reads=list(rd), writes=t_dst)
        if flag is not None:
            S.op("dve", lambda e: e.tensor_scalar(out=dst[:, :, 0:w], in0=dst[:, :, 0:w], scalar1=flag[0][:, 0:1], scalar2=None, op0=ALU.mult),
                 reads=t_dst + [flag[1]], writes=t_dst)

    def ffn(self, h_in, h_out, hhalo_d, Wg, Wu, Wd, gain, t_gain, gcol, cw, t_cw, cb, t_cb, flag=None, hrd=()):
        S = self.S
        self.alloc_A(FC)
        self.gfull = [self.sb("gfull%d" % i, [P, TT + 2], F32) for i in range(2)]
        self.t_gfull = [T("gfull0"), T("gfull1")]
        self.ghalo = self.sb("ghalo", [P, FC, 2], F32)
        self.t_ghalo = [T("ghalo%d" % c) for c in range(FC)]
        self.c1 = [self.sb("c1_%d" % i, [P, TT], F32) for i in range(2)]
        self.t_c1 = [T("c1_0"), T("c1_1")]
        self.hhT = self.sb("hhT", [P, KC, 2], F32)
        self.t_hhT = [T("hhT%d" % k) for k in range(KC)]
        self.hhn = self.sb("hhn", [P, KC, 2], BF16)
        self.t_hhn = [T("hhn%d" % k) for k in range(KC)]
        self.halo_load(self.hhT, self.t_hhT, hhalo_d, 2, flag, hrd)
        self.rmsnorm(self.hhT, self.t_hhT, gain, t_gain, gcol, self.hhn, self.t_hhn, 2)
        for it in range(NT):
            tok0 = it * TT
            self.load_h(h_in, tok0)
            self.rmsnorm(self.hT, self.t_hT, gain, t_gain, gcol, self.hn, self.t_hn, TT)
            for blk in range(FC // 2):
                wg, t_wg = self.load_w(Wg[:, blk * 256:(blk + 1) * 256].rearrange("(k p) m -> p k m", p=P), KC, 256)
                wu, t_wu = self.load_w(Wu[:, blk * 256:(blk + 1) * 256].rearrange("(k p) m -> p k m", p=P), KC, 256)
                for m in range(2):
                    c = blk * 2 + m
                    b = c % 2
                    gf, t_gf = self.gfull[b], self.t_gfull[b]
                    if it == 0:
                        psh, t_psh = self.psum()
                        self.mm_group(psh[:, 0:2], t_psh, [(wg[:, k, m * P:(m + 1) * P], self.hhn[:, k, :]) for k in range(KC)],
                                      reads=[t_wg] + self.t_hhn)
                        S.op("act", lambda e, gf=gf, psh=psh: e.activation(out=gf[:, 0:2], in_=psh[:, 0:2], func=AF.Copy),
                             reads=[t_psh], writes=[t_gf])
                    else:
                        S.op("act", lambda e, gf=gf, c=c: e.activation(out=gf[:, 0:2], in_=self.ghalo[:, c, :], func=AF.Copy),
                             reads=[self.t_ghalo[c]], writes=[t_gf])
                    psg, t_psg = self.psum()
                    self.mm_group(psg[:, 0:TT], t_psg, [(wg[:, k, m * P:(m + 1) * P], self.hn[:, k, :]) for k in range(KC)],
                                  reads=[t_wg] + self.t_hn)
                    psu, t_psu = self.psum()
                    self.mm_group(psu[:, 0:TT], t_psu, [(wu[:, k, m * P:(m + 1) * P], self.hn[:, k, :]) for k in range(KC)],
                                  reads=[t_wu] + self.t_hn)
                    S.op("act", lambda e, gf=gf, psg=psg: e.activation(out=gf[:, 2:TT + 2], in_=psg[:, 0:TT], func=AF.Copy),
                         reads=[t_psg], writes=[t_gf])
                    if it < NT - 1:
                        S.op("act", lambda e, gf=gf, c=c: e.activation(out=self.ghalo[:, c, :], in_=gf[:, TT:TT + 2], func=AF.Copy),
                             reads=[t_gf], writes=[self.t_ghalo[c]])
                    c1, t_c1 = self.c1[b], self.t_c1[b]
                    S.op("act", lambda e, gf=gf, c=c, c1=c1: e.activation(out=c1[:], in_=gf[:, 2:TT + 2], func=AF.Identity,
                                                                        bias=cb[:, c:c + 1], scale=cw[:, 2, c:c + 1]),
                         reads=[t_gf, t_cw, t_cb], writes=[t_c1])
                    S.op("dve", lambda e, gf=gf, c=c, c1=c1: e.scalar_tensor_tensor(out=c1[:], in0=gf[:, 1:TT + 1], scalar=cw[:, 1, c:c + 1], in1=c1[:],
                                                                                  op0=ALU.mult, op1=ALU.add),
                         reads=[t_gf, t_cw, t_c1], writes=[t_c1])
                    S.op("dve", lambda e, gf=gf, c=c, c1=c1: e.scalar_tensor_tensor(out=c1[:], in0=gf[:, 0:TT], scalar=cw[:, 0, c:c + 1], in1=c1[:],
                                                                                  op0=ALU.mult, op1=ALU.add),
                         reads=[t_gf, t_cw, t_c1], writes=[t_c1])
                    S.op("act", lambda e, c1=c1: e.activation(out=c1[:], in_=c1[:], func=AF.Silu), reads=[t_c1], writes=[t_c1])
                    S.op("dve", lambda e, c=c, c1=c1, psu=psu: e.tensor_tensor(out=self.bufA[:, c, :], in0=c1[:], in1=psu[:, 0:TT], op=ALU.mult),
                         reads=[t_c1, t_psu], writes=[self.t_A[c]])
            self.linear_resid(Wd, FC, self.t_A, self.bufA, 0)
            self.store_h(h_out, tok0)

    def final_norm(self, h_in, out_d, gain, t_gain, gcol):
        S = self.S
        self.fo = self.sb("fo", [P, KC, TT], F32)
        self.t_fo = [T("fo%d" % k) for k in range(KC)]
        for it in range(NT):
            tok0 = it * TT
            self.load_h(h_in, tok0)
            self.rmsnorm(self.hT, self.t_hT, gain, t_gain, gcol, self.fo, self.t_fo, TT)
            dst = out_d[:, tok0:tok0 + TT].rearrange("(k p) t -> p k t", p=P)
            S.dma("sp", [lambda e, dst=dst: e.dma_start(out=dst, in_=self.fo[:])], self.t_fo[0], reads=self.t_fo)

    def finish(self, out_tiles):
        self.S.wait_all("sp", out_tiles)
        self.S.emit()


def pk(v):
    v = np.asarray(v, dtype=np.float32)
    lead = v.shape[:-1]
    n = v.shape[-1] // P
    a = v.reshape(lead + (n, P))
    a = np.moveaxis(a, -1, 0)
    return np.ascontiguousarray(a)


def build_xattn_launch():
    nc = bass.Bass("TRN2", target_bir_lowering=False)
    dt = lambda name, shape, kind="ExternalInput": nc.dram_tensor(name, shape, F32, kind=kind).ap()
    h_in = dt("h_in", [D, NTOK]); h_out = dt("h_out", [D, NTOK], "ExternalOutput")
    memT = dt("memT", [D, N_MEM]); gmem = dt("gmem", [P, KC]); gx = dt("gx", [P, KC])
    Wq = dt("Wq", [D, D]); Wkv = dt("Wkv", [D, 2 * D]); Wo = dt("Wo", [D, D])
    with ExitStack() as st:
        pr = Prog(nc, st)
        g, t_g = pr.load_small("gx_s", gx, [P, KC])
        pr.prep_mem(memT, gmem)
        with pr.phase():
            pr.xattn(h_in, h_out, Wq, Wkv, Wo, g, t_g, 0)
        pr.finish(pr.t_hT)
    return nc


def build_ffn_launch(final=False):
    nc = bass.Bass("TRN2", target_bir_lowering=False)
    dt = lambda name, shape, kind="ExternalInput": nc.dram_tensor(name, shape, F32, kind=kind).ap()
    h_in = dt("h_in", [D, NTOK]); h_out = dt("h_out", [D, NTOK], "ExternalOutput")
    hhalo = dt("hhalo", [D, 2]); gf = dt("gf", [P, KC])
    cw = dt("cw", [P, 3, FC]); cb = dt("cb", [P, FC])
    Wg = dt("Wg", [D, D_FF]); Wu = dt("Wu", [D, D_FF]); Wd = dt("Wd", [D_FF, D])
    with ExitStack() as st:
        pr = Prog(nc, st)
        g, t_g = pr.load_small("gf_s", gf, [P, KC])
        cws, t_cw = pr.load_small("cw_s", cw, [P, 3, FC])
        cbs, t_cb = pr.load_small("cb_s", cb, [P, FC])
        with pr.phase():
            pr.ffn(h_in, h_out, hhalo, Wg, Wu, Wd, g, t_g, 0, cws, t_cw, cbs, t_cb)
        pr.finish(pr.t_hT)
    return nc


def t5_bucket_np(rel):
    n = np.maximum(rel, 0)
    nf = np.maximum(n, 16).astype(np.float32)
    large = 16 + (np.log(nf / np.float32(16)) / np.float32(math.log(128 / 16)) * np.float32(16)).astype(np.int32)
    large = np.minimum(large, 31)
    return np.where(n < 16, n, large)


def swa_consts():
    k = np.arange(128)[:, None, None]
    j = np.arange(2)[None, :, None]
    q = np.arange(128)[None, None, :]
    rel = 128 + q - j * 128 - k
    valid = (rel >= 0) & (rel < 128)
    bk = t5_bucket_np(rel)
    oh = np.zeros((128, 32, 2, 128), np.float32)
    for b in range(32):
        oh[:, b] = ((bk == b) & valid).astype(np.float32)
    maskT = np.where(valid, 0.0, -30000.0).astype(np.float32)
    return oh, maskT


def _swa_setup(self, oh_d, mask_d, relb_d, sinkrow_d, cache=None):
    S = self.S
    self.biasT = self.sb("biasT", [P, 2, 4, 4, 256], F32)
    self.t_biasT = T("biasT")
    self.esrow = self.sb("esrow", [1, 32 * 128], BF16)
    self.t_esrow = T("esrow")
    bflat = self.biasT[:].rearrange("p a b c d -> p (a b c d)")
    have = cache is not None and cache.get("valid", False)
    if have:
        S.dma("sp", [lambda e: e.dma_start(out=bflat, in_=cache["ap"])], self.t_biasT, reads=[cache["t"]], writes=[self.t_biasT])
    with self.phase():
        oh = self.sb("oh_s", [P, 8, 256], F32)
        t_oh = T("oh")
        mk, t_mk = self.load_small("mask_s", mask_d.rearrange("p j q -> p (j q)"), [P, 256])
        rb, t_rb = self.load_small("relb_s", relb_d, [P, 32, 32])
        ohv = oh_d.rearrange("p b j q -> p b (j q)")
        for bg in range(0 if have else 4):
            S.dma("sp", [lambda e, bg=bg: e.dma_start(out=oh[:], in_=ohv[:, bg * 8:(bg + 1) * 8, :])], t_oh, writes=[t_oh])
            for h in range(32):
                kh, g = h // 8, h % 8
                par, i = g % 2, g // 2
                dst = self.biasT[:, par, kh, i, :]
                if bg == 0:
                    S.op("dve", lambda e, dst=dst: e.tensor_copy(out=dst, in_=mk[:]), reads=[t_mk], writes=[self.t_biasT])
                for b8 in range(8):
                    b = bg * 8 + b8
                    S.op("dve", lambda e, dst=dst, b=b, b8=b8, h=h: e.scalar_tensor_tensor(out=dst, in0=oh[:, b8, :], scalar=rb[:, b, h:h + 1], in1=dst,
                                                                                         op0=ALU.mult, op1=ALU.add),
                         reads=[t_oh, t_rb, self.t_biasT], writes=[self.t_biasT])
        if cache is not None and not have:
            S.dma("sp", [lambda e: e.dma_start(out=cache["ap"], in_=bflat)], self.t_biasT, reads=[self.t_biasT], writes=[cache["t"]])
            cache["valid"] = True
        sk, t_sk = self.load_small("sink_s", sinkrow_d, [1, 32])
        z128 = self.sb("z128", [1, 128], F32)
        t_z = T("z128")
        S.op("dve", lambda e: e.memset(z128[:], 0.0), writes=[t_z])
        for hh in range(32):
            S.op("act", lambda e, hh=hh: e.activation(out=self.esrow[0:1, hh * 128:(hh + 1) * 128], in_=z128[:], func=AF.Exp,
                                                     bias=sk[0:1, hh:hh + 1], scale=1.0),
                 reads=[t_z, t_sk], writes=[self.t_esrow])
    self.kbuf = self.sb("kbuf", [P, 4, 128 + TT], BF16)
    self.t_kbuf = [T("kbuf%d" % i) for i in range(4)]
    self.vdup = self.sb("vdup", [P, 5, 512], BF16)
    self.t_vdup = [T("vdup%d" % i) for i in range(5)]
    self.hh128 = self.hT
    self.t_hh128 = self.t_hT
    self.alloc_A(2 * KC)
    self.hhn128 = self.sb("hhn128", [P, KC, 128], BF16)
    self.t_hhn128 = [T("hhn128_%d" % k) for k in range(KC)]
    self.sc = [self.sb("sc%d" % i, [P, 512], F32) for i in range(2)]
    self.t_sc = [T("sc0"), T("sc1")]
    self.pS = [self.sb("pS%d" % i, [P, 512], BF16) for i in range(2)]
    self.t_pS = [T("pS0"), T("pS1")]
    self.rdn = self.sb("rdn", [P, 512], F32)
    self.t_rdn = T("rdn")


def _swa(self, h_in, h_out, hhalo_d, hmask, t_hmask, Wq, Wkd, Wvd, Wo, bq, t_bq, bkd, t_bkd, bvd, t_bvd, gain, t_gain, gcol, flag=None, hrd=()):
    S = self.S
    qoff, ooff = 0, KC
    self.halo_load(self.hh128, self.t_hh128, hhalo_d, 128, flag, hrd)
    self.rmsnorm(self.hh128, self.t_hh128, gain, t_gain, gcol, self.hhn128, self.t_hhn128, 128)

    def kv_proj(src, t_src, w, kcol0, vblk):
        for half in range(2):
            wk, t_wk = self.load_w(Wkd[:, half * 256:(half + 1) * 256].rearrange("(k p) m -> p k m", p=P), KC, 256)
            for m in range(2):
                kh = half * 2 + m
                ps, t_ps = self.psum()
                self.mm_group(ps[:, 0:w], t_ps, [(wk[:, k, m * P:(m + 1) * P], src[:, k, 0:w]) for k in range(KC)], reads=[t_wk] + t_src)
                S.op("act", lambda e, kh=kh, ps=ps: e.activation(out=self.kbuf[:, kh, kcol0:kcol0 + w], in_=ps[:, 0:w], func=AF.Identity,
                                                                bias=bkd[:, kh:kh + 1], scale=1.0),
                     reads=[t_ps, t_bkd], writes=[self.t_kbuf[kh]])
        wv = []
        for half in range(2):
            wv.append(self.load_w(Wvd[:, half * 256:(half + 1) * 256].rearrange("(k p) m -> p k m", p=P), KC, 256))
        for qb in range(w // 128):
            ps, t_ps = self.psum()
            for half in range(2):
                self.mm_group(ps[:, half * 256:(half + 1) * 256], t_ps,
                              [(src[:, k, qb * 128:(qb + 1) * 128], wv[half][0][:, k, :]) for k in range(KC)], reads=[wv[half][1]] + t_src)
            S.op("dve", lambda e, qb=qb, ps=ps: e.tensor_tensor(out=self.vdup[:, vblk + qb, :], in0=ps[:, 0:512], in1=bvd[:], op=ALU.add),
                 reads=[t_ps, t_bvd], writes=[self.t_vdup[vblk + qb]])

    kv_proj(self.hhn128, self.t_hhn128, 128, 0, 0)
    for it in range(NT):
        tok0 = it * TT
        self.load_h(h_in, tok0)
        self.rmsnorm(self.hT, self.t_hT, gain, t_gain, gcol, self.hn, self.t_hn, TT)
        if it > 0:
            for kh in range(4):
                S.op("act", lambda e, kh=kh: e.activation(out=self.kbuf[:, kh, 0:128], in_=self.kbuf[:, kh, TT:TT + 128], func=AF.Copy),
                     reads=[self.t_kbuf[kh]], writes=[self.t_kbuf[kh]])
            S.op("act", lambda e: e.activation(out=self.vdup[:, 0, :], in_=self.vdup[:, 4, :], func=AF.Copy),
                 reads=[self.t_vdup[4]], writes=[self.t_vdup[0]])
        kv_proj(self.hn, self.t_hn, TT, 128, 1)

        def q_evac(c, ps, t_ps):
            S.op("act", lambda e, c=c, ps=ps: e.activation(out=self.bufA[:, qoff + c, :], in_=ps[:, 0:TT], func=AF.Identity,
                                                          bias=bq[:, c:c + 1], scale=0.125),
                 reads=[t_ps, t_bq], writes=[self.t_A[qoff + c]])
        self.linear_to(Wq, 0, D, self.t_hn, self.hn, self.bufA, qoff, self.t_A, evac=q_evac)
        for qb in range(TT // 128):
            qs = slice(qb * 128, (qb + 1) * 128)
            for kh in range(4):
                for par in range(2):
                    pr = slice(par * 64, (par + 1) * 64)
                    t_q4 = self.t_A[qoff + 4 * kh:qoff + 4 * kh + 4]
                    for j in range(2):
                        ps, t_ps = self.psum()
                        kc0 = (qb + j) * 128
                        self.mm_group(ps[:, 0:512], t_ps,
                                      [(self.kbuf[pr, kh, kc0:kc0 + 128], self.bufA[pr, qoff + 4 * kh:qoff + 4 * kh + 4, qs])],
                                      reads=[self.t_kbuf[kh]] + t_q4)
                        bsl = self.biasT[:, par, kh, :, j * 128:(j + 1) * 128]
                        psv = ps[:, 0:512].rearrange("p (i q) -> p i q", i=4)
                        scv = self.sc[j][:].rearrange("p (i q) -> p i q", i=4)
                        if it == 0 and qb == 0 and j == 0:
                            S.op("dve", lambda e, psv=psv, scv=scv, bsl=bsl: e.scalar_tensor_tensor(out=scv, in0=psv, scalar=hmask[:, 0:1], in1=bsl,
                                                                                                  op0=ALU.add, op1=ALU.add),
                                 reads=[t_ps, self.t_biasT, t_hmask], writes=[self.t_sc[j]])
                        else:
                            S.op("dve", lambda e, psv=psv, scv=scv, bsl=bsl: e.tensor_tensor(out=scv, in0=psv, in1=bsl, op=ALU.add),
                                 reads=[t_ps, self.t_biasT], writes=[self.t_sc[j]])
                        S.op("act", lambda e, j=j: e.activation(out=self.pS[j][:], in_=self.sc[j][:], func=AF.Exp),
                             reads=[self.t_sc[j]], writes=[self.t_pS[j]])
                    psd, t_psd = self.psum()
                    e0 = (par * 16 + kh * 4) * 128
                    self.mm_group(psd[:, 0:512], t_psd,
                                  [(self.ones[:], self.pS[0][:]), (self.ones[:], self.pS[1][:]), (self.ones[0:1, :], self.esrow[0:1, e0:e0 + 512])],
                                  reads=self.t_pS + [self.t_ones, self.t_esrow])
                    S.op("dve", lambda e, psd=psd: e.reciprocal(out=self.rdn[:], in_=psd[:, 0:512]), reads=[t_psd], writes=[self.t_rdn])
                    pso, t_pso = self.psum()
                    self.mm_group(pso[:, 0:512], t_pso,
                                  [(self.vdup[:, qb + j, kh * 128:(kh + 1) * 128], self.pS[j][:]) for j in range(2)],
                                  reads=self.t_pS + [self.t_vdup[qb], self.t_vdup[qb + 1]])
                    ov = self.bufA[pr, ooff + 4 * kh:ooff + 4 * kh + 4, qs]
                    S.op("dve", lambda e, ov=ov, pso=pso, pr=pr: e.tensor_tensor(out=ov, in0=pso[pr, 0:512].rearrange("p (i q) -> p i q", i=4),
                                                                                in1=self.rdn[pr, :].rearrange("p (i q) -> p i q", i=4), op=ALU.mult),
                         reads=[t_pso, self.t_rdn], writes=self.t_A[ooff + 4 * kh:ooff + 4 * kh + 4])
        self.linear_resid(Wo, KC, self.t_A[ooff:ooff + KC], self.bufA, ooff)
        self.store_h(h_out, tok0)


Prog.swa_setup = _swa_setup
Prog.swa = _swa


def swa_host_params(z, i):
    b = np.asarray(z["od_b_qkv"][i], np.float32)
    bq = pk(b[0:2048])
    bk = b[2048:2304].reshape(4, 64)
    bkd = np.ascontiguousarray(np.concatenate([bk, bk], axis=1).T)
    bv = b[2304:2560].reshape(4, 64)
    bvd = np.concatenate([bv, bv], axis=1).reshape(1, 512)
    bvd = np.ascontiguousarray(np.broadcast_to(bvd, (P, 512)))
    W = np.asarray(z["od_w_qkv"][i], np.float32)
    Wk = W[:, 2048:2304].reshape(D, 4, 64)
    Wkd = np.ascontiguousarray(np.concatenate([Wk, Wk], axis=2).reshape(D, 512))
    Wv = W[:, 2304:2560].reshape(D, 4, 64)
    Wvd = np.ascontiguousarray(np.concatenate([Wv, Wv], axis=2).reshape(D, 512))
    Wq = np.ascontiguousarray(W[:, 0:2048])
    sk = np.asarray(z["od_sinks"][i], np.float32)
    order = [kh * 8 + 2 * ii + par for par in range(2) for kh in range(4) for ii in range(4)]
    sinkrow = np.ascontiguousarray(sk[order].reshape(1, 32))
    relb = np.ascontiguousarray(np.broadcast_to(np.asarray(z["rel_bias"], np.float32)[None], (P, 32, 32)))
    return dict(bq=bq, bkd=bkd, bvd=bvd, Wq=Wq, Wkd=Wkd, Wvd=Wvd, sinkrow=sinkrow, relb=relb)


def build_swa_launch():
    nc = bass.Bass("TRN2", target_bir_lowering=False)
    dt = lambda name, shape, kind="ExternalInput": nc.dram_tensor(name, shape, F32, kind=kind).ap()
    h_in = dt("h_in", [D, NTOK]); h_out = dt("h_out", [D, NTOK], "ExternalOutput")
    hhalo = dt("hhalo", [D, 128]); gm = dt("gm", [P, KC]); hmask_d = dt("hmask", [P, 1])
    oh_d = dt("oh", [P, 32, 2, 128]); mask_d = dt("maskT", [P, 2, 128]); relb_d = dt("relb", [P, 32, 32]); sinkrow_d = dt("sinkrow", [1, 32])
    bq_d = dt("bq", [P, KC]); bkd_d = dt("bkd", [P, 4]); bvd_d = dt("bvd", [P, 512])
    Wq = dt("Wq", [D, D]); Wkd = dt("Wkd", [D, 512]); Wvd = dt("Wvd", [D, 512]); Wo = dt("Wo", [D, D])
    with ExitStack() as st:
        pr = Prog(nc, st)
        g, t_g = pr.load_small("gm_s", gm, [P, KC])
        hm, t_hm = pr.load_small("hmask_s", hmask_d, [P, 1])
        bq, t_bq = pr.load_small("bq_s", bq_d, [P, KC])
        pr.S.op("dve", lambda e: e.tensor_scalar(out=bq[:], in0=bq[:], scalar1=0.125, scalar2=None, op0=ALU.mult), reads=[t_bq], writes=[t_bq])
        bkd, t_bkd = pr.load_small("bkd_s", bkd_d, [P, 4])
        bvd, t_bvd = pr.load_small("bvd_s", bvd_d, [P, 512])
        with pr.phase():
            pr.swa_setup(oh_d, mask_d, relb_d, sinkrow_d)
            pr.swa(h_in, h_out, hhalo, hm, t_hm, Wq, Wkd, Wvd, Wo, bq, t_bq, bkd, t_bkd, bvd, t_bvd, g, t_g, 0)
        pr.finish(pr.t_hT)
    return nc


class Ew:
    def __init__(self, prog, shape, tag):
        self.pr = prog
        self.shape = shape
        self.tag = tag
        self.n = 0

    def new(self, dt=F32, shape=None):
        self.n += 1
        t = self.pr.sb("%s%d" % (self.tag, self.n), shape or self.shape, dt)
        return (t[:], T("%s%d" % (self.tag, self.n)))

    def tt(self, o, a, b, op):
        self.pr.S.op("dve", lambda e: e.tensor_tensor(out=o[0], in0=a[0], in1=b[0], op=op), reads=[a[1], b[1]], writes=[o[1]])
        return o

    def ts(self, o, a, s1, op0, s2=None, op1=None):
        if op1 is None:
            self.pr.S.op("dve", lambda e: e.tensor_scalar(out=o[0], in0=a[0], scalar1=s1, scalar2=None, op0=op0), reads=[a[1]], writes=[o[1]])
        else:
            self.pr.S.op("dve", lambda e: e.tensor_scalar(out=o[0], in0=a[0], scalar1=s1, scalar2=s2, op0=op0, op1=op1), reads=[a[1]], writes=[o[1]])
        return o

    def act(self, o, a, func, scale=1.0, bias=None, breads=()):
        if bias is None:
            self.pr.S.op("act", lambda e: e.activation(out=o[0], in_=a[0], func=func, scale=scale), reads=[a[1]], writes=[o[1]])
        else:
            self.pr.S.op("act", lambda e: e.activation(out=o[0], in_=a[0], func=func, scale=scale, bias=bias), reads=[a[1]] + list(breads), writes=[o[1]])
        return o

    def cmul(self, orr, oi, ar, ai, br, bi, t1, t2):
        self.tt(t1, ar, br, ALU.mult)
        self.tt(t2, ai, bi, ALU.mult)
        self.tt(orr, t1, t2, ALU.subtract)
        self.tt(t1, ar, bi, ALU.mult)
        self.tt(t2, ai, br, ALU.mult)
        self.tt(oi, t1, t2, ALU.add)

    def abar(self, are, aim, ldt, halfpi):
        n = self.new
        dt_ = self.act(n(), ldt, AF.Exp)
        x1 = self.tt(n(), are, dt_, ALU.mult)
        th = self.tt(n(), aim, dt_, ALU.mult)
        rho = self.act(n(), x1, AF.Exp, scale=1.0 / 32)
        sn = self.act(n(), th, AF.Sin, scale=1.0 / 32)
        cs = self.act(n(), th, AF.Sin, scale=1.0 / 32, bias=halfpi[0], breads=[halfpi[1]])
        er = self.tt(n(), rho, cs, ALU.mult)
        ei = self.tt(n(), rho, sn, ALU.mult)
        t1, t2, t3 = n(), n(), n()
        for _ in range(5):
            self.tt(t1, er, er, ALU.mult)
            self.tt(t2, ei, ei, ALU.mult)
            self.tt(t3, er, ei, ALU.mult)
            self.tt(er, t1, t2, ALU.subtract)
            self.ts(ei, t3, 2.0, ALU.mult)
        nr = self.ts(n(), er, -1.0, ALU.add)
        self.tt(t1, are, are, ALU.mult)
        self.tt(t2, aim, aim, ALU.mult)
        self.tt(t3, t1, t2, ALU.add)
        rd = n()
        self.pr.S.op("dve", lambda e: e.reciprocal(out=rd[0], in_=t3[0]), reads=[t3[1]], writes=[rd[1]])
        qr, qi = n(), n()
        self.tt(t1, nr, are, ALU.mult)
        self.tt(t2, ei, aim, ALU.mult)
        self.tt(t3, t1, t2, ALU.add)
        self.tt(qr, t3, rd, ALU.mult)
        self.tt(t1, ei, are, ALU.mult)
        self.tt(t2, nr, aim, ALU.mult)
        self.tt(t3, t1, t2, ALU.subtract)
        self.tt(qi, t3, rd, ALU.mult)
        return er, ei, qr, qi


def _s5_precompute(self, prm, dr):
    S = self.S
    self.A8r = self.sb("A8r", [P, 32], F32)
    self.A8i = self.sb("A8i", [P, 32], F32)
    self.t_A8 = T("A8")

    with self.phase():
        hp_t = self.sb("halfpi", [P, 1], F32)
        t_hp = T("halfpi")
        S.op("dve", lambda e: e.memset(hp_t[:], math.pi / 2), writes=[t_hp])
        halfpi = (hp_t[:], t_hp)
        ld = lambda name, shape: (lambda r: (r[0][:], r[1]))(self.load_small(name + "_s", prm[name], shape))
        es = Ew(self, [P, 32], "es")
        are, aim, ldt = ld("are_S", [P, 32]), ld("aim_S", [P, 32]), ld("ldt_S", [P, 32])
        ar, ai, qr, qi = es.abar(are, aim, ldt, halfpi)
        bre, bim = ld("bre_S", [P, 32, 16]), ld("bim_S", [P, 32, 16])
        cre, cim = ld("cre_S", [P, 32, 16]), ld("cim_S", [P, 32, 16])
        e3 = Ew(self, [P, 32, 16], "e3")
        bc = lambda v: (v[0].unsqueeze(2).to_broadcast([P, 32, 16]), v[1])
        Br, Bi, u1, u2 = e3.new(), e3.new(), e3.new(), e3.new()
        e3.cmul(Br, Bi, bc(qr), bc(qi), bre, bim, u1, u2)
        pw = [(es.new(), es.new()) for _ in range(9)]
        S.op("dve", lambda e: e.memset(pw[0][0][0], 1.0), writes=[pw[0][0][1]])
        S.op("dve", lambda e: e.memset(pw[0][1][0], 0.0), writes=[pw[0][1][1]])
        s1, s2 = es.new(), es.new()
        for k in range(8):
            es.cmul(pw[k + 1][0], pw[k + 1][1], pw[k][0], pw[k][1], ar, ai, s1, s2)
        S.op("dve", lambda e: e.tensor_copy(out=self.A8r[:], in_=pw[8][0][0]), reads=[pw[8][0][1]], writes=[self.t_A8])
        S.op("dve", lambda e: e.tensor_copy(out=self.A8i[:], in_=pw[8][1][0]), reads=[pw[8][1][1]], writes=[self.t_A8])
        def padded(name, dt):
            t = self.sb(name, [P, 32, 32], dt)
            tt = T(name)
            S.op("dve", lambda e: e.memset(t[:], 0.0), writes=[tt])
            return t, tt

        def to_pad(dst, t_dst, src):
            for m in range(2):
                S.op("dve", lambda e, m=m: e.tensor_copy(out=dst[m * 64:(m + 1) * 64, :, m * 16:(m + 1) * 16], in_=src[0][m * 64:(m + 1) * 64, :, :]),
                     reads=[src[1]], writes=[t_dst])
        Bpr, t_Bpr = padded("Bpr", BF16)
        Bpn, t_Bpn = padded("Bpn", BF16)
        to_pad(Bpr, t_Bpr, Br)
        nBi = e3.ts(e3.new(), Bi, -1.0, ALU.mult)
        to_pad(Bpn, t_Bpn, nBi)
        Cpr, t_Cpr = padded("Cpr", BF16)
        Cpi, t_Cpi = padded("Cpi", BF16)
        ident, t_ident = self.load_small("ident_s", prm["ident"], [P, P])
        dcol, t_dcol = self.load_small("dcol_s", prm["dcol"], [P, 8])
        Ck_r, Ck_i = e3.new(), e3.new()
        stg = self.sb("wc_stg", [P, 8, 3072], BF16)
        t_stg = T("wc_stg")
        S.op("dve", lambda e: e.memset(stg[:], 0.0), writes=[t_stg])
        f32blk = self.sb("f32blk", [P, P], F32)
        t_f32blk = T("f32blk")
        for k in range(9):
            e3.cmul(Ck_r, Ck_i, cre, cim, bc(pw[k][0]), bc(pw[k][1]), u1, u2)
            if k < 8:
                to_pad(Cpr, t_Cpr, Ck_r)
                to_pad(Cpi, t_Cpi, Ck_i)
                for c8 in range(8):
                    ps, t_ps = self.psum()
                    for q in range(4):
                        pair = c8 * 4 + q
                        self.mm_group(ps[q * 32:(q + 1) * 32, q * 32:(q + 1) * 32], t_ps,
                                      [(Bpr[:, pair, :], Cpr[:, pair, :]), (Bpn[:, pair, :], Cpi[:, pair, :])],
                                      reads=[t_Bpr, t_Bpn, t_Cpr, t_Cpi], tp=(0, q * 32))
                    dstv = stg[:, c8, k * 128:(k + 1) * 128]
                    for q in range(4):
                        sl = slice(q * 32, (q + 1) * 32)
                        if k == 0:
                            S.op("dve", lambda e, sl=sl, c8=c8, ps=ps, dstv=dstv: e.scalar_tensor_tensor(
                                out=dstv[sl, sl], in0=ident[sl, sl], scalar=dcol[sl, c8:c8 + 1], in1=ps[sl, sl], op0=ALU.mult, op1=ALU.add),
                                reads=[t_ps, t_ident, t_dcol], writes=[t_stg])
                        else:
                            S.op("act", lambda e, sl=sl, ps=ps, dstv=dstv: e.activation(out=dstv[sl, sl], in_=ps[sl, sl], func=AF.Copy),
                                 reads=[t_ps], writes=[t_stg])
            if k >= 1:
                r = k - 1
                wov = stg[:, :, 1024:3072].rearrange("p c (q r i o) -> p c q r i o", q=4, r=8, i=2)
                for m in range(2):
                    ms = slice(m * 64, (m + 1) * 64)
                    S.op("dve", lambda e, ms=ms, m=m, r=r: e.tensor_copy(
                        out=wov[ms, :, :, r, 0, m * 16:(m + 1) * 16], in_=Ck_r[0][ms, :, :].rearrange("p (c q) h -> p c q h", q=4)),
                        reads=[Ck_r[1]], writes=[t_stg])
                    S.op("dve", lambda e, ms=ms, m=m, r=r: e.tensor_scalar(
                        out=wov[ms, :, :, r, 1, m * 16:(m + 1) * 16], in0=Ck_i[0][ms, :, :].rearrange("p (c q) h -> p c q h", q=4),
                        scalar1=-1.0, scalar2=None, op0=ALU.mult),
                        reads=[Ck_i[1]], writes=[t_stg])
        S.dma("sp", [lambda e: e.dma_start(out=dr["wc"], in_=stg[:])], t_stg, reads=[t_stg], writes=[dr["t_wc"]])
    with self.phase():
        hp_t = self.sb("halfpi2", [P, 1], F32)
        t_hp = T("halfpi2")
        S.op("dve", lambda e: e.memset(hp_t[:], math.pi / 2), writes=[t_hp])
        halfpi = (hp_t[:], t_hp)
        ld = lambda name, shape: (lambda r: (r[0][:], r[1]))(self.load_small(name + "_s", prm[name], shape))
        et = Ew(self, [P, 8, 64], "et")
        are, aim, ldt = ld("are_T", [P, 8, 64]), ld("aim_T", [P, 8, 64]), ld("ldt_T", [P, 8, 64])
        ar, ai, qr, qi = et.abar(are, aim, ldt, halfpi)
        bre, bim = ld("bre_T", [P, 8, 64]), ld("bim_T", [P, 8, 64])
        QBr, QBi, u1, u2 = et.new(), et.new(), et.new(), et.new()
        et.cmul(QBr, QBi, qr, qi, bre, bim, u1, u2)
        m01, t_m01 = self.load_small("mask01_s", prm["mask01"], [P, 2])
        wzs = self.sb("wz_stg", [P, 8, 8, 2, 128], BF16)
        t_wzs = T("wz_stg")
        pr_, pi_ = et.new(), et.new()
        S.op("dve", lambda e: e.memset(pr_[0], 1.0), writes=[pr_[1]])
        S.op("dve", lambda e: e.memset(pi_[0], 0.0), writes=[pi_[1]])
        Wr, Wi, nr_, ni_ = et.new(), et.new(), et.new(), et.new()
        for k in range(8):
            s = 7 - k
            et.cmul(Wr, Wi, pr_, pi_, QBr, QBi, u1, u2)
            for ri, W in ((0, Wr), (1, Wi)):
                for m in range(2):
                    S.op("dve", lambda e, s=s, ri=ri, m=m, W=W: e.tensor_scalar(out=wzs[:, :, s, ri, m * 64:(m + 1) * 64], in0=W[0],
                                                                                 scalar1=m01[:, m:m + 1], scalar2=None, op0=ALU.mult),
                         reads=[W[1], t_m01], writes=[t_wzs])
            if k < 7:
                et.cmul(nr_, ni_, pr_, pi_, ar, ai, u1, u2)
                S.op("dve", lambda e: e.tensor_copy(out=pr_[0], in_=nr_[0]), reads=[nr_[1]], writes=[pr_[1]])
                S.op("dve", lambda e: e.tensor_copy(out=pi_[0], in_=ni_[0]), reads=[ni_[1]], writes=[pi_[1]])
        S.dma("sp", [lambda e: e.dma_start(out=dr["wz"], in_=wzs[:].rearrange("p c s i o -> p c (s i o)"))], t_wzs, reads=[t_wzs], writes=[dr["t_wz"]])


Prog.s5_precompute = _s5_precompute


def ev_host_params(z, i):
    f = lambda k: np.asarray(z[k][i], np.float32)
    a_re, a_im, ldt = f("s5_a_re"), f("s5_a_im"), f("s5_log_dt")
    b_re, b_im = f("s5_b_re"), f("s5_b_im")
    c_re, c_im = f("s5_c_re"), f("s5_c_im")
    def S2(v):
        return np.ascontiguousarray(v.reshape(32, 2, 64).transpose(1, 2, 0).reshape(128, 32))
    def S3(v):
        return np.ascontiguousarray(v.reshape(32, 2, 64, 16).transpose(1, 2, 0, 3).reshape(128, 32, 16))
    def T2(v):
        w = v.reshape(8, 4, 2, 1, 64)
        w = np.broadcast_to(w, (8, 4, 2, 16, 64)).transpose(1, 2, 3, 0, 4).reshape(128, 8, 64)
        return np.ascontiguousarray(w)
    def T3(v):
        w = v.reshape(8, 4, 2, 64, 16).transpose(1, 2, 4, 0, 3).reshape(128, 8, 64)
        return np.ascontiguousarray(w)
    ldt2 = np.broadcast_to(ldt[:, None], (64, 64))
    out = dict(are_S=S2(a_re), aim_S=S2(a_im), ldt_S=S2(ldt2), bre_S=S3(b_re), bim_S=S3(b_im),
               cre_S=S3(c_re.transpose(0, 2, 1)), cim_S=S3(c_im.transpose(0, 2, 1)),
               are_T=T2(a_re), aim_T=T2(a_im), ldt_T=T2(ldt2), bre_T=T3(b_re), bim_T=T3(b_im))
    mask01 = np.zeros((128, 2), np.float32)
    for p_ in range(128):
        mask01[p_, (p_ // 16) % 2] = 1.0
    out["mask01"] = mask01
    out["ident"] = np.eye(128, dtype=np.float32)
    out["dcol"] = pk(f("s5_d"))
    glu = f("s5_glu_w")
    gb = np.zeros((128, 8, 128), np.float32)
    for g in range(64):
        c8, gg = g // 8, g % 8
        gb[gg * 16:(gg + 1) * 16, c8, gg * 16:(gg + 1) * 16] = glu[g]
    out["gblk"] = gb
    out["convw"] = pk(f("ev_conv_w"))
    return out


def _even_mixer(self, h_in, h_out, hhalo_d, Win, Wout, prm, dr, gain, t_gain, gcol, flag, hrd, xch):
    S = self.S
    self.s5_precompute(prm, dr)
    self.alloc_A(24)
    cw, t_cw = self.load_small("convw_s", prm["convw"], [P, 3, 8])
    gblk = self.sb("gblk", [P, 8, P], BF16)
    t_gblk = T("gblk")
    S.dma("pool", [lambda e: e.dma_start(out=gblk[:], in_=prm["gblk"])], t_gblk, writes=[t_gblk])
    Xb = [self.sb("Xb%d" % i, [P, 32, 65], F32) for i in range(2)]
    t_Xb = [T("Xb0"), T("Xb1")]
    X16 = [self.sb("X16_%d" % i, [P, 32, 64], BF16) for i in range(2)]
    t_X16 = [T("X16_0"), T("X16_1")]
    X0 = [self.sb("X0_%d" % i, [P, 32, NT], F32) for i in range(2)]
    t_X0 = [T("X0_0"), T("X0_1")]
    sstg = [self.sb("sstg%d" % i, [P, 32], F32) for i in range(2)]
    t_sstg = [T("sstg0"), T("sstg1")]
    st = [self.sb("st%d" % i, [P, 32], F32) for i in range(4)]
    t_st = [T("st%d" % i) for i in range(4)]
    prod = [self.sb("prod%d" % i, [P, TT + 2], F32) for i in range(2)]
    t_prod = [T("prod0"), T("prod1")]
    phalo = self.sb("phalo", [P, 8, 2], F32)
    t_phalo = [T("phalo%d" % c) for c in range(8)]
    gcs = [self.sb("gcs%d" % i, [P, TT], F32) for i in range(2)]
    t_gcs = [T("gcs0"), T("gcs1")]
    cv = [self.sb("cv%d" % i, [P, TT], F32) for i in range(2)]
    t_cv = [T("cv0"), T("cv1")]
    hhT = self.sb("ehhT", [P, KC, 2], F32)
    t_hhT = [T("ehhT%d" % k) for k in range(KC)]
    hhn = self.sb("ehhn", [P, KC, 2], BF16)
    t_hhn = [T("ehhn%d" % k) for k in range(KC)]
    yx, t_yx = gcs, t_gcs
    yg, t_yg = cv, t_cv
    yb = [self.sb("yb%d" % i, [P, TT], BF16) for i in range(2)]
    t_yb = [T("yb0"), T("yb1")]
    NJ = TT // 8
    UO = 16
    tt_ = lambda o, t_o, a, b_, op, rd: S.op("dve", lambda e: e.tensor_tensor(out=o, in0=a, in1=b_, op=op), reads=rd, writes=[t_o])

    def zscan(it):
        for c8 in range(8):
            wz, t_wz = self.load_wz(dr, c8)
            uv = self.bufA[:, UO + c8, :].rearrange("p (j s) -> p j s", s=8)
            for q in range(4):
                pair = c8 * 4 + q
                ps, t_ps = self.psum()
                for ri in range(2):
                    n = 8
                    for s in range(8):
                        self.S.op("pe", lambda e, ps=ps, q=q, s=s, ri=ri, wz=wz, uv=uv: e.matmul(
                            ps[:, ri * NJ:(ri + 1) * NJ], wz[q * 32:(q + 1) * 32, s, ri, :], uv[q * 32:(q + 1) * 32, :, s],
                            start=(s == 0), stop=(s == n - 1), tile_position=(q * 32, 0)),
                            reads=[t_wz, self.t_A[UO + c8]], writes=[t_ps], inc=(s == n - 1))
                    S.op("act", lambda e, ps=ps, ri=ri, pair=pair: e.activation(out=Xb[ri][:, pair, 1:NJ + 1], in_=ps[:, ri * NJ:(ri + 1) * NJ], func=AF.Copy),
                         reads=[t_ps], writes=[t_Xb[ri]])
        for j in range(NJ):
            xr, xi = Xb[0][:, :, j], Xb[1][:, :, j]
            nr_, ni_ = Xb[0][:, :, j + 1], Xb[1][:, :, j + 1]
            tt_ = lambda o, t_o, a, b_, op, rd: S.op("dve", lambda e: e.tensor_tensor(out=o, in0=a, in1=b_, op=op), reads=rd, writes=[t_o])
            tt_(st[0][:], t_st[0], self.A8r[:], xr, ALU.mult, [self.t_A8, t_Xb[0]])
            tt_(st[1][:], t_st[1], self.A8i[:], xi, ALU.mult, [self.t_A8, t_Xb[1]])
            tt_(st[2][:], t_st[2], self.A8i[:], xr, ALU.mult, [self.t_A8, t_Xb[0]])
            tt_(st[3][:], t_st[3], self.A8r[:], xi, ALU.mult, [self.t_A8, t_Xb[1]])
            tt_(st[0][:], t_st[0], st[0][:], st[1][:], ALU.subtract, [t_st[0], t_st[1]])
            tt_(st[2][:], t_st[2], st[2][:], st[3][:], ALU.add, [t_st[2], t_st[3]])
            tt_(nr_, t_Xb[0], nr_, st[0][:], ALU.add, [t_Xb[0], t_st[0]])
            tt_(ni_, t_Xb[1], ni_, st[2][:], ALU.add, [t_Xb[1], t_st[2]])
    for ri in range(2):
        S.op("dve", lambda e, ri=ri: e.memset(Xb[ri][:, :, 0], 0.0), writes=[t_Xb[ri]])
    for it in range(NT):
        self.load_h(h_in, it * TT)
        self.rmsnorm(self.hT, self.t_hT, gain, t_gain, gcol, self.hn, self.t_hn, TT)
        self.linear_to(Win, 3072, 1024, self.t_hn, self.hn, self.bufA, UO, self.t_A)
        for ri in range(2):
            S.op("act", lambda e, ri=ri, it=it: e.activation(out=X0[ri][:, :, it], in_=Xb[ri][:, :, 0], func=AF.Copy), reads=[t_Xb[ri]], writes=[t_X0[ri]])
        zscan(it)
        for ri in range(2):
            if it < NT - 1:
                S.op("act", lambda e, ri=ri: e.activation(out=Xb[ri][:, :, 0], in_=Xb[ri][:, :, NJ], func=AF.Copy), reads=[t_Xb[ri]], writes=[t_Xb[ri]])
            else:
                S.op("act", lambda e, ri=ri: e.activation(out=sstg[ri][:], in_=Xb[ri][:, :, NJ], func=AF.Copy), reads=[t_Xb[ri]], writes=[t_sstg[ri]])
    for ri in range(2):
        S.dma("sp", [lambda e, ri=ri: e.dma_start(out=xch["snd_s"][ri * P:(ri + 1) * P, :], in_=sstg[ri][:])], t_sstg[ri], reads=[t_sstg[ri]], writes=[xch["t_snd_s"]])
    S.cc("pool", lambda e: e.collective_compute("AllGather", ALU.bypass, replica_groups=xch["rg"], ins=[xch["snd_s"]], outs=[xch["rcv_s"]]),
         reads=[xch["t_snd_s"]], writes=[xch["t_rcv_s"]])
    Dr = [self.sb("Dst%d" % i, [P, 32], F32) for i in range(2)]
    t_Dr = [T("Dst0"), T("Dst1")]
    for ri in range(2):
        S.dma("sp", [lambda e, ri=ri: e.dma_start(out=Dr[ri][:], in_=xch["rcv_s"][ri * P:(ri + 1) * P, :])], t_Dr[ri], reads=[xch["t_rcv_s"]], writes=[t_Dr[ri]])
        S.op("dve", lambda e, ri=ri: e.tensor_scalar(out=Dr[ri][:], in0=Dr[ri][:], scalar1=flag[0][:, 0:1], scalar2=None, op0=ALU.mult),
             reads=[t_Dr[ri], flag[1]], writes=[t_Dr[ri]])
    self.dbg("m_A8r", self.A8r[:], [self.t_A8], [P, 32])
    self.dbg("m_Dr0", Dr[0][:], [t_Dr[0]], [P, 32])
    self.dbg("m_X0", X0[0][:].rearrange("p a b -> p (a b)"), [t_X0[0]], [P, 32 * NT])
    es = Ew(self, [P, 32], "e5")
    Ar, Ai = es.new(), es.new()
    S.op("dve", lambda e: e.tensor_copy(out=Ar[0], in_=self.A8r[:]), reads=[self.t_A8], writes=[Ar[1]])
    S.op("dve", lambda e: e.tensor_copy(out=Ai[0], in_=self.A8i[:]), reads=[self.t_A8], writes=[Ai[1]])
    q1, q2, q3 = es.new(), es.new(), es.new()
    for _ in range(6):
        es.tt(q1, Ar, Ar, ALU.mult)
        es.tt(q2, Ai, Ai, ALU.mult)
        es.tt(q3, Ar, Ai, ALU.mult)
        es.tt(Ar, q1, q2, ALU.subtract)
        es.ts(Ai, q3, 2.0, ALU.mult)
    D0, D1 = (Dr[0][:], t_Dr[0]), (Dr[1][:], t_Dr[1])
    n0, n1 = es.new(), es.new()
    for it in range(NT):
        for ri, Dv in ((0, D0), (1, D1)):
            S.op("dve", lambda e, ri=ri, it=it, Dv=Dv: e.tensor_tensor(out=X0[ri][:, :, it], in0=X0[ri][:, :, it], in1=Dv[0], op=ALU.add),
                 reads=[t_X0[ri], Dv[1]], writes=[t_X0[ri]])
        if it < NT - 1:
            es.cmul(n0, n1, D0, D1, Ar, Ai, q1, q2)
            S.op("dve", lambda e: e.tensor_copy(out=D0[0], in_=n0[0]), reads=[n0[1]], writes=[D0[1]])
            S.op("dve", lambda e: e.tensor_copy(out=D1[0], in_=n1[0]), reads=[n1[1]], writes=[D1[1]])
    self.halo_load(hhT, t_hhT, hhalo_d, 2, flag, hrd)
    self.rmsnorm(hhT, t_hhT, gain, t_gain, gcol, hhn, t_hhn, 2)
    for it in range(NT):
        tok0 = it * TT
        self.load_h(h_in, tok0)
        self.rmsnorm(self.hT, self.t_hT, gain, t_gain, gcol, self.hn, self.t_hn, TT)
        for blk in range(4):
            wblk = lambda col0: self.load_w(Win[:, col0 + blk * 256:col0 + (blk + 1) * 256].rearrange("(k p) m -> p k m", p=P), KC, 256)
            wgb, t_wgb = wblk(0)
            wgc, t_wgc = wblk(1024)
            wxa, t_wxa = wblk(2048)
            for m in range(2):
                c = blk * 2 + m
                b = c % 2
                ms = slice(m * P, (m + 1) * P)
                pd, t_pd = prod[b], t_prod[b]
                if it == 0:
                    ph1, t_ph1 = self.psum()
                    self.mm_group(ph1[:, 0:2], t_ph1, [(wgc[:, k, ms], hhn[:, k, :]) for k in range(KC)], reads=[t_wgc] + t_hhn)
                    ph2, t_ph2 = self.psum()
                    self.mm_group(ph2[:, 0:2], t_ph2, [(wxa[:, k, ms], hhn[:, k, :]) for k in range(KC)], reads=[t_wxa] + t_hhn)
                    S.op("act", lambda e, ph1=ph1, b=b: e.activation(out=gcs[b][:, 0:2], in_=ph1[:, 0:2], func=AF.Copy), reads=[t_ph1], writes=[t_gcs[b]])
                    S.op("dve", lambda e, ph2=ph2, b=b, pd=pd: e.tensor_tensor(out=pd[:, 0:2], in0=gcs[b][:, 0:2], in1=ph2[:, 0:2], op=ALU.mult),
                         reads=[t_ph2, t_gcs[b]], writes=[t_pd])
                else:
                    S.op("act", lambda e, c=c, pd=pd: e.activation(out=pd[:, 0:2], in_=phalo[:, c, :], func=AF.Copy), reads=[t_phalo[c]], writes=[t_pd])
                pgc, t_pgc = self.psum()
                self.mm_group(pgc[:, 0:TT], t_pgc, [(wgc[:, k, ms], self.hn[:, k, :]) for k in range(KC)], reads=[t_wgc] + self.t_hn)
                pxa, t_pxa = self.psum()
                self.mm_group(pxa[:, 0:TT], t_pxa, [(wxa[:, k, ms], self.hn[:, k, :]) for k in range(KC)], reads=[t_wxa] + self.t_hn)
                pgb, t_pgb = self.psum()
                self.mm_group(pgb[:, 0:TT], t_pgb, [(wgb[:, k, ms], self.hn[:, k, :]) for k in range(KC)], reads=[t_wgb] + self.t_hn)
                S.op("act", lambda e, pgc=pgc, b=b: e.activation(out=gcs[b][:], in_=pgc[:, 0:TT], func=AF.Copy), reads=[t_pgc], writes=[t_gcs[b]])
                S.op("dve", lambda e, pxa=pxa, b=b, pd=pd: e.tensor_tensor(out=pd[:, 2:TT + 2], in0=gcs[b][:], in1=pxa[:, 0:TT], op=ALU.mult),
                     reads=[t_pxa, t_gcs[b]], writes=[t_pd])
                if it < NT - 1:
                    S.op("act", lambda e, c=c, pd=pd: e.activation(out=phalo[:, c, :], in_=pd[:, TT:TT + 2], func=AF.Copy), reads=[t_pd], writes=[t_phalo[c]])
                S.op("act", lambda e, c=c, b=b, pd=pd: e.activation(out=cv[b][:], in_=pd[:, 2:TT + 2], func=AF.Copy, scale=cw[:, 2, c:c + 1]),
                     reads=[t_pd, t_cw], writes=[t_cv[b]])
                S.op("dve", lambda e, c=c, b=b, pd=pd: e.scalar_tensor_tensor(out=cv[b][:], in0=pd[:, 1:TT + 1], scalar=cw[:, 1, c:c + 1], in1=cv[b][:],
                                                                           op0=ALU.mult, op1=ALU.add), reads=[t_pd, t_cw, t_cv[b]], writes=[t_cv[b]])
                S.op("dve", lambda e, c=c, b=b, pd=pd: e.scalar_tensor_tensor(out=cv[b][:], in0=pd[:, 0:TT], scalar=cw[:, 0, c:c + 1], in1=cv[b][:],
                                                                           op0=ALU.mult, op1=ALU.add), reads=[t_pd, t_cw, t_cv[b]], writes=[t_cv[b]])
                S.op("dve", lambda e, c=c, b=b, pgb=pgb: e.tensor_tensor(out=self.bufA[:, c, :], in0=cv[b][:], in1=pgb[:, 0:TT], op=ALU.mult),
                     reads=[t_cv[b], t_pgb], writes=[self.t_A[c]])
        self.linear_to(Win, 3072, 1024, self.t_hn, self.hn, self.bufA, UO, self.t_A)
        for ri in range(2):
            S.op("act", lambda e, ri=ri, it=it: e.activation(out=Xb[ri][:, :, 0], in_=X0[ri][:, :, it], func=AF.Copy), reads=[t_X0[ri]], writes=[t_Xb[ri]])
        zscan(it)
        for ri in range(2):
            S.op("act", lambda e, ri=ri: e.activation(out=X16[ri][:], in_=Xb[ri][:, :, 0:NJ], func=AF.Copy), reads=[t_Xb[ri]], writes=[t_X16[ri]])
        for c8 in range(8):
            wc, t_wc = self.load_wc(dr, c8)
            fir = wc[:, 0:1024].rearrange("p (k o) -> p k o", k=8)
            wo = wc[:, 1024:3072].rearrange("p (q r i o) -> p q r i o", q=4, r=8, i=2)
            ut = self.bufA[:, UO + c8, :]
            uv = ut.rearrange("p (j s) -> p j s", s=8)
            py, t_py = self.psum()
            pyv = py[:, 0:TT].rearrange("p (j s) -> p j s", s=8)
            rd = [t_wc, self.t_A[UO + c8]] + t_X16
            S.op("pe", lambda e, py=py, fir=fir, ut=ut: e.matmul(py[:, 0:TT], fir[:, 0, :], ut, start=True, stop=False), reads=rd, writes=[t_py], inc=False)
            for k in range(1, 8):
                S.op("pe", lambda e, pyv=pyv, fir=fir, uv=uv, k=k: e.matmul(pyv[:, :, k:8], fir[:, k, :], uv[:, :, 0:8 - k], start=False, stop=False),
                     reads=rd, writes=[t_py], inc=False)
            for q in range(4):
                for r in range(8):
                    for ri in range(2):
                        last = (q == 3 and r == 7 and ri == 1)
                        S.op("pe", lambda e, pyv=pyv, wo=wo, q=q, r=r, ri=ri, c8=c8, last=last: e.matmul(
                            pyv[q * 32:(q + 1) * 32, :, r], wo[:, q, r, ri, :], X16[ri][:, c8 * 4 + q, :], start=False, stop=last,
                            tile_position=(0, q * 32)), reads=rd, writes=[t_py], inc=last)
            b = c8 % 2
            S.op("act", lambda e, py=py, b=b: e.activation(out=yx[b][:], in_=py[:, 0:TT], func=AF.Square), reads=[t_py], writes=[t_yx[b]])
            S.op("dve", lambda e, b=b: e.tensor_scalar(out=yx[b][:], in0=yx[b][:], scalar1=0.044715, scalar2=1.0, op0=ALU.mult, op1=ALU.add),
                 reads=[t_yx[b]], writes=[t_yx[b]])
            S.op("dve", lambda e, py=py, b=b: e.tensor_tensor(out=yx[b][:], in0=yx[b][:], in1=py[:, 0:TT], op=ALU.mult), reads=[t_yx[b], t_py], writes=[t_yx[b]])
            S.op("act", lambda e, b=b: e.activation(out=yx[b][:], in_=yx[b][:], func=AF.Sigmoid, scale=1.5957691216057308), reads=[t_yx[b]], writes=[t_yx[b]])
            S.op("dve", lambda e, py=py, b=b: e.tensor_tensor(out=yg[b][:], in0=yx[b][:], in1=py[:, 0:TT], op=ALU.mult), reads=[t_yx[b], t_py], writes=[t_yg[b]])
            S.op("act", lambda e, b=b: e.activation(out=yb[b][:], in_=yg[b][:], func=AF.Copy), reads=[t_yg[b]], writes=[t_yb[b]])
            pg, t_pg = self.psum()
            self.mm_group(pg[:, 0:TT], t_pg, [(gblk[:, c8, :], yb[b][:])], reads=[t_gblk, t_yb[b]])
            S.op("act", lambda e, pg=pg, b=b: e.activation(out=yx[b][:], in_=pg[:, 0:TT], func=AF.Sigmoid), reads=[t_pg], writes=[t_yx[b]])
            S.op("dve", lambda e, b=b, c8=c8: e.tensor_tensor(out=self.bufA[:, 8 + c8, :], in0=yg[b][:], in1=yx[b][:], op=ALU.mult),
                 reads=[t_yg[b], t_yx[b]], writes=[self.t_A[8 + c8]])
        if it == 0:
            self.dbg("m_ya0", self.bufA[:, 0, :], [self.t_A[0]], [P, TT], BF16)
            self.dbg("m_ys0", self.bufA[:, 8, :], [self.t_A[8]], [P, TT], BF16)
            self.dbg("m_u0", self.bufA[:, 16, :], [self.t_A[16]], [P, TT], BF16)
            self.dbg("m_X16", X16[0][:].rearrange("p a b -> p (a b)"), [t_X16[0]], [P, 32 * 64], BF16)
        self.linear_resid(Wout, KC, self.t_A[0:KC], self.bufA, 0)
        if it == 0:
            self.dbg("m_hT0", self.hT[:, 0, :], [self.t_hT[0]], [P, TT])
        self.store_h(h_out, tok0)


def _load_wz(self, dr, c8):
    i = self.wi
    self.wi = (i + 1) % NWB
    view = self.wb[i][:, 0:2048].rearrange("p (s i o) -> p s i o", s=8, i=2)
    t = self.t_wb[i]
    self.S.dma("sp", [lambda e: e.dma_start(out=self.wb[i][:, 0:2048], in_=dr["wz"][:, c8, :])], t, reads=[dr["t_wz"]], writes=[t])
    return view, t


def _load_wc(self, dr, c8):
    i = self.wi
    self.wi = (i + 1) % NWB
    view = self.wb[i][:, 0:3072]
    t = self.t_wb[i]
    self.S.dma("sp", [lambda e: e.dma_start(out=self.wb[i][:, 0:3072], in_=dr["wc"][:, c8, :])], t, reads=[dr["t_wc"]], writes=[t])
    return view, t


Prog.even_mixer = _even_mixer
Prog.load_wz = _load_wz
Prog.load_wc = _load_wc

EV_SMALL = dict(are_S=[P, 32], aim_S=[P, 32], ldt_S=[P, 32], bre_S=[P, 32, 16], bim_S=[P, 32, 16], cre_S=[P, 32, 16], cim_S=[P, 32, 16],
                are_T=[P, 8, 64], aim_T=[P, 8, 64], ldt_T=[P, 8, 64], bre_T=[P, 8, 64], bim_T=[P, 8, 64], mask01=[P, 2], ident=[P, P],
                dcol=[P, 8], gblk=[P, 8, P], convw=[P, 3, 8])


def build_even_launch():
    nc = bass.Bass("TRN2", target_bir_lowering=False)
    dt = lambda name, shape, kind="ExternalInput": nc.dram_tensor(name, shape, F32, kind=kind).ap()
    h_in = dt("h_in", [D, NTOK]); h_out = dt("h_out", [D, NTOK], "ExternalOutput")
    hhalo = dt("hhalo", [D, 2]); gm = dt("gm", [P, KC])
    s5in = dt("s5in", [2, P, 32]); s5out = dt("s5out", [2, P, 32], "ExternalOutput")
    Win = dt("Win", [D, 4096]); Wout = dt("Wout", [D, D])
    prm = {k: dt(k, shp) for k, shp in EV_SMALL.items()}
    dr = dict(wz=nc.dram_tensor("wz_scr", [P, 8, 2048], BF16, kind="Internal").ap(), t_wz=T("wz_scr"),
              wc=nc.dram_tensor("wc_scr", [P, 8, 3072], BF16, kind="Internal").ap(), t_wc=T("wc_scr"))
    with ExitStack() as st:
        pr = Prog(nc, st)
        g, t_g = pr.load_small("gm_s", gm, [P, KC])
        with pr.phase():
            t_Xb = pr.even_mixer(h_in, h_out, hhalo, s5in, s5out, Win, Wout, prm, dr, g, t_g, 0)
            pr.S.wait_all("sp", t_Xb)
        pr.finish(pr.t_hT)
    return nc


def build_final_launch():
    nc = bass.Bass("TRN2", target_bir_lowering=False)
    dt = lambda name, shape, kind="ExternalInput": nc.dram_tensor(name, shape, F32, kind=kind).ap()
    h_in = dt("h_in", [D, NTOK]); out = dt("out", [D, NTOK], "ExternalOutput"); gfin = dt("gfin", [P, KC])
    with ExitStack() as st:
        pr = Prog(nc, st)
        g, t_g = pr.load_small("gfin_s", gfin, [P, KC])
        with pr.phase():
            pr.final_norm(h_in, out, g, t_g, 0)
            pr.S.wait_all("sp", pr.t_fo)
        pr.finish([])
    return nc


NCORES = 8


def _run(nc, maps):
    res = run_bass_kernel_spmd(nc, maps, core_ids=list(range(NCORES)))
    return res.results


def kernel_unfused(**inp):
    z = {k: np.asarray(v) for k, v in inp.items()}
    x = z["x"].astype(np.float32)
    B = x.shape[0]
    cores = [(b, hf) for b in range(B) for hf in range(2)]
    h = [np.ascontiguousarray(x[b, hf * NTOK:(hf + 1) * NTOK, :].T) for (b, hf) in cores]

    def halo(w):
        out = []
        for ci, (b, hf) in enumerate(cores):
            if hf == 0:
                out.append(np.zeros((D, w), np.float32))
            else:
                out.append(np.ascontiguousarray(h[ci - 1][:, -w:]))
        return out
    oh, maskT = swa_consts()
    nc_x = build_xattn_launch()
    nc_f = build_ffn_launch()
    nc_e = build_even_launch()
    nc_s = build_swa_launch()
    for l in range(4):
        i = l // 2
        if l % 2 == 0:
            hp = ev_host_params(z, i)
            hal = halo(2)
            s5 = [np.zeros((2, P, 32), np.float32) for _ in cores]
            for rep in range(2):
                maps = [dict(h_in=h[ci], hhalo=hal[ci], gm=pk(z["norm_mix"][l]), s5in=s5[ci], Win=z["ev_w_in"][i], Wout=z["ev_w_out"][i], **hp)
                        for ci in range(len(cores))]
                r = _run(nc_e, maps)
                if rep == 0:
                    s5 = [np.zeros((2, P, 32), np.float32) if hf == 0 else r[ci - 1]["s5out"] for ci, (b, hf) in enumerate(cores)]
            h = [r[ci]["h_out"] for ci in range(len(cores))]
        else:
            hp = swa_host_params(z, i)
            hal = halo(128)
            maps = [dict(h_in=h[ci], hhalo=hal[ci], hmask=(np.full((P, 1), -30000.0, np.float32) if hf == 0 else np.zeros((P, 1), np.float32)),
                         gm=pk(z["norm_mix"][l]), oh=oh, maskT=maskT, relb=hp["relb"], sinkrow=hp["sinkrow"], bq=hp["bq"], bkd=hp["bkd"], bvd=hp["bvd"],
                         Wq=hp["Wq"], Wkd=hp["Wkd"], Wvd=hp["Wvd"], Wo=z["od_w_out"][i]) for ci, (b, hf) in enumerate(cores)]
            r = _run(nc_s, maps)
            h = [r[ci]["h_out"] for ci in range(len(cores))]
        maps = [dict(h_in=h[ci], memT=np.ascontiguousarray(z["mem"][b].T), gmem=pk(z["norm_mem"]), gx=pk(z["norm_xattn"][l]),
                     Wq=z["xa_w_q"][l], Wkv=z["xa_w_kv"][l], Wo=z["xa_w_o"][l]) for ci, (b, hf) in enumerate(cores)]
        r = _run(nc_x, maps)
        h = [r[ci]["h_out"] for ci in range(len(cores))]
        hal = halo(2)
        maps = [dict(h_in=h[ci], hhalo=hal[ci], gf=pk(z["norm_ffn"][l]), cw=pk(z["ff_conv_w"][l]), cb=pk(z["ff_conv_b"][l]),
                     Wg=z["ff_w_gate"][l], Wu=z["ff_w_up"][l], Wd=z["ff_w_down"][l]) for ci in range(len(cores))]
        r = _run(nc_f, maps)
        h = [r[ci]["h_out"] for ci in range(len(cores))]
    nc_n = build_final_launch()
    r = _run(nc_n, [dict(h_in=h[ci], gfin=pk(z["norm_final"])) for ci in range(len(cores))])
    out = np.empty((B, 2 * NTOK, D), np.float32)
    for ci, (b, hf) in enumerate(cores):
        out[b, hf * NTOK:(hf + 1) * NTOK, :] = r[ci]["out"].T
    return out


def kernel(**inp):
    return kernel_unfused(**inp)


RG_PAIRS = [[0, 1], [2, 3], [4, 5], [6, 7]]
SWA_SMALL = dict(bq=[P, KC], bkd=[P, 4], bvd=[P, 512], sinkrow=[1, 32])


def build_fused(depth=4, final=True, stop_after=None):
    nc = bass.Bass("TRN2", target_bir_lowering=False)
    n_ev = (depth + 1) // 2
    n_od = depth // 2
    global LAST_INPUT_NAMES
    LAST_INPUT_NAMES = []

    def dt(name, shape, kind="ExternalInput"):
        if kind == "ExternalInput":
            LAST_INPUT_NAMES.append(name)
        return nc.dram_tensor(name, list(shape), F32, kind=kind).ap()
    xT = dt("xT", [D, NTOK]); xhalo = dt("xhalo", [D, 128]); flag_d = dt("flag", [P, 1]); hmask_d = dt("hmask", [P, 1])
    memT = dt("memT", [D, N_MEM]); gmem = dt("gmem", [P, KC]); gfin = dt("gfin", [P, KC])
    gmix = dt("gmix", [P, 4, KC]); gxa = dt("gxa", [P, 4, KC]); gff = dt("gff", [P, 4, KC])
    oh_d = dt("oh", [P, 32, 2, 128]); mask_d = dt("maskT", [P, 2, 128]); relb_d = dt("relb", [P, 32, 32])
    ev = []
    for i in range(n_ev):
        e = dict(Win=dt("ev_w_in%d" % i, [D, 4096]), Wout=dt("ev_w_out%d" % i, [D, D]))
        e["prm"] = {k: dt("ev%d_%s" % (i, k), shp) for k, shp in EV_SMALL.items()}
        e["dr"] = dict(wz=nc.dram_tensor("wz_scr%d" % i, [P, 8, 2048], BF16, kind="Internal").ap(), t_wz=T("wz_scr"),
                       wc=nc.dram_tensor("wc_scr%d" % i, [P, 8, 3072], BF16, kind="Internal").ap(), t_wc=T("wc_scr"))
        ev.append(e)
    od = []
    for i in range(n_od):
        o = dict(Wq=dt("od_wq%d" % i, [D, D]), Wkd=dt("od_wkd%d" % i, [D, 512]), Wvd=dt("od_wvd%d" % i, [D, 512]), Wo=dt("od_wo%d" % i, [D, D]))
        o["prm"] = {k: dt("od%d_%s" % (i, k), shp) for k, shp in SWA_SMALL.items()}
        od.append(o)
    xa = [dict(Wq=dt("xa_wq%d" % l, [D, D]), Wkv=dt("xa_wkv%d" % l, [D, 2 * D]), Wo=dt("xa_wo%d" % l, [D, D])) for l in range(depth)]
    ff = [dict(Wg=dt("ff_wg%d" % l, [D, D_FF]), Wu=dt("ff_wu%d" % l, [D, D_FF]), Wd=dt("ff_wd%d" % l, [D_FF, D]),
               cw=dt("ff_cw%d" % l, [P, 3, FC]), cb=dt("ff_cb%d" % l, [P, FC])) for l in range(depth)]
    out = dt("out", [D, NTOK], "ExternalOutput")
    if stop_after is None:
        h_scr = nc.dram_tensor("h_scr", [D, NTOK], F32, kind="Internal").ap()
    else:
        h_scr = dt("h_dbg", [D, NTOK], "ExternalOutput")
    xch = dict(rg=RG_PAIRS,
               snd_h=nc.dram_tensor("snd_h", [D, 128], F32, kind="Internal").ap(), t_snd_h=T("snd_h"),
               rcv_h=nc.dram_tensor("rcv_h", [2 * D, 128], F32, kind="Internal").ap(), t_rcv_h=T("rcv_h"),
               snd_s=nc.dram_tensor("snd_s", [2 * P, 32], F32, kind="Internal").ap(), t_snd_s=T("snd_s"),
               rcv_s=nc.dram_tensor("rcv_s", [4 * P, 32], F32, kind="Internal").ap(), t_rcv_s=T("rcv_s"))
    bias_cache = dict(ap=nc.dram_tensor("bias_scr", [P, 8192], F32, kind="Internal").ap(), t=T("bias_scr"), valid=False)
    with ExitStack() as st:
        pr = Prog(nc, st)
        S = pr.S
        gm, t_gm = pr.load_small("gmix_s", gmix, [P, 4, KC])
        gx, t_gx = pr.load_small("gxa_s", gxa, [P, 4, KC])
        gf, t_gf = pr.load_small("gff_s", gff, [P, 4, KC])
        gfn, t_gfn = pr.load_small("gfin_s", gfin, [P, KC])
        flag = pr.load_small("flag_s", flag_d, [P, 1])
        hm, t_hm = pr.load_small("hmask_s", hmask_d, [P, 1])
        gmv = gm[:].rearrange("p l k -> p (l k)")
        gxv = gx[:].rearrange("p l k -> p (l k)")
        gfv = gf[:].rearrange("p l k -> p (l k)")
        pr.prep_mem(memT, gmem)

        def exchange_h():
            S.dma("sp", [lambda e: e.dma_start(out=xch["snd_h"].rearrange("(k p) t -> p k t", p=P), in_=pr.hT[:, :, TT - 128:TT])],
                  xch["t_snd_h"], reads=pr.t_hT, writes=[xch["t_snd_h"]])
            S.cc("pool", lambda e: e.collective_compute("AllGather", ALU.bypass, replica_groups=xch["rg"], ins=[xch["snd_h"]], outs=[xch["rcv_h"]]),
                 reads=[xch["t_snd_h"]], writes=[xch["t_rcv_h"]])

        for l in range(depth):
            i = l // 2
            h_in = xT if l == 0 else h_scr
            halo_src = xhalo if l == 0 else xch["rcv_h"][0:D, :]
            hrd = [] if l == 0 else [xch["t_rcv_h"]]
            with pr.phase():
                if l % 2 == 0:
                    pr.even_mixer(h_in, h_scr, halo_src[:, 126:128], ev[i]["Win"], ev[i]["Wout"], ev[i]["prm"], ev[i]["dr"],
                                  gmv, t_gm, l * KC, flag, hrd, xch)
                else:
                    o = od[i]
                    bq, t_bq = pr.load_small("bq_s", o["prm"]["bq"], [P, KC])
                    S.op("dve", lambda e, bq=bq: e.tensor_scalar(out=bq[:], in0=bq[:], scalar1=0.125, scalar2=None, op0=ALU.mult), reads=[t_bq], writes=[t_bq])
                    bkd, t_bkd = pr.load_small("bkd_s", o["prm"]["bkd"], [P, 4])
                    bvd, t_bvd = pr.load_small("bvd_s", o["prm"]["bvd"], [P, 512])
                    pr.swa_setup(oh_d, mask_d, relb_d, o["prm"]["sinkrow"], cache=bias_cache)
                    pr.swa(h_in, h_scr, halo_src, hm, t_hm, o["Wq"], o["Wkd"], o["Wvd"], o["Wo"], bq, t_bq, bkd, t_bkd, bvd, t_bvd,
                           gmv, t_gm, l * KC, flag=flag, hrd=hrd)
            if stop_after == (l, "mix"):
                break
            with pr.phase():
                pr.xattn(h_scr, h_scr, xa[l]["Wq"], xa[l]["Wkv"], xa[l]["Wo"], gxv, t_gx, l * KC)
                exchange_h()
            if stop_after == (l, "xa"):
                break
            with pr.phase():
                cws, t_cw = pr.load_small("cw_s", ff[l]["cw"], [P, 3, FC])
                cbs, t_cb = pr.load_small("cb_s", ff[l]["cb"], [P, FC])
                pr.ffn(h_scr, h_scr, xch["rcv_h"][0:D, 126:128], ff[l]["Wg"], ff[l]["Wu"], ff[l]["Wd"], gfv, t_gf, l * KC, cws, t_cw, cbs, t_cb,
                       flag=flag, hrd=[xch["t_rcv_h"]])
                if l < depth - 1:
                    exchange_h()
        with pr.phase():
            pr.final_norm(h_scr, out, gfn, t_gfn, 0)
            S.wait_all("sp", pr.t_fo)
        pr.finish(getattr(pr, "dbg_tiles", []))
    return nc


def fused_inputs(z):
    x = np.asarray(z["x"], np.float32)
    B = x.shape[0]
    cores = [(b, hf) for b in range(B) for hf in range(2)]
    oh, maskT = swa_consts()
    common = dict(gmem=pk(z["norm_mem"]), gfin=pk(z["norm_final"]), gmix=pk(z["norm_mix"]), gxa=pk(z["norm_xattn"]), gff=pk(z["norm_ffn"]),
                  oh=oh, maskT=maskT)
    for i in range(2):
        hp = ev_host_params(z, i)
        common["ev_w_in%d" % i] = np.asarray(z["ev_w_in"][i], np.float32)
        common["ev_w_out%d" % i] = np.asarray(z["ev_w_out"][i], np.float32)
        for k in EV_SMALL:
            common["ev%d_%s" % (i, k)] = hp[k]
        sp = swa_host_params(z, i)
        common["relb"] = sp["relb"]
        common["od_wq%d" % i] = sp["Wq"]
        common["od_wkd%d" % i] = sp["Wkd"]
        common["od_wvd%d" % i] = sp["Wvd"]
        common["od_wo%d" % i] = np.asarray(z["od_w_out"][i], np.float32)
        for k in SWA_SMALL:
            common["od%d_%s" % (i, k)] = sp[k]
    for l in range(4):
        common["xa_wq%d" % l] = np.asarray(z["xa_w_q"][l], np.float32)
        common["xa_wkv%d" % l] = np.asarray(z["xa_w_kv"][l], np.float32)
        common["xa_wo%d" % l] = np.asarray(z["xa_w_o"][l], np.float32)
        common["ff_wg%d" % l] = np.asarray(z["ff_w_gate"][l], np.float32)
        common["ff_wu%d" % l] = np.asarray(z["ff_w_up"][l], np.float32)
        common["ff_wd%d" % l] = np.asarray(z["ff_w_down"][l], np.float32)
        common["ff_cw%d" % l] = pk(z["ff_conv_w"][l])
        common["ff_cb%d" % l] = pk(z["ff_conv_b"][l])
    maps = []
    for (b, hf) in cores:
        m = dict(common)
        m["xT"] = np.ascontiguousarray(x[b, hf * NTOK:(hf + 1) * NTOK, :].T)
        m["xhalo"] = np.zeros((D, 128), np.float32) if hf == 0 else np.ascontiguousarray(x[b, NTOK - 128:NTOK, :].T)
        m["flag"] = np.full((P, 1), float(hf), np.float32)
        m["hmask"] = np.full((P, 1), -30000.0 if hf == 0 else 0.0, np.float32)
        m["memT"] = np.ascontiguousarray(np.asarray(z["mem"], np.float32)[b].T)
        maps.append(m)
    return cores, maps


def kernel_fused(**inp):
    z = {k: np.asarray(v) for k, v in inp.items()}
    cores, maps = fused_inputs(z)
    nc = build_fused()
    decl = set(LAST_INPUT_NAMES)
    maps = [{k: v for k, v in m.items() if k in decl} for m in maps]
    res = run_bass_kernel_spmd(nc, maps, core_ids=list(range(len(cores))))
    B = z["x"].shape[0]
    out = np.empty((B, 2 * NTOK, D), np.float32)
    for ci, (b, hf) in enumerate(cores):
        out[b, hf * NTOK:(hf + 1) * NTOK, :] = res.results[ci]["out"].T
    return out


def kernel(**inp):
    return kernel_fused(**inp)
```

```python
import math
from contextlib import ExitStack

import numpy as np
import concourse.bass as bass
import concourse.mybir as mybir
from concourse.bass_utils import run_bass_kernel_spmd

F32 = mybir.dt.float32
BF16 = mybir.dt.bfloat16
AF = mybir.ActivationFunctionType
ALU = mybir.AluOpType

P = 128
D = 2048
KC = D // P
NTOK = 2048
TT = 512
NT = NTOK // TT
N_MEM = 256
D_FF = 5632
FC = D_FF // P
RMS_EPS = 1e-5


class T:
    __slots__ = ("name", "w", "r", "sem", "semval", "last_dma")

    def __init__(self, name):
        self.name = name
        self.w = None
        self.r = {}
        self.sem = None
        self.semval = 0
        self.last_dma = None


ENGS = ("pe", "act", "dve", "pool", "sp")


class Sched:
    def __init__(self, nc, stack):
        self.nc = nc
        self.stack = stack
        self.q = {e: [] for e in ENGS}
        self.cnt = {e: 0 for e in ENGS}
        self.waited = {e: {} for e in ENGS}
        self.semh = {}
        for e in ENGS:
            self.semh[e] = stack.enter_context(nc.semaphore("sem_" + e))
        self.ndma_sem = 0
        self.n_instr = 0
        self.dmaval = {}
        self.gd = [stack.enter_context(nc.sbuf_tensor("gd%d" % i, [P, 1], F32)) for i in range(3)]
        self.q["dve"].append(lambda e: e.memset(self.gd[2][:], 0.0))

    def barrier(self):
        cur = {e: self.cnt[e] for e in ENGS if self.cnt[e] > 0}
        cur.update(self.dmaval)
        for e in ENGS:
            self._need(e, {k: v for k, v in cur.items() if k != e})

    NDSEM = 40

    def _tile_sem(self, t):
        if t.sem is None:
            i = self.ndma_sem % self.NDSEM
            key = "dsem%d" % i
            self.ndma_sem += 1
            if key not in self.semh:
                self.semh[key] = self.stack.enter_context(self.nc.semaphore(key))
            t.sem = key
        return t.sem

    def _need(self, eng, needs):
        for key, val in needs.items():
            if key == "pe" and eng == "pe":
                continue
            if self.waited[eng].get(key, 0) >= val:
                continue
            self.waited[eng][key] = val
            h = self.semh[key]
            self.q[eng].append(lambda e, h=h, val=val: e.wait_ge(h, val))

    def _deps(self, reads, writes):
        needs = {}

        def add(m):
            if m is not None:
                if needs.get(m[0], 0) < m[1]:
                    needs[m[0]] = m[1]
        for t in reads:
            add(t.w)
        for t in writes:
            add(t.w)
            for m in t.r.items():
                add(m)
        return needs

    def op(self, eng, fn, reads=(), writes=(), inc=True, guard=False):
        needs = self._deps(reads, writes)
        self._need(eng, needs)
        if guard and eng in ("dve", "act") and inc:
            self.q[eng].append(lambda e, fn=fn: fn(e))
            self.cnt[eng] += 1
            h = self.semh[eng]
            g = self.gd
            if eng == "dve":
                self.q[eng].append(lambda e, h=h, g=g: e.memset(g[0][:], 0.0).then_inc(h, 1))
            else:
                self.q[eng].append(lambda e, h=h, g=g: e.activation(out=g[1][:], in_=g[2][:], func=AF.Copy).then_inc(h, 1))
            mark = (eng, self.cnt[eng])
            self._mark(reads, writes, mark)
            self.n_instr += 2
            return
        if inc:
            self.cnt[eng] += 1
            h = self.semh[eng]
            self.q[eng].append(lambda e, fn=fn, h=h: fn(e).then_inc(h, 1))
            mark = (eng, self.cnt[eng])
        else:
            self.q[eng].append(lambda e, fn=fn: fn(e))
            mark = (eng, self.cnt[eng] + 1)
        self._mark(reads, writes, mark)
        self.n_instr += 1

    def dma(self, eng, fns, owner, reads=(), writes=(), step=16):
        key = self._tile_sem(owner)
        needs = self._deps(reads, writes)
        cur = self.dmaval.get(key, 0)
        if cur > 0 and needs.get(key, 0) < cur:
            needs[key] = cur
        self._need(eng, needs)
        h = self.semh[key]
        for fn in fns:
            cur += step
            self.q[eng].append(lambda e, fn=fn, h=h: fn(e).then_inc(h, step))
        mark = (key, cur)
        self.dmaval[key] = cur
        self._mark(reads, writes, mark)
        self.n_instr += len(fns)

    def cc(self, eng, fn, reads=(), writes=()):
        key = "ccsem%d" % len([k for k in self.semh if k.startswith("ccsem")])
        self.semh[key] = self.stack.enter_context(self.nc.semaphore(key))
        needs = self._deps(reads, writes)
        self._need(eng, needs)
        h = self.semh[key]
        self.q[eng].append(lambda e, fn=fn, h=h: fn(e).then_inc(h, 1))
        mark = (key, 1)
        self.dmaval[key] = 1
        self._mark(reads, writes, mark)
        self.n_instr += 1

    @staticmethod
    def _mark(reads, writes, mark):
        for t in reads:
            if t.r.get(mark[0], 0) < mark[1]:
                t.r[mark[0]] = mark[1]
        for t in writes:
            t.w = mark
            t.r = {}

    def wait_all(self, eng, tiles):
        needs = {}
        for t in tiles:
            for m in ([t.w] if t.w else []) + list(t.r.items()):
                if needs.get(m[0], 0) < m[1]:
                    needs[m[0]] = m[1]
        self._need(eng, needs)

    def emit(self):
        nc = self.nc
        if not any(self.q[e] for e in ENGS):
            return
        qs = {e: self.q[e] for e in ENGS}
        self.q = {e: [] for e in ENGS}
        self._emit_block(nc, qs)

    def _emit_block(self, nc, qs):
        self_q = qs
        with nc.Block() as block:
            @block.tensor
            def _(e):
                for f in self_q["pe"]:
                    f(e)

            @block.scalar
            def _(e):
                for f in self_q["act"]:
                    f(e)

            @block.vector
            def _(e):
                for f in self_q["dve"]:
                    f(e)

            @block.gpsimd
            def _(e):
                for f in self_q["pool"]:
                    f(e)

            @block.sync
            def _(e):
                for f in self_q["sp"]:
                    f(e)


WSLOT = 5632
NWB = 4
NPS = 8


class Prog:
    def __init__(self, nc, stack):
        self.nc = nc
        self.st = stack
        self.S = Sched(nc, stack)
        self.pstack = None
        self.uid = 0

        def sb(name, shape, dt):
            self.uid += 1
            stk = self.pstack if self.pstack is not None else stack
            return stk.enter_context(nc.sbuf_tensor("%s_%d" % (name, self.uid), shape, dt))
        self.sb = sb
        self.hT = sb("hT", [P, KC, TT], F32)
        self.t_hT = [T("hT%d" % k) for k in range(KC)]
        self.hn = sb("hn", [P, KC, TT], BF16)
        self.t_hn = [T("hn%d" % k) for k in range(KC)]
        self.wb = [sb("wb%d" % i, [P, WSLOT], BF16) for i in range(NWB)]
        self.t_wb = [T("wb%d" % i) for i in range(NWB)]
        self.wi = 0
        self.ps = [stack.enter_context(nc.psum_tensor("ps%d" % i, [P, 512], F32)) for i in range(NPS)]
        self.t_ps = [T("ps%d" % i) for i in range(NPS)]
        self.pi = 0
        self.sq = [sb("sq%d" % i, [P, TT], BF16) for i in range(2)]
        self.t_sq = [T("sq0"), T("sq1")]
        self.rstd = sb("rstd", [P, TT], F32)
        self.t_rstd = T("rstd")
        self.rtmp = sb("rtmp", [P, TT], F32)
        self.t_rtmp = T("rtmp")
        self.ones = sb("ones", [P, P], BF16)
        self.t_ones = T("ones")
        self.S.op("dve", lambda e: e.memset(self.ones[:], 1.0), writes=[self.t_ones])
        self.epsc = sb("epsc", [P, 1], F32)
        self.t_eps = T("eps")
        self.S.op("dve", lambda e: e.memset(self.epsc[:], RMS_EPS), writes=[self.t_eps])
        self.evi = 0

    def phase(self):
        prog = self

        class _Ph:
            def __enter__(s):
                prog.S.barrier()
                prog.S.emit()
                s.es = ExitStack()
                s.es.__enter__()
                s.prev = prog.pstack
                prog.pstack = s.es
                return s

            def __exit__(s, *a):
                prog.S.barrier()
                prog.S.emit()
                prog.pstack = s.prev
                return s.es.__exit__(*a)
        return _Ph()

    DBG = False

    def dbg(self, name, ap, tiles, shape, dt=F32):
        if not self.DBG:
            return
        d = self.nc.dram_tensor("dbg_" + name, list(shape), dt, kind="ExternalOutput").ap()
        t = T("dbg_" + name)
        self.S.dma("sp", [lambda e: e.dma_start(out=d, in_=ap)], t, reads=list(tiles))
        self.dbg_tiles = getattr(self, "dbg_tiles", []) + [t]

    def alloc_A(self, n):
        self.bufA = self.sb("bufA", [P, n, TT], BF16)
        self.t_A = [T("A%d" % k) for k in range(n)]

    def psum(self):
        i = self.pi
        self.pi = (i + 1) % NPS
        return self.ps[i], self.t_ps[i]

    def load_w(self, src3, k, m):
        i = self.wi
        self.wi = (i + 1) % NWB
        view = self.wb[i][:, 0:k * m].rearrange("p (k m) -> p k m", k=k)
        t = self.t_wb[i]
        if k > 22:
            h = k // 2
            fns = [lambda e: e.dma_start(out=view[:, 0:h, :], in_=src3[:, 0:h, :]),
                   lambda e: e.dma_start(out=view[:, h:, :], in_=src3[:, h:, :])]
        else:
            fns = [lambda e: e.dma_start(out=view, in_=src3)]
        self.S.dma("pool", fns, t, writes=[t])
        return view, t

    def mm_group(self, out_ap, t_out, pairs, reads, tp=None):
        n = len(pairs)
        for i, (l, r) in enumerate(pairs):
            if tp is None:
                self.S.op("pe", lambda e, l=l, r=r, i=i: e.matmul(out_ap, l, r, start=(i == 0), stop=(i == n - 1)),
                          reads=reads, writes=[t_out], inc=(i == n - 1))
            else:
                self.S.op("pe", lambda e, l=l, r=r, i=i: e.matmul(out_ap, l, r, start=(i == 0), stop=(i == n - 1), tile_position=tp),
                          reads=reads, writes=[t_out], inc=(i == n - 1))

    def load_small(self, name, dram_ap, shape, dt=F32):
        t = self.sb(name, shape, dt)
        tt = T(name)
        self.S.dma("sp", [lambda e: e.dma_start(out=t[:], in_=dram_ap)], tt, writes=[tt])
        return t, tt

    def rmsnorm(self, src, t_src, gain, t_gain, gcol, dst, t_dst, w, nk=KC):
        S = self.S
        ps, t_ps = self.psum()
        for k in range(nk):
            b = k % 2
            S.op("act", lambda e, k=k, b=b: e.activation(out=self.sq[b][:, 0:w], in_=src[:, k, 0:w], func=AF.Square),
                 reads=[t_src[k]], writes=[self.t_sq[b]])
            S.op("pe", lambda e, k=k, b=b: e.matmul(ps[:, 0:w], self.ones[:], self.sq[b][:, 0:w], start=(k == 0), stop=(k == nk - 1)),
                 reads=[self.t_sq[b], self.t_ones], writes=[t_ps], inc=True)
        S.op("act", lambda e: e.activation(out=self.rtmp[:, 0:w], in_=ps[:, 0:w], func=AF.Sqrt, bias=self.epsc[:], scale=1.0 / (nk * P)),
             reads=[t_ps, self.t_eps], writes=[self.t_rtmp])
        S.op("dve", lambda e: e.reciprocal(out=self.rstd[:, 0:w], in_=self.rtmp[:, 0:w]), reads=[self.t_rtmp], writes=[self.t_rstd])
        if getattr(self, "dbg_norm", False):
            self.dbg_norm = False
            self.dbg("n_hT0", src[:, 0, 0:w], [t_src[0]], [P, w], F32)
            self.dbg("n_hT15", src[:, 15, 0:w], [t_src[15]], [P, w], F32)
            self.dbg("n_rtmp", self.rtmp[:, 0:w], [self.t_rtmp], [P, w], F32)
            self.dbg("n_rstd", self.rstd[:, 0:w], [self.t_rstd], [P, w], F32)
        for k in range(nk):
            S.op("dve", lambda e, k=k: e.scalar_tensor_tensor(out=dst[:, k, 0:w], in0=src[:, k, 0:w], scalar=gain[:, gcol + k:gcol + k + 1],
                                                              in1=self.rstd[:, 0:w], op0=ALU.mult, op1=ALU.mult),
                 reads=[t_src[k], t_gain, self.t_rstd], writes=[t_dst[k]])

    def load_h(self, h_dram, tok0):
        S = self.S
        src = h_dram[:, tok0:tok0 + TT].rearrange("(k p) t -> p k t", p=P)
        for g in range(4):
            S.dma("sp", [lambda e, g=g: e.dma_start(out=self.hT[:, 4 * g:4 * g + 4, :], in_=src[:, 4 * g:4 * g + 4, :])],
                  self.t_hT[4 * g], writes=self.t_hT[4 * g:4 * g + 4])

    def store_h(self, h_dram, tok0):
        S = self.S
        dst = h_dram[:, tok0:tok0 + TT].rearrange("(k p) t -> p k t", p=P)
        for g in range(4):
            S.dma("sp", [lambda e, g=g: e.dma_start(out=dst[:, 4 * g:4 * g + 4, :], in_=self.hT[:, 4 * g:4 * g + 4, :])],
                  self.t_hT[4 * g], reads=self.t_hT[4 * g:4 * g + 4])

    def linear_resid(self, W, kin, t_in_list, in_buf, in_off):
        S = self.S
        mb = 256
        per = mb // P
        nh = 1 if kin <= 22 else 2
        kh = kin // nh
        for blk in range(D // mb):
            ws = []
            for hf in range(nh):
                ws.append(self.load_w(W[hf * kh * P:(hf + 1) * kh * P, blk * mb:(blk + 1) * mb].rearrange("(k p) m -> p k m", p=P), kh, mb))
            for m in range(per):
                ps, t_ps = self.psum()
                self.mm_group(ps[:, 0:TT], t_ps,
                              [(ws[k // kh][0][:, k % kh, m * P:(m + 1) * P], in_buf[:, in_off + k, :]) for k in range(kin)],
                              reads=[x[1] for x in ws] + t_in_list)
                c = blk * per + m
                S.op("dve", lambda e, c=c, ps=ps: e.tensor_tensor(out=self.hT[:, c, :], in0=self.hT[:, c, :], in1=ps[:, 0:TT], op=ALU.add),
                     reads=[t_ps, self.t_hT[c]], writes=[self.t_hT[c]])

    def linear_to(self, W, col0, ncols, t_in_list, in_buf, out_buf, out_off, t_out_list, evac=None):
        S = self.S
        mb = 256
        for blk in range(ncols // mb):
            w, t_w = self.load_w(W[:, col0 + blk * mb:col0 + (blk + 1) * mb].rearrange("(k p) m -> p k m", p=P), KC, mb)
            for m in range(2):
                ps, t_ps = self.psum()
                self.mm_group(ps[:, 0:TT], t_ps, [(w[:, k, m * P:(m + 1) * P], in_buf[:, k, :]) for k in range(KC)],
                              reads=[t_w] + t_in_list)
                c = blk * 2 + m
                if evac is not None:
                    evac(c, ps, t_ps)
                else:
                    S.op("act", lambda e, c=c, ps=ps: e.activation(out=out_buf[:, out_off + c, :], in_=ps[:, 0:TT], func=AF.Copy),
                         reads=[t_ps], writes=[t_out_list[out_off + c]])

    def prep_mem(self, memT_d, gmem_d):
        S = self.S
        self.memn = self.sb("memn", [P, KC, N_MEM], BF16)
        self.t_memn = [T("memn%d" % k) for k in range(KC)]
        gm, t_gm = self.load_small("gmem_s", gmem_d, [P, KC])
        src = memT_d.rearrange("(k p) t -> p k t", p=P)
        S.dma("sp", [lambda e: e.dma_start(out=self.hT[:, :, 0:N_MEM], in_=src)], self.t_hT[0], writes=self.t_hT)
        self.rmsnorm(self.hT, self.t_hT, gm, t_gm, 0, self.memn, self.t_memn, N_MEM)

    def prep_kv(self, Wkv):
        S = self.S
        for blk in range(8):
            w, t_w = self.load_w(Wkv[:, blk * 256:(blk + 1) * 256].rearrange("(k p) m -> p k m", p=P), KC, 256)
            for m in range(2):
                ps, t_ps = self.psum()
                self.mm_group(ps[:, 0:N_MEM], t_ps, [(w[:, k, m * P:(m + 1) * P], self.memn[:, k, :]) for k in range(KC)],
                              reads=[t_w] + self.t_memn)
                c = blk * 2 + m
                S.op("act", lambda e, c=c, ps=ps: e.activation(out=self.kT[:, c, :], in_=ps[:, 0:N_MEM], func=AF.Copy),
                     reads=[t_ps], writes=[self.t_kT[c]])
        for blk in range(8):
            w, t_w = self.load_w(Wkv[:, D + blk * 256:D + (blk + 1) * 256].rearrange("(k p) m -> p k m", p=P), KC, 256)
            for mc in range(2):
                ps, t_ps = self.psum()
                self.mm_group(ps[:, 0:256], t_ps, [(self.memn[:, k, mc * P:(mc + 1) * P], w[:, k, :]) for k in range(KC)],
                              reads=[t_w] + self.t_memn)
                S.op("act", lambda e, mc=mc, blk=blk, ps=ps: e.activation(out=self.vv[:, mc, blk * 256:(blk + 1) * 256], in_=ps[:, 0:256], func=AF.Copy),
                     reads=[t_ps], writes=[self.t_vv[mc]])

    def xattn(self, h_in, h_out, Wq, Wkv, Wo, gain, t_gain, gcol):
        S = self.S
        self.alloc_A(2 * KC)
        self.kT = self.sb("kT", [P, KC, N_MEM], BF16)
        self.t_kT = [T("kT%d" % k) for k in range(KC)]
        self.vv = self.sb("vv", [P, 2, D], BF16)
        self.t_vv = [T("vv0"), T("vv1")]
        self.pT = self.sb("pT", [P, 2, TT], BF16)
        self.t_pT = [T("pT0"), T("pT1")]
        self.rden = self.sb("rden", [P, TT], F32)
        self.t_rden = T("rden")
        self.prep_kv(Wkv)
        qoff, ooff = 0, KC
        scale = 1.0 / math.sqrt(512.0)
        for it in range(NT):
            tok0 = it * TT
            self.load_h(h_in, tok0)
            self.dbg_norm = (it == 0)
            self.rmsnorm(self.hT, self.t_hT, gain, t_gain, gcol, self.hn, self.t_hn, TT)
            self.linear_to(Wq, 0, D, self.t_hn, self.hn, self.bufA, qoff, self.t_A)
            for hd in range(4):
                for mc in range(2):
                    ps, t_ps = self.psum()
                    self.mm_group(ps[:, 0:TT], t_ps,
                                  [(self.kT[:, hd * 4 + j, mc * P:(mc + 1) * P], self.bufA[:, qoff + hd * 4 + j, :]) for j in range(4)],
                                  reads=self.t_kT[hd * 4:hd * 4 + 4] + self.t_A[qoff + hd * 4:qoff + hd * 4 + 4])
                    S.op("act", lambda e, mc=mc, ps=ps: e.activation(out=self.pT[:, mc, :], in_=ps[:, 0:TT], func=AF.Exp, scale=scale),
                         reads=[t_ps], writes=[self.t_pT[mc]])
                ps, t_ps = self.psum()
                self.mm_group(ps[:, 0:TT], t_ps, [(self.ones[:], self.pT[:, mc, :]) for mc in range(2)],
                              reads=[self.t_ones] + self.t_pT)
                S.op("dve", lambda e, ps=ps: e.reciprocal(out=self.rden[:], in_=ps[:, 0:TT]), reads=[t_ps], writes=[self.t_rden])
                for j in range(4):
                    ps, t_ps = self.psum()
                    f0 = hd * 512 + j * P
                    self.mm_group(ps[:, 0:TT], t_ps, [(self.vv[:, mc, f0:f0 + P], self.pT[:, mc, :]) for mc in range(2)],
                                  reads=self.t_vv + self.t_pT)
                    c = ooff + hd * 4 + j
                    S.op("dve", lambda e, c=c, ps=ps: e.tensor_tensor(out=self.bufA[:, c, :], in0=ps[:, 0:TT], in1=self.rden[:], op=ALU.mult),
                         reads=[t_ps, self.t_rden], writes=[self.t_A[c]])
                if it == 0 and hd == 0:
                    self.dbg("hn0", self.hn[:, 0, :], [self.t_hn[0]], [P, TT], BF16)
                    self.dbg("q0", self.bufA[:, 0, :], [self.t_A[0]], [P, TT], BF16)
                    self.dbg("kT0", self.kT[:, 0, :], [self.t_kT[0]], [P, N_MEM], BF16)
                    self.dbg("vv0", self.vv[:, 0, 0:512], [self.t_vv[0]], [P, 512], BF16)
                    self.dbg("memn0", self.memn[:, 0, :], [self.t_memn[0]], [P, N_MEM], BF16)
                    self.dbg("pT0", self.pT[:, 0, :], [self.t_pT[0]], [P, TT], BF16)
                    self.dbg("rden", self.rden[:], [self.t_rden], [P, TT], F32)
                    self.dbg("o0", self.bufA[:, ooff, :], [self.t_A[ooff]], [P, TT], BF16)
            self.linear_resid(Wo, KC, self.t_A[ooff:ooff + KC], self.bufA, ooff)
            self.store_h(h_out, tok0)

    def halo_load(self, dst, t_dst, src_ap, w, flag, rd=()):
        S = self.S
        S.dma("sp", [lambda e: e.dma_start(out=dst[:, :, 0:w], in_=src_ap.rearrange("(k p) t -> p k t", p=P))], t_dst[0], reads=list(rd), writes=t_dst)
        if flag is not None:
            S.op("dve", lambda e: e.tensor_scalar(out=dst[:, :, 0:w], in0=dst[:, :, 0:w], scalar1=flag[0][:, 0:1], scalar2=None, op0=ALU.mult),
                 reads=t_dst + [flag[1]], writes=t_dst)

    def ffn(self, h_in, h_out, hhalo_d, Wg, Wu, Wd, gain, t_gain, gcol, cw, t_cw, cb, t_cb, flag=None, hrd=()):
        S = self.S
        self.alloc_A(FC)
        self.gfull = [self.sb("gfull%d" % i, [P, TT + 2], F32) for i in range(2)]
        self.t_gfull = [T("gfull0"), T("gfull1")]
        self.ghalo = self.sb("ghalo", [P, FC, 2], F32)
        self.t_ghalo = [T("ghalo%d" % c) for c in range(FC)]
        self.c1 = [self.sb("c1_%d" % i, [P, TT], F32) for i in range(2)]
        self.t_c1 = [T("c1_0"), T("c1_1")]
        self.hhT = self.sb("hhT", [P, KC, 2], F32)
        self.t_hhT = [T("hhT%d" % k) for k in range(KC)]
        self.hhn = self.sb("hhn", [P, KC, 2], BF16)
        self.t_hhn = [T("hhn%d" % k) for k in range(KC)]
        self.halo_load(self.hhT, self.t_hhT, hhalo_d, 2, flag, hrd)
        self.rmsnorm(self.hhT, self.t_hhT, gain, t_gain, gcol, self.hhn, self.t_hhn, 2)
        for it in range(NT):
            tok0 = it * TT
            self.load_h(h_in, tok0)
            self.rmsnorm(self.hT, self.t_hT, gain, t_gain, gcol, self.hn, self.t_hn, TT)
            for blk in range(FC // 2):
                wg, t_wg = self.load_w(Wg[:, blk * 256:(blk + 1) * 256].rearrange("(k p) m -> p k m", p=P), KC, 256)
                wu, t_wu = self.load_w(Wu[:, blk * 256:(blk + 1) * 256].rearrange("(k p) m -> p k m", p=P), KC, 256)
                for m in range(2):
                    c = blk * 2 + m
                    b = c % 2
                    gf, t_gf = self.gfull[b], self.t_gfull[b]
                    if it == 0:
                        psh, t_psh = self.psum()
                        self.mm_group(psh[:, 0:2], t_psh, [(wg[:, k, m * P:(m + 1) * P], self.hhn[:, k, :]) for k in range(KC)],
                                      reads=[t_wg] + self.t_hhn)
                        S.op("act", lambda e, gf=gf, psh=psh: e.activation(out=gf[:, 0:2], in_=psh[:, 0:2], func=AF.Copy),
                             reads=[t_psh], writes=[t_gf])
                    else:
                        S.op("act", lambda e, gf=gf, c=c: e.activation(out=gf[:, 0:2], in_=self.ghalo[:, c, :], func=AF.Copy),
                             reads=[self.t_ghalo[c]], writes=[t_gf])
                    psg, t_psg = self.psum()
                    self.mm_group(psg[:, 0:TT], t_psg, [(wg[:, k, m * P:(m + 1) * P], self.hn[:, k, :]) for k in range(KC)],
                                  reads=[t_wg] + self.t_hn)
                    psu, t_psu = self.psum()
                    self.mm_group(psu[:, 0:TT], t_psu, [(wu[:, k, m * P:(m + 1) * P], self.hn[:, k, :]) for k in range(KC)],
                                  reads=[t_wu] + self.t_hn)
                    S.op("act", lambda e, gf=gf, psg=psg: e.activation(out=gf[:, 2:TT + 2], in_=psg[:, 0:TT], func=AF.Copy),
                         reads=[t_psg], writes=[t_gf])
                    if it < NT - 1:
                        S.op("act", lambda e, gf=gf, c=c: e.activation(out=self.ghalo[:, c, :], in_=gf[:, TT:TT + 2], func=AF.Copy),
                             reads=[t_gf], writes=[self.t_ghalo[c]])
                    c1, t_c1 = self.c1[b], self.t_c1[b]
                    S.op("act", lambda e, gf=gf, c=c, c1=c1: e.activation(out=c1[:], in_=gf[:, 2:TT + 2], func=AF.Identity,
                                                                        bias=cb[:, c:c + 1], scale=cw[:, 2, c:c + 1]),
                         reads=[t_gf, t_cw, t_cb], writes=[t_c1])
                    S.op("dve", lambda e, gf=gf, c=c, c1=c1: e.scalar_tensor_tensor(out=c1[:], in0=gf[:, 1:TT + 1], scalar=cw[:, 1, c:c + 1], in1=c1[:],
                                                                                  op0=ALU.mult, op1=ALU.add),
                         reads=[t_gf, t_cw, t_c1], writes=[t_c1])
                    S.op("dve", lambda e, gf=gf, c=c, c1=c1: e.scalar_tensor_tensor(out=c1[:], in0=gf[:, 0:TT], scalar=cw[:, 0, c:c + 1], in1=c1[:],
                                                                                  op0=ALU.mult, op1=ALU.add),
                         reads=[t_gf, t_cw, t_c1], writes=[t_c1])
                    S.op("act", lambda e, c1=c1: e.activation(out=c1[:], in_=c1[:], func=AF.Silu), reads=[t_c1], writes=[t_c1])
                    S.op("dve", lambda e, c=c, c1=c1, psu=psu: e.tensor_tensor(out=self.bufA[:, c, :], in0=c1[:], in1=psu[:, 0:TT], op=ALU.mult),
                         reads=[t_c1, t_psu], writes=[self.t_A[c]])
            self.linear_resid(Wd, FC, self.t_A, self.bufA, 0)
            self.store_h(h_out, tok0)

    def final_norm(self, h_in, out_d, gain, t_gain, gcol):
        S = self.S
        self.fo = self.sb("fo", [P, KC, TT], F32)
        self.t_fo = [T("fo%d" % k) for k in range(KC)]
        for it in range(NT):
            tok0 = it * TT
            self.load_h(h_in, tok0)
            self.rmsnorm(self.hT, self.t_hT, gain, t_gain, gcol, self.fo, self.t_fo, TT)
            dst = out_d[:, tok0:tok0 + TT].rearrange("(k p) t -> p k t", p=P)
            S.dma("sp", [lambda e, dst=dst: e.dma_start(out=dst, in_=self.fo[:])], self.t_fo[0], reads=self.t_fo)

    def finish(self, out_tiles):
        self.S.wait_all("sp", out_tiles)
        self.S.emit()


def pk(v):
    v = np.asarray(v, dtype=np.float32)
    lead = v.shape[:-1]
    n = v.shape[-1] // P
    a = v.reshape(lead + (n, P))
    a = np.moveaxis(a, -1, 0)
    return np.ascontiguousarray(a)


def build_xattn_launch():
    nc = bass.Bass("TRN2", target_bir_lowering=False)
    dt = lambda name, shape, kind="ExternalInput": nc.dram_tensor(name, shape, F32, kind=kind).ap()
    h_in = dt("h_in", [D, NTOK]); h_out = dt("h_out", [D, NTOK], "ExternalOutput")
    memT = dt("memT", [D, N_MEM]); gmem = dt("gmem", [P, KC]); gx = dt("gx", [P, KC])
    Wq = dt("Wq", [D, D]); Wkv = dt("Wkv", [D, 2 * D]); Wo = dt("Wo", [D, D])
    with ExitStack() as st:
        pr = Prog(nc, st)
        g, t_g = pr.load_small("gx_s", gx, [P, KC])
        pr.prep_mem(memT, gmem)
        with pr.phase():
            pr.xattn(h_in, h_out, Wq, Wkv, Wo, g, t_g, 0)
        pr.finish(pr.t_hT)
    return nc


def build_ffn_launch(final=False):
    nc = bass.Bass("TRN2", target_bir_lowering=False)
    dt = lambda name, shape, kind="ExternalInput": nc.dram_tensor(name, shape, F32, kind=kind).ap()
    h_in = dt("h_in", [D, NTOK]); h_out = dt("h_out", [D, NTOK], "ExternalOutput")
    hhalo = dt("hhalo", [D, 2]); gf = dt("gf", [P, KC])
    cw = dt("cw", [P, 3, FC]); cb = dt("cb", [P, FC])
    Wg = dt("Wg", [D, D_FF]); Wu = dt("Wu", [D, D_FF]); Wd = dt("Wd", [D_FF, D])
    with ExitStack() as st:
        pr = Prog(nc, st)
        g, t_g = pr.load_small("gf_s", gf, [P, KC])
        cws, t_cw = pr.load_small("cw_s", cw, [P, 3, FC])
        cbs, t_cb = pr.load_small("cb_s", cb, [P, FC])
        with pr.phase():
            pr.ffn(h_in, h_out, hhalo, Wg, Wu, Wd, g, t_g, 0, cws, t_cw, cbs, t_cb)
        pr.finish(pr.t_hT)
    return nc


def t5_bucket_np(rel):
    n = np.maximum(rel, 0)
    nf = np.maximum(n, 16).astype(np.float32)
    large = 16 + (np.log(nf / np.float32(16)) / np.float32(math.log(128 / 16)) * np.float32(16)).astype(np.int32)
    large = np.minimum(large, 31)
    return np.where(n < 16, n, large)


def swa_consts():
    k = np.arange(128)[:, None, None]
    j = np.arange(2)[None, :, None]
    q = np.arange(128)[None, None, :]
    rel = 128 + q - j * 128 - k
    valid = (rel >= 0) & (rel < 128)
    bk = t5_bucket_np(rel)
    oh = np.zeros((128, 32, 2, 128), np.float32)
    for b in range(32):
        oh[:, b] = ((bk == b) & valid).astype(np.float32)
    maskT = np.where(valid, 0.0, -30000.0).astype(np.float32)
    return oh, maskT


def _swa_setup(self, oh_d, mask_d, relb_d, sinkrow_d, cache=None):
    S = self.S
    self.biasT = self.sb("biasT", [P, 2, 4, 4, 256], F32)
    self.t_biasT = T("biasT")
    self.esrow = self.sb("esrow", [1, 32 * 128], BF16)
    self.t_esrow = T("esrow")
    bflat = self.biasT[:].rearrange("p a b c d -> p (a b c d)")
    have = cache is not None and cache.get("valid", False)
    if have:
        S.dma("sp", [lambda e: e.dma_start(out=bflat, in_=cache["ap"])], self.t_biasT, reads=[cache["t"]], writes=[self.t_biasT])
    with self.phase():
        oh = self.sb("oh_s", [P, 8, 256], F32)
        t_oh = T("oh")
        mk, t_mk = self.load_small("mask_s", mask_d.rearrange("p j q -> p (j q)"), [P, 256])
        rb, t_rb = self.load_small("relb_s", relb_d, [P, 32, 32])
        ohv = oh_d.rearrange("p b j q -> p b (j q)")
        for bg in range(0 if have else 4):
            S.dma("sp", [lambda e, bg=bg: e.dma_start(out=oh[:], in_=ohv[:, bg * 8:(bg + 1) * 8, :])], t_oh, writes=[t_oh])
            for h in range(32):
                kh, g = h // 8, h % 8
                par, i = g % 2, g // 2
                dst = self.biasT[:, par, kh, i, :]
                if bg == 0:
                    S.op("dve", lambda e, dst=dst: e.tensor_copy(out=dst, in_=mk[:]), reads=[t_mk], writes=[self.t_biasT])
                for b8 in range(8):
                    b = bg * 8 + b8
                    S.op("dve", lambda e, dst=dst, b=b, b8=b8, h=h: e.scalar_tensor_tensor(out=dst, in0=oh[:, b8, :], scalar=rb[:, b, h:h + 1], in1=dst,
                                                                                         op0=ALU.mult, op1=ALU.add),
                         reads=[t_oh, t_rb, self.t_biasT], writes=[self.t_biasT])
        if cache is not None and not have:
            S.dma("sp", [lambda e: e.dma_start(out=cache["ap"], in_=bflat)], self.t_biasT, reads=[self.t_biasT], writes=[cache["t"]])
            cache["valid"] = True
        sk, t_sk = self.load_small("sink_s", sinkrow_d, [1, 32])
        z128 = self.sb("z128", [1, 128], F32)
        t_z = T("z128")
        S.op("dve", lambda e: e.memset(z128[:], 0.0), writes=[t_z])
        for hh in range(32):
            S.op("act", lambda e, hh=hh: e.activation(out=self.esrow[0:1, hh * 128:(hh + 1) * 128], in_=z128[:], func=AF.Exp,
                                                     bias=sk[0:1, hh:hh + 1], scale=1.0),
                 reads=[t_z, t_sk], writes=[self.t_esrow])
    self.kbuf = self.sb("kbuf", [P, 4, 128 + TT], BF16)
    self.t_kbuf = [T("kbuf%d" % i) for i in range(4)]
    self.vdup = self.sb("vdup", [P, 5, 512], BF16)
    self.t_vdup = [T("vdup%d" % i) for i in range(5)]
    self.hh128 = self.hT
    self.t_hh128 = self.t_hT
    self.alloc_A(2 * KC)
    self.hhn128 = self.hn
    self.t_hhn128 = self.t_hn
    self.sc = [self.sb("sc%d" % i, [P, 512], F32) for i in range(2)]
    self.t_sc = [T("sc0"), T("sc1")]
    self.pS = [self.sb("pS%d" % i, [P, 512], BF16) for i in range(4)]
    self.t_pS = [T("pS%d" % i) for i in range(4)]
    self.rdn = [self.sb("rdn%d" % i, [P, 512], F32) for i in range(2)]
    self.t_rdn = [T("rdn0"), T("rdn1")]
    self.swa_it = 0


def _swa(self, h_in, h_out, hhalo_d, hmask, t_hmask, Wq, Wkd, Wvd, Wo, bq, t_bq, bkd, t_bkd, bvd, t_bvd, gain, t_gain, gcol, flag=None, hrd=()):
    S = self.S
    qoff, ooff = 0, KC
    self.halo_load(self.hh128, self.t_hh128, hhalo_d, 128, flag, hrd)
    self.rmsnorm(self.hh128, self.t_hh128, gain, t_gain, gcol, self.hhn128, self.t_hhn128, 128)

    def kv_proj(src, t_src, w, kcol0, vblk):
        for half in range(2):
            wk, t_wk = self.load_w(Wkd[:, half * 256:(half + 1) * 256].rearrange("(k p) m -> p k m", p=P), KC, 256)
            for m in range(2):
                kh = half * 2 + m
                ps, t_ps = self.psum()
                self.mm_group(ps[:, 0:w], t_ps, [(wk[:, k, m * P:(m + 1) * P], src[:, k, 0:w]) for k in range(KC)], reads=[t_wk] + t_src)
                S.op("act", lambda e, kh=kh, ps=ps: e.activation(out=self.kbuf[:, kh, kcol0:kcol0 + w], in_=ps[:, 0:w], func=AF.Identity,
                                                                bias=bkd[:, kh:kh + 1], scale=1.0),
                     reads=[t_ps, t_bkd], writes=[self.t_kbuf[kh]])
        wv = []
        for half in range(2):
            wv.append(self.load_w(Wvd[:, half * 256:(half + 1) * 256].rearrange("(k p) m -> p k m", p=P), KC, 256))
        for qb in range(w // 128):
            ps, t_ps = self.psum()
            for half in range(2):
                self.mm_group(ps[:, half * 256:(half + 1) * 256], t_ps,
                              [(src[:, k, qb * 128:(qb + 1) * 128], wv[half][0][:, k, :]) for k in range(KC)], reads=[wv[half][1]] + t_src)
            S.op("dve", lambda e, qb=qb, ps=ps: e.tensor_tensor(out=self.vdup[:, vblk + qb, :], in0=ps[:, 0:512], in1=bvd[:], op=ALU.add),
                 reads=[t_ps, t_bvd], writes=[self.t_vdup[vblk + qb]])

    kv_proj(self.hhn128, self.t_hhn128, 128, 0, 0)
    for it in range(NT):
        tok0 = it * TT
        self.load_h(h_in, tok0)
        self.rmsnorm(self.hT, self.t_hT, gain, t_gain, gcol, self.hn, self.t_hn, TT)
        if it > 0:
            for kh in range(4):
                S.op("act", lambda e, kh=kh: e.activation(out=self.kbuf[:, kh, 0:128], in_=self.kbuf[:, kh, TT:TT + 128], func=AF.Copy),
                     reads=[self.t_kbuf[kh]], writes=[self.t_kbuf[kh]])
            S.op("act", lambda e: e.activation(out=self.vdup[:, 0, :], in_=self.vdup[:, 4, :], func=AF.Copy),
                 reads=[self.t_vdup[4]], writes=[self.t_vdup[0]])
        kv_proj(self.hn, self.t_hn, TT, 128, 1)

        def q_evac(c, ps, t_ps):
            S.op("act", lambda e, c=c, ps=ps: e.activation(out=self.bufA[:, qoff + c, :], in_=ps[:, 0:TT], func=AF.Identity,
                                                          bias=bq[:, c:c + 1], scale=0.125),
                 reads=[t_ps, t_bq], writes=[self.t_A[qoff + c]])
        self.linear_to(Wq, 0, D, self.t_hn, self.hn, self.bufA, qoff, self.t_A, evac=q_evac)
        for qb in range(TT // 128):
            qs = slice(qb * 128, (qb + 1) * 128)
            for kh in range(4):
                for par in range(2):
                    pr = slice(par * 64, (par + 1) * 64)
                    t_q4 = self.t_A[qoff + 4 * kh:qoff + 4 * kh + 4]
                    pb = self.swa_it % 2
                    self.swa_it += 1
                    pS = self.pS[2 * pb:2 * pb + 2]
                    t_pS = self.t_pS[2 * pb:2 * pb + 2]
                    rdn, t_rdn = self.rdn[pb], self.t_rdn[pb]
                    for j in range(2):
                        ps, t_ps = self.psum()
                        kc0 = (qb + j) * 128
                        self.mm_group(ps[:, 0:512], t_ps,
                                      [(self.kbuf[pr, kh, kc0:kc0 + 128], self.bufA[pr, qoff + 4 * kh:qoff + 4 * kh + 4, qs])],
                                      reads=[self.t_kbuf[kh]] + t_q4)
                        bsl = self.biasT[:, par, kh, :, j * 128:(j + 1) * 128]
                        psv = ps[:, 0:512].rearrange("p (i q) -> p i q", i=4)
                        scv = self.sc[j][:].rearrange("p (i q) -> p i q", i=4)
                        if it == 0 and qb == 0 and j == 0:
                            S.op("dve", lambda e, psv=psv, scv=scv, bsl=bsl: e.scalar_tensor_tensor(out=scv, in0=psv, scalar=hmask[:, 0:1], in1=bsl,
                                                                                                  op0=ALU.add, op1=ALU.add),
                                 reads=[t_ps, self.t_biasT, t_hmask], writes=[self.t_sc[j]])
                        else:
                            S.op("dve", lambda e, psv=psv, scv=scv, bsl=bsl: e.tensor_tensor(out=scv, in0=psv, in1=bsl, op=ALU.add),
                                 reads=[t_ps, self.t_biasT], writes=[self.t_sc[j]])
                        S.op("act", lambda e, j=j, pS=pS: e.activation(out=pS[j][:], in_=self.sc[j][:], func=AF.Exp),
                             reads=[self.t_sc[j]], writes=[t_pS[j]])
                    psd, t_psd = self.psum()
                    e0 = (par * 16 + kh * 4) * 128
                    self.mm_group(psd[:, 0:512], t_psd,
                                  [(self.ones[:], pS[0][:]), (self.ones[:], pS[1][:]), (self.ones[0:1, :], self.esrow[0:1, e0:e0 + 512])],
                                  reads=t_pS + [self.t_ones, self.t_esrow])
                    S.op("dve", lambda e, psd=psd, rdn=rdn: e.reciprocal(out=rdn[:], in_=psd[:, 0:512]), reads=[t_psd], writes=[t_rdn])
                    pso, t_pso = self.psum()
                    self.mm_group(pso[:, 0:512], t_pso,
                                  [(self.vdup[:, qb + j, kh * 128:(kh + 1) * 128], pS[j][:]) for j in range(2)],
                                  reads=t_pS + [self.t_vdup[qb], self.t_vdup[qb + 1]])
                    ov = self.bufA[pr, ooff + 4 * kh:ooff + 4 * kh + 4, qs]
                    S.op("dve", lambda e, ov=ov, pso=pso, pr=pr, rdn=rdn: e.tensor_tensor(out=ov, in0=pso[pr, 0:512].rearrange("p (i q) -> p i q", i=4),
                                                                                         in1=rdn[pr, :].rearrange("p (i q) -> p i q", i=4), op=ALU.mult),
                         reads=[t_pso, t_rdn], writes=self.t_A[ooff + 4 * kh:ooff + 4 * kh + 4])
        self.linear_resid(Wo, KC, self.t_A[ooff:ooff + KC], self.bufA, ooff)
        self.store_h(h_out, tok0)


Prog.swa_setup = _swa_setup
Prog.swa = _swa


def swa_host_params(z, i):
    b = np.asarray(z["od_b_qkv"][i], np.float32)
    bq = pk(b[0:2048])
    bk = b[2048:2304].reshape(4, 64)
    bkd = np.ascontiguousarray(np.concatenate([bk, bk], axis=1).T)
    bv = b[2304:2560].reshape(4, 64)
    bvd = np.concatenate([bv, bv], axis=1).reshape(1, 512)
    bvd = np.ascontiguousarray(np.broadcast_to(bvd, (P, 512)))
    W = np.asarray(z["od_w_qkv"][i], np.float32)
    Wk = W[:, 2048:2304].reshape(D, 4, 64)
    Wkd = np.ascontiguousarray(np.concatenate([Wk, Wk], axis=2).reshape(D, 512))
    Wv = W[:, 2304:2560].reshape(D, 4, 64)
    Wvd = np.ascontiguousarray(np.concatenate([Wv, Wv], axis=2).reshape(D, 512))
    Wq = np.ascontiguousarray(W[:, 0:2048])
    sk = np.asarray(z["od_sinks"][i], np.float32)
    order = [kh * 8 + 2 * ii + par for par in range(2) for kh in range(4) for ii in range(4)]
    sinkrow = np.ascontiguousarray(sk[order].reshape(1, 32))
    relb = np.ascontiguousarray(np.broadcast_to(np.asarray(z["rel_bias"], np.float32)[None], (P, 32, 32)))
    return dict(bq=bq, bkd=bkd, bvd=bvd, Wq=Wq, Wkd=Wkd, Wvd=Wvd, sinkrow=sinkrow, relb=relb)


def build_swa_launch():
    nc = bass.Bass("TRN2", target_bir_lowering=False)
    dt = lambda name, shape, kind="ExternalInput": nc.dram_tensor(name, shape, F32, kind=kind).ap()
    h_in = dt("h_in", [D, NTOK]); h_out = dt("h_out", [D, NTOK], "ExternalOutput")
    hhalo = dt("hhalo", [D, 128]); gm = dt("gm", [P, KC]); hmask_d = dt("hmask", [P, 1])
    oh_d = dt("oh", [P, 32, 2, 128]); mask_d = dt("maskT", [P, 2, 128]); relb_d = dt("relb", [P, 32, 32]); sinkrow_d = dt("sinkrow", [1, 32])
    bq_d = dt("bq", [P, KC]); bkd_d = dt("bkd", [P, 4]); bvd_d = dt("bvd", [P, 512])
    Wq = dt("Wq", [D, D]); Wkd = dt("Wkd", [D, 512]); Wvd = dt("Wvd", [D, 512]); Wo = dt("Wo", [D, D])
    with ExitStack() as st:
        pr = Prog(nc, st)
        g, t_g = pr.load_small("gm_s", gm, [P, KC])
        hm, t_hm = pr.load_small("hmask_s", hmask_d, [P, 1])
        bq, t_bq = pr.load_small("bq_s", bq_d, [P, KC])
        pr.S.op("dve", lambda e: e.tensor_scalar(out=bq[:], in0=bq[:], scalar1=0.125, scalar2=None, op0=ALU.mult), reads=[t_bq], writes=[t_bq])
        bkd, t_bkd = pr.load_small("bkd_s", bkd_d, [P, 4])
        bvd, t_bvd = pr.load_small("bvd_s", bvd_d, [P, 512])
        with pr.phase():
            pr.swa_setup(oh_d, mask_d, relb_d, sinkrow_d)
            pr.swa(h_in, h_out, hhalo, hm, t_hm, Wq, Wkd, Wvd, Wo, bq, t_bq, bkd, t_bkd, bvd, t_bvd, g, t_g, 0)
        pr.finish(pr.t_hT)
    return nc


class Ew:
    def __init__(self, prog, shape, tag):
        self.pr = prog
        self.shape = shape
        self.tag = tag
        self.n = 0

    def new(self, dt=F32, shape=None):
        self.n += 1
        t = self.pr.sb("%s%d" % (self.tag, self.n), shape or self.shape, dt)
        return (t[:], T("%s%d" % (self.tag, self.n)))

    def tt(self, o, a, b, op):
        self.pr.S.op("dve", lambda e: e.tensor_tensor(out=o[0], in0=a[0], in1=b[0], op=op), reads=[a[1], b[1]], writes=[o[1]])
        return o

    def ts(self, o, a, s1, op0, s2=None, op1=None):
        if op1 is None:
            self.pr.S.op("dve", lambda e: e.tensor_scalar(out=o[0], in0=a[0], scalar1=s1, scalar2=None, op0=op0), reads=[a[1]], writes=[o[1]])
        else:
            self.pr.S.op("dve", lambda e: e.tensor_scalar(out=o[0], in0=a[0], scalar1=s1, scalar2=s2, op0=op0, op1=op1), reads=[a[1]], writes=[o[1]])
        return o

    def act(self, o, a, func, scale=1.0, bias=None, breads=()):
        if bias is None:
            self.pr.S.op("act", lambda e: e.activation(out=o[0], in_=a[0], func=func, scale=scale), reads=[a[1]], writes=[o[1]])
        else:
            self.pr.S.op("act", lambda e: e.activation(out=o[0], in_=a[0], func=func, scale=scale, bias=bias), reads=[a[1]] + list(breads), writes=[o[1]])
        return o

    def cmul(self, orr, oi, ar, ai, br, bi, t1, t2):
        self.tt(t1, ar, br, ALU.mult)
        self.tt(t2, ai, bi, ALU.mult)
        self.tt(orr, t1, t2, ALU.subtract)
        self.tt(t1, ar, bi, ALU.mult)
        self.tt(t2, ai, br, ALU.mult)
        self.tt(oi, t1, t2, ALU.add)

    def abar(self, are, aim, ldt, halfpi):
        n = self.new
        dt_ = self.act(n(), ldt, AF.Exp)
        x1 = self.tt(n(), are, dt_, ALU.mult)
        th = self.tt(n(), aim, dt_, ALU.mult)
        rho = self.act(n(), x1, AF.Exp, scale=1.0 / 32)
        sn = self.act(n(), th, AF.Sin, scale=1.0 / 32)
        cs = self.act(n(), th, AF.Sin, scale=1.0 / 32, bias=halfpi[0], breads=[halfpi[1]])
        er = self.tt(n(), rho, cs, ALU.mult)
        ei = self.tt(n(), rho, sn, ALU.mult)
        t1, t2, t3 = n(), n(), n()
        for _ in range(5):
            self.tt(t1, er, er, ALU.mult)
            self.tt(t2, ei, ei, ALU.mult)
            self.tt(t3, er, ei, ALU.mult)
            self.tt(er, t1, t2, ALU.subtract)
            self.ts(ei, t3, 2.0, ALU.mult)
        nr = self.ts(n(), er, -1.0, ALU.add)
        self.tt(t1, are, are, ALU.mult)
        self.tt(t2, aim, aim, ALU.mult)
        self.tt(t3, t1, t2, ALU.add)
        rd = n()
        self.pr.S.op("dve", lambda e: e.reciprocal(out=rd[0], in_=t3[0]), reads=[t3[1]], writes=[rd[1]])
        qr, qi = n(), n()
        self.tt(t1, nr, are, ALU.mult)
        self.tt(t2, ei, aim, ALU.mult)
        self.tt(t3, t1, t2, ALU.add)
        self.tt(qr, t3, rd, ALU.mult)
        self.tt(t1, ei, are, ALU.mult)
        self.tt(t2, nr, aim, ALU.mult)
        self.tt(t3, t1, t2, ALU.subtract)
        self.tt(qi, t3, rd, ALU.mult)
        return er, ei, qr, qi


def _s5_precompute(self, prm, dr):
    S = self.S
    self.A8r = self.sb("A8r", [P, 32], F32)
    self.A8i = self.sb("A8i", [P, 32], F32)
    self.t_A8 = T("A8")

    with self.phase():
        hp_t = self.sb("halfpi", [P, 1], F32)
        t_hp = T("halfpi")
        S.op("dve", lambda e: e.memset(hp_t[:], math.pi / 2), writes=[t_hp])
        halfpi = (hp_t[:], t_hp)
        ld = lambda name, shape: (lambda r: (r[0][:], r[1]))(self.load_small(name + "_s", prm[name], shape))
        es = Ew(self, [P, 32], "es")
        are, aim, ldt = ld("are_S", [P, 32]), ld("aim_S", [P, 32]), ld("ldt_S", [P, 32])
        ar, ai, qr, qi = es.abar(are, aim, ldt, halfpi)
        bre, bim = ld("bre_S", [P, 32, 16]), ld("bim_S", [P, 32, 16])
        cre, cim = ld("cre_S", [P, 32, 16]), ld("cim_S", [P, 32, 16])
        e3 = Ew(self, [P, 32, 16], "e3")
        bc = lambda v: (v[0].unsqueeze(2).to_broadcast([P, 32, 16]), v[1])
        Br, Bi, u1, u2 = e3.new(), e3.new(), e3.new(), e3.new()
        e3.cmul(Br, Bi, bc(qr), bc(qi), bre, bim, u1, u2)
        pw = [(es.new(), es.new()) for _ in range(9)]
        S.op("dve", lambda e: e.memset(pw[0][0][0], 1.0), writes=[pw[0][0][1]])
        S.op("dve", lambda e: e.memset(pw[0][1][0], 0.0), writes=[pw[0][1][1]])
        s1, s2 = es.new(), es.new()
        for k in range(8):
            es.cmul(pw[k + 1][0], pw[k + 1][1], pw[k][0], pw[k][1], ar, ai, s1, s2)
        S.op("dve", lambda e: e.tensor_copy(out=self.A8r[:], in_=pw[8][0][0]), reads=[pw[8][0][1]], writes=[self.t_A8])
        S.op("dve", lambda e: e.tensor_copy(out=self.A8i[:], in_=pw[8][1][0]), reads=[pw[8][1][1]], writes=[self.t_A8])
        def padded(name, dt):
            t = self.sb(name, [P, 32, 32], dt)
            tt = T(name)
            S.op("dve", lambda e: e.memset(t[:], 0.0), writes=[tt])
            return t, tt

        def to_pad(dst, t_dst, src):
            for m in range(2):
                S.op("dve", lambda e, m=m: e.tensor_copy(out=dst[m * 64:(m + 1) * 64, :, m * 16:(m + 1) * 16], in_=src[0][m * 64:(m + 1) * 64, :, :]),
                     reads=[src[1]], writes=[t_dst])
        Bpr, t_Bpr = padded("Bpr", BF16)
        Bpn, t_Bpn = padded("Bpn", BF16)
        to_pad(Bpr, t_Bpr, Br)
        nBi = e3.ts(e3.new(), Bi, -1.0, ALU.mult)
        to_pad(Bpn, t_Bpn, nBi)
        Cpr, t_Cpr = padded("Cpr", BF16)
        Cpi, t_Cpi = padded("Cpi", BF16)
        ident, t_ident = self.load_small("ident_s", prm["ident"], [P, P])
        dcol, t_dcol = self.load_small("dcol_s", prm["dcol"], [P, 8])
        Ck_r, Ck_i = e3.new(), e3.new()
        stg = self.sb("wc_stg", [P, 8, 3072], BF16)
        t_stg = T("wc_stg")
        S.op("dve", lambda e: e.memset(stg[:], 0.0), writes=[t_stg])
        f32blk = self.sb("f32blk", [P, P], F32)
        t_f32blk = T("f32blk")
        for k in range(9):
            e3.cmul(Ck_r, Ck_i, cre, cim, bc(pw[k][0]), bc(pw[k][1]), u1, u2)
            if k < 8:
                to_pad(Cpr, t_Cpr, Ck_r)
                to_pad(Cpi, t_Cpi, Ck_i)
                for c8 in range(8):
                    ps, t_ps = self.psum()
                    for q in range(4):
                        pair = c8 * 4 + q
                        self.mm_group(ps[q * 32:(q + 1) * 32, q * 32:(q + 1) * 32], t_ps,
                                      [(Bpr[:, pair, :], Cpr[:, pair, :]), (Bpn[:, pair, :], Cpi[:, pair, :])],
                                      reads=[t_Bpr, t_Bpn, t_Cpr, t_Cpi], tp=(0, q * 32))
                    dstv = stg[:, c8, k * 128:(k + 1) * 128]
                    for q in range(4):
                        sl = slice(q * 32, (q + 1) * 32)
                        if k == 0:
                            S.op("dve", lambda e, sl=sl, c8=c8, ps=ps, dstv=dstv: e.scalar_tensor_tensor(
                                out=dstv[sl, sl], in0=ident[sl, sl], scalar=dcol[sl, c8:c8 + 1], in1=ps[sl, sl], op0=ALU.mult, op1=ALU.add),
                                reads=[t_ps, t_ident, t_dcol], writes=[t_stg])
                        else:
                            S.op("act", lambda e, sl=sl, ps=ps, dstv=dstv: e.activation(out=dstv[sl, sl], in_=ps[sl, sl], func=AF.Copy),
                                 reads=[t_ps], writes=[t_stg])
            if k >= 1:
                r = k - 1
                wov = stg[:, :, 1024:3072].rearrange("p c (q r i o) -> p c q r i o", q=4, r=8, i=2)
                for m in range(2):
                    ms = slice(m * 64, (m + 1) * 64)
                    S.op("dve", lambda e, ms=ms, m=m, r=r: e.tensor_copy(
                        out=wov[ms, :, :, r, 0, m * 16:(m + 1) * 16], in_=Ck_r[0][ms, :, :].rearrange("p (c q) h -> p c q h", q=4)),
                        reads=[Ck_r[1]], writes=[t_stg])
                    S.op("dve", lambda e, ms=ms, m=m, r=r: e.tensor_scalar(
                        out=wov[ms, :, :, r, 1, m * 16:(m + 1) * 16], in0=Ck_i[0][ms, :, :].rearrange("p (c q) h -> p c q h", q=4),
                        scalar1=-1.0, scalar2=None, op0=ALU.mult),
                        reads=[Ck_i[1]], writes=[t_stg])
        S.dma("sp", [lambda e: e.dma_start(out=dr["wc"], in_=stg[:])], t_stg, reads=[t_stg], writes=[dr["t_wc"]])
    with self.phase():
        hp_t = self.sb("halfpi2", [P, 1], F32)
        t_hp = T("halfpi2")
        S.op("dve", lambda e: e.memset(hp_t[:], math.pi / 2), writes=[t_hp])
        halfpi = (hp_t[:], t_hp)
        ld = lambda name, shape: (lambda r: (r[0][:], r[1]))(self.load_small(name + "_s", prm[name], shape))
        et = Ew(self, [P, 8, 64], "et")
        are, aim, ldt = ld("are_T", [P, 8, 64]), ld("aim_T", [P, 8, 64]), ld("ldt_T", [P, 8, 64])
        ar, ai, qr, qi = et.abar(are, aim, ldt, halfpi)
        bre, bim = ld("bre_T", [P, 8, 64]), ld("bim_T", [P, 8, 64])
        QBr, QBi, u1, u2 = et.new(), et.new(), et.new(), et.new()
        et.cmul(QBr, QBi, qr, qi, bre, bim, u1, u2)
        m01, t_m01 = self.load_small("mask01_s", prm["mask01"], [P, 2])
        wzs = self.sb("wz_stg", [P, 8, 8, 2, 128], BF16)
        t_wzs = T("wz_stg")
        pr_, pi_ = et.new(), et.new()
        S.op("dve", lambda e: e.memset(pr_[0], 1.0), writes=[pr_[1]])
        S.op("dve", lambda e: e.memset(pi_[0], 0.0), writes=[pi_[1]])
        Wr, Wi, nr_, ni_ = et.new(), et.new(), et.new(), et.new()
        for k in range(8):
            s = 7 - k
            et.cmul(Wr, Wi, pr_, pi_, QBr, QBi, u1, u2)
            for ri, W in ((0, Wr), (1, Wi)):
                for m in range(2):
                    S.op("dve", lambda e, s=s, ri=ri, m=m, W=W: e.tensor_scalar(out=wzs[:, :, s, ri, m * 64:(m + 1) * 64], in0=W[0],
                                                                                 scalar1=m01[:, m:m + 1], scalar2=None, op0=ALU.mult),
                         reads=[W[1], t_m01], writes=[t_wzs])
            if k < 7:
                et.cmul(nr_, ni_, pr_, pi_, ar, ai, u1, u2)
                S.op("dve", lambda e: e.tensor_copy(out=pr_[0], in_=nr_[0]), reads=[nr_[1]], writes=[pr_[1]])
                S.op("dve", lambda e: e.tensor_copy(out=pi_[0], in_=ni_[0]), reads=[ni_[1]], writes=[pi_[1]])
        S.dma("sp", [lambda e: e.dma_start(out=dr["wz"], in_=wzs[:].rearrange("p c s i o -> p c (s i o)"))], t_wzs, reads=[t_wzs], writes=[dr["t_wz"]])


Prog.s5_precompute = _s5_precompute


def ev_host_params(z, i):
    f = lambda k: np.asarray(z[k][i], np.float32)
    a_re, a_im, ldt = f("s5_a_re"), f("s5_a_im"), f("s5_log_dt")
    b_re, b_im = f("s5_b_re"), f("s5_b_im")
    c_re, c_im = f("s5_c_re"), f("s5_c_im")
    def S2(v):
        return np.ascontiguousarray(v.reshape(32, 2, 64).transpose(1, 2, 0).reshape(128, 32))
    def S3(v):
        return np.ascontiguousarray(v.reshape(32, 2, 64, 16).transpose(1, 2, 0, 3).reshape(128, 32, 16))
    def T2(v):
        w = v.reshape(8, 4, 2, 1, 64)
        w = np.broadcast_to(w, (8, 4, 2, 16, 64)).transpose(1, 2, 3, 0, 4).reshape(128, 8, 64)
        return np.ascontiguousarray(w)
    def T3(v):
        w = v.reshape(8, 4, 2, 64, 16).transpose(1, 2, 4, 0, 3).reshape(128, 8, 64)
        return np.ascontiguousarray(w)
    ldt2 = np.broadcast_to(ldt[:, None], (64, 64))
    out = dict(are_S=S2(a_re), aim_S=S2(a_im), ldt_S=S2(ldt2), bre_S=S3(b_re), bim_S=S3(b_im),
               cre_S=S3(c_re.transpose(0, 2, 1)), cim_S=S3(c_im.transpose(0, 2, 1)),
               are_T=T2(a_re), aim_T=T2(a_im), ldt_T=T2(ldt2), bre_T=T3(b_re), bim_T=T3(b_im))
    mask01 = np.zeros((128, 2), np.float32)
    for p_ in range(128):
        mask01[p_, (p_ // 16) % 2] = 1.0
    out["mask01"] = mask01
    out["ident"] = np.eye(128, dtype=np.float32)
    out["dcol"] = pk(f("s5_d"))
    glu = f("s5_glu_w")
    gb = np.zeros((128, 8, 128), np.float32)
    for g in range(64):
        c8, gg = g // 8, g % 8
        gb[gg * 16:(gg + 1) * 16, c8, gg * 16:(gg + 1) * 16] = glu[g]
    out["gblk"] = gb
    out["convw"] = pk(f("ev_conv_w"))
    return out


def _even_mixer(self, h_in, h_out, hhalo_d, Win, Wout, prm, dr, gain, t_gain, gcol, flag, hrd, xch):
    S = self.S
    self.s5_precompute(prm, dr)
    self.alloc_A(24)
    cw, t_cw = self.load_small("convw_s", prm["convw"], [P, 3, 8])
    gblk = self.sb("gblk", [P, 8, P], BF16)
    t_gblk = T("gblk")
    S.dma("pool", [lambda e: e.dma_start(out=gblk[:], in_=prm["gblk"])], t_gblk, writes=[t_gblk])
    Xb = [self.sb("Xb%d" % i, [P, 32, 65], F32) for i in range(2)]
    t_Xb = [T("Xb0"), T("Xb1")]
    X16 = [self.sb("X16_%d" % i, [P, 32, 64], BF16) for i in range(2)]
    t_X16 = [T("X16_0"), T("X16_1")]
    X0 = [self.sb("X0_%d" % i, [P, 32, NT], F32) for i in range(2)]
    t_X0 = [T("X0_0"), T("X0_1")]
    sstg = [self.sb("sstg%d" % i, [P, 32], F32) for i in range(2)]
    t_sstg = [T("sstg0"), T("sstg1")]
    st = [self.sb("st%d" % i, [P, 32], F32) for i in range(4)]
    t_st = [T("st%d" % i) for i in range(4)]
    prod = [self.sb("prod%d" % i, [P, TT + 2], F32) for i in range(2)]
    t_prod = [T("prod0"), T("prod1")]
    phalo = self.sb("phalo", [P, 8, 2], F32)
    t_phalo = [T("phalo%d" % c) for c in range(8)]
    gcs = [self.sb("gcs%d" % i, [P, TT], F32) for i in range(2)]
    t_gcs = [T("gcs0"), T("gcs1")]
    cv = [self.sb("cv%d" % i, [P, TT], F32) for i in range(2)]
    t_cv = [T("cv0"), T("cv1")]
    hhT = self.sb("ehhT", [P, KC, 2], F32)
    t_hhT = [T("ehhT%d" % k) for k in range(KC)]
    hhn = self.sb("ehhn", [P, KC, 2], BF16)
    t_hhn = [T("ehhn%d" % k) for k in range(KC)]
    yx, t_yx = gcs, t_gcs
    yg, t_yg = cv, t_cv
    yb = [self.sb("yb%d" % i, [P, TT], BF16) for i in range(2)]
    t_yb = [T("yb0"), T("yb1")]
    NJ = TT // 8
    UO = 16
    tt_ = lambda o, t_o, a, b_, op, rd: S.op("dve", lambda e: e.tensor_tensor(out=o, in0=a, in1=b_, op=op), reads=rd, writes=[t_o])

    def zpart(it):
        for c8 in range(8):
            wz, t_wz = self.load_wz(dr, c8)
            uv = self.bufA[:, UO + c8, :].rearrange("p (j s) -> p j s", s=8)
            for q in range(4):
                pair = c8 * 4 + q
                ps, t_ps = self.psum()
                for ri in range(2):
                    n = 8
                    for s in range(8):
                        self.S.op("pe", lambda e, ps=ps, q=q, s=s, ri=ri, wz=wz, uv=uv: e.matmul(
                            ps[:, ri * NJ:(ri + 1) * NJ], wz[q * 32:(q + 1) * 32, s, ri, :], uv[q * 32:(q + 1) * 32, :, s],
                            start=(s == 0), stop=(s == n - 1), tile_position=(q * 32, 0)),
                            reads=[t_wz, self.t_A[UO + c8]], writes=[t_ps], inc=(s == n - 1))
                    S.op("act", lambda e, ps=ps, ri=ri, pair=pair: e.activation(out=Xb[ri][:, pair, 1:NJ + 1], in_=ps[:, ri * NJ:(ri + 1) * NJ], func=AF.Copy),
                         reads=[t_ps], writes=[t_Xb[ri]])
    def scan_steps(j0, j1):
        for j in range(j0, j1):
            xr, xi = Xb[0][:, :, j], Xb[1][:, :, j]
            nr_, ni_ = Xb[0][:, :, j + 1], Xb[1][:, :, j + 1]
            tt_ = lambda o, t_o, a, b_, op, rd: S.op("dve", lambda e: e.tensor_tensor(out=o, in0=a, in1=b_, op=op), reads=rd, writes=[t_o])
            tt_(st[0][:], t_st[0], self.A8r[:], xr, ALU.mult, [self.t_A8, t_Xb[0]])
            tt_(st[1][:], t_st[1], self.A8i[:], xi, ALU.mult, [self.t_A8, t_Xb[1]])
            tt_(st[2][:], t_st[2], self.A8i[:], xr, ALU.mult, [self.t_A8, t_Xb[0]])
            tt_(st[3][:], t_st[3], self.A8r[:], xi, ALU.mult, [self.t_A8, t_Xb[1]])
            tt_(st[0][:], t_st[0], st[0][:], st[1][:], ALU.subtract, [t_st[0], t_st[1]])
            tt_(st[2][:], t_st[2], st[2][:], st[3][:], ALU.add, [t_st[2], t_st[3]])
            tt_(nr_, t_Xb[0], nr_, st[0][:], ALU.add, [t_Xb[0], t_st[0]])
            tt_(ni_, t_Xb[1], ni_, st[2][:], ALU.add, [t_Xb[1], t_st[2]])
    for ri in range(2):
        S.op("dve", lambda e, ri=ri: e.memset(Xb[ri][:, :, 0], 0.0), writes=[t_Xb[ri]])
    for it in range(NT):
        self.load_h(h_in, it * TT)
        self.rmsnorm(self.hT, self.t_hT, gain, t_gain, gcol, self.hn, self.t_hn, TT)
        self.linear_to(Win, 3072, 1024, self.t_hn, self.hn, self.bufA, UO, self.t_A)
        for ri in range(2):
            S.op("act", lambda e, ri=ri, it=it: e.activation(out=X0[ri][:, :, it], in_=Xb[ri][:, :, 0], func=AF.Copy), reads=[t_Xb[ri]], writes=[t_X0[ri]])
        zpart(it)
        scan_steps(0, NJ)
        for ri in range(2):
            if it < NT - 1:
                S.op("act", lambda e, ri=ri: e.activation(out=Xb[ri][:, :, 0], in_=Xb[ri][:, :, NJ], func=AF.Copy), reads=[t_Xb[ri]], writes=[t_Xb[ri]])
            else:
                S.op("act", lambda e, ri=ri: e.activation(out=sstg[ri][:], in_=Xb[ri][:, :, NJ], func=AF.Copy), reads=[t_Xb[ri]], writes=[t_sstg[ri]])
    for ri in range(2):
        S.dma("sp", [lambda e, ri=ri: e.dma_start(out=xch["snd_s"][ri * P:(ri + 1) * P, :], in_=sstg[ri][:])], t_sstg[ri], reads=[t_sstg[ri]], writes=[xch["t_snd_s"]])
    S.cc("pool", lambda e: e.collective_compute("AllGather", ALU.bypass, replica_groups=xch["rg"], ins=[xch["snd_s"]], outs=[xch["rcv_s"]]),
         reads=[xch["t_snd_s"]], writes=[xch["t_rcv_s"]])
    Dr = [self.sb("Dst%d" % i, [P, 32], F32) for i in range(2)]
    t_Dr = [T("Dst0"), T("Dst1")]
    for ri in range(2):
        S.dma("sp", [lambda e, ri=ri: e.dma_start(out=Dr[ri][:], in_=xch["rcv_s"][ri * P:(ri + 1) * P, :])], t_Dr[ri], reads=[xch["t_rcv_s"]], writes=[t_Dr[ri]])
        S.op("dve", lambda e, ri=ri: e.tensor_scalar(out=Dr[ri][:], in0=Dr[ri][:], scalar1=flag[0][:, 0:1], scalar2=None, op0=ALU.mult),
             reads=[t_Dr[ri], flag[1]], writes=[t_Dr[ri]])
    self.dbg("m_A8r", self.A8r[:], [self.t_A8], [P, 32])
    self.dbg("m_Dr0", Dr[0][:], [t_Dr[0]], [P, 32])
    self.dbg("m_X0", X0[0][:].rearrange("p a b -> p (a b)"), [t_X0[0]], [P, 32 * NT])
    es = Ew(self, [P, 32], "e5")
    Ar, Ai = es.new(), es.new()
    S.op("dve", lambda e: e.tensor_copy(out=Ar[0], in_=self.A8r[:]), reads=[self.t_A8], writes=[Ar[1]])
    S.op("dve", lambda e: e.tensor_copy(out=Ai[0], in_=self.A8i[:]), reads=[self.t_A8], writes=[Ai[1]])
    q1, q2, q3 = es.new(), es.new(), es.new()
    for _ in range(6):
        es.tt(q1, Ar, Ar, ALU.mult)
        es.tt(q2, Ai, Ai, ALU.mult)
        es.tt(q3, Ar, Ai, ALU.mult)
        es.tt(Ar, q1, q2, ALU.subtract)
        es.ts(Ai, q3, 2.0, ALU.mult)
    D0, D1 = (Dr[0][:], t_Dr[0]), (Dr[1][:], t_Dr[1])
    n0, n1 = es.new(), es.new()
    for it in range(NT):
        for ri, Dv in ((0, D0), (1, D1)):
            S.op("dve", lambda e, ri=ri, it=it, Dv=Dv: e.tensor_tensor(out=X0[ri][:, :, it], in0=X0[ri][:, :, it], in1=Dv[0], op=ALU.add),
                 reads=[t_X0[ri], Dv[1]], writes=[t_X0[ri]])
        if it < NT - 1:
            es.cmul(n0, n1, D0, D1, Ar, Ai, q1, q2)
            S.op("dve", lambda e: e.tensor_copy(out=D0[0], in_=n0[0]), reads=[n0[1]], writes=[D0[1]])
            S.op("dve", lambda e: e.tensor_copy(out=D1[0], in_=n1[0]), reads=[n1[1]], writes=[D1[1]])
    self.halo_load(hhT, t_hhT, hhalo_d, 2, flag, hrd)
    self.rmsnorm(hhT, t_hhT, gain, t_gain, gcol, hhn, t_hhn, 2)
    for it in range(NT):
        tok0 = it * TT
        self.load_h(h_in, tok0)
        self.rmsnorm(self.hT, self.t_hT, gain, t_gain, gcol, self.hn, self.t_hn, TT)
        self.linear_to(Win, 3072, 1024, self.t_hn, self.hn, self.bufA, UO, self.t_A)
        for ri in range(2):
            S.op("act", lambda e, ri=ri, it=it: e.activation(out=Xb[ri][:, :, 0], in_=X0[ri][:, :, it], func=AF.Copy), reads=[t_X0[ri]], writes=[t_Xb[ri]])
        zpart(it)
        for blk in range(4):
            wblk = lambda col0: self.load_w(Win[:, col0 + blk * 256:col0 + (blk + 1) * 256].rearrange("(k p) m -> p k m", p=P), KC, 256)
            wgb, t_wgb = wblk(0)
            wgc, t_wgc = wblk(1024)
            wxa, t_wxa = wblk(2048)
            for m in range(2):
                c = blk * 2 + m
                b = c % 2
                ms = slice(m * P, (m + 1) * P)
                pd, t_pd = prod[b], t_prod[b]
                if it == 0:
                    ph1, t_ph1 = self.psum()
                    self.mm_group(ph1[:, 0:2], t_ph1, [(wgc[:, k, ms], hhn[:, k, :]) for k in range(KC)], reads=[t_wgc] + t_hhn)
                    ph2, t_ph2 = self.psum()
                    self.mm_group(ph2[:, 0:2], t_ph2, [(wxa[:, k, ms], hhn[:, k, :]) for k in range(KC)], reads=[t_wxa] + t_hhn)
                    S.op("act", lambda e, ph1=ph1, b=b: e.activation(out=gcs[b][:, 0:2], in_=ph1[:, 0:2], func=AF.Copy), reads=[t_ph1], writes=[t_gcs[b]])
                    S.op("dve", lambda e, ph2=ph2, b=b, pd=pd: e.tensor_tensor(out=pd[:, 0:2], in0=gcs[b][:, 0:2], in1=ph2[:, 0:2], op=ALU.mult),
                         reads=[t_ph2, t_gcs[b]], writes=[t_pd])
                else:
                    S.op("act", lambda e, c=c, pd=pd: e.activation(out=pd[:, 0:2], in_=phalo[:, c, :], func=AF.Copy), reads=[t_phalo[c]], writes=[t_pd])
                pgc, t_pgc = self.psum()
                self.mm_group(pgc[:, 0:TT], t_pgc, [(wgc[:, k, ms], self.hn[:, k, :]) for k in range(KC)], reads=[t_wgc] + self.t_hn)
                pxa, t_pxa = self.psum()
                self.mm_group(pxa[:, 0:TT], t_pxa, [(wxa[:, k, ms], self.hn[:, k, :]) for k in range(KC)], reads=[t_wxa] + self.t_hn)
                pgb, t_pgb = self.psum()
                self.mm_group(pgb[:, 0:TT], t_pgb, [(wgb[:, k, ms], self.hn[:, k, :]) for k in range(KC)], reads=[t_wgb] + self.t_hn)
                S.op("act", lambda e, pgc=pgc, b=b: e.activation(out=gcs[b][:], in_=pgc[:, 0:TT], func=AF.Copy), reads=[t_pgc], writes=[t_gcs[b]])
                S.op("dve", lambda e, pxa=pxa, b=b, pd=pd: e.tensor_tensor(out=pd[:, 2:TT + 2], in0=gcs[b][:], in1=pxa[:, 0:TT], op=ALU.mult),
                     reads=[t_pxa, t_gcs[b]], writes=[t_pd])
                if it < NT - 1:
                    S.op("act", lambda e, c=c, pd=pd: e.activation(out=phalo[:, c, :], in_=pd[:, TT:TT + 2], func=AF.Copy), reads=[t_pd], writes=[t_phalo[c]])
                S.op("act", lambda e, c=c, b=b, pd=pd: e.activation(out=cv[b][:], in_=pd[:, 2:TT + 2], func=AF.Copy, scale=cw[:, 2, c:c + 1]),
                     reads=[t_pd, t_cw], writes=[t_cv[b]])
                S.op("dve", lambda e, c=c, b=b, pd=pd: e.scalar_tensor_tensor(out=cv[b][:], in0=pd[:, 1:TT + 1], scalar=cw[:, 1, c:c + 1], in1=cv[b][:],
                                                                           op0=ALU.mult, op1=ALU.add), reads=[t_pd, t_cw, t_cv[b]], writes=[t_cv[b]])
                S.op("dve", lambda e, c=c, b=b, pd=pd: e.scalar_tensor_tensor(out=cv[b][:], in0=pd[:, 0:TT], scalar=cw[:, 0, c:c + 1], in1=cv[b][:],
                                                                           op0=ALU.mult, op1=ALU.add), reads=[t_pd, t_cw, t_cv[b]], writes=[t_cv[b]])
                S.op("dve", lambda e, c=c, b=b, pgb=pgb: e.tensor_tensor(out=self.bufA[:, c, :], in0=cv[b][:], in1=pgb[:, 0:TT], op=ALU.mult),
                     reads=[t_cv[b], t_pgb], writes=[self.t_A[c]])
                scan_steps(c * (NJ // 8), (c + 1) * (NJ // 8))
        for ri in range(2):
            S.op("act", lambda e, ri=ri: e.activation(out=X16[ri][:], in_=Xb[ri][:, :, 0:NJ], func=AF.Copy), reads=[t_Xb[ri]], writes=[t_X16[ri]])
        for c8 in range(8):
            wc, t_wc = self.load_wc(dr, c8)
            fir = wc[:, 0:1024].rearrange("p (k o) -> p k o", k=8)
            wo = wc[:, 1024:3072].rearrange("p (q r i o) -> p q r i o", q=4, r=8, i=2)
            ut = self.bufA[:, UO + c8, :]
            uv = ut.rearrange("p (j s) -> p j s", s=8)
            py, t_py = self.psum()
            pyv = py[:, 0:TT].rearrange("p (j s) -> p j s", s=8)
            rd = [t_wc, self.t_A[UO + c8]] + t_X16
            S.op("pe", lambda e, py=py, fir=fir, ut=ut: e.matmul(py[:, 0:TT], fir[:, 0, :], ut, start=True, stop=False), reads=rd, writes=[t_py], inc=False)
            for k in range(1, 8):
                S.op("pe", lambda e, pyv=pyv, fir=fir, uv=uv, k=k: e.matmul(pyv[:, :, k:8], fir[:, k, :], uv[:, :, 0:8 - k], start=False, stop=False),
                     reads=rd, writes=[t_py], inc=False)
            for q in range(4):
                for r in range(8):
                    for ri in range(2):
                        last = (q == 3 and r == 7 and ri == 1)
                        S.op("pe", lambda e, pyv=pyv, wo=wo, q=q, r=r, ri=ri, c8=c8, last=last: e.matmul(
                            pyv[q * 32:(q + 1) * 32, :, r], wo[:, q, r, ri, :], X16[ri][:, c8 * 4 + q, :], start=False, stop=last,
                            tile_position=(0, q * 32)), reads=rd, writes=[t_py], inc=last)
            b = c8 % 2
            S.op("act", lambda e, py=py, b=b: e.activation(out=yx[b][:], in_=py[:, 0:TT], func=AF.Square), reads=[t_py], writes=[t_yx[b]])
            S.op("dve", lambda e, b=b: e.tensor_scalar(out=yx[b][:], in0=yx[b][:], scalar1=0.044715, scalar2=1.0, op0=ALU.mult, op1=ALU.add),
                 reads=[t_yx[b]], writes=[t_yx[b]])
            S.op("dve", lambda e, py=py, b=b: e.tensor_tensor(out=yx[b][:], in0=yx[b][:], in1=py[:, 0:TT], op=ALU.mult), reads=[t_yx[b], t_py], writes=[t_yx[b]])
            S.op("act", lambda e, b=b: e.activation(out=yx[b][:], in_=yx[b][:], func=AF.Sigmoid, scale=1.5957691216057308), reads=[t_yx[b]], writes=[t_yx[b]])
            S.op("dve", lambda e, py=py, b=b: e.tensor_tensor(out=yg[b][:], in0=yx[b][:], in1=py[:, 0:TT], op=ALU.mult), reads=[t_yx[b], t_py], writes=[t_yg[b]])
            S.op("act", lambda e, b=b: e.activation(out=yb[b][:], in_=yg[b][:], func=AF.Copy), reads=[t_yg[b]], writes=[t_yb[b]])
            pg, t_pg = self.psum()
            self.mm_group(pg[:, 0:TT], t_pg, [(gblk[:, c8, :], yb[b][:])], reads=[t_gblk, t_yb[b]])
            S.op("act", lambda e, pg=pg, b=b: e.activation(out=yx[b][:], in_=pg[:, 0:TT], func=AF.Sigmoid), reads=[t_pg], writes=[t_yx[b]])
            S.op("dve", lambda e, b=b, c8=c8: e.tensor_tensor(out=self.bufA[:, 8 + c8, :], in0=yg[b][:], in1=yx[b][:], op=ALU.mult),
                 reads=[t_yg[b], t_yx[b]], writes=[self.t_A[8 + c8]])
        if it == 0:
            self.dbg("m_ya0", self.bufA[:, 0, :], [self.t_A[0]], [P, TT], BF16)
            self.dbg("m_ys0", self.bufA[:, 8, :], [self.t_A[8]], [P, TT], BF16)
            self.dbg("m_u0", self.bufA[:, 16, :], [self.t_A[16]], [P, TT], BF16)
            self.dbg("m_X16", X16[0][:].rearrange("p a b -> p (a b)"), [t_X16[0]], [P, 32 * 64], BF16)
        self.linear_resid(Wout, KC, self.t_A[0:KC], self.bufA, 0)
        if it == 0:
            self.dbg("m_hT0", self.hT[:, 0, :], [self.t_hT[0]], [P, TT])
        self.store_h(h_out, tok0)


def _load_wz(self, dr, c8):
    i = self.wi
    self.wi = (i + 1) % NWB
    view = self.wb[i][:, 0:2048].rearrange("p (s i o) -> p s i o", s=8, i=2)
    t = self.t_wb[i]
    self.S.dma("sp", [lambda e: e.dma_start(out=self.wb[i][:, 0:2048], in_=dr["wz"][:, c8, :])], t, reads=[dr["t_wz"]], writes=[t])
    return view, t


def _load_wc(self, dr, c8):
    i = self.wi
    self.wi = (i + 1) % NWB
    view = self.wb[i][:, 0:3072]
    t = self.t_wb[i]
    self.S.dma("sp", [lambda e: e.dma_start(out=self.wb[i][:, 0:3072], in_=dr["wc"][:, c8, :])], t, reads=[dr["t_wc"]], writes=[t])
    return view, t


Prog.even_mixer = _even_mixer
Prog.load_wz = _load_wz
Prog.load_wc = _load_wc

EV_SMALL = dict(are_S=[P, 32], aim_S=[P, 32], ldt_S=[P, 32], bre_S=[P, 32, 16], bim_S=[P, 32, 16], cre_S=[P, 32, 16], cim_S=[P, 32, 16],
                are_T=[P, 8, 64], aim_T=[P, 8, 64], ldt_T=[P, 8, 64], bre_T=[P, 8, 64], bim_T=[P, 8, 64], mask01=[P, 2], ident=[P, P],
                dcol=[P, 8], gblk=[P, 8, P], convw=[P, 3, 8])


def build_even_launch():
    nc = bass.Bass("TRN2", target_bir_lowering=False)
    dt = lambda name, shape, kind="ExternalInput": nc.dram_tensor(name, shape, F32, kind=kind).ap()
    h_in = dt("h_in", [D, NTOK]); h_out = dt("h_out", [D, NTOK], "ExternalOutput")
    hhalo = dt("hhalo", [D, 2]); gm = dt("gm", [P, KC])
    s5in = dt("s5in", [2, P, 32]); s5out = dt("s5out", [2, P, 32], "ExternalOutput")
    Win = dt("Win", [D, 4096]); Wout = dt("Wout", [D, D])
    prm = {k: dt(k, shp) for k, shp in EV_SMALL.items()}
    dr = dict(wz=nc.dram_tensor("wz_scr", [P, 8, 2048], BF16, kind="Internal").ap(), t_wz=T("wz_scr"),
              wc=nc.dram_tensor("wc_scr", [P, 8, 3072], BF16, kind="Internal").ap(), t_wc=T("wc_scr"))
    with ExitStack() as st:
        pr = Prog(nc, st)
        g, t_g = pr.load_small("gm_s", gm, [P, KC])
        with pr.phase():
            t_Xb = pr.even_mixer(h_in, h_out, hhalo, s5in, s5out, Win, Wout, prm, dr, g, t_g, 0)
            pr.S.wait_all("sp", t_Xb)
        pr.finish(pr.t_hT)
    return nc


def build_final_launch():
    nc = bass.Bass("TRN2", target_bir_lowering=False)
    dt = lambda name, shape, kind="ExternalInput": nc.dram_tensor(name, shape, F32, kind=kind).ap()
    h_in = dt("h_in", [D, NTOK]); out = dt("out", [D, NTOK], "ExternalOutput"); gfin = dt("gfin", [P, KC])
    with ExitStack() as st:
        pr = Prog(nc, st)
        g, t_g = pr.load_small("gfin_s", gfin, [P, KC])
        with pr.phase():
            pr.final_norm(h_in, out, g, t_g, 0)
            pr.S.wait_all("sp", pr.t_fo)
        pr.finish([])
    return nc


NCORES = 8


def _run(nc, maps):
    res = run_bass_kernel_spmd(nc, maps, core_ids=list(range(NCORES)))
    return res.results


def kernel_unfused(**inp):
    z = {k: np.asarray(v) for k, v in inp.items()}
    x = z["x"].astype(np.float32)
    B = x.shape[0]
    cores = [(b, hf) for b in range(B) for hf in range(2)]
    h = [np.ascontiguousarray(x[b, hf * NTOK:(hf + 1) * NTOK, :].T) for (b, hf) in cores]

    def halo(w):
        out = []
        for ci, (b, hf) in enumerate(cores):
            if hf == 0:
                out.append(np.zeros((D, w), np.float32))
            else:
                out.append(np.ascontiguousarray(h[ci - 1][:, -w:]))
        return out
    oh, maskT = swa_consts()
    nc_x = build_xattn_launch()
    nc_f = build_ffn_launch()
    nc_e = build_even_launch()
    nc_s = build_swa_launch()
    for l in range(4):
        i = l // 2
        if l % 2 == 0:
            hp = ev_host_params(z, i)
            hal = halo(2)
            s5 = [np.zeros((2, P, 32), np.float32) for _ in cores]
            for rep in range(2):
                maps = [dict(h_in=h[ci], hhalo=hal[ci], gm=pk(z["norm_mix"][l]), s5in=s5[ci], Win=z["ev_w_in"][i], Wout=z["ev_w_out"][i], **hp)
                        for ci in range(len(cores))]
                r = _run(nc_e, maps)
                if rep == 0:
                    s5 = [np.zeros((2, P, 32), np.float32) if hf == 0 else r[ci - 1]["s5out"] for ci, (b, hf) in enumerate(cores)]
            h = [r[ci]["h_out"] for ci in range(len(cores))]
        else:
            hp = swa_host_params(z, i)
            hal = halo(128)
            maps = [dict(h_in=h[ci], hhalo=hal[ci], hmask=(np.full((P, 1), -30000.0, np.float32) if hf == 0 else np.zeros((P, 1), np.float32)),
                         gm=pk(z["norm_mix"][l]), oh=oh, maskT=maskT, relb=hp["relb"], sinkrow=hp["sinkrow"], bq=hp["bq"], bkd=hp["bkd"], bvd=hp["bvd"],
                         Wq=hp["Wq"], Wkd=hp["Wkd"], Wvd=hp["Wvd"], Wo=z["od_w_out"][i]) for ci, (b, hf) in enumerate(cores)]
            r = _run(nc_s, maps)
            h = [r[ci]["h_out"] for ci in range(len(cores))]
        maps = [dict(h_in=h[ci], memT=np.ascontiguousarray(z["mem"][b].T), gmem=pk(z["norm_mem"]), gx=pk(z["norm_xattn"][l]),
                     Wq=z["xa_w_q"][l], Wkv=z["xa_w_kv"][l], Wo=z["xa_w_o"][l]) for ci, (b, hf) in enumerate(cores)]
        r = _run(nc_x, maps)
        h = [r[ci]["h_out"] for ci in range(len(cores))]
        hal = halo(2)
        maps = [dict(h_in=h[ci], hhalo=hal[ci], gf=pk(z["norm_ffn"][l]), cw=pk(z["ff_conv_w"][l]), cb=pk(z["ff_conv_b"][l]),
                     Wg=z["ff_w_gate"][l], Wu=z["ff_w_up"][l], Wd=z["ff_w_down"][l]) for ci in range(len(cores))]
        r = _run(nc_f, maps)
        h = [r[ci]["h_out"] for ci in range(len(cores))]
    nc_n = build_final_launch()
    r = _run(nc_n, [dict(h_in=h[ci], gfin=pk(z["norm_final"])) for ci in range(len(cores))])
    out = np.empty((B, 2 * NTOK, D), np.float32)
    for ci, (b, hf) in enumerate(cores):
        out[b, hf * NTOK:(hf + 1) * NTOK, :] = r[ci]["out"].T
    return out


def kernel(**inp):
    return kernel_unfused(**inp)


RG_PAIRS = [[0, 1], [2, 3], [4, 5], [6, 7]]
SWA_SMALL = dict(bq=[P, KC], bkd=[P, 4], bvd=[P, 512], sinkrow=[1, 32])


def build_fused(depth=4, final=True, stop_after=None):
    nc = bass.Bass("TRN2", target_bir_lowering=False)
    n_ev = (depth + 1) // 2
    n_od = depth // 2
    global LAST_INPUT_NAMES
    LAST_INPUT_NAMES = []

    def dt(name, shape, kind="ExternalInput"):
        if kind == "ExternalInput":
            LAST_INPUT_NAMES.append(name)
        return nc.dram_tensor(name, list(shape), F32, kind=kind).ap()
    xT = dt("xT", [D, NTOK]); xhalo = dt("xhalo", [D, 128]); flag_d = dt("flag", [P, 1]); hmask_d = dt("hmask", [P, 1])
    memT = dt("memT", [D, N_MEM]); gmem = dt("gmem", [P, KC]); gfin = dt("gfin", [P, KC])
    gmix = dt("gmix", [P, 4, KC]); gxa = dt("gxa", [P, 4, KC]); gff = dt("gff", [P, 4, KC])
    oh_d = dt("oh", [P, 32, 2, 128]); mask_d = dt("maskT", [P, 2, 128]); relb_d = dt("relb", [P, 32, 32])
    ev = []
    for i in range(n_ev):
        e = dict(Win=dt("ev_w_in%d" % i, [D, 4096]), Wout=dt("ev_w_out%d" % i, [D, D]))
        e["prm"] = {k: dt("ev%d_%s" % (i, k), shp) for k, shp in EV_SMALL.items()}
        e["dr"] = dict(wz=nc.dram_tensor("wz_scr%d" % i, [P, 8, 2048], BF16, kind="Internal").ap(), t_wz=T("wz_scr"),
                       wc=nc.dram_tensor("wc_scr%d" % i, [P, 8, 3072], BF16, kind="Internal").ap(), t_wc=T("wc_scr"))
        ev.append(e)
    od = []
    for i in range(n_od):
        o = dict(Wq=dt("od_wq%d" % i, [D, D]), Wkd=dt("od_wkd%d" % i, [D, 512]), Wvd=dt("od_wvd%d" % i, [D, 512]), Wo=dt("od_wo%d" % i, [D, D]))
        o["prm"] = {k: dt("od%d_%s" % (i, k), shp) for k, shp in SWA_SMALL.items()}
        od.append(o)
    xa = [dict(Wq=dt("xa_wq%d" % l, [D, D]), Wkv=dt("xa_wkv%d" % l, [D, 2 * D]), Wo=dt("xa_wo%d" % l, [D, D])) for l in range(depth)]
    ff = [dict(Wg=dt("ff_wg%d" % l, [D, D_FF]), Wu=dt("ff_wu%d" % l, [D, D_FF]), Wd=dt("ff_wd%d" % l, [D_FF, D]),
               cw=dt("ff_cw%d" % l, [P, 3, FC]), cb=dt("ff_cb%d" % l, [P, FC])) for l in range(depth)]
    out = dt("out", [D, NTOK], "ExternalOutput")
    if stop_after is None:
        h_scr = nc.dram_tensor("h_scr", [D, NTOK], F32, kind="Internal").ap()
    else:
        h_scr = dt("h_dbg", [D, NTOK], "ExternalOutput")
    xch = dict(rg=RG_PAIRS,
               snd_h=nc.dram_tensor("snd_h", [D, 128], F32, kind="Internal").ap(), t_snd_h=T("snd_h"),
               rcv_h=nc.dram_tensor("rcv_h", [2 * D, 128], F32, kind="Internal").ap(), t_rcv_h=T("rcv_h"),
               snd_s=nc.dram_tensor("snd_s", [2 * P, 32], F32, kind="Internal").ap(), t_snd_s=T("snd_s"),
               rcv_s=nc.dram_tensor("rcv_s", [4 * P, 32], F32, kind="Internal").ap(), t_rcv_s=T("rcv_s"))
    bias_cache = dict(ap=nc.dram_tensor("bias_scr", [P, 8192], F32, kind="Internal").ap(), t=T("bias_scr"), valid=False)
    with ExitStack() as st:
        pr = Prog(nc, st)
        S = pr.S
        gm, t_gm = pr.load_small("gmix_s", gmix, [P, 4, KC])
        gx, t_gx = pr.load_small("gxa_s", gxa, [P, 4, KC])
        gf, t_gf = pr.load_small("gff_s", gff, [P, 4, KC])
        gfn, t_gfn = pr.load_small("gfin_s", gfin, [P, KC])
        flag = pr.load_small("flag_s", flag_d, [P, 1])
        hm, t_hm = pr.load_small("hmask_s", hmask_d, [P, 1])
        gmv = gm[:].rearrange("p l k -> p (l k)")
        gxv = gx[:].rearrange("p l k -> p (l k)")
        gfv = gf[:].rearrange("p l k -> p (l k)")
        pr.prep_mem(memT, gmem)

        def exchange_h():
            S.dma("sp", [lambda e: e.dma_start(out=xch["snd_h"].rearrange("(k p) t -> p k t", p=P), in_=pr.hT[:, :, TT - 128:TT])],
                  xch["t_snd_h"], reads=pr.t_hT, writes=[xch["t_snd_h"]])
            S.cc("pool", lambda e: e.collective_compute("AllGather", ALU.bypass, replica_groups=xch["rg"], ins=[xch["snd_h"]], outs=[xch["rcv_h"]]),
                 reads=[xch["t_snd_h"]], writes=[xch["t_rcv_h"]])

        for l in range(depth):
            i = l // 2
            h_in = xT if l == 0 else h_scr
            halo_src = xhalo if l == 0 else xch["rcv_h"][0:D, :]
            hrd = [] if l == 0 else [xch["t_rcv_h"]]
            with pr.phase():
                if l % 2 == 0:
                    pr.even_mixer(h_in, h_scr, halo_src[:, 126:128], ev[i]["Win"], ev[i]["Wout"], ev[i]["prm"], ev[i]["dr"],
                                  gmv, t_gm, l * KC, flag, hrd, xch)
                else:
                    o = od[i]
                    bq, t_bq = pr.load_small("bq_s", o["prm"]["bq"], [P, KC])
                    S.op("dve", lambda e, bq=bq: e.tensor_scalar(out=bq[:], in0=bq[:], scalar1=0.125, scalar2=None, op0=ALU.mult), reads=[t_bq], writes=[t_bq])
                    bkd, t_bkd = pr.load_small("bkd_s", o["prm"]["bkd"], [P, 4])
                    bvd, t_bvd = pr.load_small("bvd_s", o["prm"]["bvd"], [P, 512])
                    pr.swa_setup(oh_d, mask_d, relb_d, o["prm"]["sinkrow"], cache=bias_cache)
                    pr.swa(h_in, h_scr, halo_src, hm, t_hm, o["Wq"], o["Wkd"], o["Wvd"], o["Wo"], bq, t_bq, bkd, t_bkd, bvd, t_bvd,
                           gmv, t_gm, l * KC, flag=flag, hrd=hrd)
            if stop_after == (l, "mix"):
                break
            with pr.phase():
                pr.xattn(h_scr, h_scr, xa[l]["Wq"], xa[l]["Wkv"], xa[l]["Wo"], gxv, t_gx, l * KC)
                exchange_h()
            if stop_after == (l, "xa"):
                break
            with pr.phase():
                cws, t_cw = pr.load_small("cw_s", ff[l]["cw"], [P, 3, FC])
                cbs, t_cb = pr.load_small("cb_s", ff[l]["cb"], [P, FC])
                pr.ffn(h_scr, h_scr, xch["rcv_h"][0:D, 126:128], ff[l]["Wg"], ff[l]["Wu"], ff[l]["Wd"], gfv, t_gf, l * KC, cws, t_cw, cbs, t_cb,
                       flag=flag, hrd=[xch["t_rcv_h"]])
                if l < depth - 1:
                    exchange_h()
        with pr.phase():
            pr.final_norm(h_scr, out, gfn, t_gfn, 0)
            S.wait_all("sp", pr.t_fo)
        pr.finish(getattr(pr, "dbg_tiles", []))
    return nc


def fused_inputs(z):
    x = np.asarray(z["x"], np.float32)
    B = x.shape[0]
    cores = [(b, hf) for b in range(B) for hf in range(2)]
    oh, maskT = swa_consts()
    common = dict(gmem=pk(z["norm_mem"]), gfin=pk(z["norm_final"]), gmix=pk(z["norm_mix"]), gxa=pk(z["norm_xattn"]), gff=pk(z["norm_ffn"]),
                  oh=oh, maskT=maskT)
    for i in range(2):
        hp = ev_host_params(z, i)
        common["ev_w_in%d" % i] = np.asarray(z["ev_w_in"][i], np.float32)
        common["ev_w_out%d" % i] = np.asarray(z["ev_w_out"][i], np.float32)
        for k in EV_SMALL:
            common["ev%d_%s" % (i, k)] = hp[k]
        sp = swa_host_params(z, i)
        common["relb"] = sp["relb"]
        common["od_wq%d" % i] = sp["Wq"]
        common["od_wkd%d" % i] = sp["Wkd"]
        common["od_wvd%d" % i] = sp["Wvd"]
        common["od_wo%d" % i] = np.asarray(z["od_w_out"][i], np.float32)
        for k in SWA_SMALL:
            common["od%d_%s" % (i, k)] = sp[k]
    for l in range(4):
        common["xa_wq%d" % l] = np.asarray(z["xa_w_q"][l], np.float32)
        common["xa_wkv%d" % l] = np.asarray(z["xa_w_kv"][l], np.float32)
        common["xa_wo%d" % l] = np.asarray(z["xa_w_o"][l], np.float32)
        common["ff_wg%d" % l] = np.asarray(z["ff_w_gate"][l], np.float32)
        common["ff_wu%d" % l] = np.asarray(z["ff_w_up"][l], np.float32)
        common["ff_wd%d" % l] = np.asarray(z["ff_w_down"][l], np.float32)
        common["ff_cw%d" % l] = pk(z["ff_conv_w"][l])
        common["ff_cb%d" % l] = pk(z["ff_conv_b"][l])
    maps = []
    for (b, hf) in cores:
        m = dict(common)
        m["xT"] = np.ascontiguousarray(x[b, hf * NTOK:(hf + 1) * NTOK, :].T)
        m["xhalo"] = np.zeros((D, 128), np.float32) if hf == 0 else np.ascontiguousarray(x[b, NTOK - 128:NTOK, :].T)
        m["flag"] = np.full((P, 1), float(hf), np.float32)
        m["hmask"] = np.full((P, 1), -30000.0 if hf == 0 else 0.0, np.float32)
        m["memT"] = np.ascontiguousarray(np.asarray(z["mem"], np.float32)[b].T)
        maps.append(m)
    return cores, maps


def kernel_fused(**inp):
    z = {k: np.asarray(v) for k, v in inp.items()}
    cores, maps = fused_inputs(z)
    nc = build_fused()
    decl = set(LAST_INPUT_NAMES)
    maps = [{k: v for k, v in m.items() if k in decl} for m in maps]
    res = run_bass_kernel_spmd(nc, maps, core_ids=list(range(len(cores))))
    B = z["x"].shape[0]
    out = np.empty((B, 2 * NTOK, D), np.float32)
    for ci, (b, hf) in enumerate(cores):
        out[b, hf * NTOK:(hf + 1) * NTOK, :] = res.results[ci]["out"].T
    return out


def kernel(**inp):
    return kernel_fused(**inp)
```

```python
import math
from contextlib import ExitStack

import numpy as np
import concourse.bass as bass
import concourse.mybir as mybir
from concourse.bass_utils import run_bass_kernel_spmd

F32 = mybir.dt.float32
BF16 = mybir.dt.bfloat16
AF = mybir.ActivationFunctionType
ALU = mybir.AluOpType

P = 128
D = 2048
KC = D // P
NTOK = 2048
TT = 512
NT = NTOK // TT
N_MEM = 256
D_FF = 5632
FC = D_FF // P
RMS_EPS = 1e-5


class T:
    __slots__ = ("name", "w", "r", "sem", "semval", "last_dma")

    def __init__(self, name):
        self.name = name
        self.w = None
        self.r = {}
        self.sem = None
        self.semval = 0
        self.last_dma = None


ENGS = ("pe", "act", "dve", "pool", "sp")


class Sched:
    def __init__(self, nc, stack):
        self.nc = nc
        self.stack = stack
        self.q = {e: [] for e in ENGS}
        self.cnt = {e: 0 for e in ENGS}
        self.waited = {e: {} for e in ENGS}
        self.semh = {}
        for e in ENGS:
            self.semh[e] = stack.enter_context(nc.semaphore("sem_" + e))
        self.ndma_sem = 0
        self.n_instr = 0
        self.dmaval = {}
        self.gd = [stack.enter_context(nc.sbuf_tensor("gd%d" % i, [P, 1], F32)) for i in range(3)]
        self.q["dve"].append(lambda e: e.memset(self.gd[2][:], 0.0))

    def barrier(self):
        cur = {e: self.cnt[e] for e in ENGS if self.cnt[e] > 0}
        cur.update(self.dmaval)
        for e in ENGS:
            self._need(e, {k: v for k, v in cur.items() if k != e})

    NDSEM = 40

    def _tile_sem(self, t):
        if t.sem is None:
            i = self.ndma_sem % self.NDSEM
            key = "dsem%d" % i
            self.ndma_sem += 1
            if key not in self.semh:
                self.semh[key] = self.stack.enter_context(self.nc.semaphore(key))
            t.sem = key
        return t.sem

    def _need(self, eng, needs):
        for key, val in needs.items():
            if key == "pe" and eng == "pe":
                continue
            if self.waited[eng].get(key, 0) >= val:
                continue
            self.waited[eng][key] = val
            h = self.semh[key]
            self.q[eng].append(lambda e, h=h, val=val: e.wait_ge(h, val))

    def _deps(self, reads, writes):
        needs = {}

        def add(m):
            if m is not None:
                if needs.get(m[0], 0) < m[1]:
                    needs[m[0]] = m[1]
        for t in reads:
            add(t.w)
        for t in writes:
            add(t.w)
            for m in t.r.items():
                add(m)
        return needs

    def op(self, eng, fn, reads=(), writes=(), inc=True, guard=False):
        needs = self._deps(reads, writes)
        self._need(eng, needs)
        if guard and eng in ("dve", "act") and inc:
            self.q[eng].append(lambda e, fn=fn: fn(e))
            self.cnt[eng] += 1
            h = self.semh[eng]
            g = self.gd
            if eng == "dve":
                self.q[eng].append(lambda e, h=h, g=g: e.memset(g[0][:], 0.0).then_inc(h, 1))
            else:
                self.q[eng].append(lambda e, h=h, g=g: e.activation(out=g[1][:], in_=g[2][:], func=AF.Copy).then_inc(h, 1))
            mark = (eng, self.cnt[eng])
            self._mark(reads, writes, mark)
            self.n_instr += 2
            return
        if inc:
            self.cnt[eng] += 1
            h = self.semh[eng]
            self.q[eng].append(lambda e, fn=fn, h=h: fn(e).then_inc(h, 1))
            mark = (eng, self.cnt[eng])
        else:
            self.q[eng].append(lambda e, fn=fn: fn(e))
            mark = (eng, self.cnt[eng] + 1)
        self._mark(reads, writes, mark)
        self.n_instr += 1

    def dma(self, eng, fns, owner, reads=(), writes=(), step=16):
        key = self._tile_sem(owner)
        needs = self._deps(reads, writes)
        cur = self.dmaval.get(key, 0)
        if cur > 0 and needs.get(key, 0) < cur:
            needs[key] = cur
        self._need(eng, needs)
        h = self.semh[key]
        for fn in fns:
            cur += step
            self.q[eng].append(lambda e, fn=fn, h=h: fn(e).then_inc(h, step))
        mark = (key, cur)
        self.dmaval[key] = cur
        self._mark(reads, writes, mark)
        self.n_instr += len(fns)

    def cc(self, eng, fn, reads=(), writes=()):
        key = "ccsem%d" % len([k for k in self.semh if k.startswith("ccsem")])
        self.semh[key] = self.stack.enter_context(self.nc.semaphore(key))
        needs = self._deps(reads, writes)
        self._need(eng, needs)
        h = self.semh[key]
        self.q[eng].append(lambda e, fn=fn, h=h: fn(e).then_inc(h, 1))
        mark = (key, 1)
        self.dmaval[key] = 1
        self._mark(reads, writes, mark)
        self.n_instr += 1

    @staticmethod
    def _mark(reads, writes, mark):
        for t in reads:
            if t.r.get(mark[0], 0) < mark[1]:
                t.r[mark[0]] = mark[1]
        for t in writes:
            t.w = mark
            t.r = {}

    def wait_all(self, eng, tiles):
        needs = {}
        for t in tiles:
            for m in ([t.w] if t.w else []) + list(t.r.items()):
                if needs.get(m[0], 0) < m[1]:
                    needs[m[0]] = m[1]
        self._need(eng, needs)

    def emit(self):
        nc = self.nc
        if not any(self.q[e] for e in ENGS):
            return
        qs = {e: self.q[e] for e in ENGS}
        self.q = {e: [] for e in ENGS}
        self._emit_block(nc, qs)

    def _emit_block(self, nc, qs):
        self_q = qs
        with nc.Block() as block:
            @block.tensor
            def _(e):
                for f in self_q["pe"]:
                    f(e)

            @block.scalar
            def _(e):
                for f in self_q["act"]:
                    f(e)

            @block.vector
            def _(e):
                for f in self_q["dve"]:
                    f(e)

            @block.gpsimd
            def _(e):
                for f in self_q["pool"]:
                    f(e)

            @block.sync
            def _(e):
                for f in self_q["sp"]:
                    f(e)


WSLOT = 5632
NWB = 4
NPS = 8


class Prog:
    def __init__(self, nc, stack):
        self.nc = nc
        self.st = stack
        self.S = Sched(nc, stack)
        self.pstack = None
        self.uid = 0

        def sb(name, shape, dt):
            self.uid += 1
            stk = self.pstack if self.pstack is not None else stack
            return stk.enter_context(nc.sbuf_tensor("%s_%d" % (name, self.uid), shape, dt))
        self.sb = sb
        self.hT = sb("hT", [P, KC, TT], F32)
        self.t_hT = [T("hT%d" % k) for k in range(KC)]
        self.hn = sb("hn", [P, KC, TT], BF16)
        self.t_hn = [T("hn%d" % k) for k in range(KC)]
        self.wb = [sb("wb%d" % i, [P, WSLOT], BF16) for i in range(NWB)]
        self.t_wb = [T("wb%d" % i) for i in range(NWB)]
        self.wi = 0
        self.ps = [stack.enter_context(nc.psum_tensor("ps%d" % i, [P, 512], F32)) for i in range(NPS)]
        self.t_ps = [T("ps%d" % i) for i in range(NPS)]
        self.pi = 0
        self.sq = [sb("sq%d" % i, [P, TT], BF16) for i in range(2)]
        self.t_sq = [T("sq0"), T("sq1")]
        self.rstd = sb("rstd", [P, TT], F32)
        self.t_rstd = T("rstd")
        self.rtmp = sb("rtmp", [P, TT], F32)
        self.t_rtmp = T("rtmp")
        self.ones = sb("ones", [P, P], BF16)
        self.t_ones = T("ones")
        self.S.op("dve", lambda e: e.memset(self.ones[:], 1.0), writes=[self.t_ones])
        self.epsc = sb("epsc", [P, 1], F32)
        self.t_eps = T("eps")
        self.S.op("dve", lambda e: e.memset(self.epsc[:], RMS_EPS), writes=[self.t_eps])
        self.evi = 0

    def phase(self):
        prog = self

        class _Ph:
            def __enter__(s):
                prog.S.barrier()
                prog.S.emit()
                s.es = ExitStack()
                s.es.__enter__()
                s.prev = prog.pstack
                prog.pstack = s.es
                return s

            def __exit__(s, *a):
                prog.S.barrier()
                prog.S.emit()
                prog.pstack = s.prev
                return s.es.__exit__(*a)
        return _Ph()

    DBG = False

    def dbg(self, name, ap, tiles, shape, dt=F32):
        if not self.DBG:
            return
        d = self.nc.dram_tensor("dbg_" + name, list(shape), dt, kind="ExternalOutput").ap()
        t = T("dbg_" + name)
        self.S.dma("sp", [lambda e: e.dma_start(out=d, in_=ap)], t, reads=list(tiles))
        self.dbg_tiles = getattr(self, "dbg_tiles", []) + [t]

    def alloc_A(self, n):
        self.bufA = self.sb("bufA", [P, n, TT], BF16)
        self.t_A = [T("A%d" % k) for k in range(n)]

    def psum(self):
        i = self.pi
        self.pi = (i + 1) % NPS
        return self.ps[i], self.t_ps[i]

    def load_w(self, src3, k, m):
        i = self.wi
        self.wi = (i + 1) % NWB
        view = self.wb[i][:, 0:k * m].rearrange("p (k m) -> p k m", k=k)
        t = self.t_wb[i]
        if k > 22:
            h = k // 2
            fns = [lambda e: e.dma_start(out=view[:, 0:h, :], in_=src3[:, 0:h, :]),
                   lambda e: e.dma_start(out=view[:, h:, :], in_=src3[:, h:, :])]
        else:
            fns = [lambda e: e.dma_start(out=view, in_=src3)]
        self.S.dma("pool", fns, t, writes=[t])
        return view, t

    def mm_group(self, out_ap, t_out, pairs, reads, tp=None):
        n = len(pairs)
        for i, (l, r) in enumerate(pairs):
            if tp is None:
                self.S.op("pe", lambda e, l=l, r=r, i=i: e.matmul(out_ap, l, r, start=(i == 0), stop=(i == n - 1)),
                          reads=reads, writes=[t_out], inc=(i == n - 1))
            else:
                self.S.op("pe", lambda e, l=l, r=r, i=i: e.matmul(out_ap, l, r, start=(i == 0), stop=(i == n - 1), tile_position=tp),
                          reads=reads, writes=[t_out], inc=(i == n - 1))

    def load_small(self, name, dram_ap, shape, dt=F32):
        t = self.sb(name, shape, dt)
        tt = T(name)
        self.S.dma("sp", [lambda e: e.dma_start(out=t[:], in_=dram_ap)], tt, writes=[tt])
        return t, tt

    def rmsnorm(self, src, t_src, gain, t_gain, gcol, dst, t_dst, w, nk=KC):
        S = self.S
        ps, t_ps = self.psum()
        for k in range(nk):
            b = k % 2
            S.op("act", lambda e, k=k, b=b: e.activation(out=self.sq[b][:, 0:w], in_=src[:, k, 0:w], func=AF.Square),
                 reads=[t_src[k]], writes=[self.t_sq[b]])
            S.op("pe", lambda e, k=k, b=b: e.matmul(ps[:, 0:w], self.ones[:], self.sq[b][:, 0:w], start=(k == 0), stop=(k == nk - 1)),
                 reads=[self.t_sq[b], self.t_ones], writes=[t_ps], inc=True)
        S.op("act", lambda e: e.activation(out=self.rtmp[:, 0:w], in_=ps[:, 0:w], func=AF.Sqrt, bias=self.epsc[:], scale=1.0 / (nk * P)),
             reads=[t_ps, self.t_eps], writes=[self.t_rtmp])
        S.op("dve", lambda e: e.reciprocal(out=self.rstd[:, 0:w], in_=self.rtmp[:, 0:w]), reads=[self.t_rtmp], writes=[self.t_rstd])
        if getattr(self, "dbg_norm", False):
            self.dbg_norm = False
            self.dbg("n_hT0", src[:, 0, 0:w], [t_src[0]], [P, w], F32)
            self.dbg("n_hT15", src[:, 15, 0:w], [t_src[15]], [P, w], F32)
            self.dbg("n_rtmp", self.rtmp[:, 0:w], [self.t_rtmp], [P, w], F32)
            self.dbg("n_rstd", self.rstd[:, 0:w], [self.t_rstd], [P, w], F32)
        for k in range(nk):
            S.op("dve", lambda e, k=k: e.scalar_tensor_tensor(out=dst[:, k, 0:w], in0=src[:, k, 0:w], scalar=gain[:, gcol + k:gcol + k + 1],
                                                              in1=self.rstd[:, 0:w], op0=ALU.mult, op1=ALU.mult),
                 reads=[t_src[k], t_gain, self.t_rstd], writes=[t_dst[k]])

    def load_h(self, h_dram, tok0):
        S = self.S
        src = h_dram[:, tok0:tok0 + TT].rearrange("(k p) t -> p k t", p=P)
        for g in range(4):
            S.dma("sp", [lambda e, g=g: e.dma_start(out=self.hT[:, 4 * g:4 * g + 4, :], in_=src[:, 4 * g:4 * g + 4, :])],
                  self.t_hT[4 * g], writes=self.t_hT[4 * g:4 * g + 4])

    def store_h(self, h_dram, tok0):
        S = self.S
        dst = h_dram[:, tok0:tok0 + TT].rearrange("(k p) t -> p k t", p=P)
        for g in range(4):
            S.dma("sp", [lambda e, g=g: e.dma_start(out=dst[:, 4 * g:4 * g + 4, :], in_=self.hT[:, 4 * g:4 * g + 4, :])],
                  self.t_hT[4 * g], reads=self.t_hT[4 * g:4 * g + 4])

    def linear_resid(self, W, kin, t_in_list, in_buf, in_off):
        S = self.S
        mb = 256
        per = mb // P
        nh = 1 if kin <= 22 else 2
        kh = kin // nh
        for blk in range(D // mb):
            ws = []
            for hf in range(nh):
                ws.append(self.load_w(W[hf * kh * P:(hf + 1) * kh * P, blk * mb:(blk + 1) * mb].rearrange("(k p) m -> p k m", p=P), kh, mb))
            for m in range(per):
                ps, t_ps = self.psum()
                self.mm_group(ps[:, 0:TT], t_ps,
                              [(ws[k // kh][0][:, k % kh, m * P:(m + 1) * P], in_buf[:, in_off + k, :]) for k in range(kin)],
                              reads=[x[1] for x in ws] + t_in_list)
                c = blk * per + m
                S.op("dve", lambda e, c=c, ps=ps: e.tensor_tensor(out=self.hT[:, c, :], in0=self.hT[:, c, :], in1=ps[:, 0:TT], op=ALU.add),
                     reads=[t_ps, self.t_hT[c]], writes=[self.t_hT[c]])

    def linear_to(self, W, col0, ncols, t_in_list, in_buf, out_buf, out_off, t_out_list, evac=None):
        S = self.S
        mb = 256
        for blk in range(ncols // mb):
            w, t_w = self.load_w(W[:, col0 + blk * mb:col0 + (blk + 1) * mb].rearrange("(k p) m -> p k m", p=P), KC, mb)
            for m in range(2):
                ps, t_ps = self.psum()
                self.mm_group(ps[:, 0:TT], t_ps, [(w[:, k, m * P:(m + 1) * P], in_buf[:, k, :]) for k in range(KC)],
                              reads=[t_w] + t_in_list)
                c = blk * 2 + m
                if evac is not None:
                    evac(c, ps, t_ps)
                else:
                    S.op("act", lambda e, c=c, ps=ps: e.activation(out=out_buf[:, out_off + c, :], in_=ps[:, 0:TT], func=AF.Copy),
                         reads=[t_ps], writes=[t_out_list[out_off + c]])

    def prep_mem(self, memT_d, gmem_d):
        S = self.S
        self.memn = self.sb("memn", [P, KC, N_MEM], BF16)
        self.t_memn = [T("memn%d" % k) for k in range(KC)]
        gm, t_gm = self.load_small("gmem_s", gmem_d, [P, KC])
        src = memT_d.rearrange("(k p) t -> p k t", p=P)
        S.dma("sp", [lambda e: e.dma_start(out=self.hT[:, :, 0:N_MEM], in_=src)], self.t_hT[0], writes=self.t_hT)
        self.rmsnorm(self.hT, self.t_hT, gm, t_gm, 0, self.memn, self.t_memn, N_MEM)

    def prep_kv(self, Wkv):
        S = self.S
        for blk in range(8):
            w, t_w = self.load_w(Wkv[:, blk * 256:(blk + 1) * 256].rearrange("(k p) m -> p k m", p=P), KC, 256)
            for m in range(2):
                ps, t_ps = self.psum()
                self.mm_group(ps[:, 0:N_MEM], t_ps, [(w[:, k, m * P:(m + 1) * P], self.memn[:, k, :]) for k in range(KC)],
                              reads=[t_w] + self.t_memn)
                c = blk * 2 + m
                S.op("act", lambda e, c=c, ps=ps: e.activation(out=self.kT[:, c, :], in_=ps[:, 0:N_MEM], func=AF.Copy),
                     reads=[t_ps], writes=[self.t_kT[c]])
        for blk in range(8):
            w, t_w = self.load_w(Wkv[:, D + blk * 256:D + (blk + 1) * 256].rearrange("(k p) m -> p k m", p=P), KC, 256)
            for mc in range(2):
                ps, t_ps = self.psum()
                self.mm_group(ps[:, 0:256], t_ps, [(self.memn[:, k, mc * P:(mc + 1) * P], w[:, k, :]) for k in range(KC)],
                              reads=[t_w] + self.t_memn)
                S.op("act", lambda e, mc=mc, blk=blk, ps=ps: e.activation(out=self.vv[:, mc, blk * 256:(blk + 1) * 256], in_=ps[:, 0:256], func=AF.Copy),
                     reads=[t_ps], writes=[self.t_vv[mc]])

    def xattn(self, h_in, h_out, Wq, Wkv, Wo, gain, t_gain, gcol):
        S = self.S
        self.alloc_A(2 * KC)
        self.kT = self.sb("kT", [P, KC, N_MEM], BF16)
        self.t_kT = [T("kT%d" % k) for k in range(KC)]
        self.vv = self.sb("vv", [P, 2, D], BF16)
        self.t_vv = [T("vv0"), T("vv1")]
        self.pT = self.sb("pT", [P, 2, TT], BF16)
        self.t_pT = [T("pT0"), T("pT1")]
        self.rden = self.sb("rden", [P, TT], F32)
        self.t_rden = T("rden")
        self.prep_kv(Wkv)
        qoff, ooff = 0, KC
        scale = 1.0 / math.sqrt(512.0)
        for it in range(NT):
            tok0 = it * TT
            self.load_h(h_in, tok0)
            self.dbg_norm = (it == 0)
            self.rmsnorm(self.hT, self.t_hT, gain, t_gain, gcol, self.hn, self.t_hn, TT)
            self.linear_to(Wq, 0, D, self.t_hn, self.hn, self.bufA, qoff, self.t_A)
            for hd in range(4):
                for mc in range(2):
                    ps, t_ps = self.psum()
                    self.mm_group(ps[:, 0:TT], t_ps,
                                  [(self.kT[:, hd * 4 + j, mc * P:(mc + 1) * P], self.bufA[:, qoff + hd * 4 + j, :]) for j in range(4)],
                                  reads=self.t_kT[hd * 4:hd * 4 + 4] + self.t_A[qoff + hd * 4:qoff + hd * 4 + 4])
                    S.op("act", lambda e, mc=mc, ps=ps: e.activation(out=self.pT[:, mc, :], in_=ps[:, 0:TT], func=AF.Exp, scale=scale),
                         reads=[t_ps], writes=[self.t_pT[mc]])
                ps, t_ps = self.psum()
                self.mm_group(ps[:, 0:TT], t_ps, [(self.ones[:], self.pT[:, mc, :]) for mc in range(2)],
                              reads=[self.t_ones] + self.t_pT)
                S.op("dve", lambda e, ps=ps: e.reciprocal(out=self.rden[:], in_=ps[:, 0:TT]), reads=[t_ps], writes=[self.t_rden])
                for j in range(4):
                    ps, t_ps = self.psum()
                    f0 = hd * 512 + j * P
                    self.mm_group(ps[:, 0:TT], t_ps, [(self.vv[:, mc, f0:f0 + P], self.pT[:, mc, :]) for mc in range(2)],
                                  reads=self.t_vv + self.t_pT)
                    c = ooff + hd * 4 + j
                    S.op("dve", lambda e, c=c, ps=ps: e.tensor_tensor(out=self.bufA[:, c, :], in0=ps[:, 0:TT], in1=self.rden[:], op=ALU.mult),
                         reads=[t_ps, self.t_rden], writes=[self.t_A[c]])
                if it == 0 and hd == 0:
                    self.dbg("hn0", self.hn[:, 0, :], [self.t_hn[0]], [P, TT], BF16)
                    self.dbg("q0", self.bufA[:, 0, :], [self.t_A[0]], [P, TT], BF16)
                    self.dbg("kT0", self.kT[:, 0, :], [self.t_kT[0]], [P, N_MEM], BF16)
                    self.dbg("vv0", self.vv[:, 0, 0:512], [self.t_vv[0]], [P, 512], BF16)
                    self.dbg("memn0", self.memn[:, 0, :], [self.t_memn[0]], [P, N_MEM], BF16)
                    self.dbg("pT0", self.pT[:, 0, :], [self.t_pT[0]], [P, TT], BF16)
                    self.dbg("rden", self.rden[:], [self.t_rden], [P, TT], F32)
                    self.dbg("o0", self.bufA[:, ooff, :], [self.t_A[ooff]], [P, TT], BF16)
            self.linear_resid(Wo, KC, self.t_A[ooff:ooff + KC], self.bufA, ooff)
            self.store_h(h_out, tok0)

    def halo_load(self, dst, t_dst, src_ap, w, flag, rd=()):
        S = self.S
        S.dma("sp", [lambda e: e.dma_start(out=dst[:, :, 0:w], in_=src_ap.rearrange("(k p) t -> p k t", p=P))], t_dst[0], reads=list(rd), writes=t_dst)
        if flag is not None:
            S.op("dve", lambda e: e.tensor_scalar(out=dst[:, :, 0:w], in0=dst[:, :, 0:w], scalar1=flag[0][:, 0:1], scalar2=None, op0=ALU.mult),
                 reads=t_dst + [flag[1]], writes=t_dst)

    def ffn(self, h_in, h_out, hhalo_d, Wg, Wu, Wd, gain, t_gain, gcol, cw, t_cw, cb, t_cb, flag=None, hrd=()):
        S = self.S
        self.alloc_A(FC)
        self.gfull = [self.sb("gfull%d" % i, [P, TT + 2], F32) for i in range(2)]
        self.t_gfull = [T("gfull0"), T("gfull1")]
        self.ghalo = self.sb("ghalo", [P, FC, 2], F32)
        self.t_ghalo = [T("ghalo%d" % c) for c in range(FC)]
        self.c1 = [self.sb("c1_%d" % i, [P, TT], F32) for i in range(2)]
        self.t_c1 = [T("c1_0"), T("c1_1")]
        self.hhT = self.sb("hhT", [P, KC, 2], F32)
        self.t_hhT = [T("hhT%d" % k) for k in range(KC)]
        self.hhn = self.sb("hhn", [P, KC, 2], BF16)
        self.t_hhn = [T("hhn%d" % k) for k in range(KC)]
        self.halo_load(self.hhT, self.t_hhT, hhalo_d, 2, flag, hrd)
        self.rmsnorm(self.hhT, self.t_hhT, gain, t_gain, gcol, self.hhn, self.t_hhn, 2)
        for it in range(NT):
            tok0 = it * TT
            self.load_h(h_in, tok0)
            self.rmsnorm(self.hT, self.t_hT, gain, t_gain, gcol, self.hn, self.t_hn, TT)
            for blk in range(FC // 2):
                wg, t_wg = self.load_w(Wg[:, blk * 256:(blk + 1) * 256].rearrange("(k p) m -> p k m", p=P), KC, 256)
                wu, t_wu = self.load_w(Wu[:, blk * 256:(blk + 1) * 256].rearrange("(k p) m -> p k m", p=P), KC, 256)
                for m in range(2):
                    c = blk * 2 + m
                    b = c % 2
                    gf, t_gf = self.gfull[b], self.t_gfull[b]
                    if it == 0:
                        psh, t_psh = self.psum()
                        self.mm_group(psh[:, 0:2], t_psh, [(wg[:, k, m * P:(m + 1) * P], self.hhn[:, k, :]) for k in range(KC)],
                                      reads=[t_wg] + self.t_hhn)
                        S.op("act", lambda e, gf=gf, psh=psh: e.activation(out=gf[:, 0:2], in_=psh[:, 0:2], func=AF.Copy),
                             reads=[t_psh], writes=[t_gf])
                    else:
                        S.op("act", lambda e, gf=gf, c=c: e.activation(out=gf[:, 0:2], in_=self.ghalo[:, c, :], func=AF.Copy),
                             reads=[self.t_ghalo[c]], writes=[t_gf])
                    psg, t_psg = self.psum()
                    self.mm_group(psg[:, 0:TT], t_psg, [(wg[:, k, m * P:(m + 1) * P], self.hn[:, k, :]) for k in range(KC)],
                                  reads=[t_wg] + self.t_hn)
                    psu, t_psu = self.psum()
                    self.mm_group(psu[:, 0:TT], t_psu, [(wu[:, k, m * P:(m + 1) * P], self.hn[:, k, :]) for k in range(KC)],
                                  reads=[t_wu] + self.t_hn)
                    S.op("act", lambda e, gf=gf, psg=psg: e.activation(out=gf[:, 2:TT + 2], in_=psg[:, 0:TT], func=AF.Copy),
                         reads=[t_psg], writes=[t_gf])
                    if it < NT - 1:
                        S.op("act", lambda e, gf=gf, c=c: e.activation(out=self.ghalo[:, c, :], in_=gf[:, TT:TT + 2], func=AF.Copy),
                             reads=[t_gf], writes=[self.t_ghalo[c]])
                    c1, t_c1 = self.c1[b], self.t_c1[b]
                    S.op("act", lambda e, gf=gf, c=c, c1=c1: e.activation(out=c1[:], in_=gf[:, 2:TT + 2], func=AF.Identity,
                                                                        bias=cb[:, c:c + 1], scale=cw[:, 2, c:c + 1]),
                         reads=[t_gf, t_cw, t_cb], writes=[t_c1])
                    S.op("dve", lambda e, gf=gf, c=c, c1=c1: e.scalar_tensor_tensor(out=c1[:], in0=gf[:, 1:TT + 1], scalar=cw[:, 1, c:c + 1], in1=c1[:],
                                                                                  op0=ALU.mult, op1=ALU.add),
                         reads=[t_gf, t_cw, t_c1], writes=[t_c1])
                    S.op("dve", lambda e, gf=gf, c=c, c1=c1: e.scalar_tensor_tensor(out=c1[:], in0=gf[:, 0:TT], scalar=cw[:, 0, c:c + 1], in1=c1[:],
                                                                                  op0=ALU.mult, op1=ALU.add),
                         reads=[t_gf, t_cw, t_c1], writes=[t_c1])
                    S.op("act", lambda e, c1=c1: e.activation(out=c1[:], in_=c1[:], func=AF.Silu), reads=[t_c1], writes=[t_c1])
                    S.op("dve", lambda e, c=c, c1=c1, psu=psu: e.tensor_tensor(out=self.bufA[:, c, :], in0=c1[:], in1=psu[:, 0:TT], op=ALU.mult),
                         reads=[t_c1, t_psu], writes=[self.t_A[c]])
            self.linear_resid(Wd, FC, self.t_A, self.bufA, 0)
            self.store_h(h_out, tok0)

    def final_norm(self, h_in, out_d, gain, t_gain, gcol):
        S = self.S
        self.fo = self.sb("fo", [P, KC, TT], F32)
        self.t_fo = [T("fo%d" % k) for k in range(KC)]
        for it in range(NT):
            tok0 = it * TT
            self.load_h(h_in, tok0)
            self.rmsnorm(self.hT, self.t_hT, gain, t_gain, gcol, self.fo, self.t_fo, TT)
            dst = out_d[:, tok0:tok0 + TT].rearrange("(k p) t -> p k t", p=P)
            S.dma("sp", [lambda e, dst=dst: e.dma_start(out=dst, in_=self.fo[:])], self.t_fo[0], reads=self.t_fo)

    def finish(self, out_tiles):
        self.S.wait_all("sp", out_tiles)
        self.S.emit()


def pk(v):
    v = np.asarray(v, dtype=np.float32)
    lead = v.shape[:-1]
    n = v.shape[-1] // P
    a = v.reshape(lead + (n, P))
    a = np.moveaxis(a, -1, 0)
    return np.ascontiguousarray(a)


def build_xattn_launch():
    nc = bass.Bass("TRN2", target_bir_lowering=False)
    dt = lambda name, shape, kind="ExternalInput": nc.dram_tensor(name, shape, F32, kind=kind).ap()
    h_in = dt("h_in", [D, NTOK]); h_out = dt("h_out", [D, NTOK], "ExternalOutput")
    memT = dt("memT", [D, N_MEM]); gmem = dt("gmem", [P, KC]); gx = dt("gx", [P, KC])
    Wq = dt("Wq", [D, D]); Wkv = dt("Wkv", [D, 2 * D]); Wo = dt("Wo", [D, D])
    with ExitStack() as st:
        pr = Prog(nc, st)
        g, t_g = pr.load_small("gx_s", gx, [P, KC])
        pr.prep_mem(memT, gmem)
        with pr.phase():
            pr.xattn(h_in, h_out, Wq, Wkv, Wo, g, t_g, 0)
        pr.finish(pr.t_hT)
    return nc


def build_ffn_launch(final=False):
    nc = bass.Bass("TRN2", target_bir_lowering=False)
    dt = lambda name, shape, kind="ExternalInput": nc.dram_tensor(name, shape, F32, kind=kind).ap()
    h_in = dt("h_in", [D, NTOK]); h_out = dt("h_out", [D, NTOK], "ExternalOutput")
    hhalo = dt("hhalo", [D, 2]); gf = dt("gf", [P, KC])
    cw = dt("cw", [P, 3, FC]); cb = dt("cb", [P, FC])
    Wg = dt("Wg", [D, D_FF]); Wu = dt("Wu", [D, D_FF]); Wd = dt("Wd", [D_FF, D])
    with ExitStack() as st:
        pr = Prog(nc, st)
        g, t_g = pr.load_small("gf_s", gf, [P, KC])
        cws, t_cw = pr.load_small("cw_s", cw, [P, 3, FC])
        cbs, t_cb = pr.load_small("cb_s", cb, [P, FC])
        with pr.phase():
            pr.ffn(h_in, h_out, hhalo, Wg, Wu, Wd, g, t_g, 0, cws, t_cw, cbs, t_cb)
        pr.finish(pr.t_hT)
    return nc


def t5_bucket_np(rel):
    n = np.maximum(rel, 0)
    nf = np.maximum(n, 16).astype(np.float32)
    large = 16 + (np.log(nf / np.float32(16)) / np.float32(math.log(128 / 16)) * np.float32(16)).astype(np.int32)
    large = np.minimum(large, 31)
    return np.where(n < 16, n, large)


def swa_consts():
    k = np.arange(128)[:, None, None]
    j = np.arange(2)[None, :, None]
    q = np.arange(128)[None, None, :]
    rel = 128 + q - j * 128 - k
    valid = (rel >= 0) & (rel < 128)
    bk = t5_bucket_np(rel)
    oh = np.zeros((128, 32, 2, 128), np.float32)
    for b in range(32):
        oh[:, b] = ((bk == b) & valid).astype(np.float32)
    maskT = np.where(valid, 0.0, -30000.0).astype(np.float32)
    return oh, maskT


def _swa_setup(self, oh_d, mask_d, relb_d, sinkrow_d, cache=None):
    S = self.S
    self.biasT = self.sb("biasT", [P, 2, 4, 4, 256], F32)
    self.t_biasT = T("biasT")
    self.esrow = self.sb("esrow", [1, 32 * 128], BF16)
    self.t_esrow = T("esrow")
    bflat = self.biasT[:].rearrange("p a b c d -> p (a b c d)")
    have = cache is not None and cache.get("valid", False)
    if have:
        S.dma("sp", [lambda e: e.dma_start(out=bflat, in_=cache["ap"])], self.t_biasT, reads=[cache["t"]], writes=[self.t_biasT])
    with self.phase():
        oh = self.sb("oh_s", [P, 8, 256], F32)
        t_oh = T("oh")
        mk, t_mk = self.load_small("mask_s", mask_d.rearrange("p j q -> p (j q)"), [P, 256])
        rb, t_rb = self.load_small("relb_s", relb_d, [P, 32, 32])
        ohv = oh_d.rearrange("p b j q -> p b (j q)")
        for bg in range(0 if have else 4):
            S.dma("sp", [lambda e, bg=bg: e.dma_start(out=oh[:], in_=ohv[:, bg * 8:(bg + 1) * 8, :])], t_oh, writes=[t_oh])
            for h in range(32):
                kh, g = h // 8, h % 8
                par, i = g % 2, g // 2
                dst = self.biasT[:, par, kh, i, :]
                if bg == 0:
                    S.op("dve", lambda e, dst=dst: e.tensor_copy(out=dst, in_=mk[:]), reads=[t_mk], writes=[self.t_biasT])
                for b8 in range(8):
                    b = bg * 8 + b8
                    S.op("dve", lambda e, dst=dst, b=b, b8=b8, h=h: e.scalar_tensor_tensor(out=dst, in0=oh[:, b8, :], scalar=rb[:, b, h:h + 1], in1=dst,
                                                                                         op0=ALU.mult, op1=ALU.add),
                         reads=[t_oh, t_rb, self.t_biasT], writes=[self.t_biasT])
        if cache is not None and not have:
            S.dma("sp", [lambda e: e.dma_start(out=cache["ap"], in_=bflat)], self.t_biasT, reads=[self.t_biasT], writes=[cache["t"]])
            cache["valid"] = True
        sk, t_sk = self.load_small("sink_s", sinkrow_d, [1, 32])
        z128 = self.sb("z128", [1, 128], F32)
        t_z = T("z128")
        S.op("dve", lambda e: e.memset(z128[:], 0.0), writes=[t_z])
        for hh in range(32):
            S.op("act", lambda e, hh=hh: e.activation(out=self.esrow[0:1, hh * 128:(hh + 1) * 128], in_=z128[:], func=AF.Exp,
                                                     bias=sk[0:1, hh:hh + 1], scale=1.0),
                 reads=[t_z, t_sk], writes=[self.t_esrow])
    self.kbuf = self.sb("kbuf", [P, 4, 128 + TT], BF16)
    self.t_kbuf = [T("kbuf%d" % i) for i in range(4)]
    self.vdup = self.sb("vdup", [P, 5, 512], BF16)
    self.t_vdup = [T("vdup%d" % i) for i in range(5)]
    self.hh128 = self.hT
    self.t_hh128 = self.t_hT
    self.alloc_A(2 * KC)
    self.hhn128 = self.hn
    self.t_hhn128 = self.t_hn
    self.sc = [self.sb("sc%d" % i, [P, 512], F32) for i in range(2)]
    self.t_sc = [T("sc0"), T("sc1")]
    self.pS = [self.sb("pS%d" % i, [P, 512], BF16) for i in range(4)]
    self.t_pS = [T("pS%d" % i) for i in range(4)]
    self.rdn = [self.sb("rdn%d" % i, [P, 512], F32) for i in range(2)]
    self.t_rdn = [T("rdn0"), T("rdn1")]
    self.swa_it = 0


def _swa(self, h_in, h_out, hhalo_d, hmask, t_hmask, Wq, Wkd, Wvd, Wo, bq, t_bq, bkd, t_bkd, bvd, t_bvd, gain, t_gain, gcol, flag=None, hrd=()):
    S = self.S
    qoff, ooff = 0, KC
    self.halo_load(self.hh128, self.t_hh128, hhalo_d, 128, flag, hrd)
    self.rmsnorm(self.hh128, self.t_hh128, gain, t_gain, gcol, self.hhn128, self.t_hhn128, 128)

    def kv_proj(src, t_src, w, kcol0, vblk):
        for half in range(2):
            wk, t_wk = self.load_w(Wkd[:, half * 256:(half + 1) * 256].rearrange("(k p) m -> p k m", p=P), KC, 256)
            for m in range(2):
                kh = half * 2 + m
                ps, t_ps = self.psum()
                self.mm_group(ps[:, 0:w], t_ps, [(wk[:, k, m * P:(m + 1) * P], src[:, k, 0:w]) for k in range(KC)], reads=[t_wk] + t_src)
                S.op("act", lambda e, kh=kh, ps=ps: e.activation(out=self.kbuf[:, kh, kcol0:kcol0 + w], in_=ps[:, 0:w], func=AF.Identity,
                                                                bias=bkd[:, kh:kh + 1], scale=1.0),
                     reads=[t_ps, t_bkd], writes=[self.t_kbuf[kh]])
        wv = []
        for half in range(2):
            wv.append(self.load_w(Wvd[:, half * 256:(half + 1) * 256].rearrange("(k p) m -> p k m", p=P), KC, 256))
        for qb in range(w // 128):
            ps, t_ps = self.psum()
            for half in range(2):
                self.mm_group(ps[:, half * 256:(half + 1) * 256], t_ps,
                              [(src[:, k, qb * 128:(qb + 1) * 128], wv[half][0][:, k, :]) for k in range(KC)], reads=[wv[half][1]] + t_src)
            S.op("dve", lambda e, qb=qb, ps=ps: e.tensor_tensor(out=self.vdup[:, vblk + qb, :], in0=ps[:, 0:512], in1=bvd[:], op=ALU.add),
                 reads=[t_ps, t_bvd], writes=[self.t_vdup[vblk + qb]])

    kv_proj(self.hhn128, self.t_hhn128, 128, 0, 0)
    for it in range(NT):
        tok0 = it * TT
        self.load_h(h_in, tok0)
        self.rmsnorm(self.hT, self.t_hT, gain, t_gain, gcol, self.hn, self.t_hn, TT)
        if it > 0:
            for kh in range(4):
                S.op("act", lambda e, kh=kh: e.activation(out=self.kbuf[:, kh, 0:128], in_=self.kbuf[:, kh, TT:TT + 128], func=AF.Copy),
                     reads=[self.t_kbuf[kh]], writes=[self.t_kbuf[kh]])
            S.op("act", lambda e: e.activation(out=self.vdup[:, 0, :], in_=self.vdup[:, 4, :], func=AF.Copy),
                 reads=[self.t_vdup[4]], writes=[self.t_vdup[0]])
        kv_proj(self.hn, self.t_hn, TT, 128, 1)

        def q_evac(c, ps, t_ps):
            S.op("act", lambda e, c=c, ps=ps: e.activation(out=self.bufA[:, qoff + c, :], in_=ps[:, 0:TT], func=AF.Identity,
                                                          bias=bq[:, c:c + 1], scale=0.125),
                 reads=[t_ps, t_bq], writes=[self.t_A[qoff + c]])
        self.linear_to(Wq, 0, D, self.t_hn, self.hn, self.bufA, qoff, self.t_A, evac=q_evac)
        for qb in range(TT // 128):
            qs = slice(qb * 128, (qb + 1) * 128)
            for kh in range(4):
                for par in range(2):
                    pr = slice(par * 64, (par + 1) * 64)
                    t_q4 = self.t_A[qoff + 4 * kh:qoff + 4 * kh + 4]
                    pb = self.swa_it % 2
                    self.swa_it += 1
                    pS = self.pS[2 * pb:2 * pb + 2]
                    t_pS = self.t_pS[2 * pb:2 * pb + 2]
                    rdn, t_rdn = self.rdn[pb], self.t_rdn[pb]
                    for j in range(2):
                        ps, t_ps = self.psum()
                        kc0 = (qb + j) * 128
                        self.mm_group(ps[:, 0:512], t_ps,
                                      [(self.kbuf[pr, kh, kc0:kc0 + 128], self.bufA[pr, qoff + 4 * kh:qoff + 4 * kh + 4, qs])],
                                      reads=[self.t_kbuf[kh]] + t_q4)
                        bsl = self.biasT[:, par, kh, :, j * 128:(j + 1) * 128]
                        psv = ps[:, 0:512].rearrange("p (i q) -> p i q", i=4)
                        scv = self.sc[j][:].rearrange("p (i q) -> p i q", i=4)
                        if it == 0 and qb == 0 and j == 0:
                            S.op("dve", lambda e, psv=psv, scv=scv, bsl=bsl: e.scalar_tensor_tensor(out=scv, in0=psv, scalar=hmask[:, 0:1], in1=bsl,
                                                                                                  op0=ALU.add, op1=ALU.add),
                                 reads=[t_ps, self.t_biasT, t_hmask], writes=[self.t_sc[j]])
                        else:
                            S.op("dve", lambda e, psv=psv, scv=scv, bsl=bsl: e.tensor_tensor(out=scv, in0=psv, in1=bsl, op=ALU.add),
                                 reads=[t_ps, self.t_biasT], writes=[self.t_sc[j]])
                        S.op("act", lambda e, j=j, pS=pS: e.activation(out=pS[j][:], in_=self.sc[j][:], func=AF.Exp),
                             reads=[self.t_sc[j]], writes=[t_pS[j]])
                    psd, t_psd = self.psum()
                    e0 = (par * 16 + kh * 4) * 128
                    self.mm_group(psd[:, 0:512], t_psd,
                                  [(self.ones[:], pS[0][:]), (self.ones[:], pS[1][:]), (self.ones[0:1, :], self.esrow[0:1, e0:e0 + 512])],
                                  reads=t_pS + [self.t_ones, self.t_esrow])
                    S.op("dve", lambda e, psd=psd, rdn=rdn: e.reciprocal(out=rdn[:], in_=psd[:, 0:512]), reads=[t_psd], writes=[t_rdn])
                    pso, t_pso = self.psum()
                    self.mm_group(pso[:, 0:512], t_pso,
                                  [(self.vdup[:, qb + j, kh * 128:(kh + 1) * 128], pS[j][:]) for j in range(2)],
                                  reads=t_pS + [self.t_vdup[qb], self.t_vdup[qb + 1]])
                    ov = self.bufA[pr, ooff + 4 * kh:ooff + 4 * kh + 4, qs]
                    S.op("dve", lambda e, ov=ov, pso=pso, pr=pr, rdn=rdn: e.tensor_tensor(out=ov, in0=pso[pr, 0:512].rearrange("p (i q) -> p i q", i=4),
                                                                                         in1=rdn[pr, :].rearrange("p (i q) -> p i q", i=4), op=ALU.mult),
                         reads=[t_pso, t_rdn], writes=self.t_A[ooff + 4 * kh:ooff + 4 * kh + 4])
        self.linear_resid(Wo, KC, self.t_A[ooff:ooff + KC], self.bufA, ooff)
        self.store_h(h_out, tok0)


Prog.swa_setup = _swa_setup
Prog.swa = _swa


def swa_host_params(z, i):
    b = np.asarray(z["od_b_qkv"][i], np.float32)
    bq = pk(b[0:2048])
    bk = b[2048:2304].reshape(4, 64)
    bkd = np.ascontiguousarray(np.concatenate([bk, bk], axis=1).T)
    bv = b[2304:2560].reshape(4, 64)
    bvd = np.concatenate([bv, bv], axis=1).reshape(1, 512)
    bvd = np.ascontiguousarray(np.broadcast_to(bvd, (P, 512)))
    W = np.asarray(z["od_w_qkv"][i], np.float32)
    Wk = W[:, 2048:2304].reshape(D, 4, 64)
    Wkd = np.ascontiguousarray(np.concatenate([Wk, Wk], axis=2).reshape(D, 512))
    Wv = W[:, 2304:2560].reshape(D, 4, 64)
    Wvd = np.ascontiguousarray(np.concatenate([Wv, Wv], axis=2).reshape(D, 512))
    Wq = np.ascontiguousarray(W[:, 0:2048])
    sk = np.asarray(z["od_sinks"][i], np.float32)
    order = [kh * 8 + 2 * ii + par for par in range(2) for kh in range(4) for ii in range(4)]
    sinkrow = np.ascontiguousarray(sk[order].reshape(1, 32))
    relb = np.ascontiguousarray(np.broadcast_to(np.asarray(z["rel_bias"], np.float32)[None], (P, 32, 32)))
    return dict(bq=bq, bkd=bkd, bvd=bvd, Wq=Wq, Wkd=Wkd, Wvd=Wvd, sinkrow=sinkrow, relb=relb)


def build_swa_launch():
    nc = bass.Bass("TRN2", target_bir_lowering=False)
    dt = lambda name, shape, kind="ExternalInput": nc.dram_tensor(name, shape, F32, kind=kind).ap()
    h_in = dt("h_in", [D, NTOK]); h_out = dt("h_out", [D, NTOK], "ExternalOutput")
    hhalo = dt("hhalo", [D, 128]); gm = dt("gm", [P, KC]); hmask_d = dt("hmask", [P, 1])
    oh_d = dt("oh", [P, 32, 2, 128]); mask_d = dt("maskT", [P, 2, 128]); relb_d = dt("relb", [P, 32, 32]); sinkrow_d = dt("sinkrow", [1, 32])
    bq_d = dt("bq", [P, KC]); bkd_d = dt("bkd", [P, 4]); bvd_d = dt("bvd", [P, 512])
    Wq = dt("Wq", [D, D]); Wkd = dt("Wkd", [D, 512]); Wvd = dt("Wvd", [D, 512]); Wo = dt("Wo", [D, D])
    with ExitStack() as st:
        pr = Prog(nc, st)
        g, t_g = pr.load_small("gm_s", gm, [P, KC])
        hm, t_hm = pr.load_small("hmask_s", hmask_d, [P, 1])
        bq, t_bq = pr.load_small("bq_s", bq_d, [P, KC])
        pr.S.op("dve", lambda e: e.tensor_scalar(out=bq[:], in0=bq[:], scalar1=0.125, scalar2=None, op0=ALU.mult), reads=[t_bq], writes=[t_bq])
        bkd, t_bkd = pr.load_small("bkd_s", bkd_d, [P, 4])
        bvd, t_bvd = pr.load_small("bvd_s", bvd_d, [P, 512])
        with pr.phase():
            pr.swa_setup(oh_d, mask_d, relb_d, sinkrow_d)
            pr.swa(h_in, h_out, hhalo, hm, t_hm, Wq, Wkd, Wvd, Wo, bq, t_bq, bkd, t_bkd, bvd, t_bvd, g, t_g, 0)
        pr.finish(pr.t_hT)
    return nc


class Ew:
    def __init__(self, prog, shape, tag):
        self.pr = prog
        self.shape = shape
        self.tag = tag
        self.n = 0

    def new(self, dt=F32, shape=None):
        self.n += 1
        t = self.pr.sb("%s%d" % (self.tag, self.n), shape or self.shape, dt)
        return (t[:], T("%s%d" % (self.tag, self.n)))

    def tt(self, o, a, b, op):
        self.pr.S.op("dve", lambda e: e.tensor_tensor(out=o[0], in0=a[0], in1=b[0], op=op), reads=[a[1], b[1]], writes=[o[1]])
        return o

    def ts(self, o, a, s1, op0, s2=None, op1=None):
        if op1 is None:
            self.pr.S.op("dve", lambda e: e.tensor_scalar(out=o[0], in0=a[0], scalar1=s1, scalar2=None, op0=op0), reads=[a[1]], writes=[o[1]])
        else:
            self.pr.S.op("dve", lambda e: e.tensor_scalar(out=o[0], in0=a[0], scalar1=s1, scalar2=s2, op0=op0, op1=op1), reads=[a[1]], writes=[o[1]])
        return o

    def act(self, o, a, func, scale=1.0, bias=None, breads=()):
        if bias is None:
            self.pr.S.op("act", lambda e: e.activation(out=o[0], in_=a[0], func=func, scale=scale), reads=[a[1]], writes=[o[1]])
        else:
            self.pr.S.op("act", lambda e: e.activation(out=o[0], in_=a[0], func=func, scale=scale, bias=bias), reads=[a[1]] + list(breads), writes=[o[1]])
        return o

    def cmul(self, orr, oi, ar, ai, br, bi, t1, t2):
        self.tt(t1, ar, br, ALU.mult)
        self.tt(t2, ai, bi, ALU.mult)
        self.tt(orr, t1, t2, ALU.subtract)
        self.tt(t1, ar, bi, ALU.mult)
        self.tt(t2, ai, br, ALU.mult)
        self.tt(oi, t1, t2, ALU.add)

    def abar(self, are, aim, ldt, halfpi):
        n = self.new
        dt_ = self.act(n(), ldt, AF.Exp)
        x1 = self.tt(n(), are, dt_, ALU.mult)
        th = self.tt(n(), aim, dt_, ALU.mult)
        rho = self.act(n(), x1, AF.Exp, scale=1.0 / 32)
        sn = self.act(n(), th, AF.Sin, scale=1.0 / 32)
        cs = self.act(n(), th, AF.Sin, scale=1.0 / 32, bias=halfpi[0], breads=[halfpi[1]])
        er = self.tt(n(), rho, cs, ALU.mult)
        ei = self.tt(n(), rho, sn, ALU.mult)
        t1, t2, t3 = n(), n(), n()
        for _ in range(5):
            self.tt(t1, er, er, ALU.mult)
            self.tt(t2, ei, ei, ALU.mult)
            self.tt(t3, er, ei, ALU.mult)
            self.tt(er, t1, t2, ALU.subtract)
            self.ts(ei, t3, 2.0, ALU.mult)
        nr = self.ts(n(), er, -1.0, ALU.add)
        self.tt(t1, are, are, ALU.mult)
        self.tt(t2, aim, aim, ALU.mult)
        self.tt(t3, t1, t2, ALU.add)
        rd = n()
        self.pr.S.op("dve", lambda e: e.reciprocal(out=rd[0], in_=t3[0]), reads=[t3[1]], writes=[rd[1]])
        qr, qi = n(), n()
        self.tt(t1, nr, are, ALU.mult)
        self.tt(t2, ei, aim, ALU.mult)
        self.tt(t3, t1, t2, ALU.add)
        self.tt(qr, t3, rd, ALU.mult)
        self.tt(t1, ei, are, ALU.mult)
        self.tt(t2, nr, aim, ALU.mult)
        self.tt(t3, t1, t2, ALU.subtract)
        self.tt(qi, t3, rd, ALU.mult)
        return er, ei, qr, qi


def _s5_precompute(self, prm, dr):
    S = self.S
    self.A8r = self.sb("A8r", [P, 32], F32)
    self.A8i = self.sb("A8i", [P, 32], F32)
    self.t_A8 = T("A8")

    with self.phase():
        hp_t = self.sb("halfpi", [P, 1], F32)
        t_hp = T("halfpi")
        S.op("dve", lambda e: e.memset(hp_t[:], math.pi / 2), writes=[t_hp])
        halfpi = (hp_t[:], t_hp)
        ld = lambda name, shape: (lambda r: (r[0][:], r[1]))(self.load_small(name + "_s", prm[name], shape))
        es = Ew(self, [P, 32], "es")
        are, aim, ldt = ld("are_S", [P, 32]), ld("aim_S", [P, 32]), ld("ldt_S", [P, 32])
        ar, ai, qr, qi = es.abar(are, aim, ldt, halfpi)
        bre, bim = ld("bre_S", [P, 32, 16]), ld("bim_S", [P, 32, 16])
        cre, cim = ld("cre_S", [P, 32, 16]), ld("cim_S", [P, 32, 16])
        e3 = Ew(self, [P, 32, 16], "e3")
        bc = lambda v: (v[0].unsqueeze(2).to_broadcast([P, 32, 16]), v[1])
        Br, Bi, u1, u2 = e3.new(), e3.new(), e3.new(), e3.new()
        e3.cmul(Br, Bi, bc(qr), bc(qi), bre, bim, u1, u2)
        pw = [(es.new(), es.new()) for _ in range(9)]
        S.op("dve", lambda e: e.memset(pw[0][0][0], 1.0), writes=[pw[0][0][1]])
        S.op("dve", lambda e: e.memset(pw[0][1][0], 0.0), writes=[pw[0][1][1]])
        s1, s2 = es.new(), es.new()
        for k in range(8):
            es.cmul(pw[k + 1][0], pw[k + 1][1], pw[k][0], pw[k][1], ar, ai, s1, s2)
        S.op("dve", lambda e: e.tensor_copy(out=self.A8r[:], in_=pw[8][0][0]), reads=[pw[8][0][1]], writes=[self.t_A8])
        S.op("dve", lambda e: e.tensor_copy(out=self.A8i[:], in_=pw[8][1][0]), reads=[pw[8][1][1]], writes=[self.t_A8])
        def padded(name, dt):
            t = self.sb(name, [P, 32, 32], dt)
            tt = T(name)
            S.op("dve", lambda e: e.memset(t[:], 0.0), writes=[tt])
            return t, tt

        def to_pad(dst, t_dst, src):
            for m in range(2):
                S.op("dve", lambda e, m=m: e.tensor_copy(out=dst[m * 64:(m + 1) * 64, :, m * 16:(m + 1) * 16], in_=src[0][m * 64:(m + 1) * 64, :, :]),
                     reads=[src[1]], writes=[t_dst])
        Bpr, t_Bpr = padded("Bpr", BF16)
        Bpn, t_Bpn = padded("Bpn", BF16)
        to_pad(Bpr, t_Bpr, Br)
        nBi = e3.ts(e3.new(), Bi, -1.0, ALU.mult)
        to_pad(Bpn, t_Bpn, nBi)
        Cpr, t_Cpr = padded("Cpr", BF16)
        Cpi, t_Cpi = padded("Cpi", BF16)
        ident, t_ident = self.load_small("ident_s", prm["ident"], [P, P])
        dcol, t_dcol = self.load_small("dcol_s", prm["dcol"], [P, 8])
        Ck_r, Ck_i = e3.new(), e3.new()
        stg = self.sb("wc_stg", [P, 8, 3072], BF16)
        t_stg = T("wc_stg")
        S.op("dve", lambda e: e.memset(stg[:], 0.0), writes=[t_stg])
        f32blk = self.sb("f32blk", [P, P], F32)
        t_f32blk = T("f32blk")
        for k in range(9):
            e3.cmul(Ck_r, Ck_i, cre, cim, bc(pw[k][0]), bc(pw[k][1]), u1, u2)
            if k < 8:
                to_pad(Cpr, t_Cpr, Ck_r)
                to_pad(Cpi, t_Cpi, Ck_i)
                for c8 in range(8):
                    ps, t_ps = self.psum()
                    for q in range(4):
                        pair = c8 * 4 + q
                        self.mm_group(ps[q * 32:(q + 1) * 32, q * 32:(q + 1) * 32], t_ps,
                                      [(Bpr[:, pair, :], Cpr[:, pair, :]), (Bpn[:, pair, :], Cpi[:, pair, :])],
                                      reads=[t_Bpr, t_Bpn, t_Cpr, t_Cpi], tp=(0, q * 32))
                    dstv = stg[:, c8, k * 128:(k + 1) * 128]
                    for q in range(4):
                        sl = slice(q * 32, (q + 1) * 32)
                        if k == 0:
                            S.op("dve", lambda e, sl=sl, c8=c8, ps=ps, dstv=dstv: e.scalar_tensor_tensor(
                                out=dstv[sl, sl], in0=ident[sl, sl], scalar=dcol[sl, c8:c8 + 1], in1=ps[sl, sl], op0=ALU.mult, op1=ALU.add),
                                reads=[t_ps, t_ident, t_dcol], writes=[t_stg])
                        else:
                            S.op("act", lambda e, sl=sl, ps=ps, dstv=dstv: e.activation(out=dstv[sl, sl], in_=ps[sl, sl], func=AF.Copy),
                                 reads=[t_ps], writes=[t_stg])
            if k >= 1:
                r = k - 1
                wov = stg[:, :, 1024:3072].rearrange("p c (q r i o) -> p c q r i o", q=4, r=8, i=2)
                for m in range(2):
                    ms = slice(m * 64, (m + 1) * 64)
                    S.op("dve", lambda e, ms=ms, m=m, r=r: e.tensor_copy(
                        out=wov[ms, :, :, r, 0, m * 16:(m + 1) * 16], in_=Ck_r[0][ms, :, :].rearrange("p (c q) h -> p c q h", q=4)),
                        reads=[Ck_r[1]], writes=[t_stg])
                    S.op("dve", lambda e, ms=ms, m=m, r=r: e.tensor_scalar(
                        out=wov[ms, :, :, r, 1, m * 16:(m + 1) * 16], in0=Ck_i[0][ms, :, :].rearrange("p (c q) h -> p c q h", q=4),
                        scalar1=-1.0, scalar2=None, op0=ALU.mult),
                        reads=[Ck_i[1]], writes=[t_stg])
        S.dma("sp", [lambda e: e.dma_start(out=dr["wc"], in_=stg[:])], t_stg, reads=[t_stg], writes=[dr["t_wc"]])
    with self.phase():
        hp_t = self.sb("halfpi2", [P, 1], F32)
        t_hp = T("halfpi2")
        S.op("dve", lambda e: e.memset(hp_t[:], math.pi / 2), writes=[t_hp])
        halfpi = (hp_t[:], t_hp)
        ld = lambda name, shape: (lambda r: (r[0][:], r[1]))(self.load_small(name + "_s", prm[name], shape))
        et = Ew(self, [P, 8, 64], "et")
        are, aim, ldt = ld("are_T", [P, 8, 64]), ld("aim_T", [P, 8, 64]), ld("ldt_T", [P, 8, 64])
        ar, ai, qr, qi = et.abar(are, aim, ldt, halfpi)
        bre, bim = ld("bre_T", [P, 8, 64]), ld("bim_T", [P, 8, 64])
        QBr, QBi, u1, u2 = et.new(), et.new(), et.new(), et.new()
        et.cmul(QBr, QBi, qr, qi, bre, bim, u1, u2)
        m01, t_m01 = self.load_small("mask01_s", prm["mask01"], [P, 2])
        wzs = self.sb("wz_stg", [P, 8, 8, 2, 128], BF16)
        t_wzs = T("wz_stg")
        pr_, pi_ = et.new(), et.new()
        S.op("dve", lambda e: e.memset(pr_[0], 1.0), writes=[pr_[1]])
        S.op("dve", lambda e: e.memset(pi_[0], 0.0), writes=[pi_[1]])
        Wr, Wi, nr_, ni_ = et.new(), et.new(), et.new(), et.new()
        for k in range(8):
            s = 7 - k
            et.cmul(Wr, Wi, pr_, pi_, QBr, QBi, u1, u2)
            for ri, W in ((0, Wr), (1, Wi)):
                for m in range(2):
                    S.op("dve", lambda e, s=s, ri=ri, m=m, W=W: e.tensor_scalar(out=wzs[:, :, s, ri, m * 64:(m + 1) * 64], in0=W[0],
                                                                                 scalar1=m01[:, m:m + 1], scalar2=None, op0=ALU.mult),
                         reads=[W[1], t_m01], writes=[t_wzs])
            if k < 7:
                et.cmul(nr_, ni_, pr_, pi_, ar, ai, u1, u2)
                S.op("dve", lambda e: e.tensor_copy(out=pr_[0], in_=nr_[0]), reads=[nr_[1]], writes=[pr_[1]])
                S.op("dve", lambda e: e.tensor_copy(out=pi_[0], in_=ni_[0]), reads=[ni_[1]], writes=[pi_[1]])
        S.dma("sp", [lambda e: e.dma_start(out=dr["wz"], in_=wzs[:].rearrange("p c s i o -> p c (s i o)"))], t_wzs, reads=[t_wzs], writes=[dr["t_wz"]])


Prog.s5_precompute = _s5_precompute


def ev_host_params(z, i):
    f = lambda k: np.asarray(z[k][i], np.float32)
    a_re, a_im, ldt = f("s5_a_re"), f("s5_a_im"), f("s5_log_dt")
    b_re, b_im = f("s5_b_re"), f("s5_b_im")
    c_re, c_im = f("s5_c_re"), f("s5_c_im")
    def S2(v):
        return np.ascontiguousarray(v.reshape(32, 2, 64).transpose(1, 2, 0).reshape(128, 32))
    def S3(v):
        return np.ascontiguousarray(v.reshape(32, 2, 64, 16).transpose(1, 2, 0, 3).reshape(128, 32, 16))
    def T2(v):
        w = v.reshape(8, 4, 2, 1, 64)
        w = np.broadcast_to(w, (8, 4, 2, 16, 64)).transpose(1, 2, 3, 0, 4).reshape(128, 8, 64)
        return np.ascontiguousarray(w)
    def T3(v):
        w = v.reshape(8, 4, 2, 64, 16).transpose(1, 2, 4, 0, 3).reshape(128, 8, 64)
        return np.ascontiguousarray(w)
    ldt2 = np.broadcast_to(ldt[:, None], (64, 64))
    out = dict(are_S=S2(a_re), aim_S=S2(a_im), ldt_S=S2(ldt2), bre_S=S3(b_re), bim_S=S3(b_im),
               cre_S=S3(c_re.transpose(0, 2, 1)), cim_S=S3(c_im.transpose(0, 2, 1)),
               are_T=T2(a_re), aim_T=T2(a_im), ldt_T=T2(ldt2), bre_T=T3(b_re), bim_T=T3(b_im))
    mask01 = np.zeros((128, 2), np.float32)
    for p_ in range(128):
        mask01[p_, (p_ // 16) % 2] = 1.0
    out["mask01"] = mask01
    out["ident"] = np.eye(128, dtype=np.float32)
    out["dcol"] = pk(f("s5_d"))
    glu = f("s5_glu_w")
    gb = np.zeros((128, 8, 128), np.float32)
    for g in range(64):
        c8, gg = g // 8, g % 8
        gb[gg * 16:(gg + 1) * 16, c8, gg * 16:(gg + 1) * 16] = glu[g]
    out["gblk"] = gb
    out["convw"] = pk(f("ev_conv_w"))
    return out


def _even_mixer(self, h_in, h_out, hhalo_d, Win, Wout, prm, dr, gain, t_gain, gcol, flag, hrd, xch):
    S = self.S
    self.s5_precompute(prm, dr)
    self.alloc_A(24)
    cw, t_cw = self.load_small("convw_s", prm["convw"], [P, 3, 8])
    gblk = self.sb("gblk", [P, 8, P], BF16)
    t_gblk = T("gblk")
    S.dma("pool", [lambda e: e.dma_start(out=gblk[:], in_=prm["gblk"])], t_gblk, writes=[t_gblk])
    Xb = [self.sb("Xb%d" % i, [P, 32, 65], F32) for i in range(2)]
    t_Xb = [T("Xb0"), T("Xb1")]
    X16 = [self.sb("X16_%d" % i, [P, 32, 64], BF16) for i in range(2)]
    t_X16 = [T("X16_0"), T("X16_1")]
    X0 = [self.sb("X0_%d" % i, [P, 32, NT], F32) for i in range(2)]
    t_X0 = [T("X0_0"), T("X0_1")]
    sstg = [self.sb("sstg%d" % i, [P, 32], F32) for i in range(2)]
    t_sstg = [T("sstg0"), T("sstg1")]
    st = [self.sb("st%d" % i, [P, 32], F32) for i in range(4)]
    t_st = [T("st%d" % i) for i in range(4)]
    prod = [self.sb("prod%d" % i, [P, TT + 2], F32) for i in range(2)]
    t_prod = [T("prod0"), T("prod1")]
    phalo = self.sb("phalo", [P, 8, 2], F32)
    t_phalo = [T("phalo%d" % c) for c in range(8)]
    gcs = [self.sb("gcs%d" % i, [P, TT], F32) for i in range(2)]
    t_gcs = [T("gcs0"), T("gcs1")]
    cv = [self.sb("cv%d" % i, [P, TT], F32) for i in range(2)]
    t_cv = [T("cv0"), T("cv1")]
    hhT = self.sb("ehhT", [P, KC, 2], F32)
    t_hhT = [T("ehhT%d" % k) for k in range(KC)]
    hhn = self.sb("ehhn", [P, KC, 2], BF16)
    t_hhn = [T("ehhn%d" % k) for k in range(KC)]
    yx, t_yx = gcs, t_gcs
    yg, t_yg = cv, t_cv
    yb = [self.sb("yb%d" % i, [P, TT], BF16) for i in range(2)]
    t_yb = [T("yb0"), T("yb1")]
    NJ = TT // 8
    UO = 16
    tt_ = lambda o, t_o, a, b_, op, rd: S.op("dve", lambda e: e.tensor_tensor(out=o, in0=a, in1=b_, op=op), reads=rd, writes=[t_o])

    def zpart(it):
        for c8 in range(8):
            wz, t_wz = self.load_wz(dr, c8)
            uv = self.bufA[:, UO + c8, :].rearrange("p (j s) -> p j s", s=8)
            for q in range(4):
                pair = c8 * 4 + q
                ps, t_ps = self.psum()
                for ri in range(2):
                    n = 8
                    for s in range(8):
                        self.S.op("pe", lambda e, ps=ps, q=q, s=s, ri=ri, wz=wz, uv=uv: e.matmul(
                            ps[:, ri * NJ:(ri + 1) * NJ], wz[q * 32:(q + 1) * 32, s, ri, :], uv[q * 32:(q + 1) * 32, :, s],
                            start=(s == 0), stop=(s == n - 1), tile_position=(q * 32, 0)),
                            reads=[t_wz, self.t_A[UO + c8]], writes=[t_ps], inc=(s == n - 1))
                    S.op("act", lambda e, ps=ps, ri=ri, pair=pair: e.activation(out=Xb[ri][:, pair, 1:NJ + 1], in_=ps[:, ri * NJ:(ri + 1) * NJ], func=AF.Copy),
                         reads=[t_ps], writes=[t_Xb[ri]])
    def scan_steps(j0, j1):
        for j in range(j0, j1):
            xr, xi = Xb[0][:, :, j], Xb[1][:, :, j]
            nr_, ni_ = Xb[0][:, :, j + 1], Xb[1][:, :, j + 1]
            tt_ = lambda o, t_o, a, b_, op, rd: S.op("dve", lambda e: e.tensor_tensor(out=o, in0=a, in1=b_, op=op), reads=rd, writes=[t_o])
            tt_(st[0][:], t_st[0], self.A8r[:], xr, ALU.mult, [self.t_A8, t_Xb[0]])
            tt_(st[1][:], t_st[1], self.A8i[:], xi, ALU.mult, [self.t_A8, t_Xb[1]])
            tt_(st[2][:], t_st[2], self.A8i[:], xr, ALU.mult, [self.t_A8, t_Xb[0]])
            tt_(st[3][:], t_st[3], self.A8r[:], xi, ALU.mult, [self.t_A8, t_Xb[1]])
            tt_(st[0][:], t_st[0], st[0][:], st[1][:], ALU.subtract, [t_st[0], t_st[1]])
            tt_(st[2][:], t_st[2], st[2][:], st[3][:], ALU.add, [t_st[2], t_st[3]])
            tt_(nr_, t_Xb[0], nr_, st[0][:], ALU.add, [t_Xb[0], t_st[0]])
            tt_(ni_, t_Xb[1], ni_, st[2][:], ALU.add, [t_Xb[1], t_st[2]])
    for ri in range(2):
        S.op("dve", lambda e, ri=ri: e.memset(Xb[ri][:, :, 0], 0.0), writes=[t_Xb[ri]])
    self.load_h(h_in, 0)
    self.rmsnorm(self.hT, self.t_hT, gain, t_gain, gcol, self.hn, self.t_hn, TT)
    self.linear_to(Win, 3072, 1024, self.t_hn, self.hn, self.bufA, UO, self.t_A)
    for it in range(NT):
        for ri in range(2):
            S.op("act", lambda e, ri=ri, it=it: e.activation(out=X0[ri][:, :, it], in_=Xb[ri][:, :, 0], func=AF.Copy), reads=[t_Xb[ri]], writes=[t_X0[ri]])
        zpart(it)
        if it < NT - 1:
            self.load_h(h_in, (it + 1) * TT)
            scan_steps(0, 16)
            self.rmsnorm(self.hT, self.t_hT, gain, t_gain, gcol, self.hn, self.t_hn, TT)
            for b4 in range(4):
                self.linear_to(Win, 3072 + b4 * 256, 256, self.t_hn, self.hn, self.bufA, UO + 2 * b4, self.t_A)
                scan_steps(16 + 12 * b4, 16 + 12 * (b4 + 1))
        else:
            scan_steps(0, NJ)
        for ri in range(2):
            if it < NT - 1:
                S.op("act", lambda e, ri=ri: e.activation(out=Xb[ri][:, :, 0], in_=Xb[ri][:, :, NJ], func=AF.Copy), reads=[t_Xb[ri]], writes=[t_Xb[ri]])
            else:
                S.op("act", lambda e, ri=ri: e.activation(out=sstg[ri][:], in_=Xb[ri][:, :, NJ], func=AF.Copy), reads=[t_Xb[ri]], writes=[t_sstg[ri]])
    for ri in range(2):
        S.dma("sp", [lambda e, ri=ri: e.dma_start(out=xch["snd_s"][ri * P:(ri + 1) * P, :], in_=sstg[ri][:])], t_sstg[ri], reads=[t_sstg[ri]], writes=[xch["t_snd_s"]])
    S.cc("pool", lambda e: e.collective_compute("AllGather", ALU.bypass, replica_groups=xch["rg"], ins=[xch["snd_s"]], outs=[xch["rcv_s"]]),
         reads=[xch["t_snd_s"]], writes=[xch["t_rcv_s"]])
    Dr = [self.sb("Dst%d" % i, [P, 32], F32) for i in range(2)]
    t_Dr = [T("Dst0"), T("Dst1")]
    for ri in range(2):
        S.dma("sp", [lambda e, ri=ri: e.dma_start(out=Dr[ri][:], in_=xch["rcv_s"][ri * P:(ri + 1) * P, :])], t_Dr[ri], reads=[xch["t_rcv_s"]], writes=[t_Dr[ri]])
        S.op("dve", lambda e, ri=ri: e.tensor_scalar(out=Dr[ri][:], in0=Dr[ri][:], scalar1=flag[0][:, 0:1], scalar2=None, op0=ALU.mult),
             reads=[t_Dr[ri], flag[1]], writes=[t_Dr[ri]])
    self.dbg("m_A8r", self.A8r[:], [self.t_A8], [P, 32])
    self.dbg("m_Dr0", Dr[0][:], [t_Dr[0]], [P, 32])
    self.dbg("m_X0", X0[0][:].rearrange("p a b -> p (a b)"), [t_X0[0]], [P, 32 * NT])
    es = Ew(self, [P, 32], "e5")
    Ar, Ai = es.new(), es.new()
    S.op("dve", lambda e: e.tensor_copy(out=Ar[0], in_=self.A8r[:]), reads=[self.t_A8], writes=[Ar[1]])
    S.op("dve", lambda e: e.tensor_copy(out=Ai[0], in_=self.A8i[:]), reads=[self.t_A8], writes=[Ai[1]])
    q1, q2, q3 = es.new(), es.new(), es.new()
    for _ in range(6):
        es.tt(q1, Ar, Ar, ALU.mult)
        es.tt(q2, Ai, Ai, ALU.mult)
        es.tt(q3, Ar, Ai, ALU.mult)
        es.tt(Ar, q1, q2, ALU.subtract)
        es.ts(Ai, q3, 2.0, ALU.mult)
    D0, D1 = (Dr[0][:], t_Dr[0]), (Dr[1][:], t_Dr[1])
    n0, n1 = es.new(), es.new()
    for it in range(NT):
        for ri, Dv in ((0, D0), (1, D1)):
            S.op("dve", lambda e, ri=ri, it=it, Dv=Dv: e.tensor_tensor(out=X0[ri][:, :, it], in0=X0[ri][:, :, it], in1=Dv[0], op=ALU.add),
                 reads=[t_X0[ri], Dv[1]], writes=[t_X0[ri]])
        if it < NT - 1:
            es.cmul(n0, n1, D0, D1, Ar, Ai, q1, q2)
            S.op("dve", lambda e: e.tensor_copy(out=D0[0], in_=n0[0]), reads=[n0[1]], writes=[D0[1]])
            S.op("dve", lambda e: e.tensor_copy(out=D1[0], in_=n1[0]), reads=[n1[1]], writes=[D1[1]])
    self.halo_load(hhT, t_hhT, hhalo_d, 2, flag, hrd)
    self.rmsnorm(hhT, t_hhT, gain, t_gain, gcol, hhn, t_hhn, 2)
    for it in range(NT):
        tok0 = it * TT
        self.load_h(h_in, tok0)
        self.rmsnorm(self.hT, self.t_hT, gain, t_gain, gcol, self.hn, self.t_hn, TT)
        self.linear_to(Win, 3072, 1024, self.t_hn, self.hn, self.bufA, UO, self.t_A)
        for ri in range(2):
            S.op("act", lambda e, ri=ri, it=it: e.activation(out=Xb[ri][:, :, 0], in_=X0[ri][:, :, it], func=AF.Copy), reads=[t_X0[ri]], writes=[t_Xb[ri]])
        zpart(it)
        for blk in range(4):
            wblk = lambda col0: self.load_w(Win[:, col0 + blk * 256:col0 + (blk + 1) * 256].rearrange("(k p) m -> p k m", p=P), KC, 256)
            wgb, t_wgb = wblk(0)
            wgc, t_wgc = wblk(1024)
            wxa, t_wxa = wblk(2048)
            for m in range(2):
                c = blk * 2 + m
                b = c % 2
                ms = slice(m * P, (m + 1) * P)
                pd, t_pd = prod[b], t_prod[b]
                if it == 0:
                    ph1, t_ph1 = self.psum()
                    self.mm_group(ph1[:, 0:2], t_ph1, [(wgc[:, k, ms], hhn[:, k, :]) for k in range(KC)], reads=[t_wgc] + t_hhn)
                    ph2, t_ph2 = self.psum()
                    self.mm_group(ph2[:, 0:2], t_ph2, [(wxa[:, k, ms], hhn[:, k, :]) for k in range(KC)], reads=[t_wxa] + t_hhn)
                    S.op("act", lambda e, ph1=ph1, b=b: e.activation(out=gcs[b][:, 0:2], in_=ph1[:, 0:2], func=AF.Copy), reads=[t_ph1], writes=[t_gcs[b]])
                    S.op("dve", lambda e, ph2=ph2, b=b, pd=pd: e.tensor_tensor(out=pd[:, 0:2], in0=gcs[b][:, 0:2], in1=ph2[:, 0:2], op=ALU.mult),
                         reads=[t_ph2, t_gcs[b]], writes=[t_pd])
                else:
                    S.op("act", lambda e, c=c, pd=pd: e.activation(out=pd[:, 0:2], in_=phalo[:, c, :], func=AF.Copy), reads=[t_phalo[c]], writes=[t_pd])
                pgc, t_pgc = self.psum()
                self.mm_group(pgc[:, 0:TT], t_pgc, [(wgc[:, k, ms], self.hn[:, k, :]) for k in range(KC)], reads=[t_wgc] + self.t_hn)
                pxa, t_pxa = self.psum()
                self.mm_group(pxa[:, 0:TT], t_pxa, [(wxa[:, k, ms], self.hn[:, k, :]) for k in range(KC)], reads=[t_wxa] + self.t_hn)
                pgb, t_pgb = self.psum()
                self.mm_group(pgb[:, 0:TT], t_pgb, [(wgb[:, k, ms], self.hn[:, k, :]) for k in range(KC)], reads=[t_wgb] + self.t_hn)
                S.op("act", lambda e, pgc=pgc, b=b: e.activation(out=gcs[b][:], in_=pgc[:, 0:TT], func=AF.Copy), reads=[t_pgc], writes=[t_gcs[b]])
                S.op("dve", lambda e, pxa=pxa, b=b, pd=pd: e.tensor_tensor(out=pd[:, 2:TT + 2], in0=gcs[b][:], in1=pxa[:, 0:TT], op=ALU.mult),
                     reads=[t_pxa, t_gcs[b]], writes=[t_pd])
                if it < NT - 1:
                    S.op("act", lambda e, c=c, pd=pd: e.activation(out=phalo[:, c, :], in_=pd[:, TT:TT + 2], func=AF.Copy), reads=[t_pd], writes=[t_phalo[c]])
                S.op("act", lambda e, c=c, b=b, pd=pd: e.activation(out=cv[b][:], in_=pd[:, 2:TT + 2], func=AF.Copy, scale=cw[:, 2, c:c + 1]),
                     reads=[t_pd, t_cw], writes=[t_cv[b]])
                S.op("dve", lambda e, c=c, b=b, pd=pd: e.scalar_tensor_tensor(out=cv[b][:], in0=pd[:, 1:TT + 1], scalar=cw[:, 1, c:c + 1], in1=cv[b][:],
                                                                           op0=ALU.mult, op1=ALU.add), reads=[t_pd, t_cw, t_cv[b]], writes=[t_cv[b]])
                S.op("dve", lambda e, c=c, b=b, pd=pd: e.scalar_tensor_tensor(out=cv[b][:], in0=pd[:, 0:TT], scalar=cw[:, 0, c:c + 1], in1=cv[b][:],
                                                                           op0=ALU.mult, op1=ALU.add), reads=[t_pd, t_cw, t_cv[b]], writes=[t_cv[b]])
                S.op("dve", lambda e, c=c, b=b, pgb=pgb: e.tensor_tensor(out=self.bufA[:, c, :], in0=cv[b][:], in1=pgb[:, 0:TT], op=ALU.mult),
                     reads=[t_cv[b], t_pgb], writes=[self.t_A[c]])
                scan_steps(c * (NJ // 8), (c + 1) * (NJ // 8))
        for ri in range(2):
            S.op("act", lambda e, ri=ri: e.activation(out=X16[ri][:], in_=Xb[ri][:, :, 0:NJ], func=AF.Copy), reads=[t_Xb[ri]], writes=[t_X16[ri]])
        for c8 in range(8):
            wc, t_wc = self.load_wc(dr, c8)
            fir = wc[:, 0:1024].rearrange("p (k o) -> p k o", k=8)
            wo = wc[:, 1024:3072].rearrange("p (q r i o) -> p q r i o", q=4, r=8, i=2)
            ut = self.bufA[:, UO + c8, :]
            uv = ut.rearrange("p (j s) -> p j s", s=8)
            py, t_py = self.psum()
            pyv = py[:, 0:TT].rearrange("p (j s) -> p j s", s=8)
            rd = [t_wc, self.t_A[UO + c8]] + t_X16
            S.op("pe", lambda e, py=py, fir=fir, ut=ut: e.matmul(py[:, 0:TT], fir[:, 0, :], ut, start=True, stop=False), reads=rd, writes=[t_py], inc=False)
            for k in range(1, 8):
                S.op("pe", lambda e, pyv=pyv, fir=fir, uv=uv, k=k: e.matmul(pyv[:, :, k:8], fir[:, k, :], uv[:, :, 0:8 - k], start=False, stop=False),
                     reads=rd, writes=[t_py], inc=False)
            for q in range(4):
                for r in range(8):
                    for ri in range(2):
                        last = (q == 3 and r == 7 and ri == 1)
                        S.op("pe", lambda e, pyv=pyv, wo=wo, q=q, r=r, ri=ri, c8=c8, last=last: e.matmul(
                            pyv[q * 32:(q + 1) * 32, :, r], wo[:, q, r, ri, :], X16[ri][:, c8 * 4 + q, :], start=False, stop=last,
                            tile_position=(0, q * 32)), reads=rd, writes=[t_py], inc=last)
            b = c8 % 2
            S.op("act", lambda e, py=py, b=b: e.activation(out=yx[b][:], in_=py[:, 0:TT], func=AF.Square), reads=[t_py], writes=[t_yx[b]])
            S.op("dve", lambda e, b=b: e.tensor_scalar(out=yx[b][:], in0=yx[b][:], scalar1=0.044715, scalar2=1.0, op0=ALU.mult, op1=ALU.add),
                 reads=[t_yx[b]], writes=[t_yx[b]])
            S.op("dve", lambda e, py=py, b=b: e.tensor_tensor(out=yx[b][:], in0=yx[b][:], in1=py[:, 0:TT], op=ALU.mult), reads=[t_yx[b], t_py], writes=[t_yx[b]])
            S.op("act", lambda e, b=b: e.activation(out=yx[b][:], in_=yx[b][:], func=AF.Sigmoid, scale=1.5957691216057308), reads=[t_yx[b]], writes=[t_yx[b]])
            S.op("dve", lambda e, py=py, b=b: e.tensor_tensor(out=yg[b][:], in0=yx[b][:], in1=py[:, 0:TT], op=ALU.mult), reads=[t_yx[b], t_py], writes=[t_yg[b]])
            S.op("act", lambda e, b=b: e.activation(out=yb[b][:], in_=yg[b][:], func=AF.Copy), reads=[t_yg[b]], writes=[t_yb[b]])
            pg, t_pg = self.psum()
            self.mm_group(pg[:, 0:TT], t_pg, [(gblk[:, c8, :], yb[b][:])], reads=[t_gblk, t_yb[b]])
            S.op("act", lambda e, pg=pg, b=b: e.activation(out=yx[b][:], in_=pg[:, 0:TT], func=AF.Sigmoid), reads=[t_pg], writes=[t_yx[b]])
            S.op("dve", lambda e, b=b, c8=c8: e.tensor_tensor(out=self.bufA[:, 8 + c8, :], in0=yg[b][:], in1=yx[b][:], op=ALU.mult),
                 reads=[t_yg[b], t_yx[b]], writes=[self.t_A[8 + c8]])
        if it == 0:
            self.dbg("m_ya0", self.bufA[:, 0, :], [self.t_A[0]], [P, TT], BF16)
            self.dbg("m_ys0", self.bufA[:, 8, :], [self.t_A[8]], [P, TT], BF16)
            self.dbg("m_u0", self.bufA[:, 16, :], [self.t_A[16]], [P, TT], BF16)
            self.dbg("m_X16", X16[0][:].rearrange("p a b -> p (a b)"), [t_X16[0]], [P, 32 * 64], BF16)
        self.linear_resid(Wout, KC, self.t_A[0:KC], self.bufA, 0)
        if it == 0:
            self.dbg("m_hT0", self.hT[:, 0, :], [self.t_hT[0]], [P, TT])
        self.store_h(h_out, tok0)


def _load_wz(self, dr, c8):
    i = self.wi
    self.wi = (i + 1) % NWB
    view = self.wb[i][:, 0:2048].rearrange("p (s i o) -> p s i o", s=8, i=2)
    t = self.t_wb[i]
    self.S.dma("sp", [lambda e: e.dma_start(out=self.wb[i][:, 0:2048], in_=dr["wz"][:, c8, :])], t, reads=[dr["t_wz"]], writes=[t])
    return view, t


def _load_wc(self, dr, c8):
    i = self.wi
    self.wi = (i + 1) % NWB
    view = self.wb[i][:, 0:3072]
    t = self.t_wb[i]
    self.S.dma("sp", [lambda e: e.dma_start(out=self.wb[i][:, 0:3072], in_=dr["wc"][:, c8, :])], t, reads=[dr["t_wc"]], writes=[t])
    return view, t


Prog.even_mixer = _even_mixer
Prog.load_wz = _load_wz
Prog.load_wc = _load_wc

EV_SMALL = dict(are_S=[P, 32], aim_S=[P, 32], ldt_S=[P, 32], bre_S=[P, 32, 16], bim_S=[P, 32, 16], cre_S=[P, 32, 16], cim_S=[P, 32, 16],
                are_T=[P, 8, 64], aim_T=[P, 8, 64], ldt_T=[P, 8, 64], bre_T=[P, 8, 64], bim_T=[P, 8, 64], mask01=[P, 2], ident=[P, P],
                dcol=[P, 8], gblk=[P, 8, P], convw=[P, 3, 8])


def build_even_launch():
    nc = bass.Bass("TRN2", target_bir_lowering=False)
    dt = lambda name, shape, kind="ExternalInput": nc.dram_tensor(name, shape, F32, kind=kind).ap()
    h_in = dt("h_in", [D, NTOK]); h_out = dt("h_out", [D, NTOK], "ExternalOutput")
    hhalo = dt("hhalo", [D, 2]); gm = dt("gm", [P, KC])
    s5in = dt("s5in", [2, P, 32]); s5out = dt("s5out", [2, P, 32], "ExternalOutput")
    Win = dt("Win", [D, 4096]); Wout = dt("Wout", [D, D])
    prm = {k: dt(k, shp) for k, shp in EV_SMALL.items()}
    dr = dict(wz=nc.dram_tensor("wz_scr", [P, 8, 2048], BF16, kind="Internal").ap(), t_wz=T("wz_scr"),
              wc=nc.dram_tensor("wc_scr", [P, 8, 3072], BF16, kind="Internal").ap(), t_wc=T("wc_scr"))
    with ExitStack() as st:
        pr = Prog(nc, st)
        g, t_g = pr.load_small("gm_s", gm, [P, KC])
        with pr.phase():
            t_Xb = pr.even_mixer(h_in, h_out, hhalo, s5in, s5out, Win, Wout, prm, dr, g, t_g, 0)
            pr.S.wait_all("sp", t_Xb)
        pr.finish(pr.t_hT)
    return nc


def build_final_launch():
    nc = bass.Bass("TRN2", target_bir_lowering=False)
    dt = lambda name, shape, kind="ExternalInput": nc.dram_tensor(name, shape, F32, kind=kind).ap()
    h_in = dt("h_in", [D, NTOK]); out = dt("out", [D, NTOK], "ExternalOutput"); gfin = dt("gfin", [P, KC])
    with ExitStack() as st:
        pr = Prog(nc, st)
        g, t_g = pr.load_small("gfin_s", gfin, [P, KC])
        with pr.phase():
            pr.final_norm(h_in, out, g, t_g, 0)
            pr.S.wait_all("sp", pr.t_fo)
        pr.finish([])
    return nc


NCORES = 8


def _run(nc, maps):
    res = run_bass_kernel_spmd(nc, maps, core_ids=list(range(NCORES)))
    return res.results


def kernel_unfused(**inp):
    z = {k: np.asarray(v) for k, v in inp.items()}
    x = z["x"].astype(np.float32)
    B = x.shape[0]
    cores = [(b, hf) for b in range(B) for hf in range(2)]
    h = [np.ascontiguousarray(x[b, hf * NTOK:(hf + 1) * NTOK, :].T) for (b, hf) in cores]

    def halo(w):
        out = []
        for ci, (b, hf) in enumerate(cores):
            if hf == 0:
                out.append(np.zeros((D, w), np.float32))
            else:
                out.append(np.ascontiguousarray(h[ci - 1][:, -w:]))
        return out
    oh, maskT = swa_consts()
    nc_x = build_xattn_launch()
    nc_f = build_ffn_launch()
    nc_e = build_even_launch()
    nc_s = build_swa_launch()
    for l in range(4):
        i = l // 2
        if l % 2 == 0:
            hp = ev_host_params(z, i)
            hal = halo(2)
            s5 = [np.zeros((2, P, 32), np.float32) for _ in cores]
            for rep in range(2):
                maps = [dict(h_in=h[ci], hhalo=hal[ci], gm=pk(z["norm_mix"][l]), s5in=s5[ci], Win=z["ev_w_in"][i], Wout=z["ev_w_out"][i], **hp)
                        for ci in range(len(cores))]
                r = _run(nc_e, maps)
                if rep == 0:
                    s5 = [np.zeros((2, P, 32), np.float32) if hf == 0 else r[ci - 1]["s5out"] for ci, (b, hf) in enumerate(cores)]
            h = [r[ci]["h_out"] for ci in range(len(cores))]
        else:
            hp = swa_host_params(z, i)
            hal = halo(128)
            maps = [dict(h_in=h[ci], hhalo=hal[ci], hmask=(np.full((P, 1), -30000.0, np.float32) if hf == 0 else np.zeros((P, 1), np.float32)),
                         gm=pk(z["norm_mix"][l]), oh=oh, maskT=maskT, relb=hp["relb"], sinkrow=hp["sinkrow"], bq=hp["bq"], bkd=hp["bkd"], bvd=hp["bvd"],
                         Wq=hp["Wq"], Wkd=hp["Wkd"], Wvd=hp["Wvd"], Wo=z["od_w_out"][i]) for ci, (b, hf) in enumerate(cores)]
            r = _run(nc_s, maps)
            h = [r[ci]["h_out"] for ci in range(len(cores))]
        maps = [dict(h_in=h[ci], memT=np.ascontiguousarray(z["mem"][b].T), gmem=pk(z["norm_mem"]), gx=pk(z["norm_xattn"][l]),
                     Wq=z["xa_w_q"][l], Wkv=z["xa_w_kv"][l], Wo=z["xa_w_o"][l]) for ci, (b, hf) in enumerate(cores)]
        r = _run(nc_x, maps)
        h = [r[ci]["h_out"] for ci in range(len(cores))]
        hal = halo(2)
        maps = [dict(h_in=h[ci], hhalo=hal[ci], gf=pk(z["norm_ffn"][l]), cw=pk(z["ff_conv_w"][l]), cb=pk(z["ff_conv_b"][l]),
                     Wg=z["ff_w_gate"][l], Wu=z["ff_w_up"][l], Wd=z["ff_w_down"][l]) for ci in range(len(cores))]
        r = _run(nc_f, maps)
        h = [r[ci]["h_out"] for ci in range(len(cores))]
    nc_n = build_final_launch()
    r = _run(nc_n, [dict(h_in=h[ci], gfin=pk(z["norm_final"])) for ci in range(len(cores))])
    out = np.empty((B, 2 * NTOK, D), np.float32)
    for ci, (b, hf) in enumerate(cores):
        out[b, hf * NTOK:(hf + 1) * NTOK, :] = r[ci]["out"].T
    return out


def kernel(**inp):
    return kernel_unfused(**inp)


RG_PAIRS = [[0, 1], [2, 3], [4, 5], [6, 7]]
SWA_SMALL = dict(bq=[P, KC], bkd=[P, 4], bvd=[P, 512], sinkrow=[1, 32])


def build_fused(depth=4, final=True, stop_after=None):
    nc = bass.Bass("TRN2", target_bir_lowering=False)
    n_ev = (depth + 1) // 2
    n_od = depth // 2
    global LAST_INPUT_NAMES
    LAST_INPUT_NAMES = []

    def dt(name, shape, kind="ExternalInput"):
        if kind == "ExternalInput":
            LAST_INPUT_NAMES.append(name)
        return nc.dram_tensor(name, list(shape), F32, kind=kind).ap()
    xT = dt("xT", [D, NTOK]); xhalo = dt("xhalo", [D, 128]); flag_d = dt("flag", [P, 1]); hmask_d = dt("hmask", [P, 1])
    memT = dt("memT", [D, N_MEM]); gmem = dt("gmem", [P, KC]); gfin = dt("gfin", [P, KC])
    gmix = dt("gmix", [P, 4, KC]); gxa = dt("gxa", [P, 4, KC]); gff = dt("gff", [P, 4, KC])
    oh_d = dt("oh", [P, 32, 2, 128]); mask_d = dt("maskT", [P, 2, 128]); relb_d = dt("relb", [P, 32, 32])
    ev = []
    for i in range(n_ev):
        e = dict(Win=dt("ev_w_in%d" % i, [D, 4096]), Wout=dt("ev_w_out%d" % i, [D, D]))
        e["prm"] = {k: dt("ev%d_%s" % (i, k), shp) for k, shp in EV_SMALL.items()}
        e["dr"] = dict(wz=nc.dram_tensor("wz_scr%d" % i, [P, 8, 2048], BF16, kind="Internal").ap(), t_wz=T("wz_scr"),
                       wc=nc.dram_tensor("wc_scr%d" % i, [P, 8, 3072], BF16, kind="Internal").ap(), t_wc=T("wc_scr"))
        ev.append(e)
    od = []
    for i in range(n_od):
        o = dict(Wq=dt("od_wq%d" % i, [D, D]), Wkd=dt("od_wkd%d" % i, [D, 512]), Wvd=dt("od_wvd%d" % i, [D, 512]), Wo=dt("od_wo%d" % i, [D, D]))
        o["prm"] = {k: dt("od%d_%s" % (i, k), shp) for k, shp in SWA_SMALL.items()}
        od.append(o)
    xa = [dict(Wq=dt("xa_wq%d" % l, [D, D]), Wkv=dt("xa_wkv%d" % l, [D, 2 * D]), Wo=dt("xa_wo%d" % l, [D, D])) for l in range(depth)]
    ff = [dict(Wg=dt("ff_wg%d" % l, [D, D_FF]), Wu=dt("ff_wu%d" % l, [D, D_FF]), Wd=dt("ff_wd%d" % l, [D_FF, D]),
               cw=dt("ff_cw%d" % l, [P, 3, FC]), cb=dt("ff_cb%d" % l, [P, FC])) for l in range(depth)]
    out = dt("out", [D, NTOK], "ExternalOutput")
    if stop_after is None:
        h_scr = nc.dram_tensor("h_scr", [D, NTOK], F32, kind="Internal").ap()
    else:
        h_scr = dt("h_dbg", [D, NTOK], "ExternalOutput")
    xch = dict(rg=RG_PAIRS,
               snd_h=nc.dram_tensor("snd_h", [D, 128], F32, kind="Internal").ap(), t_snd_h=T("snd_h"),
               rcv_h=nc.dram_tensor("rcv_h", [2 * D, 128], F32, kind="Internal").ap(), t_rcv_h=T("rcv_h"),
               snd_s=nc.dram_tensor("snd_s", [2 * P, 32], F32, kind="Internal").ap(), t_snd_s=T("snd_s"),
               rcv_s=nc.dram_tensor("rcv_s", [4 * P, 32], F32, kind="Internal").ap(), t_rcv_s=T("rcv_s"))
    bias_cache = dict(ap=nc.dram_tensor("bias_scr", [P, 8192], F32, kind="Internal").ap(), t=T("bias_scr"), valid=False)
    with ExitStack() as st:
        pr = Prog(nc, st)
        S = pr.S
        gm, t_gm = pr.load_small("gmix_s", gmix, [P, 4, KC])
        gx, t_gx = pr.load_small("gxa_s", gxa, [P, 4, KC])
        gf, t_gf = pr.load_small("gff_s", gff, [P, 4, KC])
        gfn, t_gfn = pr.load_small("gfin_s", gfin, [P, KC])
        flag = pr.load_small("flag_s", flag_d, [P, 1])
        hm, t_hm = pr.load_small("hmask_s", hmask_d, [P, 1])
        gmv = gm[:].rearrange("p l k -> p (l k)")
        gxv = gx[:].rearrange("p l k -> p (l k)")
        gfv = gf[:].rearrange("p l k -> p (l k)")
        pr.prep_mem(memT, gmem)

        def exchange_h():
            S.dma("sp", [lambda e: e.dma_start(out=xch["snd_h"].rearrange("(k p) t -> p k t", p=P), in_=pr.hT[:, :, TT - 128:TT])],
                  xch["t_snd_h"], reads=pr.t_hT, writes=[xch["t_snd_h"]])
            S.cc("pool", lambda e: e.collective_compute("AllGather", ALU.bypass, replica_groups=xch["rg"], ins=[xch["snd_h"]], outs=[xch["rcv_h"]]),
                 reads=[xch["t_snd_h"]], writes=[xch["t_rcv_h"]])

        for l in range(depth):
            i = l // 2
            h_in = xT if l == 0 else h_scr
            halo_src = xhalo if l == 0 else xch["rcv_h"][0:D, :]
            hrd = [] if l == 0 else [xch["t_rcv_h"]]
            with pr.phase():
                if l % 2 == 0:
                    pr.even_mixer(h_in, h_scr, halo_src[:, 126:128], ev[i]["Win"], ev[i]["Wout"], ev[i]["prm"], ev[i]["dr"],
                                  gmv, t_gm, l * KC, flag, hrd, xch)
                else:
                    o = od[i]
                    bq, t_bq = pr.load_small("bq_s", o["prm"]["bq"], [P, KC])
                    S.op("dve", lambda e, bq=bq: e.tensor_scalar(out=bq[:], in0=bq[:], scalar1=0.125, scalar2=None, op0=ALU.mult), reads=[t_bq], writes=[t_bq])
                    bkd, t_bkd = pr.load_small("bkd_s", o["prm"]["bkd"], [P, 4])
                    bvd, t_bvd = pr.load_small("bvd_s", o["prm"]["bvd"], [P, 512])
                    pr.swa_setup(oh_d, mask_d, relb_d, o["prm"]["sinkrow"], cache=bias_cache)
                    pr.swa(h_in, h_scr, halo_src, hm, t_hm, o["Wq"], o["Wkd"], o["Wvd"], o["Wo"], bq, t_bq, bkd, t_bkd, bvd, t_bvd,
                           gmv, t_gm, l * KC, flag=flag, hrd=hrd)
            if stop_after == (l, "mix"):
                break
            with pr.phase():
                pr.xattn(h_scr, h_scr, xa[l]["Wq"], xa[l]["Wkv"], xa[l]["Wo"], gxv, t_gx, l * KC)
                exchange_h()
            if stop_after == (l, "xa"):
                break
            with pr.phase():
                cws, t_cw = pr.load_small("cw_s", ff[l]["cw"], [P, 3, FC])
                cbs, t_cb = pr.load_small("cb_s", ff[l]["cb"], [P, FC])
                pr.ffn(h_scr, h_scr, xch["rcv_h"][0:D, 126:128], ff[l]["Wg"], ff[l]["Wu"], ff[l]["Wd"], gfv, t_gf, l * KC, cws, t_cw, cbs, t_cb,
                       flag=flag, hrd=[xch["t_rcv_h"]])
                if l < depth - 1:
                    exchange_h()
        with pr.phase():
            pr.final_norm(h_scr, out, gfn, t_gfn, 0)
            S.wait_all("sp", pr.t_fo)
        pr.finish(getattr(pr, "dbg_tiles", []))
    return nc


def fused_inputs(z):
    x = np.asarray(z["x"], np.float32)
    B = x.shape[0]
    cores = [(b, hf) for b in range(B) for hf in range(2)]
    oh, maskT = swa_consts()
    common = dict(gmem=pk(z["norm_mem"]), gfin=pk(z["norm_final"]), gmix=pk(z["norm_mix"]), gxa=pk(z["norm_xattn"]), gff=pk(z["norm_ffn"]),
                  oh=oh, maskT=maskT)
    for i in range(2):
        hp = ev_host_params(z, i)
        common["ev_w_in%d" % i] = np.asarray(z["ev_w_in"][i], np.float32)
        common["ev_w_out%d" % i] = np.asarray(z["ev_w_out"][i], np.float32)
        for k in EV_SMALL:
            common["ev%d_%s" % (i, k)] = hp[k]
        sp = swa_host_params(z, i)
        common["relb"] = sp["relb"]
        common["od_wq%d" % i] = sp["Wq"]
        common["od_wkd%d" % i] = sp["Wkd"]
        common["od_wvd%d" % i] = sp["Wvd"]
        common["od_wo%d" % i] = np.asarray(z["od_w_out"][i], np.float32)
        for k in SWA_SMALL:
            common["od%d_%s" % (i, k)] = sp[k]
    for l in range(4):
        common["xa_wq%d" % l] = np.asarray(z["xa_w_q"][l], np.float32)
        common["xa_wkv%d" % l] = np.asarray(z["xa_w_kv"][l], np.float32)
        common["xa_wo%d" % l] = np.asarray(z["xa_w_o"][l], np.float32)
        common["ff_wg%d" % l] = np.asarray(z["ff_w_gate"][l], np.float32)
        common["ff_wu%d" % l] = np.asarray(z["ff_w_up"][l], np.float32)
        common["ff_wd%d" % l] = np.asarray(z["ff_w_down"][l], np.float32)
        common["ff_cw%d" % l] = pk(z["ff_conv_w"][l])
        common["ff_cb%d" % l] = pk(z["ff_conv_b"][l])
    maps = []
    for (b, hf) in cores:
        m = dict(common)
        m["xT"] = np.ascontiguousarray(x[b, hf * NTOK:(hf + 1) * NTOK, :].T)
        m["xhalo"] = np.zeros((D, 128), np.float32) if hf == 0 else np.ascontiguousarray(x[b, NTOK - 128:NTOK, :].T)
        m["flag"] = np.full((P, 1), float(hf), np.float32)
        m["hmask"] = np.full((P, 1), -30000.0 if hf == 0 else 0.0, np.float32)
        m["memT"] = np.ascontiguousarray(np.asarray(z["mem"], np.float32)[b].T)
        maps.append(m)
    return cores, maps


def kernel_fused(**inp):
    z = {k: np.asarray(v) for k, v in inp.items()}
    cores, maps = fused_inputs(z)
    nc = build_fused()
    decl = set(LAST_INPUT_NAMES)
    maps = [{k: v for k, v in m.items() if k in decl} for m in maps]
    res = run_bass_kernel_spmd(nc, maps, core_ids=list(range(len(cores))))
    B = z["x"].shape[0]
    out = np.empty((B, 2 * NTOK, D), np.float32)
    for ci, (b, hf) in enumerate(cores):
        out[b, hf * NTOK:(hf + 1) * NTOK, :] = res.results[ci]["out"].T
    return out


def kernel(**inp):
    return kernel_fused(**inp)
```
